# Optimizing a Trainium2 kernel written in Bass

```python
import math
import jax, jax.numpy as jnp
from jax import lax
import numpy as np

D_MODEL = 1024
BATCH = 2
SEQ = 8192
DEPTH = 2

N_MIXERS = 2
N_SB_LAYERS = (DEPTH + N_MIXERS - 1) // N_MIXERS
N_NSA_LAYERS = DEPTH // N_MIXERS
N_HEADS = 16
HEAD_DIM = 64
MIX_WIDTH = N_HEADS * HEAD_DIM
N_KV_HEADS = 4
GROUP = N_HEADS // N_KV_HEADS
SB_IN = 3 * MIX_WIDTH
NSA_IN = MIX_WIDTH + 6 * N_KV_HEADS * HEAD_DIM + 3 * N_HEADS
D_FF = ((8 * D_MODEL + 2) // 3 + 255) // 256 * 256
PLE_DIM = 256
Q_BLOCK = 128
REL_BUCKETS = 32
REL_MAX_DIST = 128
CMP_LEN = 32
CMP_STRIDE = 16
CMP_HIDDEN = 256
SEL_LEN = 64
SEL_TOPK = 16
WINDOW = 512
FORCED_SCORE = 100.0
EPS = 1e-6

kernel_name = "hybrid_stickbreak_nsa_trunk"


def rms_norm(x, g):
    xf = x.astype(jnp.float32)
    y = xf * lax.rsqrt(jnp.mean(xf * xf, axis=-1, keepdims=True) + EPS)
    return (y * g.astype(jnp.float32)).astype(x.dtype)


def rel_bucket(dist):
    n = jnp.maximum(dist, 0)
    exact = REL_BUCKETS // 2
    nf = jnp.maximum(n, 1).astype(jnp.float32)
    large = exact + (jnp.log(nf / exact) / math.log(REL_MAX_DIST / exact)
                     * (REL_BUCKETS - exact)).astype(jnp.int32)
    large = jnp.minimum(large, REL_BUCKETS - 1)
    return jnp.where(n < exact, n, large)


def swiglu(x, w_in, w_out):
    a, b = jnp.split(x @ w_in, 2, axis=-1)
    return (jax.nn.silu(a) * b) @ w_out


def stick_breaking_attention(h, w_in, w_out):
    B, S, _ = h.shape
    qkv = (h @ w_in).reshape(B, S, 3, N_HEADS, HEAD_DIM).transpose(2, 0, 3, 1, 4)
    q, k, v = qkv[0], qkv[1], qkv[2]
    scale = HEAD_DIM ** -0.5
    kpos = jnp.arange(S)

    def block(q0):
        qb = lax.dynamic_slice_in_dim(q, q0, Q_BLOCK, axis=2)
        z = jnp.einsum('bhqd,bhkd->bhqk', qb, k).astype(jnp.float32) * scale
        t = q0 + jnp.arange(Q_BLOCK)
        causal = kpos[None, :] < t[:, None]
        log_fail = jnp.where(causal, jax.nn.log_sigmoid(-z), 0.0)
        after = lax.cumsum(log_fail, axis=3, reverse=True) - log_fail
        a = jnp.where(causal, jnp.exp(jax.nn.log_sigmoid(z) + after), 0.0)
        return jnp.einsum('bhqk,bhkd->bhqd', a, v.astype(jnp.float32))

    out = lax.map(block, jnp.arange(0, S, Q_BLOCK))
    out = out.transpose(1, 0, 3, 2, 4).reshape(B, S, MIX_WIDTH).astype(h.dtype)
    return out @ w_out


def native_sparse_attention(h, w_in, w_out, rel_bias, pe_k, pe_v,
                            ck_w1, ck_w2, cv_w1, cv_w2):
    B, S, _ = h.shape
    scale = HEAD_DIM ** -0.5
    proj = h @ w_in
    q = proj[..., :MIX_WIDTH].reshape(B, S, N_KV_HEADS, GROUP, HEAD_DIM)
    q = q.transpose(0, 2, 3, 1, 4)
    kv_end = MIX_WIDTH + 6 * N_KV_HEADS * HEAD_DIM
    kv = proj[..., MIX_WIDTH:kv_end].reshape(B, S, 6, N_KV_HEADS, HEAD_DIM)
    kv = kv.transpose(2, 0, 3, 1, 4)
    k_cmp, v_cmp, k_sel, v_sel, k_win, v_win = (kv[i] for i in range(6))
    gates = jax.nn.sigmoid(proj[..., kv_end:].astype(jnp.float32))
    gates = gates.reshape(B, S, 3, N_KV_HEADS, GROUP).transpose(2, 0, 3, 4, 1)

    n_cmp = (S - CMP_LEN) // CMP_STRIDE + 1
    win_idx = jnp.arange(n_cmp)[:, None] * CMP_STRIDE + jnp.arange(CMP_LEN)[None, :]

    def compress(kx, pe, w1, w2):
        blocks = kx[:, :, win_idx] + pe
        flat = blocks.reshape(B, N_KV_HEADS, n_cmp, CMP_LEN * HEAD_DIM)
        return jax.nn.gelu(flat @ w1) @ w2

    kc = compress(k_cmp, pe_k, ck_w1, ck_w2)
    vc = compress(v_cmp, pe_v, cv_w1, cv_w2)
    cmp_start = jnp.arange(n_cmp) * CMP_STRIDE
    cmp_end = cmp_start + CMP_LEN - 1

    n_sel = S // SEL_LEN
    top_k = min(SEL_TOPK, n_sel)
    sel_ids = jnp.arange(n_sel)
    sel_start = sel_ids * SEL_LEN
    overlap = ((cmp_start[:, None] < sel_start[None, :] + SEL_LEN)
               & (cmp_start[:, None] + CMP_LEN > sel_start[None, :])).astype(jnp.float32)
    ks_b = k_sel.reshape(B, N_KV_HEADS, n_sel, SEL_LEN, HEAD_DIM)
    vs_b = v_sel.reshape(B, N_KV_HEADS, n_sel, SEL_LEN, HEAD_DIM)
    b_i = jnp.arange(B)[:, None, None, None]
    h_i = jnp.arange(N_KV_HEADS)[None, :, None, None]
    h_i5 = h_i[..., None]
    table_t = rel_bias.reshape(REL_BUCKETS, N_KV_HEADS, GROUP).transpose(1, 0, 2)

    pad = ((0, 0), (0, 0), (WINDOW, 0), (0, 0))
    kw_pad = jnp.pad(k_win, pad)
    vw_pad = jnp.pad(v_win, pad)
    slab = Q_BLOCK + WINDOW
    dist_w = jnp.arange(Q_BLOCK)[:, None] - jnp.arange(slab)[None, :] + WINDOW
    band = (dist_w >= 0) & (dist_w < WINDOW)
    bias_w = rel_bias[rel_bucket(dist_w)].reshape(Q_BLOCK, slab, N_KV_HEADS, GROUP)
    bias_w = bias_w.transpose(2, 3, 0, 1).astype(jnp.float32)

    def block(q0):
        t = q0 + jnp.arange(Q_BLOCK)
        qb = lax.dynamic_slice_in_dim(q, q0, Q_BLOCK, axis=3)

        dist_c = t[:, None] - cmp_end[None, :]
        bias_c = rel_bias[rel_bucket(dist_c)].reshape(Q_BLOCK, n_cmp, N_KV_HEADS, GROUP)
        bias_c = bias_c.transpose(2, 3, 0, 1).astype(jnp.float32)
        s_c = jnp.einsum('bhgqd,bhnd->bhgqn', qb, kc).astype(jnp.float32) * scale + bias_c
        s_c = jnp.where(dist_c >= 0, s_c, -jnp.inf)
        m_c = jnp.max(s_c, axis=-1, keepdims=True)
        e_c = jnp.exp(s_c - jnp.where(jnp.isfinite(m_c), m_c, 0.0))
        den_c = jnp.sum(e_c, axis=-1, keepdims=True)
        p_c = e_c / jnp.where(den_c > 0, den_c, 1.0)
        o_c = jnp.einsum('bhgqn,bhnd->bhgqd', p_c, vc.astype(jnp.float32))

        imp = jnp.einsum('bhgqn,ns->bhqs', p_c, overlap)
        cur = t // SEL_LEN
        forced = ((sel_ids[None, :] == 0) | (sel_ids[None, :] == cur[:, None])
                  | (sel_ids[None, :] == cur[:, None] - 1))
        causal_blk = sel_ids[None, :] <= cur[:, None]
        imp = jnp.where(forced, FORCED_SCORE, jnp.where(causal_blk, imp, -1.0))
        _, idx = lax.top_k(imp, top_k)
        kg = ks_b[b_i, h_i, idx]
        vg = vs_b[b_i, h_i, idx]
        pos = idx[..., None] * SEL_LEN + jnp.arange(SEL_LEN)
        dist_s = t[None, None, :, None, None] - pos
        bias_s = jnp.moveaxis(table_t[h_i5, rel_bucket(dist_s)], -1, 2).astype(jnp.float32)
        s_s = jnp.einsum('bhgqd,bhqkld->bhgqkl', qb, kg).astype(jnp.float32) * scale + bias_s
        s_s = jnp.where((dist_s >= 0)[:, :, None], s_s, -jnp.inf)
        s_s = s_s.reshape(B, N_KV_HEADS, GROUP, Q_BLOCK, top_k * SEL_LEN)
        p_s = jax.nn.softmax(s_s, axis=-1).reshape(B, N_KV_HEADS, GROUP, Q_BLOCK, top_k, SEL_LEN)
        o_s = jnp.einsum('bhgqkl,bhqkld->bhgqd', p_s, vg.astype(jnp.float32))

        kw = lax.dynamic_slice_in_dim(kw_pad, q0, slab, axis=2)
        vw = lax.dynamic_slice_in_dim(vw_pad, q0, slab, axis=2)
        ok_w = band & ((q0 - WINDOW + jnp.arange(slab)) >= 0)[None, :]
        s_w = jnp.einsum('bhgqd,bhkd->bhgqk', qb, kw).astype(jnp.float32) * scale + bias_w
        p_w = jax.nn.softmax(jnp.where(ok_w, s_w, -jnp.inf), axis=-1)
        o_w = jnp.einsum('bhgqk,bhkd->bhgqd', p_w, vw.astype(jnp.float32))

        g = lax.dynamic_slice_in_dim(gates, q0, Q_BLOCK, axis=4)[..., None]
        return g[0] * o_c + g[1] * o_s + g[2] * o_w

    out = lax.map(block, jnp.arange(0, S, Q_BLOCK))
    out = out.transpose(1, 0, 4, 2, 3, 5).reshape(B, S, MIX_WIDTH).astype(h.dtype)
    return out @ w_out


def setup_inputs(seed: int = 0) -> dict:
    key = jax.random.key(seed)
    ks = jax.random.split(key, 24)
    f32 = jnp.float32

    def nrm(k, shape, scale):
        return jax.random.normal(k, shape, f32) * scale

    def gain(k, shape):
        return 1.0 + 0.01 * jax.random.normal(k, shape, f32)

    return {
        "x": nrm(ks[0], (BATCH, SEQ, D_MODEL), 1.0),
        "p": nrm(ks[1], (DEPTH, BATCH, SEQ, PLE_DIM), 1.0),
        "rel_bias": nrm(ks[2], (REL_BUCKETS, N_HEADS), 0.5),
        "norm_mix": gain(ks[3], (DEPTH, D_MODEL)),
        "norm_ffn": gain(ks[4], (DEPTH, D_MODEL)),
        "norm_ple": gain(ks[5], (DEPTH, D_MODEL)),
        "final_norm": gain(ks[6], (D_MODEL,)),
        "sb_w_in": nrm(ks[7], (N_SB_LAYERS, D_MODEL, SB_IN), D_MODEL ** -0.5),
        "sb_w_out": nrm(ks[8], (N_SB_LAYERS, MIX_WIDTH, D_MODEL), MIX_WIDTH ** -0.5),
        "nsa_w_in": nrm(ks[9], (N_NSA_LAYERS, D_MODEL, NSA_IN), D_MODEL ** -0.5),
        "nsa_w_out": nrm(ks[10], (N_NSA_LAYERS, MIX_WIDTH, D_MODEL), MIX_WIDTH ** -0.5),
        "nsa_pe_k": nrm(ks[11], (N_NSA_LAYERS, CMP_LEN, HEAD_DIM), 0.02),
        "nsa_pe_v": nrm(ks[12], (N_NSA_LAYERS, CMP_LEN, HEAD_DIM), 0.02),
        "nsa_ck_w1": nrm(ks[13], (N_NSA_LAYERS, CMP_LEN * HEAD_DIM, CMP_HIDDEN), (CMP_LEN * HEAD_DIM) ** -0.5),
        "nsa_ck_w2": nrm(ks[14], (N_NSA_LAYERS, CMP_HIDDEN, HEAD_DIM), CMP_HIDDEN ** -0.5),
        "nsa_cv_w1": nrm(ks[15], (N_NSA_LAYERS, CMP_LEN * HEAD_DIM, CMP_HIDDEN), (CMP_LEN * HEAD_DIM) ** -0.5),
        "nsa_cv_w2": nrm(ks[16], (N_NSA_LAYERS, CMP_HIDDEN, HEAD_DIM), CMP_HIDDEN ** -0.5),
        "ffn_w_in": nrm(ks[17], (DEPTH, D_MODEL, 2 * D_FF), D_MODEL ** -0.5),
        "ffn_w_out": nrm(ks[18], (DEPTH, D_FF, D_MODEL), D_FF ** -0.5),
        "ple_w_proj": nrm(ks[19], (DEPTH, PLE_DIM, D_MODEL), PLE_DIM ** -0.5),
        "ple_w_gate": nrm(ks[20], (DEPTH, D_MODEL, D_MODEL), D_MODEL ** -0.5),
    }


def reference(x, p, rel_bias, norm_mix, norm_ffn, norm_ple, final_norm,
              sb_w_in, sb_w_out, nsa_w_in, nsa_w_out, nsa_pe_k, nsa_pe_v,
              nsa_ck_w1, nsa_ck_w2, nsa_cv_w1, nsa_cv_w2,
              ffn_w_in, ffn_w_out, ple_w_proj, ple_w_gate):
    h = x
    for i in range(DEPTH):
        j = i // N_MIXERS
        hn = rms_norm(h, norm_mix[i])
        if i % N_MIXERS == 0:
            mix = stick_breaking_attention(hn, sb_w_in[j], sb_w_out[j])
        else:
            mix = native_sparse_attention(hn, nsa_w_in[j], nsa_w_out[j], rel_bias,
                                          nsa_pe_k[j], nsa_pe_v[j], nsa_ck_w1[j],
                                          nsa_ck_w2[j], nsa_cv_w1[j], nsa_cv_w2[j])
        h = h + mix
        h = h + swiglu(rms_norm(h, norm_ffn[i]), ffn_w_in[i], ffn_w_out[i])
        gate = jax.nn.sigmoid(rms_norm(h, norm_ple[i]) @ ple_w_gate[i])
        h = h + gate * (p[i] @ ple_w_proj[i])
    return rms_norm(h, final_norm)
```

```python
import contextlib
import math
import numpy as np
import ml_dtypes
import concourse.bass as bass
import concourse.mybir as mybir
from concourse.alu_op_type import AluOpType as ALU
from concourse.bass_utils import run_bass_kernel_spmd

F32 = mybir.dt.float32
BF16 = mybir.dt.bfloat16
AF = mybir.ActivationFunctionType
NPBF = ml_dtypes.bfloat16

NCORES = 8
S = 8192
D = 1024
TOK = 2048
FC = 8
DFF = 2816
NJ = DFF // 128
EPS = 1e-6
NEG = -30000.0
NDC = 4352
DOFF = 2063


class Buf:
    __slots__ = ("name", "w", "r", "dsem", "dcnt", "dkey")

    def __init__(self, name):
        self.name = name
        self.w = None
        self.r = {}
        self.dsem = None
        self.dcnt = 0
        self.dkey = None


class Eng:
    def __init__(self, name, h, sem):
        self.name = name
        self.gen = 0
        self.key = ("e", name, 0)
        self.h = h
        self.sem = sem
        self.count = 0
        self.seen = {}


SEM_ROTATE = 6000


class KB:
    def __init__(self, nc):
        self.nc = nc
        self.stack = contextlib.ExitStack()
        self.engs = {}
        for key, h in (("pe", nc.tensor), ("act", nc.scalar), ("dve", nc.vector),
                       ("pool", nc.gpsimd), ("sp", nc.sync)):
            sem = self.stack.enter_context(nc.semaphore("s_" + key)) if key != "sp" else None
            self.engs[key] = Eng(key, h, sem)
        self.csem = self.stack.enter_context(nc.semaphore("s_coll"))
        self.hsem = self.stack.enter_context(nc.semaphore("s_hand"))
        self.hcnt = 0
        self.dma_bufs = []
        self.nbuf = 0
        self.free_dsems = {"hw": [], "sw": []}
        self.nsem = 0
        self.ccnt = 0

    def sb(self, name, shape, dtype, stack=None):
        self.nbuf += 1
        return (stack or self.stack).enter_context(self.nc.sbuf_tensor("S%d_%s" % (self.nbuf, name), list(shape), dtype))

    def ps(self, name, shape, dtype, stack=None):
        self.nbuf += 1
        return (stack or self.stack).enter_context(self.nc.psum_tensor("P%d_%s" % (self.nbuf, name), list(shape), dtype))

    def buf(self, name=None):
        self.nbuf += 1
        return Buf("%s_%d" % (name or "b", self.nbuf))

    def bufs(self, n, name=None):
        return [self.buf(name) for _ in range(n)]

    def _deps(self, E, r, w):
        deps = {}

        def need(k, s, v):
            if k[0] == "e" and k[1] == "pe" and E.name == "pe":
                return
            if E.seen.get(k, 0) >= v:
                return
            if k not in deps or deps[k][1] < v:
                deps[k] = (s, v)

        for b in r:
            if b.w is not None:
                need(*b.w)
        for b in w:
            if b.w is not None:
                need(*b.w)
            for kk, (s, v) in b.r.items():
                need(kk, s, v)
        for kk, (s, v) in deps.items():
            E.h.wait_ge(s, v)
            E.seen[kk] = v

    @staticmethod
    def _mark(ev, r, w):
        kk, s, v = ev
        for b in r:
            b.r[kk] = (s, v)
        for b in w:
            b.w = ev
            b.r = {}

    def op(self, eng, fn, r=(), w=()):
        E = self.engs[eng]
        self._deps(E, r, w)
        ins = fn(E.h)
        E.count += 1
        ins.then_inc(E.sem, 1)
        self._mark((E.key, E.sem, E.count), r, w)
        return ins

    def _dsem(self, sbuf, cls):
        if sbuf.dsem is None:
            sbuf.dsem = {}
        if cls not in sbuf.dsem:
            pool = self.free_dsems[cls]
            if pool:
                ent = pool.pop()
            else:
                self.nsem += 1
                sem = self.stack.enter_context(self.nc.semaphore("d%s%d" % (cls, self.nsem)))
                ent = [sem, ("d", self.nsem), 0]
            sbuf.dsem[cls] = ent
            self.dma_bufs.append((sbuf, cls))
        return sbuf.dsem[cls]

    def dma(self, q, out, in_, r=(), w=(), sem_buf=None, **kw):
        E = self.engs[q]
        self._deps(E, r, w)
        sbuf = sem_buf or (w[0] if w else r[0])
        ent = self._dsem(sbuf, "sw" if q == "pool" else "hw")
        ins = E.h.dma_start(out=out, in_=in_, **kw)
        ent[2] += 16
        ins.then_inc(ent[0], 16)
        self._mark((ent[1], ent[0], ent[2]), r, w)
        return ins

    def coll(self, kind, in_ap, out_ap, groups, r=(), w=()):
        E = self.engs["pool"]
        self._deps(E, r, w)
        ins = E.h.collective_compute(kind, ALU.bypass, replica_groups=groups, ins=[in_ap], outs=[out_ap])
        self.ccnt += 1
        ins.then_inc(self.csem, 1)
        self._mark((("c", 0), self.csem, self.ccnt), r, w)
        return ins

    def barrier(self, release=True):
        for E in self.engs.values():
            for Fg in self.engs.values():
                if Fg.count == 0:
                    continue
                if Fg is E and E.name == "pe":
                    continue
                if E.seen.get(Fg.key, 0) < Fg.count:
                    E.h.wait_ge(Fg.sem, Fg.count)
                    E.seen[Fg.key] = Fg.count
            for b, cls in self.dma_bufs:
                sem, dkey, cnt = b.dsem[cls]
                if cnt and E.seen.get(dkey, 0) < cnt:
                    E.h.wait_ge(sem, cnt)
                    E.seen[dkey] = cnt
            if self.ccnt and E.seen.get(("c", 0), 0) < self.ccnt:
                E.h.wait_ge(self.csem, self.ccnt)
                E.seen[("c", 0)] = self.ccnt
        if any(E.count > SEM_ROTATE for E in self.engs.values()):
            parts = list(self.engs.values())
            for rnd in range(2):
                self.hcnt += len(parts)
                for E in parts:
                    E.h.sem_inc(self.hsem, 1)
                for E in parts:
                    E.h.wait_ge(self.hsem, self.hcnt)
                if rnd == 0:
                    for E in parts:
                        if E.sem is not None and E.count > 0:
                            E.h.sem_clear(E.sem)
                            E.gen += 1
                            E.key = ("e", E.name, E.gen)
                            E.count = 0
        if release:
            for b, cls in self.dma_bufs:
                self.free_dsems[cls].append(b.dsem.pop(cls))
                b.w = None
                b.r = {}
            self.dma_bufs = []

    def close(self):
        self.stack.close()


def _rel_bucket_np(dist):
    n = np.maximum(dist, 0)
    nf = np.maximum(n, 1).astype(np.float32)
    large = 16 + (np.log(nf / np.float32(16)) / np.float32(math.log(128 / 16)) * np.float32(16)).astype(np.int32)
    large = np.minimum(large, 31)
    return np.where(n < 16, n, large)


def host_consts():
    c = {}
    i = np.arange(128)
    c["ident_f"] = np.eye(128, dtype=np.float32)
    c["ident_b"] = np.eye(128, dtype=np.float32).astype(NPBF)
    c["ones_b"] = np.ones((128, 128), np.float32).astype(NPBF)
    c["negu_b"] = (-(i[:, None] >= i[None, :]).astype(np.float32)).astype(NPBF)
    c["negones_b"] = (-np.ones((128, 2), np.float32)).astype(NPBF)
    c["mask_sb"] = (i[:, None] < i[None, :]).astype(np.float32).astype(NPBF)
    return c


class Ctx:
    pass


def load_consts(k, cdram, names, stack=None):
    cx = Ctx()
    for nm in names:
        ap = cdram[nm]
        t = k.sb("sc_" + nm, list(ap.shape), ap.dtype, stack)
        b = k.buf("c_" + nm)
        k.dma("sp", t[:], ap, w=[b])
        setattr(cx, nm, t)
        setattr(cx, nm + "_b", b)
    return cx


def rmsnorm_chunk(k, cx, W, h_aps, h_buf, gcol, gcol_b, out_aps, out_buf, n=512):
    sq, sqb, ps, psb, rs, rsb = W["sq"], W["sq_b"], W["ps_n"], W["ps_n_b"], W["rs"], W["rs_b"]
    for fc in range(FC):
        k.op("act", lambda e, fc=fc: e.activation(out=sq[:, fc, :n], in_=h_aps[fc], func=AF.Square),
             r=[h_buf], w=[sqb])
    for fc in range(FC):
        k.op("pe", lambda e, fc=fc: e.matmul(ps[:, :n], lhsT=cx.ones_b[:], rhs=sq[:, fc, :n],
                                              start=(fc == 0), stop=(fc == FC - 1)),
             r=[sqb, cx.ones_b_b], w=[psb])
    k.op("act", lambda e: e.activation(out=rs[:, :n], in_=ps[:, :n], func=AF.Sqrt, bias=cx.eps_col[:, 0:1], scale=1.0 / D),
         r=[psb, cx.eps_col_b], w=[rsb])
    k.op("dve", lambda e: e.reciprocal(out=rs[:, :n], in_=rs[:, :n]), r=[rsb], w=[rsb])
    for fc in range(FC):
        k.op("dve", lambda e, fc=fc: e.scalar_tensor_tensor(out=out_aps[fc], in0=h_aps[fc], scalar=gcol[:, fc:fc + 1],
                                                            in1=rs[:, :n], op0=ALU.mult, op1=ALU.mult),
             r=[h_buf, gcol_b, rsb], w=[out_buf])


def norm_work(k, stack=None):
    W = {}
    W["sq"] = k.sb("n_sq", [128, FC, 512], BF16, stack)
    W["sq_b"] = k.buf("n_sq")
    W["ps_n"] = k.ps("n_ps", [128, 512], F32, stack)
    W["ps_n_b"] = k.buf("n_ps")
    W["rs"] = k.sb("n_rs", [128, 512], F32, stack)
    W["rs_b"] = k.buf("n_rs")
    return W


def make_eps(k, cx, stack=None):
    cx.eps_col = k.sb("eps_col", [128, 1], F32, stack)
    cx.eps_col_b = k.buf("eps")
    k.op("pool", lambda e: e.memset(cx.eps_col[:], EPS), w=[cx.eps_col_b])


def load_gain_cols(k, gain_ap, name, stack=None):
    t = k.sb(name, [128, FC], F32, stack)
    b = k.buf(name)
    k.dma("sp", t[:], gain_ap.rearrange("(fc p) -> p fc", p=128), w=[b], allow_slow_non_contiguous=True)
    return t, b


def phase_norm_out(k, cx, hT, hb, gain_ap, dst, dst_b, stack):
    W = norm_work(k, stack)
    gcol, gcol_b = load_gain_cols(k, gain_ap, "g_mix", stack)
    hn = [k.sb("hn%d" % i, [128, FC, 512], BF16, stack) for i in range(2)]
    hnb = k.bufs(2, "hn")
    dv = dst.rearrange("(fc p) t -> p fc t", p=128)
    for c in range(4):
        cs = slice(c * 512, (c + 1) * 512)
        rmsnorm_chunk(k, cx, W, [hT[:, fc, cs] for fc in range(FC)], hb[c], gcol, gcol_b,
                      [hn[c % 2][:, fc, :] for fc in range(FC)], hnb[c % 2])
        k.dma("sp", dv[:, :, cs], hn[c % 2][:], r=[hnb[c % 2]], w=[dst_b])


def phase_sb(k, cx, hn_all, hn_all_b, w_ap, a2a, a2a_b, stack, nsb=16):
    scale = 0.125
    QT = [k.sb("QT%d" % i, [128, S], BF16, stack) for i in range(2)]
    KT = [k.sb("KT%d" % i, [128, S], BF16, stack) for i in range(2)]
    V = k.sb("V", [128, 64, 256], BF16, stack)
    qkb = k.buf("qkv")
    ps = [k.ps("ps%d" % i, [128, 512], F32, stack) for i in range(8)]
    psb = k.bufs(8, "ps")
    pst = contextlib.ExitStack()
    wq = k.sb("sb_w", [128, FC, 768], BF16, pst)
    wqb = k.buf("sb_w")
    wv = w_ap.rearrange("(fc p) n -> p fc n", p=128)
    for fc in range(FC):
        k.dma("pool", wq[:, fc, :], wv[:, fc, :], w=[wqb])
    hnc = [k.sb("hnc%d" % i, [128, FC, 512], BF16, pst) for i in range(2)]
    hncb = k.bufs(2, "hnc")
    for c in range(16):
        rho, off = c // 4, (c % 4) * 512
        src = hn_all.rearrange("(q r h p) t -> r p q h t", q=4, r=4, h=2, p=128)[rho][:, :, :, off:off + 512]
        hb_ = hncb[c % 2]
        for q_ in range(4):
            k.dma("sp", hnc[c % 2][:, 2 * q_:2 * q_ + 2, :], src[:, q_, :, :], r=[hn_all_b], w=[hb_])
        x = hnc[c % 2]
        cs = slice(c * 512, (c + 1) * 512)
        j = 0
        for which, dstT in ((0, QT), (1, KT)):
            for hp in range(2):
                p_, pb_ = ps[j % 4], psb[j % 4]
                j += 1
                for fc in range(FC):
                    k.op("pe", lambda e, fc=fc, p_=p_, which=which, hp=hp: e.matmul(
                        p_[:, :], lhsT=wq[:, fc, which * 256 + hp * 128: which * 256 + (hp + 1) * 128],
                        rhs=x[:, fc, :], start=(fc == 0), stop=(fc == FC - 1)), r=[wqb, hb_], w=[pb_])
                if which == 0:
                    k.op("act", lambda e, p_=p_, hp=hp: e.activation(out=QT[hp][:, cs], in_=p_[:, :], func=AF.Copy, scale=scale),
                         r=[pb_], w=[qkb])
                else:
                    k.op("dve", lambda e, p_=p_, hp=hp: e.tensor_copy(out=KT[hp][:, cs], in_=p_[:, :]), r=[pb_], w=[qkb])
        for tt in range(4):
            p_, pb_ = ps[4 + tt % 2], psb[4 + tt % 2]
            for fc in range(FC):
                k.op("pe", lambda e, fc=fc, p_=p_, tt=tt: e.matmul(
                    p_[:, 0:256], lhsT=x[:, fc, tt * 128:(tt + 1) * 128], rhs=wq[:, fc, 512:768],
                    start=(fc == 0), stop=(fc == FC - 1)), r=[wqb, hb_], w=[pb_])
            k.op("dve" if tt % 2 else "act",
                 (lambda e, p_=p_, tt=tt: e.tensor_copy(out=V[:, c * 4 + tt, :], in_=p_[:, 0:256])) if tt % 2 else
                 (lambda e, p_=p_, tt=tt: e.copy(out=V[:, c * 4 + tt, :], in_=p_[:, 0:256])),
                 r=[pb_], w=[qkb])
    k.barrier()
    pst.close()
    e1 = [k.sb("e1_%d" % i, [128, 512], F32, stack) for i in range(2)]
    e1b = k.bufs(2, "e1")
    sp = [k.sb("sp_%d" % i, [128, 512], BF16, stack) for i in range(2)]
    spb = k.bufs(2, "sp")
    aa = [k.sb("aa_%d" % i, [128, 512], BF16, stack) for i in range(2)]
    aab = k.bufs(2, "aa")
    acc = k.sb("acc", [128, 4, 256], F32, stack)
    accb = k.bufs(4, "acc")
    Cc = k.sb("Cc", [128, 4, 4], F32, stack)
    Ccb = k.bufs(4, "Cc")
    eC = [k.sb("eC%d" % i, [128, 4], F32, stack) for i in range(2)]
    eCb = k.bufs(2, "eC")
    oT = [k.sb("oT%d" % i, [128, 2, 512], BF16, stack) for i in range(2)]
    oTb = k.bufs(2, "oT")
    psA, psAb = ps[0:2], psb[0:2]
    psB, psBb = ps[2:4], psb[2:4]
    psO, psOb = ps[4:6], psb[4:6]
    psT, psTb = ps[6:8], psb[6:8]
    un = 0
    for sbk in range(nsb):
        for hh in range(4):
            hp, base = hh // 2, 64 * (hh % 2)
            k.op("pool", lambda e, hh=hh: e.memset(acc[:, :, hh * 64:(hh + 1) * 64], 0.0), w=[accb[hh]])
            k.op("pool", lambda e, hh=hh: e.memset(Cc[:, hh, :], 0.0), w=[Ccb[hh]])
            for u in range(4 * sbk + 4):
                kb = 4 * sbk + 3 - u
                i0 = max(0, kb - 4 * sbk)
                N = 512 - 128 * i0
                diag = kb >= 4 * sbk
                x2 = un % 2
                un += 1
                qc = QT[hp][base:base + 64, 512 * sbk + 128 * i0: 512 * (sbk + 1)]
                kc = KT[hp][base:base + 64, 128 * kb:128 * (kb + 1)]
                A, Ab = psA[x2], psAb[x2]
                Bp, Bb = psB[x2], psBb[x2]
                O, Ob = psO[x2], psOb[x2]
                k.op("pe", lambda e: e.matmul(A[:, :N], lhsT=kc, rhs=qc, start=True, stop=True), r=[qkb], w=[Ab])
                k.op("act", lambda e: e.activation(out=e1[x2][:, :N], in_=A[:, :N], func=AF.Exp), r=[Ab], w=[e1b[x2]])
                k.op("act", lambda e: e.activation(out=sp[x2][:, :N], in_=e1[x2][:, :N], func=AF.Ln, bias=cx.one_col[:, 0:1], scale=1.0),
                     r=[e1b[x2], cx.one_col_b], w=[spb[x2]])
                if diag:
                    k.op("pool", lambda e: e.tensor_tensor(out=sp[x2][:, 0:128], in0=sp[x2][:, 0:128], in1=cx.mask_sb[:], op=ALU.mult),
                         r=[spb[x2], cx.mask_sb_b], w=[spb[x2]])
                k.op("pe", lambda e: e.matmul(Bp[:, :N], lhsT=cx.negu_b[:], rhs=sp[x2][:, :N], start=True, stop=False),
                     r=[spb[x2], cx.negu_b_b], w=[Bb])
                k.op("pe", lambda e: e.matmul(Bp[:, :N], lhsT=kc, rhs=qc, start=False, stop=True), r=[qkb], w=[Bb])
                k.op("act", lambda e: e.activation(out=aa[x2][:, :N], in_=Bp[:, :N], func=AF.Exp), r=[Bb], w=[aab[x2]])
                if diag:
                    k.op("pool", lambda e: e.tensor_tensor(out=aa[x2][:, 0:128], in0=aa[x2][:, 0:128], in1=cx.mask_sb[:], op=ALU.mult),
                         r=[aab[x2], cx.mask_sb_b], w=[aab[x2]])
                for i in range(i0, 4):
                    cl = slice((i - i0) * 128, (i - i0 + 1) * 128)
                    k.op("pe", lambda e, i=i, cl=cl: e.matmul(O[:, i * 65:i * 65 + 64], lhsT=aa[x2][:, cl],
                                                              rhs=V[:, kb, hh * 64:(hh + 1) * 64], start=True, stop=True),
                         r=[aab[x2], qkb], w=[Ob])
                    k.op("pe", lambda e, i=i, cl=cl: e.matmul(O[:, i * 65 + 64:i * 65 + 65], lhsT=sp[x2][:, cl],
                                                              rhs=cx.negones_b[:, 0:1], start=True, stop=True),
                         r=[spb[x2], cx.negones_b_b], w=[Ob])
                k.op("act", lambda e: e.activation(out=eC[x2][:, :], in_=Cc[:, hh, :], func=AF.Exp), r=[Ccb[hh]], w=[eCb[x2]])
                for i in range(i0, 4):
                    k.op("dve", lambda e, i=i: e.scalar_tensor_tensor(
                        out=acc[:, i, hh * 64:(hh + 1) * 64], in0=O[:, i * 65:i * 65 + 64], scalar=eC[x2][:, i:i + 1],
                        in1=acc[:, i, hh * 64:(hh + 1) * 64], op0=ALU.mult, op1=ALU.add),
                        r=[Ob, eCb[x2], accb[hh]], w=[accb[hh]])
                Ov = O[:, 0:260].rearrange("p (i c) -> p i c", c=65)
                k.op("dve", lambda e: e.tensor_tensor(out=Cc[:, hh, i0:4], in0=Cc[:, hh, i0:4], in1=Ov[:, i0:4, 64], op=ALU.add),
                     r=[Ob, Ccb[hh]], w=[Ccb[hh]])
        y2 = sbk % 2
        for i in range(4):
            for fh in range(2):
                T_, Tb_ = psT[(i * 2 + fh) % 2], psTb[(i * 2 + fh) % 2]
                k.op("pe", lambda e, i=i, fh=fh, T_=T_: e.transpose(out=T_[:, 0:128], in_=acc[:, i, fh * 128:(fh + 1) * 128], identity=cx.ident_f[:]),
                     r=accb + [cx.ident_f_b], w=[Tb_])
                k.op("dve", lambda e, i=i, fh=fh, T_=T_: e.tensor_copy(out=oT[y2][:, fh, i * 128:(i + 1) * 128], in_=T_[:, 0:128]),
                     r=[Tb_], w=[oTb[y2]])
        dest, off = sbk // 4, (sbk % 4) * 512
        k.dma("sp", a2a[dest].rearrange("(fh p) t -> p fh t", p=128)[:, :, off:off + 512], oT[y2][:], r=[oTb[y2]], w=[a2a_b])
        if k.engs["pe"].count > SEM_ROTATE:
            k.barrier()


def make_one(k, cx, stack=None):
    cx.one_col = k.sb("one_col", [128, 1], F32, stack)
    cx.one_col_b = k.buf("one")
    k.op("pool", lambda e: e.memset(cx.one_col[:], 1.0), w=[cx.one_col_b])


def dram_consts(nc, consts, names):
    out = {}
    for nm in names:
        a = consts[nm]
        dt_ = BF16 if a.dtype == NPBF else F32
        out[nm] = nc.dram_tensor("c_" + nm, list(a.shape), dt_, kind="ExternalInput").ap()
    return out


def build_prog_norm0():
    nc = bass.Bass("TRN2", target_bir_lowering=False)
    consts = host_consts()
    names = ["ones_b"]
    cd = dram_consts(nc, consts, names)
    xT = nc.dram_tensor("xT", [D, TOK], F32, kind="ExternalInput").ap()
    gain = nc.dram_tensor("gain", [D], F32, kind="ExternalInput").ap()
    hn = nc.dram_tensor("hnT", [D, TOK], BF16, kind="ExternalOutput").ap()
    k = KB(nc)
    cx = load_consts(k, cd, names)
    make_eps(k, cx)
    hT = k.sb("hT", [128, FC, TOK], F32)
    hb = k.bufs(4, "hT")
    xv = xT.rearrange("(fc p) t -> p fc t", p=128)
    for c in range(4):
        k.dma("sp", hT[:, :, c * 512:(c + 1) * 512], xv[:, :, c * 512:(c + 1) * 512], w=[hb[c]])
    hn_b = k.buf("hn_dram")
    phase_norm_out(k, cx, hT, hb, gain, hn, hn_b, k.stack)
    k.barrier()
    k.close()
    return nc, {nm: consts[nm] for nm in names}


def build_prog_sb(nsb=16):
    nc = bass.Bass("TRN2", target_bir_lowering=False)
    consts = host_consts()
    names = ["ident_f", "negu_b", "negones_b", "mask_sb"]
    cd = dram_consts(nc, consts, names)
    hn_all = nc.dram_tensor("hn_all", [4 * D, TOK], BF16, kind="ExternalInput").ap()
    w = nc.dram_tensor("sb_w", [D, 768], F32, kind="ExternalInput").ap()
    a2a = nc.dram_tensor("a2a", [4, 256, TOK], BF16, kind="ExternalOutput").ap()
    k = KB(nc)
    cx = load_consts(k, cd, names)
    make_one(k, cx)
    phase_sb(k, cx, hn_all, k.buf("hn_all"), w, a2a, k.buf("a2a"), k.stack, nsb=nsb)
    k.barrier()
    k.close()
    return nc, {nm: consts[nm] for nm in names}


def load_w_cast(k, name, src_view, shape, stack, nsplit=None):
    t = k.sb(name, shape, BF16, stack)
    b = k.buf(name)
    a = shape[1]
    for i in range(a):
        k.dma("pool", t[:, i, :], src_view[:, i, :], w=[b])
    return t, b


def phase_tail(k, cx, hT, hb, oT_dram, oT_b, w_out_ap, g_ffn_ap, w1_ap, w2_ap, g_ple_ap, wg_ap, wp_ap, pT_ap, oT_loader=None):
    ps_names = ["tp%d" % i for i in range(6)]
    with contextlib.ExitStack() as st0:
        ps = [k.ps(n_, [128, 512], F32, st0) for n_ in ps_names]
        psb = k.bufs(6, "tp")
        W = norm_work(k, st0)
        with contextlib.ExitStack() as st:
            wo, wob = load_w_cast(k, "wo", w_out_ap.rearrange("(fc p) n -> p fc n", p=128), [128, FC, D], st)
            oc = [k.sb("oc%d" % i, [128, FC, 512], BF16, st) for i in range(2)]
            ocb = k.bufs(2, "oc")
            ov = oT_dram.rearrange("(fc p) t -> p fc t", p=128) if oT_loader is None else None
            n_ = 0
            for c in range(4):
                cs = slice(c * 512, (c + 1) * 512)
                if oT_loader is None:
                    k.dma("sp", oc[c % 2][:], ov[:, :, cs], r=[oT_b], w=[ocb[c % 2]])
                else:
                    oT_loader(oc[c % 2], ocb[c % 2], cs)
                for of in range(FC):
                    p_, pb_ = ps[n_ % 4], psb[n_ % 4]
                    n_ += 1
                    for fc in range(FC):
                        k.op("pe", lambda e, fc=fc, of=of, p_=p_, c=c: e.matmul(
                            p_[:, :], lhsT=wo[:, fc, of * 128:(of + 1) * 128], rhs=oc[c % 2][:, fc, :],
                            start=(fc == 0), stop=(fc == FC - 1)), r=[wob, ocb[c % 2]], w=[pb_])
                    k.op("dve", lambda e, of=of, p_=p_, cs=cs: e.tensor_tensor(out=hT[:, of, cs], in0=hT[:, of, cs], in1=p_[:, :], op=ALU.add),
                         r=[pb_, hb[c]], w=[hb[c]])
            k.barrier()
        with contextlib.ExitStack() as st:
            gcol, gcol_b = load_gain_cols(k, g_ffn_ap, "g_ffn", st)
            hn = k.sb("f_hn", [128, FC, 1024], BF16, st)
            hnb = k.bufs(2, "f_hn")
            gT = k.sb("f_gT", [128, NJ, 1024], BF16, st)
            gTb = k.buf("f_gT")
            wab = [k.sb("f_wab%d" % i, [128, FC, 256], BF16, st) for i in range(2)]
            wabb = k.bufs(2, "f_wab")
            w2t = [k.sb("f_w2%d" % i, [128, NJ, 128], BF16, st) for i in range(2)]
            w2b = k.bufs(2, "f_w2")
            sl = [k.sb("f_sl%d" % i, [128, 512], F32, st) for i in range(2)]
            slb = k.bufs(2, "f_sl")
            w1v = w1_ap.rearrange("(fc p) n -> p fc n", p=128)
            w2v = w2_ap.rearrange("(j p) n -> p j n", p=128)
            n_ = 0
            nw = 0
            nw2 = 0
            for tc in range(2):
                for hf in range(2):
                    c = tc * 2 + hf
                    cs = slice(c * 512, (c + 1) * 512)
                    rmsnorm_chunk(k, cx, W, [hT[:, fc, cs] for fc in range(FC)], hb[c], gcol, gcol_b,
                                  [hn[:, fc, hf * 512:(hf + 1) * 512] for fc in range(FC)], hnb[hf])
                for j in range(NJ):
                    wt, wtb = wab[nw % 2], wabb[nw % 2]
                    nw += 1
                    for fc in range(FC):
                        k.dma("pool", wt[:, fc, 0:128], w1v[:, fc, j * 128:(j + 1) * 128], w=[wtb])
                        k.dma("pool", wt[:, fc, 128:256], w1v[:, fc, DFF + j * 128:DFF + (j + 1) * 128], w=[wtb])
                    for hf in range(2):
                        hs = slice(hf * 512, (hf + 1) * 512)
                        pa, pab = ps[n_ % 2], psb[n_ % 2]
                        pb2, pbb = ps[2 + n_ % 2], psb[2 + n_ % 2]
                        s_, sb_ = sl[n_ % 2], slb[n_ % 2]
                        n_ += 1
                        for fc in range(FC):
                            k.op("pe", lambda e, fc=fc, pa=pa, wt=wt, hs=hs: e.matmul(pa[:, :], lhsT=wt[:, fc, 0:128], rhs=hn[:, fc, hs],
                                                                                      start=(fc == 0), stop=(fc == FC - 1)), r=[wtb, hnb[hf]], w=[pab])
                        for fc in range(FC):
                            k.op("pe", lambda e, fc=fc, pb2=pb2, wt=wt, hs=hs: e.matmul(pb2[:, :], lhsT=wt[:, fc, 128:256], rhs=hn[:, fc, hs],
                                                                                        start=(fc == 0), stop=(fc == FC - 1)), r=[wtb, hnb[hf]], w=[pbb])
                        k.op("act", lambda e, pa=pa, s_=s_: e.activation(out=s_[:, :], in_=pa[:, :], func=AF.Silu), r=[pab], w=[sb_])
                        k.op("dve", lambda e, pb2=pb2, s_=s_, j=j, hs=hs: e.tensor_tensor(out=gT[:, j, hs], in0=s_[:, :], in1=pb2[:, :], op=ALU.mult),
                             r=[sb_, pbb], w=[gTb])
                for of in range(FC):
                    wt, wtb = w2t[nw2 % 2], w2b[nw2 % 2]
                    nw2 += 1
                    for j in range(NJ):
                        k.dma("pool", wt[:, j, :], w2v[:, j, of * 128:(of + 1) * 128], w=[wtb])
                    for hf in range(2):
                        c = tc * 2 + hf
                        cs = slice(c * 512, (c + 1) * 512)
                        p_, pb_ = ps[4 + n_ % 2], psb[4 + n_ % 2]
                        n_ += 1
                        for j in range(NJ):
                            k.op("pe", lambda e, j=j, p_=p_, wt=wt, hf=hf: e.matmul(p_[:, :], lhsT=wt[:, j, :], rhs=gT[:, j, hf * 512:(hf + 1) * 512],
                                                                                    start=(j == 0), stop=(j == NJ - 1)), r=[wtb, gTb], w=[pb_])
                        k.op("dve", lambda e, of=of, p_=p_, cs=cs: e.tensor_tensor(out=hT[:, of, cs], in0=hT[:, of, cs], in1=p_[:, :], op=ALU.add),
                             r=[pb_, hb[c]], w=[hb[c]])
            k.barrier()
        with contextlib.ExitStack() as st:
            gcol, gcol_b = load_gain_cols(k, g_ple_ap, "g_ple", st)
            wg, wgb = load_w_cast(k, "wg", wg_ap.rearrange("(fc p) n -> p fc n", p=128), [128, FC, D], st)
            wp, wpb = load_w_cast(k, "wp", wp_ap.rearrange("(kc p) n -> p kc n", p=128), [128, 2, D], st)
            hn = [k.sb("p_hn%d" % i, [128, FC, 512], BF16, st) for i in range(2)]
            hnb = k.bufs(2, "p_hn")
            pt = [k.sb("p_pt%d" % i, [128, 2, 512], BF16, st) for i in range(2)]
            ptb = k.bufs(2, "p_pt")
            sg = [k.sb("p_sg%d" % i, [128, 512], F32, st) for i in range(2)]
            sgb = k.bufs(2, "p_sg")
            pv = pT_ap.rearrange("(kc p) t -> p kc t", p=128)
            n_ = 0
            for c in range(4):
                cs = slice(c * 512, (c + 1) * 512)
                x_, xb_ = hn[c % 2], hnb[c % 2]
                rmsnorm_chunk(k, cx, W, [hT[:, fc, cs] for fc in range(FC)], hb[c], gcol, gcol_b,
                              [x_[:, fc, :] for fc in range(FC)], xb_)
                q_, qb_ = pt[c % 2], ptb[c % 2]
                for kc in range(2):
                    k.dma("pool", q_[:, kc, :], pv[:, kc, cs], w=[qb_])
                for of in range(FC):
                    pg, pgb = ps[n_ % 2], psb[n_ % 2]
                    pp, ppb = ps[2 + n_ % 2], psb[2 + n_ % 2]
                    s_, sb_ = sg[n_ % 2], sgb[n_ % 2]
                    n_ += 1
                    for fc in range(FC):
                        k.op("pe", lambda e, fc=fc, of=of, pg=pg, x_=x_: e.matmul(pg[:, :], lhsT=wg[:, fc, of * 128:(of + 1) * 128], rhs=x_[:, fc, :],
                                                                                  start=(fc == 0), stop=(fc == FC - 1)), r=[wgb, xb_], w=[pgb])
                    for kc in range(2):
                        k.op("pe", lambda e, kc=kc, of=of, pp=pp, q_=q_: e.matmul(pp[:, :], lhsT=wp[:, kc, of * 128:(of + 1) * 128], rhs=q_[:, kc, :],
                                                                                  start=(kc == 0), stop=(kc == 1)), r=[wpb, qb_], w=[ppb])
                    k.op("act", lambda e, pg=pg, s_=s_: e.activation(out=s_[:, :], in_=pg[:, :], func=AF.Sigmoid), r=[pgb], w=[sb_])
                    k.op("dve", lambda e, pp=pp, s_=s_: e.tensor_tensor(out=s_[:, :], in0=s_[:, :], in1=pp[:, :], op=ALU.mult),
                         r=[sb_, ppb], w=[sb_])
                    k.op("pool", lambda e, of=of, s_=s_, cs=cs: e.tensor_tensor(out=hT[:, of, cs], in0=hT[:, of, cs], in1=s_[:, :], op=ALU.add),
                         r=[sb_, hb[c]], w=[hb[c]])
            k.barrier()


def phase_final_norm(k, cx, hT, hb, gain_ap, dst, dst_b, stack):
    W = norm_work(k, stack)
    gcol, gcol_b = load_gain_cols(k, gain_ap, "g_fin", stack)
    on = [k.sb("fn%d" % i, [128, FC, 512], F32, stack) for i in range(2)]
    onb = k.bufs(2, "fn")
    dv = dst.rearrange("(fc p) t -> p fc t", p=128)
    for c in range(4):
        cs = slice(c * 512, (c + 1) * 512)
        rmsnorm_chunk(k, cx, W, [hT[:, fc, cs] for fc in range(FC)], hb[c], gcol, gcol_b,
                      [on[c % 2][:, fc, :] for fc in range(FC)], onb[c % 2])
        k.dma("sp", dv[:, :, cs], on[c % 2][:], r=[onb[c % 2]], w=[dst_b])


def build_prog_tail(final):
    nc = bass.Bass("TRN2", target_bir_lowering=False)
    consts = host_consts()
    names = ["ones_b"]
    cd = dram_consts(nc, consts, names)

    def din(nm, shape, dt_=F32):
        return nc.dram_tensor(nm, shape, dt_, kind="ExternalInput").ap()
    hin = din("hT_in", [D, TOK])
    oT = din("oT", [D, TOK], BF16)
    w_out = din("w_out", [D, D])
    g_ffn = din("g_ffn", [D])
    w1 = din("w1", [D, 2 * DFF])
    w2 = din("w2", [DFF, D])
    g_ple = din("g_ple", [D])
    wg = din("wg", [D, D])
    wp = din("wp", [256, D])
    pT = din("pT", [256, TOK])
    g_next = din("g_next", [D])
    k = KB(nc)
    cx = load_consts(k, cd, names)
    make_eps(k, cx)
    hT = k.sb("hT", [128, FC, TOK], F32)
    hb = k.bufs(4, "hT")
    xv = hin.rearrange("(fc p) t -> p fc t", p=128)
    for c in range(4):
        k.dma("sp", hT[:, :, c * 512:(c + 1) * 512], xv[:, :, c * 512:(c + 1) * 512], w=[hb[c]])
    phase_tail(k, cx, hT, hb, oT, k.buf("oT"), w_out, g_ffn, w1, w2, g_ple, wg, wp, pT)
    with contextlib.ExitStack() as st:
        if final:
            out = nc.dram_tensor("outT", [D, TOK], F32, kind="ExternalOutput").ap()
            phase_final_norm(k, cx, hT, hb, g_next, out, k.buf("outT"), st)
        else:
            hout = nc.dram_tensor("hT_out", [D, TOK], F32, kind="ExternalOutput").ap()
            hn = nc.dram_tensor("hnT", [D, TOK], BF16, kind="ExternalOutput").ap()
            hob = k.buf("hT_out")
            ov = hout.rearrange("(fc p) t -> p fc t", p=128)
            for c in range(4):
                k.dma("sp", ov[:, :, c * 512:(c + 1) * 512], hT[:, :, c * 512:(c + 1) * 512], r=[hb[c]], w=[hob])
            phase_norm_out(k, cx, hT, hb, g_next, hn, k.buf("hn_dram"), st)
        k.barrier()
    k.close()
    return nc, {nm: consts[nm] for nm in names}


def nsa_host_consts():
    c = {}
    i = np.arange(128)
    c["ident_f"] = np.eye(128, dtype=np.float32)
    c["jflip"] = np.ascontiguousarray(np.eye(128, dtype=np.float32)[::-1])
    dist = np.arange(NDC) - DOFF
    oh = np.zeros((33, NDC), np.float32)
    bk = _rel_bucket_np(dist)
    for ii in range(NDC):
        if dist[ii] < 0:
            oh[32, ii] = 1.0
        else:
            oh[bk[ii], ii] = 1.0
    c["ohc"] = oh
    dm = np.zeros((33, 33), np.float32)
    for b in range(32):
        dm[b, b] += 1.0
        dm[31, b] -= 1.0
    dm[32, 32] = NEG
    c["dm"] = dm
    keys = np.arange(S)
    c["ex"] = (np.arange(128)[:, None] == (keys[None, :] // 64)).astype(np.float32).astype(NPBF)
    t4 = np.where(i[None, :] < i[:, None], 0.0, NEG).astype(np.float32)
    c["t4"] = np.ascontiguousarray(np.tile(t4, (1, 4)))
    n = np.arange(512)
    cs_, ss_ = n * 16, np.arange(128) * 64
    ov = ((cs_[:, None] < ss_[None, :] + 64) & (cs_[:, None] + 32 > ss_[None, :])).astype(np.float32)
    ov[511, :] = 0.0
    c["ov"] = ov.astype(NPBF)
    q = np.arange(128)
    cq = (q >= 64).astype(np.int64)[:, None]
    rel = (np.arange(256) - 128)[None, :]
    forced = (rel == cq) | (rel == cq - 1)
    causal = rel <= cq
    c["mc_rel"] = (causal & ~forced).astype(np.float32)
    c["ma_rel"] = np.where(forced, 100.0, np.where(causal, 0.0, -1.0)).astype(np.float32)
    return c


GELU_C = 1.5957691216057308


def phase_nsa(k, cx, hn_all, hn_all_b, w_ap, rb_ap, peT_ap, w1_ap, w2k_ap, w2v_ap, bdz, a2a, a2a_b, stack, nqt=64, stop=99):
    scale = 0.125
    ps = [k.ps("np%d" % i, [128, 512], F32, stack) for i in range(8)]
    psb = k.bufs(8, "np")
    QT = [k.sb("nQT%d" % i, [128, S], BF16, stack) for i in range(2)]
    KST = k.sb("nKST", [128, S], BF16, stack)
    KWT = k.sb("nKWT", [128, S], BF16, stack)
    VS = k.sb("nVS", [128, 64, 65], BF16, stack)
    VW = k.sb("nVW", [128, 64, 65], BF16, stack)
    G = k.sb("nG", [128, 64, 12], F32, stack)
    kcT = k.sb("nkcT", [128, 512], BF16, stack)
    VCX = k.sb("nVCX", [128, 4, 193], BF16, stack)
    T0 = k.sb("nT0", [128, 512], F32, stack)
    T1 = k.sb("nT1", [128, 512], F32, stack)
    pb = k.buf("nsa_persist")
    k.op("pool", lambda e: e.memset(VS[:, :, 64:65], 1.0), w=[pb])
    k.op("pool", lambda e: e.memset(VW[:, :, 64:65], 1.0), w=[pb])
    k.op("pool", lambda e: e.memset(VCX[:], 0.0), w=[pb])
    k.op("pool", lambda e: e.memset(VCX[:, :, 64:65], 1.0), w=[pb])
    k.op("pool", lambda e: e.memset(kcT[:], 0.0), w=[pb])
    k.dma("sp", VCX[:, :, 65:193], cx.ov_dram.rearrange("(nb p) s -> p nb s", p=128), w=[pb])
    with contextlib.ExitStack() as st:
        rbe = k.sb("rbe", [33, 4], F32, st)
        rbeb = k.buf("rbe")
        k.op("pool", lambda e: e.memset(rbe[32:33, :], 1.0), w=[rbeb])
        k.dma("sp", rbe[0:32, :], rb_ap, w=[rbeb])
        ohc = k.sb("ohc", [33, NDC], F32, st)
        ohcb = k.buf("ohc")
        k.dma("sp", ohc[:], cx.ohc_dram, w=[ohcb])
        dm = k.sb("dm", [33, 33], F32, st)
        dmb = k.buf("dm")
        k.dma("sp", dm[:], cx.dm_dram, w=[dmb])
        rbx = k.sb("rbx", [33, 4], F32, st)
        rbxb = k.buf("rbx")
        k.op("pe", lambda e: e.matmul(ps[0][0:33, 0:4], lhsT=dm[:], rhs=rbe[:], start=True, stop=True), r=[dmb, rbeb], w=[psb[0]])
        k.op("dve", lambda e: e.tensor_copy(out=rbx[:], in_=ps[0][0:33, 0:4]), r=[psb[0]], w=[rbxb])
        bds = k.sb("bds", [4, NDC], F32, st)
        bdsb = k.buf("bds")
        nchunk = (NDC + 511) // 512
        for ci in range(nchunk):
            lo, hi = ci * 512, min(NDC, (ci + 1) * 512)
            p_, pb_ = ps[1 + ci % 2], psb[1 + ci % 2]
            k.op("pe", lambda e, p_=p_, lo=lo, hi=hi: e.matmul(p_[0:4, 0:hi - lo], lhsT=rbx[:], rhs=ohc[:, lo:hi], start=True, stop=True),
                 r=[rbxb, ohcb], w=[pb_])
            k.op("dve", lambda e, p_=p_, lo=lo, hi=hi: e.tensor_copy(out=bds[:, lo:hi], in_=p_[0:4, 0:hi - lo]), r=[pb_], w=[bdsb])
        bdzb = k.buf("bdz")
        k.dma("sp", bdz, bds[:], r=[bdsb], w=[bdzb])
        cx.bdz_b = bdzb
        U = k.sb("U", [128, 512], F32, st)
        Ub = k.buf("U")
        for m, Tm in ((0, T0), (1, T1)):
            src = bass.AP(bdz.tensor, DOFF - 127 + 128 * m, [[1, 128], [NDC, 4], [1, 128]])
            k.dma("sp", U[:].rearrange("p (g q) -> p g q", g=4), src, r=[bdzb], w=[Ub])
            k.op("pe", lambda e: e.matmul(ps[3][:, :], lhsT=cx.jflip[:], rhs=U[:], start=True, stop=True), r=[Ub, cx.jflip_b], w=[psb[3]])
            k.op("dve", lambda e, Tm=Tm: e.tensor_copy(out=Tm[:], in_=ps[3][:, :]), r=[psb[3]], w=[pb])
        k.barrier()
    if stop <= 0:
        return
    with contextlib.ExitStack() as st:
        w = k.sb("nw", [128, FC, 780], BF16, st)
        wb = k.buf("nw")
        wv = w_ap.rearrange("(fc p) n -> p fc n", p=128)
        for fc in range(FC):
            k.dma("pool", w[:, fc, :], wv[:, fc, :], w=[wb])
        KAT = k.sb("nKAT", [128, S], BF16, st)
        hnc = [k.sb("nhnc%d" % i, [128, FC, 512], BF16, st) for i in range(2)]
        hncb = k.bufs(2, "nhnc")
        for c in range(16):
            rho, off = c // 4, (c % 4) * 512
            src = hn_all.rearrange("(q r h p) t -> r p q h t", q=4, r=4, h=2, p=128)[rho][:, :, :, off:off + 512]
            hb_ = hncb[c % 2]
            for q_ in range(4):
                k.dma("sp", hnc[c % 2][:, 2 * q_:2 * q_ + 2, :], src[:, q_, :, :], r=[hn_all_b], w=[hb_])
            x = hnc[c % 2]
            cs = slice(c * 512, (c + 1) * 512)
            for j, (c0, dst, sc) in enumerate(((0, QT[0], scale), (128, QT[1], scale), (256, KAT, None), (384, KST, None), (512, KWT, None))):
                p_, pb_ = ps[j % 4], psb[j % 4]
                for fc in range(FC):
                    k.op("pe", lambda e, fc=fc, p_=p_, c0=c0: e.matmul(p_[:, :], lhsT=w[:, fc, c0:c0 + 128], rhs=x[:, fc, :],
                                                                         start=(fc == 0), stop=(fc == FC - 1)), r=[wb, hb_], w=[pb_])
                if sc is not None:
                    k.op("act", lambda e, p_=p_, dst=dst, sc=sc: e.activation(out=dst[:, cs], in_=p_[:, :], func=AF.Copy, scale=sc), r=[pb_], w=[pb])
                else:
                    k.op("dve", lambda e, p_=p_, dst=dst: e.tensor_copy(out=dst[:, cs], in_=p_[:, :]), r=[pb_], w=[pb])
            for tt in range(4):
                p_, pb_ = ps[4 + tt % 2], psb[4 + tt % 2]
                kb = c * 4 + tt
                for fc in range(FC):
                    k.op("pe", lambda e, fc=fc, p_=p_, tt=tt: e.matmul(p_[:, 0:140], lhsT=x[:, fc, tt * 128:(tt + 1) * 128], rhs=w[:, fc, 640:780],
                                                                         start=(fc == 0), stop=(fc == FC - 1)), r=[wb, hb_], w=[pb_])
                k.op("dve", lambda e, p_=p_, kb=kb: e.tensor_copy(out=VS[:, kb, 0:64], in_=p_[:, 0:64]), r=[pb_], w=[pb])
                k.op("dve", lambda e, p_=p_, kb=kb: e.tensor_copy(out=VW[:, kb, 0:64], in_=p_[:, 64:128]), r=[pb_], w=[pb])
                k.op("act", lambda e, p_=p_, kb=kb: e.activation(out=G[:, kb, :], in_=p_[:, 128:140], func=AF.Sigmoid), r=[pb_], w=[pb])
        if stop <= 1:
            k.barrier()
            return
        w1 = k.sb("nw1", [128, 32, 256], BF16, st)
        w1b = k.buf("nw1")
        for l in range(32):
            k.dma("pool", w1[:, l, :], w1_ap[:, l, :], w=[w1b])
        w2kd = k.sb("nw2k", [128, 2, 128], BF16, st)
        w2v = k.sb("nw2v", [128, 2, 64], BF16, st)
        w2b = k.buf("nw2")
        w2kv_ = w2k_ap.rearrange("(hc p) d -> p hc d", p=128)
        for hc in range(2):
            k.dma("pool", w2kd[:, hc, 0:64], w2kv_[:, hc, :], w=[w2b])
            k.dma("pool", w2kd[:, hc, 64:128], w2kv_[:, hc, :], w=[w2b])
            k.dma("pool", w2v[:, hc, :], w2v_ap.rearrange("(hc p) d -> p hc d", p=128)[:, hc, :], w=[w2b])
        peT = k.sb("npeT", [128, 32], BF16, st)
        peb = k.buf("npeT")
        k.dma("pool", peT[:], peT_ap, w=[peb])
        cb = k.sb("ncb", [128, 4], F32, st)
        cbb = k.buf("ncb")
        hid = k.sb("nhid", [128, 4, 512], BF16, st)
        hidb = k.buf("nhid")
        xs = k.sb("nxs", [128, 512], F32, st)
        x2 = k.sb("nx2", [128, 512], F32, st)
        xsb, x2b = k.buf("nxs"), k.buf("nx2")
        for kv in range(2):
            b0 = 64 * kv
            for hc in range(2):
                idx = kv * 2 + hc
                p_, pb_ = ps[idx % 2], psb[idx % 2]
                pc, pcb = ps[2 + idx % 2], psb[2 + idx % 2]
                for l in range(32):
                    k.op("pe", lambda e, l=l, pc=pc: e.matmul(pc[:, 0:1], lhsT=w1[b0:b0 + 64, l, hc * 128:(hc + 1) * 128], rhs=peT[b0:b0 + 64, l:l + 1],
                                                                start=(l == 0), stop=(l == 31)), r=[w1b, peb], w=[pcb])
                k.op("dve", lambda e, pc=pc, idx=idx: e.tensor_copy(out=cb[:, idx:idx + 1], in_=pc[:, 0:1]), r=[pcb], w=[cbb])
                for l in range(32):
                    k.op("pe", lambda e, l=l, p_=p_: e.matmul(p_[:, 0:511], lhsT=w1[b0:b0 + 64, l, hc * 128:(hc + 1) * 128],
                                                                rhs=KAT[b0:b0 + 64, l:l + 8161:16], start=(l == 0), stop=(l == 31)), r=[w1b, pb], w=[pb_])
                k.op("act", lambda e, p_=p_, idx=idx: e.activation(out=xs[:, 0:511], in_=p_[:, 0:511], func=AF.Identity, bias=cb[:, idx:idx + 1], scale=1.0),
                     r=[pb_, cbb], w=[xsb])
                k.op("act", lambda e, p_=p_, idx=idx: e.activation(out=x2[:, 0:511], in_=p_[:, 0:511], func=AF.Square, bias=cb[:, idx:idx + 1], scale=1.0),
                     r=[pb_, cbb], w=[x2b])
                k.op("dve", lambda e: e.tensor_scalar(out=x2[:, 0:511], in0=x2[:, 0:511], scalar1=0.044715, scalar2=1.0, op0=ALU.mult, op1=ALU.add),
                     r=[x2b], w=[x2b])
                k.op("dve", lambda e: e.tensor_tensor(out=x2[:, 0:511], in0=x2[:, 0:511], in1=xs[:, 0:511], op=ALU.mult), r=[x2b, xsb], w=[x2b])
                k.op("act", lambda e: e.activation(out=x2[:, 0:511], in_=x2[:, 0:511], func=AF.Sigmoid, scale=GELU_C), r=[x2b], w=[x2b])
                k.op("dve", lambda e, idx=idx: e.tensor_tensor(out=hid[:, idx, 0:511], in0=x2[:, 0:511], in1=xs[:, 0:511], op=ALU.mult),
                     r=[x2b, xsb], w=[hidb])
        for hc in range(2):
            k.op("pe", lambda e, hc=hc: e.matmul(ps[4][:, 0:511], lhsT=w2kd[:, hc, :], rhs=hid[:, hc, 0:511], start=(hc == 0), stop=(hc == 1)),
                 r=[w2b, hidb], w=[psb[4]])
        k.op("dve", lambda e: e.tensor_copy(out=kcT[:, 0:511], in_=ps[4][:, 0:511]), r=[psb[4]], w=[pb])
        for nb in range(4):
            M = 128 if nb < 3 else 127
            p_, pb_ = ps[5 + nb % 2], psb[5 + nb % 2]
            for hc in range(2):
                k.op("pe", lambda e, hc=hc, nb=nb, M=M, p_=p_: e.matmul(p_[0:M, 0:64], lhsT=hid[:, 2 + hc, nb * 128:nb * 128 + M], rhs=w2v[:, hc, :],
                                                                          start=(hc == 0), stop=(hc == 1)), r=[w2b, hidb], w=[pb_])
            k.op("dve", lambda e, nb=nb, M=M, p_=p_: e.tensor_copy(out=VCX[0:M, nb, 0:64], in_=p_[0:M, 0:64]), r=[pb_], w=[pb])
        k.barrier()
    if stop <= 2:
        return
    with contextlib.ExitStack() as st:
        psZ, psZb = ps[0:2], psb[0:2]
        psOc, psOcb = ps[2:4], psb[2:4]
        psOs, psOsb = ps[4], psb[4]
        psOw, psOwb = ps[5], psb[5]
        psM, psMb = ps[6:8], psb[6:8]
        Wc = [k.sb("nWc%d" % i, [128, 512], F32, st) for i in range(2)]
        Wcb = k.bufs(2, "nWc")
        ee = [k.sb("nee%d" % i, [128, 512], BF16, st) for i in range(3)]
        eeb = k.bufs(3, "nee")
        sf = [k.sb("nsf%d" % i, [128, 512], F32, st) for i in range(2)]
        sfb = k.bufs(2, "nsf")
        nmT = k.sb("nnmT", [128, 512], BF16, st)
        nmTb = k.buf("nnmT")
        imp = k.sb("nimp", [128, 128], F32, st)
        imp3 = k.sb("nimp3", [128, 128], F32, st)
        negm = k.sb("nnegm", [128, 128], F32, st)
        impb, imp3b, negmb = k.buf("imp"), k.buf("imp3"), k.buf("negm")
        m8 = k.sb("nm8", [128, 16], F32, st)
        m8b = k.buf("m8")
        dn = k.sb("ndn", [128, 12], F32, st)
        dnb = k.buf("dn")
        ot = k.sb("not", [128, 256], F32, st)
        otb = k.buf("ot")
        oT = [k.sb("noT%d" % i, [128, 2, 512], BF16, st) for i in range(2)]
        oTb = k.bufs(2, "noT")
        zc = 0
        ec = 0
        wcn = 0

        bd = [k.sb("nbd%d" % i, [128, 512], BF16, st) for i in range(2)]
        bdb = k.bufs(2, "nbd")
        for i in range(2):
            k.op("pool", lambda e, i=i: e.memset(bd[i][:], 0.0), w=[bdb[i]])
        cur = {}

        def qk(Z, Zb, KT_, kcols, tcols, first_start, extra_r=()):
            k.op("pe", lambda e: e.matmul(Z[:, :], lhsT=KT_[:, kcols], rhs=cur["bd"][:, :], start=first_start, stop=True, skip_group_check=True),
                 r=[pb, cur["bdb"]] + list(extra_r), w=[Zb])

        for qt in range(nqt):
            tcols = slice(128 * qt, 128 * (qt + 1))
            cur["bd"], cur["bdb"] = bd[qt % 2], bdb[qt % 2]
            for g in range(4):
                b0 = 64 * (g % 2)
                k.op("pool", lambda e, g=g, b0=b0: e.tensor_copy(out=cur["bd"][b0:b0 + 64, g * 128:(g + 1) * 128], in_=QT[g // 2][b0:b0 + 64, tcols]),
                     r=[pb], w=[cur["bdb"]])
            NB = (8 * qt + 6) // 128 + 1
            for nb in range(NB):
                Z, Zb = psZ[zc % 2], psZb[zc % 2]
                zc += 1
                o_idx = qt - 16 * nb
                if o_idx <= 16:
                    W_, Wb_ = Wc[wcn % 2], Wcb[wcn % 2]
                    wcn += 1
                    src = bass.AP(bdz.tensor, 128 * o_idx, [[16, 128], [NDC, 4], [1, 128]])
                    import os
                    if os.environ.get("NSA_DBG") == "1":
                        k.op("pool", lambda e, W_=W_: e.memset(W_[:], 0.0), w=[Wb_])
                    else:
                        k.dma("sp", W_[:].rearrange("p (g q) -> p g q", g=4), src, r=[cx.bdz_b], w=[Wb_])
                    k.op("pe", lambda e, W_=W_: e.matmul(psM[0][:, :], lhsT=cx.jflip[:], rhs=W_[:], start=True, stop=True),
                         r=[Wb_, cx.jflip_b], w=[psMb[0]])
                    k.op("act", lambda e, W_=W_: e.copy(out=W_[:], in_=psM[0][:, :]), r=[psMb[0]], w=[Wb_])
                    qk(Z, Zb, kcT, slice(nb * 128, (nb + 1) * 128), tcols, True)
                    E_, Eb_ = ee[ec % 3], eeb[ec % 3]
                    ec += 1
                    s_, sb_ = sf[ec % 2], sfb[ec % 2]
                    k.op("dve", lambda e, s_=s_, Z=Z, W_=W_: e.tensor_tensor(out=s_[:], in0=Z[:, :], in1=W_[:], op=ALU.add), r=[Zb, Wb_], w=[sb_])
                    k.op("act", lambda e, E_=E_, s_=s_: e.activation(out=E_[:, :], in_=s_[:], func=AF.Exp), r=[sb_], w=[Eb_])
                else:
                    qk(Z, Zb, kcT, slice(nb * 128, (nb + 1) * 128), tcols, True)
                    E_, Eb_ = ee[ec % 3], eeb[ec % 3]
                    ec += 1
                    k.op("act", lambda e, E_=E_, Z=Z: e.activation(out=E_[:, :], in_=Z[:, :], func=AF.Exp), r=[Zb], w=[Eb_])
                for g in range(4):
                    bank, bb = psOc[g // 2], psOcb[g // 2]
                    c0 = (g % 2) * 193
                    k.op("pe", lambda e, g=g, bank=bank, c0=c0, E_=E_, nb=nb: e.matmul(
                        bank[:, c0:c0 + 193], lhsT=E_[:, g * 128:(g + 1) * 128], rhs=VCX[:, nb, :],
                        start=(nb == 0 and g % 2 == 0), stop=(nb == NB - 1), skip_group_check=True), r=[Eb_, pb], w=[bb])
            if stop <= 3:
                continue
            for h2 in range(2):
                bank, bb = psOc[h2], psOcb[h2]
                bv = bank[:, 0:386].rearrange("p (g c) -> p g c", c=193)
                k.op("dve", lambda e, h2=h2, bv=bv: e.tensor_scalar(out=dn[:, 2 * h2:2 * h2 + 2], in0=bv[:, :, 64], scalar1=1e-30, scalar2=None, op0=ALU.max),
                     r=[bb], w=[dnb])
            k.op("dve", lambda e: e.reciprocal(out=dn[:, 0:4], in_=dn[:, 0:4]), r=[dnb], w=[dnb])
            for g in range(4):
                bank, bb = psOc[g // 2], psOcb[g // 2]
                c0 = (g % 2) * 193 + 65
                if g == 0:
                    k.op("dve", lambda e, bank=bank, c0=c0: e.tensor_scalar(out=imp[:], in0=bank[:, c0:c0 + 128], scalar1=dn[:, 0:1], scalar2=None, op0=ALU.mult),
                         r=[bb, dnb], w=[impb])
                else:
                    k.op("dve", lambda e, g=g, bank=bank, c0=c0: e.scalar_tensor_tensor(out=imp[:], in0=bank[:, c0:c0 + 128], scalar=dn[:, g:g + 1], in1=imp[:],
                                                                                        op0=ALU.mult, op1=ALU.add), r=[bb, dnb, impb], w=[impb])
            sl = slice(128 - 2 * qt, 256 - 2 * qt)
            k.op("dve", lambda e: e.tensor_tensor(out=imp[:], in0=imp[:], in1=cx.mc_rel[:, sl], op=ALU.mult), r=[impb, cx.mc_rel_b], w=[impb])
            k.op("dve", lambda e: e.tensor_tensor(out=imp[:], in0=imp[:], in1=cx.ma_rel[:, sl], op=ALU.add), r=[impb, cx.ma_rel_b], w=[impb])
            k.op("dve", lambda e: e.memset(imp[:, 0:1], 100.0), r=[impb], w=[impb])
            k.op("dve", lambda e: e.max(out=m8[:, 0:8], in_=imp[:]), r=[impb], w=[m8b])
            k.op("dve", lambda e: e.match_replace(out=imp3[:], in_to_replace=m8[:, 0:8], in_values=imp[:], imm_value=-1e30), r=[impb, m8b], w=[imp3b])
            k.op("dve", lambda e: e.max(out=m8[:, 8:16], in_=imp3[:]), r=[imp3b], w=[m8b])
            k.op("dve", lambda e: e.tensor_scalar(out=negm[:], in0=imp[:], scalar1=m8[:, 15:16], scalar2=NEG, op0=ALU.is_lt, op1=ALU.mult),
                 r=[impb, m8b], w=[negmb])
            k.op("pe", lambda e: e.transpose(out=psM[0][:, 0:128], in_=negm[:], identity=cx.ident_f[:]), r=[negmb, cx.ident_f_b], w=[psMb[0]])
            for g in range(4):
                if g % 2 == 0:
                    k.op("dve", lambda e, g=g: e.tensor_copy(out=nmT[:, g * 128:(g + 1) * 128], in_=psM[0][:, 0:128]), r=[psMb[0]], w=[nmTb])
                else:
                    k.op("act", lambda e, g=g: e.copy(out=nmT[:, g * 128:(g + 1) * 128], in_=psM[0][:, 0:128]), r=[psMb[0]], w=[nmTb])
            if stop <= 4:
                continue
            for kb in range(qt + 1):
                Z, Zb = psZ[zc % 2], psZb[zc % 2]
                zc += 1
                m = qt - kb
                k.op("pe", lambda e, Z=Z, kb=kb: e.matmul(Z[:, :], lhsT=cx.ex[:, 128 * kb:128 * (kb + 1)], rhs=nmT[:], start=True, stop=False, skip_group_check=True),
                     r=[nmTb, cx.ex_b], w=[Zb])
                qk(Z, Zb, KST, slice(128 * kb, 128 * (kb + 1)), tcols, False)
                E_, Eb_ = ee[ec % 3], eeb[ec % 3]
                ec += 1
                if m <= 1:
                    Tm = T0 if m == 0 else T1
                    s_, sb_ = sf[ec % 2], sfb[ec % 2]
                    k.op("dve", lambda e, s_=s_, Z=Z, Tm=Tm: e.tensor_tensor(out=s_[:], in0=Z[:, :], in1=Tm[:], op=ALU.add), r=[Zb, pb], w=[sb_])
                    k.op("act", lambda e, E_=E_, s_=s_: e.activation(out=E_[:, :], in_=s_[:], func=AF.Exp), r=[sb_], w=[Eb_])
                else:
                    k.op("act", lambda e, E_=E_, Z=Z: e.activation(out=E_[:, :], in_=Z[:, :], func=AF.Exp), r=[Zb], w=[Eb_])
                for g in range(4):
                    k.op("pe", lambda e, g=g, E_=E_, kb=kb: e.matmul(psOs[:, g * 65:(g + 1) * 65], lhsT=E_[:, g * 128:(g + 1) * 128], rhs=VS[:, kb, :],
                                                                      start=(kb == 0 and g == 0), stop=(kb == qt), skip_group_check=True), r=[Eb_, pb], w=[psOsb])
            if stop <= 5:
                continue
            kb0 = max(0, qt - 4)
            for kb in range(kb0, qt + 1):
                Z, Zb = psZ[zc % 2], psZb[zc % 2]
                zc += 1
                m = qt - kb
                qk(Z, Zb, KWT, slice(128 * kb, 128 * (kb + 1)), tcols, True)
                E_, Eb_ = ee[ec % 3], eeb[ec % 3]
                ec += 1
                if m in (0, 1, 4):
                    Tm, Tmb = {0: (T0, pb), 1: (T1, pb), 4: (cx.t4, cx.t4_b)}[m]
                    s_, sb_ = sf[ec % 2], sfb[ec % 2]
                    k.op("dve", lambda e, s_=s_, Z=Z, Tm=Tm: e.tensor_tensor(out=s_[:], in0=Z[:, :], in1=Tm[:], op=ALU.add), r=[Zb, Tmb], w=[sb_])
                    k.op("act", lambda e, E_=E_, s_=s_: e.activation(out=E_[:, :], in_=s_[:], func=AF.Exp), r=[sb_], w=[Eb_])
                else:
                    k.op("act", lambda e, E_=E_, Z=Z: e.activation(out=E_[:, :], in_=Z[:, :], func=AF.Exp), r=[Zb], w=[Eb_])
                for g in range(4):
                    k.op("pe", lambda e, g=g, E_=E_, kb=kb: e.matmul(psOw[:, g * 65:(g + 1) * 65], lhsT=E_[:, g * 128:(g + 1) * 128], rhs=VW[:, kb, :],
                                                                      start=(kb == kb0 and g == 0), stop=(kb == qt), skip_group_check=True), r=[Eb_, pb], w=[psOwb])
            if stop <= 6:
                continue
            osv = psOs[:, 0:260].rearrange("p (g c) -> p g c", c=65)
            owv = psOw[:, 0:260].rearrange("p (g c) -> p g c", c=65)
            k.op("dve", lambda e: e.tensor_scalar(out=dn[:, 4:8], in0=osv[:, :, 64], scalar1=1e-30, scalar2=None, op0=ALU.max), r=[psOsb], w=[dnb])
            k.op("dve", lambda e: e.tensor_scalar(out=dn[:, 8:12], in0=owv[:, :, 64], scalar1=1e-30, scalar2=None, op0=ALU.max), r=[psOwb], w=[dnb])
            k.op("dve", lambda e: e.reciprocal(out=dn[:, 4:12], in_=dn[:, 4:12]), r=[dnb], w=[dnb])
            k.op("dve", lambda e: e.tensor_tensor(out=dn[:, :], in0=dn[:, :], in1=G[:, qt, :], op=ALU.mult), r=[dnb, pb], w=[dnb])
            for g in range(4):
                bank, bb = psOc[g // 2], psOcb[g // 2]
                c0 = (g % 2) * 193
                og = ot[:, g * 64:(g + 1) * 64]
                k.op("dve", lambda e, g=g, bank=bank, c0=c0, og=og: e.tensor_scalar(out=og, in0=bank[:, c0:c0 + 64], scalar1=dn[:, g:g + 1], scalar2=None, op0=ALU.mult),
                     r=[bb, dnb], w=[otb])
                k.op("dve", lambda e, g=g, og=og: e.scalar_tensor_tensor(out=og, in0=psOs[:, g * 65:g * 65 + 64], scalar=dn[:, 4 + g:5 + g], in1=og, op0=ALU.mult, op1=ALU.add),
                     r=[psOsb, dnb, otb], w=[otb])
                k.op("dve", lambda e, g=g, og=og: e.scalar_tensor_tensor(out=og, in0=psOw[:, g * 65:g * 65 + 64], scalar=dn[:, 8 + g:9 + g], in1=og, op0=ALU.mult, op1=ALU.add),
                     r=[psOwb, dnb, otb], w=[otb])
            y2 = (qt // 4) % 2
            for fh in range(2):
                k.op("pe", lambda e, fh=fh: e.transpose(out=psM[1][:, fh * 128:(fh + 1) * 128], in_=ot[:, fh * 128:(fh + 1) * 128], identity=cx.ident_f[:]),
                     r=[otb, cx.ident_f_b], w=[psMb[1]])
            k.op("act", lambda e, y2=y2: e.copy(out=oT[y2][:, :, (qt % 4) * 128:(qt % 4 + 1) * 128],
                                                in_=psM[1][:, 0:256].rearrange("p (fh t) -> p fh t", fh=2)), r=[psMb[1]], w=[oTb[y2]])
            if qt % 4 == 3:
                sbk = qt // 4
                dest, off = sbk // 4, (sbk % 4) * 512
                k.dma("sp", a2a[dest].rearrange("(fh p) t -> p fh t", p=128)[:, :, off:off + 512], oT[y2][:], r=[oTb[y2]], w=[a2a_b])
                if k.engs["pe"].count > SEM_ROTATE:
                    k.barrier()
        k.barrier()


def build_prog_nsa(nqt=64, stop=99):
    nc = bass.Bass("TRN2", target_bir_lowering=False)
    consts = nsa_host_consts()
    names = ["ident_f", "jflip", "ex", "t4", "mc_rel", "ma_rel"]
    dnames = ["ohc", "dm", "ov"]
    cd = dram_consts(nc, consts, names + dnames)

    def din(nm, shape, dt_=F32):
        return nc.dram_tensor(nm, shape, dt_, kind="ExternalInput").ap()
    hn_all = din("hn_all", [4 * D, TOK], BF16)
    w = din("nsa_w", [D, 780])
    rb = din("rb", [32, 4])
    peT = din("peT", [128, 32])
    w1 = din("cw1", [128, 32, 256])
    w2k = din("cw2k", [256, 64])
    w2v = din("cw2v", [256, 64])
    a2a = nc.dram_tensor("a2a", [4, 256, TOK], BF16, kind="ExternalOutput").ap()
    bdz = nc.dram_tensor("bdz", [4, NDC], F32, kind="Internal").ap()
    k = KB(nc)
    cx = load_consts(k, cd, names)
    for nm in dnames:
        setattr(cx, nm + "_dram", cd[nm])
    phase_nsa(k, cx, hn_all, k.buf("hn_all"), w, rb, peT, w1, w2k, w2v, bdz, a2a, k.buf("a2a"), k.stack, nqt=nqt, stop=stop)
    k.barrier()
    k.close()
    return nc, {nm: consts[nm] for nm in names + dnames}


def nsa_core_inputs(inp, r):
    w_in = inp["nsa_w_in"][0]
    kv0 = 1024

    def kvc(i):
        return w_in[:, kv0 + i * 256 + r * 64: kv0 + i * 256 + (r + 1) * 64]
    gcols = [2560 + j * 16 + r * 4 + g for j in range(3) for g in range(4)]
    w = np.concatenate([w_in[:, 256 * r:256 * (r + 1)], kvc(0), kvc(1), kvc(2), kvc(2), kvc(4), kvc(4), kvc(3), kvc(5), w_in[:, gcols]], axis=1)
    peT = np.concatenate([inp["nsa_pe_k"][0].T, inp["nsa_pe_v"][0].T], axis=0)
    w1k = inp["nsa_ck_w1"][0].reshape(32, 64, 256).transpose(1, 0, 2)
    w1v = inp["nsa_cv_w1"][0].reshape(32, 64, 256).transpose(1, 0, 2)
    return {"nsa_w": np.ascontiguousarray(w), "rb": np.ascontiguousarray(inp["rel_bias"][:, 4 * r:4 * r + 4]),
            "peT": np.ascontiguousarray(peT), "cw1": np.ascontiguousarray(np.concatenate([w1k, w1v], axis=0)),
            "cw2k": inp["nsa_ck_w2"][0], "cw2v": inp["nsa_cv_w2"][0]}


_PROGS = {}


def _prog(name, fn):
    if name not in _PROGS:
        _PROGS[name] = fn()
    return _PROGS[name]


def _run(name, fn, maps):
    nc, cst = _prog(name, fn)
    full = []
    for m in maps:
        mm = dict(m)
        mm.update({"c_" + kk: v for kk, v in cst.items()})
        full.append(mm)
    res = run_bass_kernel_spmd(nc, full, core_ids=list(range(NCORES)))
    return res.results


def _c(a):
    return np.ascontiguousarray(a)


def _allgather(parts):
    out = []
    for b in range(2):
        cat = np.concatenate([np.asarray(parts[4 * b + r])[256 * q:256 * (q + 1)] for q in range(4) for r in range(4)], axis=0)
        out += [cat] * 4
    return out


def _alltoall(parts):
    out = []
    for b in range(2):
        for j in range(4):
            out.append(_c(np.concatenate([np.asarray(parts[4 * b + i])[j] for i in range(4)], axis=0)))
    return out


def kernel_unfused(**inp):
    inp = {kk: np.asarray(v) for kk, v in inp.items()}
    x, p = inp["x"], inp["p"]
    cores = [(c // 4, c % 4) for c in range(NCORES)]
    tsl = [slice(TOK * r, TOK * (r + 1)) for (_, r) in cores]
    xT = [_c(x[b, tsl[c]].T) for c, (b, r) in enumerate(cores)]
    res = _run("norm0", build_prog_norm0, [{"xT": xT[c], "gain": _c(inp["norm_mix"][0])} for c in range(NCORES)])
    hn_all = _allgather([r_["hnT"] for r_ in res])
    w_in = inp["sb_w_in"][0]
    maps = []
    for c, (b, r) in enumerate(cores):
        wq = np.concatenate([w_in[:, 256 * r:256 * (r + 1)], w_in[:, 1024 + 256 * r:1024 + 256 * (r + 1)],
                             w_in[:, 2048 + 256 * r:2048 + 256 * (r + 1)]], axis=1)
        maps.append({"hn_all": hn_all[c], "sb_w": _c(wq)})
    res = _run("sb", build_prog_sb, maps)
    oT = _alltoall([r_["a2a"] for r_ in res])

    def tail_maps(i, hT_in, oT_, g_next):
        out = []
        for c, (b, r) in enumerate(cores):
            out.append({"hT_in": hT_in[c], "oT": oT_[c],
                        "w_out": _c((inp["sb_w_out"] if i == 0 else inp["nsa_w_out"])[0]),
                        "g_ffn": _c(inp["norm_ffn"][i]), "w1": _c(inp["ffn_w_in"][i]), "w2": _c(inp["ffn_w_out"][i]),
                        "g_ple": _c(inp["norm_ple"][i]), "wg": _c(inp["ple_w_gate"][i]), "wp": _c(inp["ple_w_proj"][i]),
                        "pT": _c(p[i, b, tsl[c]].T), "g_next": _c(g_next)})
        return out
    res = _run("tail0", lambda: build_prog_tail(False), tail_maps(0, xT, oT, inp["norm_mix"][1]))
    h1T = [np.asarray(r_["hT_out"]) for r_ in res]
    hn_all = _allgather([r_["hnT"] for r_ in res])
    maps = []
    for c, (b, r) in enumerate(cores):
        m = {"hn_all": hn_all[c]}
        m.update(nsa_core_inputs(inp, r))
        maps.append(m)
    res = _run("nsa", build_prog_nsa, maps)
    oT = _alltoall([r_["a2a"] for r_ in res])
    res = _run("tail1", lambda: build_prog_tail(True), tail_maps(1, h1T, oT, inp["final_norm"]))
    out = np.empty((2, S, D), np.float32)
    for c, (b, r) in enumerate(cores):
        out[b, tsl[c], :] = np.asarray(res[c]["outT"]).T
    return out


I32 = mybir.dt.int32
TAIL_KEYS = (("w_out", [D, D]), ("g_ffn", [D]), ("w1", [D, 2 * DFF]), ("w2", [DFF, D]), ("g_ple", [D]), ("wg", [D, D]), ("wp", [256, D]))


def build_prog_fused():
    import os
    CUT = int(os.environ.get("FUSED_CUT", "99"))
    nc = bass.Bass("TRN2", target_bir_lowering=False)
    consts = dict(host_consts())
    consts.update(nsa_host_consts())
    small = ["ident_f", "ones_b", "negu_b", "negones_b", "mask_sb", "jflip"]
    nsa_sb = ["ex", "t4", "mc_rel", "ma_rel"]
    nsa_dr = ["ohc", "dm", "ov"]
    cd = dram_consts(nc, consts, small + nsa_sb + nsa_dr)

    def din(nm, shape, dt_=F32):
        return nc.dram_tensor(nm, shape, dt_, kind="ExternalInput").ap()

    def dint(nm, shape, dt_):
        return nc.dram_tensor(nm, shape, dt_, kind="Internal").ap()
    xT = din("xT", [D, TOK])
    pT = [din("pT0", [256, TOK]), din("pT1", [256, TOK])]
    rk = din("rk", [1, 2], I32)
    g_mix = [din("g_mix0", [D]), din("g_mix1", [D])]
    g_fin = din("g_fin", [D])
    sb_w = din("sb_w", [D, 768])
    tails = [{nm: din("%s_%d" % (nm, i), shp) for nm, shp in TAIL_KEYS} for i in range(2)]
    nsa_w = din("nsa_w", [D, 780])
    rb = din("rb", [32, 4])
    peT = din("peT", [128, 32])
    cw1 = din("cw1", [128, 32, 256])
    cw2k = din("cw2k", [256, 64])
    cw2v = din("cw2v", [256, 64])
    outT = nc.dram_tensor("outT", [D, TOK], F32, kind="ExternalOutput").ap()
    hn_loc = dint("hn_loc", [D, TOK], BF16)
    hn_all = dint("hn_all", [4 * D, TOK], BF16)
    a2a_loc = dint("a2a_loc", [4, 256, TOK], BF16)
    a2a_all = dint("a2a_all", [4 * D, TOK], BF16)
    bdz = dint("bdz", [4, NDC], F32)
    hsp = dint("hsp", [D, TOK], F32)
    k = KB(nc)
    cx = load_consts(k, cd, small)
    for nm in nsa_dr:
        setattr(cx, nm + "_dram", cd[nm])
    make_eps(k, cx)
    make_one(k, cx)
    groups = [[0, 1, 2, 3], [4, 5, 6, 7]]
    spq = k.engs["sp"].h
    reg = spq.alloc_register("rk")
    spq.reg_load(reg, rk[0:1, 0:1])
    crk = spq.snap(reg, min_val=0, max_val=3)
    hn_loc_b, hn_all_b, a2a_loc_b, a2a_all_b, hsp_b, out_b = [k.buf(n_) for n_ in ("hn_loc", "hn_all", "a2a_loc", "a2a_all", "hsp", "outT")]
    g4 = a2a_all.rearrange("(j f) t -> j f t", j=4)

    def oT_loader(tile, tb, cs):
        src = g4[crk].rearrange("(fc p) t -> p fc t", p=128)
        k.dma("sp", tile[:], src[:, :, cs], r=[a2a_all_b], w=[tb])

    def gather_hn():
        for q in range(4):
            k.coll("AllGather", hn_loc[256 * q:256 * (q + 1), :], hn_all[1024 * q:1024 * (q + 1), :], groups, r=[hn_loc_b], w=[hn_all_b])

    def gather_o():
        for j in range(4):
            k.coll("AllGather", a2a_loc[j], a2a_all[1024 * j:1024 * (j + 1), :], groups, r=[a2a_loc_b], w=[a2a_all_b])

    def tail(i, hT, hb):
        t = tails[i]
        phase_tail(k, cx, hT, hb, None, a2a_all_b, t["w_out"], t["g_ffn"], t["w1"], t["w2"], t["g_ple"], t["wg"], t["wp"], pT[i],
                   oT_loader=oT_loader)

    hv = hsp.rearrange("(fc p) t -> p fc t", p=128)
    with contextlib.ExitStack() as stA:
        hT = k.sb("hT", [128, FC, TOK], F32, stA)
        hb = k.bufs(4, "hT")
        xv = xT.rearrange("(fc p) t -> p fc t", p=128)
        for c in range(4):
            k.dma("sp", hT[:, :, c * 512:(c + 1) * 512], xv[:, :, c * 512:(c + 1) * 512], w=[hb[c]])
        with contextlib.ExitStack() as st:
            phase_norm_out(k, cx, hT, hb, g_mix[0], hn_loc, hn_loc_b, st)
            k.barrier()
        gather_hn()
        if CUT >= 2:
            with contextlib.ExitStack() as st:
                phase_sb(k, cx, hn_all, hn_all_b, sb_w, a2a_loc, a2a_loc_b, st)
                k.barrier()
            gather_o()
        if CUT >= 3:
            tail(0, hT, hb)
        with contextlib.ExitStack() as st:
            phase_norm_out(k, cx, hT, hb, g_mix[1], hn_loc, hn_loc_b, st)
            for c in range(4):
                k.dma("sp", hv[:, :, c * 512:(c + 1) * 512], hT[:, :, c * 512:(c + 1) * 512], r=[hb[c]], w=[hsp_b])
            k.barrier()
    if CUT >= 4:
        gather_hn()
    with contextlib.ExitStack() as stB:
      if CUT >= 5:
        cxb = load_consts(k, cd, nsa_sb, stB)
        for nm in nsa_sb:
            setattr(cx, nm, getattr(cxb, nm))
            setattr(cx, nm + "_b", getattr(cxb, nm + "_b"))
        phase_nsa(k, cx, hn_all, hn_all_b, nsa_w, rb, peT, cw1, cw2k, cw2v, bdz, a2a_loc, a2a_loc_b, stB)
        k.barrier()
    if CUT >= 5:
        gather_o()
    with contextlib.ExitStack() as stC:
        hT = k.sb("hT2", [128, FC, TOK], F32, stC)
        hb = k.bufs(4, "hT2")
        for c in range(4):
            k.dma("sp", hT[:, :, c * 512:(c + 1) * 512], hv[:, :, c * 512:(c + 1) * 512], r=[hsp_b], w=[hb[c]])
        if CUT >= 6:
            tail(1, hT, hb)
        with contextlib.ExitStack() as st:
            phase_final_norm(k, cx, hT, hb, g_fin, outT, out_b, st)
            k.barrier()
    k.barrier()
    k.close()
    return nc, {nm: consts[nm] for nm in small + nsa_sb + nsa_dr}


def fused_maps(inp):
    x, p = inp["x"], inp["p"]
    maps = []
    w_in = inp["sb_w_in"][0]
    for c in range(NCORES):
        b, r = c // 4, c % 4
        ts = slice(TOK * r, TOK * (r + 1))
        m = {"xT": _c(x[b, ts].T), "pT0": _c(p[0, b, ts].T), "pT1": _c(p[1, b, ts].T),
             "rk": np.array([[r, 0]], np.int32),
             "g_mix0": _c(inp["norm_mix"][0]), "g_mix1": _c(inp["norm_mix"][1]), "g_fin": _c(inp["final_norm"]),
             "sb_w": _c(np.concatenate([w_in[:, 256 * r:256 * (r + 1)], w_in[:, 1024 + 256 * r:1024 + 256 * (r + 1)],
                                        w_in[:, 2048 + 256 * r:2048 + 256 * (r + 1)]], axis=1))}
        for i in range(2):
            m["w_out_%d" % i] = _c((inp["sb_w_out"] if i == 0 else inp["nsa_w_out"])[0])
            m["g_ffn_%d" % i] = _c(inp["norm_ffn"][i])
            m["w1_%d" % i] = _c(inp["ffn_w_in"][i])
            m["w2_%d" % i] = _c(inp["ffn_w_out"][i])
            m["g_ple_%d" % i] = _c(inp["norm_ple"][i])
            m["wg_%d" % i] = _c(inp["ple_w_gate"][i])
            m["wp_%d" % i] = _c(inp["ple_w_proj"][i])
        m.update(nsa_core_inputs(inp, r))
        maps.append(m)
    return maps


def kernel(**inp):
    inp = {kk: np.asarray(v) for kk, v in inp.items()}
    res = _run("fused", build_prog_fused, fused_maps(inp))
    out = np.empty((2, S, D), np.float32)
    for c in range(NCORES):
        b, r = c // 4, c % 4
        out[b, TOK * r:TOK * (r + 1), :] = np.asarray(res[c]["outT"]).T
    return out
```

```python
import contextlib
import math
import numpy as np
import ml_dtypes
import concourse.bass as bass
import concourse.mybir as mybir
from concourse.alu_op_type import AluOpType as ALU
from concourse.bass_utils import run_bass_kernel_spmd

F32 = mybir.dt.float32
BF16 = mybir.dt.bfloat16
AF = mybir.ActivationFunctionType
NPBF = ml_dtypes.bfloat16

NCORES = 8
S = 8192
D = 1024
TOK = 2048
FC = 8
DFF = 2816
NJ = DFF // 128
EPS = 1e-6
NEG = -30000.0
NDC = 4352
DOFF = 2063


class Buf:
    __slots__ = ("name", "w", "r", "dsem", "dcnt", "dkey")

    def __init__(self, name):
        self.name = name
        self.w = None
        self.r = {}
        self.dsem = None
        self.dcnt = 0
        self.dkey = None


class Eng:
    def __init__(self, name, h, sem):
        self.name = name
        self.gen = 0
        self.key = ("e", name, 0)
        self.h = h
        self.sem = sem
        self.count = 0
        self.seen = {}


SEM_ROTATE = 6000


class KB:
    def __init__(self, nc):
        self.nc = nc
        self.stack = contextlib.ExitStack()
        self.engs = {}
        for key, h in (("pe", nc.tensor), ("act", nc.scalar), ("dve", nc.vector),
                       ("pool", nc.gpsimd), ("sp", nc.sync)):
            sem = self.stack.enter_context(nc.semaphore("s_" + key)) if key != "sp" else None
            self.engs[key] = Eng(key, h, sem)
        self.csem = self.stack.enter_context(nc.semaphore("s_coll"))
        self.hsem = self.stack.enter_context(nc.semaphore("s_hand"))
        self.hcnt = 0
        self.dma_bufs = []
        self.nbuf = 0
        self.free_dsems = {"hw": [], "sw": []}
        self.nsem = 0
        self.ccnt = 0

    def sb(self, name, shape, dtype, stack=None):
        self.nbuf += 1
        return (stack or self.stack).enter_context(self.nc.sbuf_tensor("S%d_%s" % (self.nbuf, name), list(shape), dtype))

    def ps(self, name, shape, dtype, stack=None):
        self.nbuf += 1
        return (stack or self.stack).enter_context(self.nc.psum_tensor("P%d_%s" % (self.nbuf, name), list(shape), dtype))

    def buf(self, name=None):
        self.nbuf += 1
        return Buf("%s_%d" % (name or "b", self.nbuf))

    def bufs(self, n, name=None):
        return [self.buf(name) for _ in range(n)]

    def _deps(self, E, r, w):
        deps = {}

        def need(k, s, v):
            if k[0] == "e" and k[1] == "pe" and E.name == "pe":
                return
            if E.seen.get(k, 0) >= v:
                return
            if k not in deps or deps[k][1] < v:
                deps[k] = (s, v)

        for b in r:
            if b.w is not None:
                need(*b.w)
        for b in w:
            if b.w is not None:
                need(*b.w)
            for kk, (s, v) in b.r.items():
                need(kk, s, v)
        for kk, (s, v) in deps.items():
            E.h.wait_ge(s, v)
            E.seen[kk] = v

    @staticmethod
    def _mark(ev, r, w):
        kk, s, v = ev
        for b in r:
            b.r[kk] = (s, v)
        for b in w:
            b.w = ev
            b.r = {}

    def op(self, eng, fn, r=(), w=()):
        E = self.engs[eng]
        self._deps(E, r, w)
        ins = fn(E.h)
        E.count += 1
        ins.then_inc(E.sem, 1)
        self._mark((E.key, E.sem, E.count), r, w)
        return ins

    def _dsem(self, sbuf, cls):
        if sbuf.dsem is None:
            sbuf.dsem = {}
        if cls not in sbuf.dsem:
            pool = self.free_dsems[cls]
            if pool:
                ent = pool.pop()
            else:
                self.nsem += 1
                sem = self.stack.enter_context(self.nc.semaphore("d%s%d" % (cls, self.nsem)))
                ent = [sem, ("d", self.nsem), 0]
            sbuf.dsem[cls] = ent
            self.dma_bufs.append((sbuf, cls))
        return sbuf.dsem[cls]

    def dma(self, q, out, in_, r=(), w=(), sem_buf=None, **kw):
        E = self.engs[q]
        self._deps(E, r, w)
        sbuf = sem_buf or (w[0] if w else r[0])
        ent = self._dsem(sbuf, "sw" if q == "pool" else "hw")
        ins = E.h.dma_start(out=out, in_=in_, **kw)
        ent[2] += 16
        ins.then_inc(ent[0], 16)
        self._mark((ent[1], ent[0], ent[2]), r, w)
        return ins

    def coll(self, kind, in_ap, out_ap, groups, r=(), w=()):
        E = self.engs["pool"]
        self._deps(E, r, w)
        ins = E.h.collective_compute(kind, ALU.bypass, replica_groups=groups, ins=[in_ap], outs=[out_ap])
        self.ccnt += 1
        ins.then_inc(self.csem, 1)
        self._mark((("c", 0), self.csem, self.ccnt), r, w)
        return ins

    def barrier(self, release=True):
        for E in self.engs.values():
            for Fg in self.engs.values():
                if Fg.count == 0:
                    continue
                if Fg is E and E.name == "pe":
                    continue
                if E.seen.get(Fg.key, 0) < Fg.count:
                    E.h.wait_ge(Fg.sem, Fg.count)
                    E.seen[Fg.key] = Fg.count
            for b, cls in self.dma_bufs:
                sem, dkey, cnt = b.dsem[cls]
                if cnt and E.seen.get(dkey, 0) < cnt:
                    E.h.wait_ge(sem, cnt)
                    E.seen[dkey] = cnt
            if self.ccnt and E.seen.get(("c", 0), 0) < self.ccnt:
                E.h.wait_ge(self.csem, self.ccnt)
                E.seen[("c", 0)] = self.ccnt
        if any(E.count > SEM_ROTATE for E in self.engs.values()):
            parts = list(self.engs.values())
            for rnd in range(2):
                self.hcnt += len(parts)
                for E in parts:
                    E.h.sem_inc(self.hsem, 1)
                for E in parts:
                    E.h.wait_ge(self.hsem, self.hcnt)
                if rnd == 0:
                    for E in parts:
                        if E.sem is not None and E.count > 0:
                            E.h.sem_clear(E.sem)
                            E.gen += 1
                            E.key = ("e", E.name, E.gen)
                            E.count = 0
        if release:
            for b, cls in self.dma_bufs:
                self.free_dsems[cls].append(b.dsem.pop(cls))
                b.w = None
                b.r = {}
            self.dma_bufs = []

    def close(self):
        self.stack.close()


def _rel_bucket_np(dist):
    n = np.maximum(dist, 0)
    nf = np.maximum(n, 1).astype(np.float32)
    large = 16 + (np.log(nf / np.float32(16)) / np.float32(math.log(128 / 16)) * np.float32(16)).astype(np.int32)
    large = np.minimum(large, 31)
    return np.where(n < 16, n, large)


def host_consts():
    c = {}
    i = np.arange(128)
    c["ident_f"] = np.eye(128, dtype=np.float32)
    c["ident_b"] = np.eye(128, dtype=np.float32).astype(NPBF)
    c["ones_b"] = np.ones((128, 128), np.float32).astype(NPBF)
    c["negu_b"] = (-(i[:, None] >= i[None, :]).astype(np.float32)).astype(NPBF)
    c["negones_b"] = (-np.ones((128, 2), np.float32)).astype(NPBF)
    c["mask_sb"] = (i[:, None] < i[None, :]).astype(np.float32).astype(NPBF)
    return c


class Ctx:
    pass


def load_consts(k, cdram, names, stack=None):
    cx = Ctx()
    for nm in names:
        ap = cdram[nm]
        t = k.sb("sc_" + nm, list(ap.shape), ap.dtype, stack)
        b = k.buf("c_" + nm)
        k.dma("sp", t[:], ap, w=[b])
        setattr(cx, nm, t)
        setattr(cx, nm + "_b", b)
    return cx


def rmsnorm_chunk(k, cx, W, h_aps, h_buf, gcol, gcol_b, out_aps, out_buf, n=512):
    sq, sqb, ps, psb, rs, rsb = W["sq"], W["sq_b"], W["ps_n"], W["ps_n_b"], W["rs"], W["rs_b"]
    for fc in range(FC):
        k.op("act", lambda e, fc=fc: e.activation(out=sq[:, fc, :n], in_=h_aps[fc], func=AF.Square),
             r=[h_buf], w=[sqb])
    for fc in range(FC):
        k.op("pe", lambda e, fc=fc: e.matmul(ps[:, :n], lhsT=cx.ones_b[:], rhs=sq[:, fc, :n],
                                              start=(fc == 0), stop=(fc == FC - 1)),
             r=[sqb, cx.ones_b_b], w=[psb])
    k.op("act", lambda e: e.activation(out=rs[:, :n], in_=ps[:, :n], func=AF.Sqrt, bias=cx.eps_col[:, 0:1], scale=1.0 / D),
         r=[psb, cx.eps_col_b], w=[rsb])
    k.op("dve", lambda e: e.reciprocal(out=rs[:, :n], in_=rs[:, :n]), r=[rsb], w=[rsb])
    for fc in range(FC):
        k.op("dve", lambda e, fc=fc: e.scalar_tensor_tensor(out=out_aps[fc], in0=h_aps[fc], scalar=gcol[:, fc:fc + 1],
                                                            in1=rs[:, :n], op0=ALU.mult, op1=ALU.mult),
             r=[h_buf, gcol_b, rsb], w=[out_buf])


def norm_work(k, stack=None):
    W = {}
    W["sq"] = k.sb("n_sq", [128, FC, 512], BF16, stack)
    W["sq_b"] = k.buf("n_sq")
    W["ps_n"] = k.ps("n_ps", [128, 512], F32, stack)
    W["ps_n_b"] = k.buf("n_ps")
    W["rs"] = k.sb("n_rs", [128, 512], F32, stack)
    W["rs_b"] = k.buf("n_rs")
    return W


def make_eps(k, cx, stack=None):
    cx.eps_col = k.sb("eps_col", [128, 1], F32, stack)
    cx.eps_col_b = k.buf("eps")
    k.op("pool", lambda e: e.memset(cx.eps_col[:], EPS), w=[cx.eps_col_b])


def load_gain_cols(k, gain_ap, name, stack=None):
    t = k.sb(name, [128, FC], F32, stack)
    b = k.buf(name)
    k.dma("sp", t[:], gain_ap.rearrange("(fc p) -> p fc", p=128), w=[b], allow_slow_non_contiguous=True)
    return t, b


def phase_norm_out(k, cx, hT, hb, gain_ap, dst, dst_b, stack):
    W = norm_work(k, stack)
    gcol, gcol_b = load_gain_cols(k, gain_ap, "g_mix", stack)
    hn = [k.sb("hn%d" % i, [128, FC, 512], BF16, stack) for i in range(2)]
    hnb = k.bufs(2, "hn")
    dv = dst.rearrange("(fc p) t -> p fc t", p=128)
    for c in range(4):
        cs = slice(c * 512, (c + 1) * 512)
        rmsnorm_chunk(k, cx, W, [hT[:, fc, cs] for fc in range(FC)], hb[c], gcol, gcol_b,
                      [hn[c % 2][:, fc, :] for fc in range(FC)], hnb[c % 2])
        k.dma("sp", dv[:, :, cs], hn[c % 2][:], r=[hnb[c % 2]], w=[dst_b])


def phase_sb(k, cx, hn_all, hn_all_b, w_ap, a2a, a2a_b, stack, nsb=16):
    scale = 0.125
    QT = [k.sb("QT%d" % i, [128, S], BF16, stack) for i in range(2)]
    KT = [k.sb("KT%d" % i, [128, S], BF16, stack) for i in range(2)]
    V = k.sb("V", [128, 64, 256], BF16, stack)
    qkb = k.buf("qkv")
    ps = [k.ps("ps%d" % i, [128, 512], F32, stack) for i in range(8)]
    psb = k.bufs(8, "ps")
    pst = contextlib.ExitStack()
    wq = k.sb("sb_w", [128, FC, 768], BF16, pst)
    wqb = k.buf("sb_w")
    wv = w_ap.rearrange("(fc p) n -> p fc n", p=128)
    for fc in range(FC):
        k.dma("pool", wq[:, fc, :], wv[:, fc, :], w=[wqb])
    hnc = [k.sb("hnc%d" % i, [128, FC, 512], BF16, pst) for i in range(2)]
    hncb = k.bufs(2, "hnc")
    for c in range(16):
        rho, off = c // 4, (c % 4) * 512
        src = hn_all.rearrange("(q r h p) t -> r p q h t", q=4, r=4, h=2, p=128)[rho][:, :, :, off:off + 512]
        hb_ = hncb[c % 2]
        for q_ in range(4):
            k.dma("sp", hnc[c % 2][:, 2 * q_:2 * q_ + 2, :], src[:, q_, :, :], r=[hn_all_b], w=[hb_])
        x = hnc[c % 2]
        cs = slice(c * 512, (c + 1) * 512)
        j = 0
        for which, dstT in ((0, QT), (1, KT)):
            for hp in range(2):
                p_, pb_ = ps[j % 4], psb[j % 4]
                j += 1
                for fc in range(FC):
                    k.op("pe", lambda e, fc=fc, p_=p_, which=which, hp=hp: e.matmul(
                        p_[:, :], lhsT=wq[:, fc, which * 256 + hp * 128: which * 256 + (hp + 1) * 128],
                        rhs=x[:, fc, :], start=(fc == 0), stop=(fc == FC - 1)), r=[wqb, hb_], w=[pb_])
                if which == 0:
                    k.op("act", lambda e, p_=p_, hp=hp: e.activation(out=QT[hp][:, cs], in_=p_[:, :], func=AF.Copy, scale=scale),
                         r=[pb_], w=[qkb])
                else:
                    k.op("dve", lambda e, p_=p_, hp=hp: e.tensor_copy(out=KT[hp][:, cs], in_=p_[:, :]), r=[pb_], w=[qkb])
        for tt in range(4):
            p_, pb_ = ps[4 + tt % 2], psb[4 + tt % 2]
            for fc in range(FC):
                k.op("pe", lambda e, fc=fc, p_=p_, tt=tt: e.matmul(
                    p_[:, 0:256], lhsT=x[:, fc, tt * 128:(tt + 1) * 128], rhs=wq[:, fc, 512:768],
                    start=(fc == 0), stop=(fc == FC - 1)), r=[wqb, hb_], w=[pb_])
            k.op("dve" if tt % 2 else "act",
                 (lambda e, p_=p_, tt=tt: e.tensor_copy(out=V[:, c * 4 + tt, :], in_=p_[:, 0:256])) if tt % 2 else
                 (lambda e, p_=p_, tt=tt: e.copy(out=V[:, c * 4 + tt, :], in_=p_[:, 0:256])),
                 r=[pb_], w=[qkb])
    k.barrier()
    pst.close()
    e1 = [k.sb("e1_%d" % i, [128, 512], F32, stack) for i in range(2)]
    e1b = k.bufs(2, "e1")
    sp = [k.sb("sp_%d" % i, [128, 512], BF16, stack) for i in range(3)]
    spb = k.bufs(3, "sp")
    aa = [k.sb("aa_%d" % i, [128, 512], BF16, stack) for i in range(2)]
    aab = k.bufs(2, "aa")
    acc = k.sb("acc", [128, 4, 256], F32, stack)
    accb = k.bufs(4, "acc")
    Cc = k.sb("Cc", [128, 4, 4], F32, stack)
    Ccb = k.bufs(4, "Cc")
    eC = [k.sb("eC%d" % i, [128, 4], F32, stack) for i in range(2)]
    eCb = k.bufs(2, "eC")
    oT = [k.sb("oT%d" % i, [128, 2, 512], BF16, stack) for i in range(2)]
    oTb = k.bufs(2, "oT")
    psA, psAb = ps[0:2], psb[0:2]
    psB, psBb = ps[2:4], psb[2:4]
    psO, psOb = ps[4:6], psb[4:6]
    psT, psTb = ps[6:8], psb[6:8]
    units = []
    for sbk in range(nsb):
        for hh in range(4):
            nu = 4 * sbk + 4
            for u in range(nu):
                kb = 4 * sbk + 3 - u
                units.append(dict(sbk=sbk, hh=hh, kb=kb, i0=max(0, kb - 4 * sbk), diag=kb >= 4 * sbk,
                                  first=(u == 0), last_sb=(hh == 3 and u == nu - 1)))
    for n_, U in enumerate(units):
        U["n"] = n_
        hp, base = U["hh"] // 2, 64 * (U["hh"] % 2)
        U["N"] = 512 - 128 * U["i0"]
        U["qc"] = QT[hp][base:base + 64, 512 * U["sbk"] + 128 * U["i0"]: 512 * (U["sbk"] + 1)]
        U["kc"] = KT[hp][base:base + 64, 128 * U["kb"]:128 * (U["kb"] + 1)]

    def stage1a(U):
        n_, N = U["n"], U["N"]
        A, Ab = psA[n_ % 2], psAb[n_ % 2]
        k.op("pe", lambda e: e.matmul(A[:, :N], lhsT=U["kc"], rhs=U["qc"], start=True, stop=True), r=[qkb], w=[Ab])
        k.op("act", lambda e: e.activation(out=e1[n_ % 2][:, :N], in_=A[:, :N], func=AF.Exp), r=[Ab], w=[e1b[n_ % 2]])

    def stage1b(U):
        n_, N = U["n"], U["N"]
        s_, sb_ = sp[n_ % 3], spb[n_ % 3]
        k.op("act", lambda e: e.activation(out=s_[:, :N], in_=e1[n_ % 2][:, :N], func=AF.Ln, bias=cx.one_col[:, 0:1], scale=1.0),
             r=[e1b[n_ % 2], cx.one_col_b], w=[sb_])
        if U["diag"]:
            k.op("pool", lambda e: e.tensor_tensor(out=s_[:, 0:128], in0=s_[:, 0:128], in1=cx.mask_sb[:], op=ALU.mult),
                 r=[sb_, cx.mask_sb_b], w=[sb_])

    def stage2(U):
        n_, N = U["n"], U["N"]
        s_, sb_ = sp[n_ % 3], spb[n_ % 3]
        Bp, Bb = psB[n_ % 2], psBb[n_ % 2]
        a_, ab_ = aa[n_ % 2], aab[n_ % 2]
        k.op("pe", lambda e: e.matmul(Bp[:, :N], lhsT=cx.negu_b[:], rhs=s_[:, :N], start=True, stop=False),
             r=[sb_, cx.negu_b_b], w=[Bb])
        k.op("pe", lambda e: e.matmul(Bp[:, :N], lhsT=U["kc"], rhs=U["qc"], start=False, stop=True), r=[qkb], w=[Bb])
        k.op("act", lambda e: e.activation(out=a_[:, :N], in_=Bp[:, :N], func=AF.Exp), r=[Bb], w=[ab_])
        if U["diag"]:
            k.op("pool", lambda e: e.tensor_tensor(out=a_[:, 0:128], in0=a_[:, 0:128], in1=cx.mask_sb[:], op=ALU.mult),
                 r=[ab_, cx.mask_sb_b], w=[ab_])

    def stage3(U):
        n_, hh, i0, kb, sbk = U["n"], U["hh"], U["i0"], U["kb"], U["sbk"]
        s_, sb_ = sp[n_ % 3], spb[n_ % 3]
        a_, ab_ = aa[n_ % 2], aab[n_ % 2]
        O, Ob = psO[n_ % 2], psOb[n_ % 2]
        x2 = n_ % 2
        if U["first"]:
            k.op("pool", lambda e: e.memset(acc[:, :, hh * 64:(hh + 1) * 64], 0.0), w=[accb[hh]])
            k.op("pool", lambda e: e.memset(Cc[:, hh, :], 0.0), w=[Ccb[hh]])
        for i in range(i0, 4):
            cl = slice((i - i0) * 128, (i - i0 + 1) * 128)
            k.op("pe", lambda e, i=i, cl=cl: e.matmul(O[:, i * 65:i * 65 + 64], lhsT=a_[:, cl],
                                                      rhs=V[:, kb, hh * 64:(hh + 1) * 64], start=True, stop=True),
                 r=[ab_, qkb], w=[Ob])
            k.op("pe", lambda e, i=i, cl=cl: e.matmul(O[:, i * 65 + 64:i * 65 + 65], lhsT=s_[:, cl],
                                                      rhs=cx.negones_b[:, 0:1], start=True, stop=True),
                 r=[sb_, cx.negones_b_b], w=[Ob])
        k.op("act", lambda e: e.activation(out=eC[x2][:, :], in_=Cc[:, hh, :], func=AF.Exp), r=[Ccb[hh]], w=[eCb[x2]])
        for i in range(i0, 4):
            k.op("dve", lambda e, i=i: e.scalar_tensor_tensor(
                out=acc[:, i, hh * 64:(hh + 1) * 64], in0=O[:, i * 65:i * 65 + 64], scalar=eC[x2][:, i:i + 1],
                in1=acc[:, i, hh * 64:(hh + 1) * 64], op0=ALU.mult, op1=ALU.add),
                r=[Ob, eCb[x2], accb[hh]], w=[accb[hh]])
        Ov = O[:, 0:260].rearrange("p (i c) -> p i c", c=65)
        k.op("dve", lambda e: e.tensor_tensor(out=Cc[:, hh, i0:4], in0=Cc[:, hh, i0:4], in1=Ov[:, i0:4, 64], op=ALU.add),
             r=[Ob, Ccb[hh]], w=[Ccb[hh]])
        if U["last_sb"]:
            y2 = sbk % 2
            for i in range(4):
                for fh in range(2):
                    T_, Tb_ = psT[(i * 2 + fh) % 2], psTb[(i * 2 + fh) % 2]
                    k.op("pe", lambda e, i=i, fh=fh, T_=T_: e.transpose(out=T_[:, 0:128], in_=acc[:, i, fh * 128:(fh + 1) * 128], identity=cx.ident_f[:]),
                         r=accb + [cx.ident_f_b], w=[Tb_])
                    k.op("dve", lambda e, i=i, fh=fh, T_=T_: e.tensor_copy(out=oT[y2][:, fh, i * 128:(i + 1) * 128], in_=T_[:, 0:128]),
                         r=[Tb_], w=[oTb[y2]])
            dest, off = sbk // 4, (sbk % 4) * 512
            k.dma("sp", a2a[dest].rearrange("(fh p) t -> p fh t", p=128)[:, :, off:off + 512], oT[y2][:], r=[oTb[y2]], w=[a2a_b])

    nun = len(units)
    for it in range(nun + 2):
        if it < nun:
            stage1a(units[it])
        if 0 <= it - 1 < nun:
            stage2(units[it - 1])
        if it < nun:
            stage1b(units[it])
        if 0 <= it - 2 < nun:
            stage3(units[it - 2])
            if units[it - 2]["last_sb"] and k.engs["pe"].count > SEM_ROTATE:
                k.barrier()


def make_one(k, cx, stack=None):
    cx.one_col = k.sb("one_col", [128, 1], F32, stack)
    cx.one_col_b = k.buf("one")
    k.op("pool", lambda e: e.memset(cx.one_col[:], 1.0), w=[cx.one_col_b])


def dram_consts(nc, consts, names):
    out = {}
    for nm in names:
        a = consts[nm]
        dt_ = BF16 if a.dtype == NPBF else F32
        out[nm] = nc.dram_tensor("c_" + nm, list(a.shape), dt_, kind="ExternalInput").ap()
    return out


def build_prog_norm0():
    nc = bass.Bass("TRN2", target_bir_lowering=False)
    consts = host_consts()
    names = ["ones_b"]
    cd = dram_consts(nc, consts, names)
    xT = nc.dram_tensor("xT", [D, TOK], F32, kind="ExternalInput").ap()
    gain = nc.dram_tensor("gain", [D], F32, kind="ExternalInput").ap()
    hn = nc.dram_tensor("hnT", [D, TOK], BF16, kind="ExternalOutput").ap()
    k = KB(nc)
    cx = load_consts(k, cd, names)
    make_eps(k, cx)
    hT = k.sb("hT", [128, FC, TOK], F32)
    hb = k.bufs(4, "hT")
    xv = xT.rearrange("(fc p) t -> p fc t", p=128)
    for c in range(4):
        k.dma("sp", hT[:, :, c * 512:(c + 1) * 512], xv[:, :, c * 512:(c + 1) * 512], w=[hb[c]])
    hn_b = k.buf("hn_dram")
    phase_norm_out(k, cx, hT, hb, gain, hn, hn_b, k.stack)
    k.barrier()
    k.close()
    return nc, {nm: consts[nm] for nm in names}


def build_prog_sb(nsb=16):
    nc = bass.Bass("TRN2", target_bir_lowering=False)
    consts = host_consts()
    names = ["ident_f", "negu_b", "negones_b", "mask_sb"]
    cd = dram_consts(nc, consts, names)
    hn_all = nc.dram_tensor("hn_all", [4 * D, TOK], BF16, kind="ExternalInput").ap()
    w = nc.dram_tensor("sb_w", [D, 768], F32, kind="ExternalInput").ap()
    a2a = nc.dram_tensor("a2a", [4, 256, TOK], BF16, kind="ExternalOutput").ap()
    k = KB(nc)
    cx = load_consts(k, cd, names)
    make_one(k, cx)
    phase_sb(k, cx, hn_all, k.buf("hn_all"), w, a2a, k.buf("a2a"), k.stack, nsb=nsb)
    k.barrier()
    k.close()
    return nc, {nm: consts[nm] for nm in names}


def load_w_cast(k, name, src_view, shape, stack, nsplit=None):
    t = k.sb(name, shape, BF16, stack)
    b = k.buf(name)
    a = shape[1]
    for i in range(a):
        k.dma("pool", t[:, i, :], src_view[:, i, :], w=[b])
    return t, b


def phase_tail(k, cx, hT, hb, oT_dram, oT_b, w_out_ap, g_ffn_ap, w1_ap, w2_ap, g_ple_ap, wg_ap, wp_ap, pT_ap, oT_loader=None):
    ps_names = ["tp%d" % i for i in range(6)]
    with contextlib.ExitStack() as st0:
        ps = [k.ps(n_, [128, 512], F32, st0) for n_ in ps_names]
        psb = k.bufs(6, "tp")
        W = norm_work(k, st0)
        with contextlib.ExitStack() as st:
            wo, wob = load_w_cast(k, "wo", w_out_ap.rearrange("(fc p) n -> p fc n", p=128), [128, FC, D], st)
            oc = [k.sb("oc%d" % i, [128, FC, 512], BF16, st) for i in range(2)]
            ocb = k.bufs(2, "oc")
            ov = oT_dram.rearrange("(fc p) t -> p fc t", p=128) if oT_loader is None else None
            n_ = 0
            for c in range(4):
                cs = slice(c * 512, (c + 1) * 512)
                if oT_loader is None:
                    k.dma("sp", oc[c % 2][:], ov[:, :, cs], r=[oT_b], w=[ocb[c % 2]])
                else:
                    oT_loader(oc[c % 2], ocb[c % 2], cs)
                for of in range(FC):
                    p_, pb_ = ps[n_ % 4], psb[n_ % 4]
                    n_ += 1
                    for fc in range(FC):
                        k.op("pe", lambda e, fc=fc, of=of, p_=p_, c=c: e.matmul(
                            p_[:, :], lhsT=wo[:, fc, of * 128:(of + 1) * 128], rhs=oc[c % 2][:, fc, :],
                            start=(fc == 0), stop=(fc == FC - 1)), r=[wob, ocb[c % 2]], w=[pb_])
                    k.op("dve", lambda e, of=of, p_=p_, cs=cs: e.tensor_tensor(out=hT[:, of, cs], in0=hT[:, of, cs], in1=p_[:, :], op=ALU.add),
                         r=[pb_, hb[c]], w=[hb[c]])
            k.barrier()
        with contextlib.ExitStack() as st:
            gcol, gcol_b = load_gain_cols(k, g_ffn_ap, "g_ffn", st)
            hn = k.sb("f_hn", [128, FC, 1024], BF16, st)
            hnb = k.bufs(2, "f_hn")
            gT = k.sb("f_gT", [128, NJ, 1024], BF16, st)
            gTb = k.buf("f_gT")
            wab = [k.sb("f_wab%d" % i, [128, FC, 256], BF16, st) for i in range(2)]
            wabb = k.bufs(2, "f_wab")
            w2t = [k.sb("f_w2%d" % i, [128, NJ, 128], BF16, st) for i in range(2)]
            w2b = k.bufs(2, "f_w2")
            sl = [k.sb("f_sl%d" % i, [128, 512], F32, st) for i in range(2)]
            slb = k.bufs(2, "f_sl")
            w1v = w1_ap.rearrange("(fc p) n -> p fc n", p=128)
            w2v = w2_ap.rearrange("(j p) n -> p j n", p=128)
            n_ = 0
            nw = 0
            nw2 = 0
            for tc in range(2):
                for hf in range(2):
                    c = tc * 2 + hf
                    cs = slice(c * 512, (c + 1) * 512)
                    rmsnorm_chunk(k, cx, W, [hT[:, fc, cs] for fc in range(FC)], hb[c], gcol, gcol_b,
                                  [hn[:, fc, hf * 512:(hf + 1) * 512] for fc in range(FC)], hnb[hf])
                for j in range(NJ):
                    wt, wtb = wab[nw % 2], wabb[nw % 2]
                    nw += 1
                    for fc in range(FC):
                        k.dma("pool", wt[:, fc, 0:128], w1v[:, fc, j * 128:(j + 1) * 128], w=[wtb])
                        k.dma("pool", wt[:, fc, 128:256], w1v[:, fc, DFF + j * 128:DFF + (j + 1) * 128], w=[wtb])
                    for hf in range(2):
                        hs = slice(hf * 512, (hf + 1) * 512)
                        pa, pab = ps[n_ % 2], psb[n_ % 2]
                        pb2, pbb = ps[2 + n_ % 2], psb[2 + n_ % 2]
                        s_, sb_ = sl[n_ % 2], slb[n_ % 2]
                        n_ += 1
                        for fc in range(FC):
                            k.op("pe", lambda e, fc=fc, pa=pa, wt=wt, hs=hs: e.matmul(pa[:, :], lhsT=wt[:, fc, 0:128], rhs=hn[:, fc, hs],
                                                                                      start=(fc == 0), stop=(fc == FC - 1)), r=[wtb, hnb[hf]], w=[pab])
                        for fc in range(FC):
                            k.op("pe", lambda e, fc=fc, pb2=pb2, wt=wt, hs=hs: e.matmul(pb2[:, :], lhsT=wt[:, fc, 128:256], rhs=hn[:, fc, hs],
                                                                                        start=(fc == 0), stop=(fc == FC - 1)), r=[wtb, hnb[hf]], w=[pbb])
                        k.op("act", lambda e, pa=pa, s_=s_: e.activation(out=s_[:, :], in_=pa[:, :], func=AF.Silu), r=[pab], w=[sb_])
                        k.op("dve", lambda e, pb2=pb2, s_=s_, j=j, hs=hs: e.tensor_tensor(out=gT[:, j, hs], in0=s_[:, :], in1=pb2[:, :], op=ALU.mult),
                             r=[sb_, pbb], w=[gTb])
                for of in range(FC):
                    wt, wtb = w2t[nw2 % 2], w2b[nw2 % 2]
                    nw2 += 1
                    for j in range(NJ):
                        k.dma("pool", wt[:, j, :], w2v[:, j, of * 128:(of + 1) * 128], w=[wtb])
                    for hf in range(2):
                        c = tc * 2 + hf
                        cs = slice(c * 512, (c + 1) * 512)
                        p_, pb_ = ps[4 + n_ % 2], psb[4 + n_ % 2]
                        n_ += 1
                        for j in range(NJ):
                            k.op("pe", lambda e, j=j, p_=p_, wt=wt, hf=hf: e.matmul(p_[:, :], lhsT=wt[:, j, :], rhs=gT[:, j, hf * 512:(hf + 1) * 512],
                                                                                    start=(j == 0), stop=(j == NJ - 1)), r=[wtb, gTb], w=[pb_])
                        k.op("dve", lambda e, of=of, p_=p_, cs=cs: e.tensor_tensor(out=hT[:, of, cs], in0=hT[:, of, cs], in1=p_[:, :], op=ALU.add),
                             r=[pb_, hb[c]], w=[hb[c]])
            k.barrier()
        with contextlib.ExitStack() as st:
            gcol, gcol_b = load_gain_cols(k, g_ple_ap, "g_ple", st)
            wg, wgb = load_w_cast(k, "wg", wg_ap.rearrange("(fc p) n -> p fc n", p=128), [128, FC, D], st)
            wp, wpb = load_w_cast(k, "wp", wp_ap.rearrange("(kc p) n -> p kc n", p=128), [128, 2, D], st)
            hn = [k.sb("p_hn%d" % i, [128, FC, 512], BF16, st) for i in range(2)]
            hnb = k.bufs(2, "p_hn")
            pt = [k.sb("p_pt%d" % i, [128, 2, 512], BF16, st) for i in range(2)]
            ptb = k.bufs(2, "p_pt")
            sg = [k.sb("p_sg%d" % i, [128, 512], F32, st) for i in range(2)]
            sgb = k.bufs(2, "p_sg")
            pv = pT_ap.rearrange("(kc p) t -> p kc t", p=128)
            n_ = 0
            for c in range(4):
                cs = slice(c * 512, (c + 1) * 512)
                x_, xb_ = hn[c % 2], hnb[c % 2]
                rmsnorm_chunk(k, cx, W, [hT[:, fc, cs] for fc in range(FC)], hb[c], gcol, gcol_b,
                              [x_[:, fc, :] for fc in range(FC)], xb_)
                q_, qb_ = pt[c % 2], ptb[c % 2]
                for kc in range(2):
                    k.dma("pool", q_[:, kc, :], pv[:, kc, cs], w=[qb_])
                for of in range(FC):
                    pg, pgb = ps[n_ % 2], psb[n_ % 2]
                    pp, ppb = ps[2 + n_ % 2], psb[2 + n_ % 2]
                    s_, sb_ = sg[n_ % 2], sgb[n_ % 2]
                    n_ += 1
                    for fc in range(FC):
                        k.op("pe", lambda e, fc=fc, of=of, pg=pg, x_=x_: e.matmul(pg[:, :], lhsT=wg[:, fc, of * 128:(of + 1) * 128], rhs=x_[:, fc, :],
                                                                                  start=(fc == 0), stop=(fc == FC - 1)), r=[wgb, xb_], w=[pgb])
                    for kc in range(2):
                        k.op("pe", lambda e, kc=kc, of=of, pp=pp, q_=q_: e.matmul(pp[:, :], lhsT=wp[:, kc, of * 128:(of + 1) * 128], rhs=q_[:, kc, :],
                                                                                  start=(kc == 0), stop=(kc == 1)), r=[wpb, qb_], w=[ppb])
                    k.op("act", lambda e, pg=pg, s_=s_: e.activation(out=s_[:, :], in_=pg[:, :], func=AF.Sigmoid), r=[pgb], w=[sb_])
                    k.op("dve", lambda e, pp=pp, s_=s_: e.tensor_tensor(out=s_[:, :], in0=s_[:, :], in1=pp[:, :], op=ALU.mult),
                         r=[sb_, ppb], w=[sb_])
                    k.op("pool", lambda e, of=of, s_=s_, cs=cs: e.tensor_tensor(out=hT[:, of, cs], in0=hT[:, of, cs], in1=s_[:, :], op=ALU.add),
                         r=[sb_, hb[c]], w=[hb[c]])
            k.barrier()


def phase_final_norm(k, cx, hT, hb, gain_ap, dst, dst_b, stack):
    W = norm_work(k, stack)
    gcol, gcol_b = load_gain_cols(k, gain_ap, "g_fin", stack)
    on = [k.sb("fn%d" % i, [128, FC, 512], F32, stack) for i in range(2)]
    onb = k.bufs(2, "fn")
    dv = dst.rearrange("(fc p) t -> p fc t", p=128)
    for c in range(4):
        cs = slice(c * 512, (c + 1) * 512)
        rmsnorm_chunk(k, cx, W, [hT[:, fc, cs] for fc in range(FC)], hb[c], gcol, gcol_b,
                      [on[c % 2][:, fc, :] for fc in range(FC)], onb[c % 2])
        k.dma("sp", dv[:, :, cs], on[c % 2][:], r=[onb[c % 2]], w=[dst_b])


def build_prog_tail(final):
    nc = bass.Bass("TRN2", target_bir_lowering=False)
    consts = host_consts()
    names = ["ones_b"]
    cd = dram_consts(nc, consts, names)

    def din(nm, shape, dt_=F32):
        return nc.dram_tensor(nm, shape, dt_, kind="ExternalInput").ap()
    hin = din("hT_in", [D, TOK])
    oT = din("oT", [D, TOK], BF16)
    w_out = din("w_out", [D, D])
    g_ffn = din("g_ffn", [D])
    w1 = din("w1", [D, 2 * DFF])
    w2 = din("w2", [DFF, D])
    g_ple = din("g_ple", [D])
    wg = din("wg", [D, D])
    wp = din("wp", [256, D])
    pT = din("pT", [256, TOK])
    g_next = din("g_next", [D])
    k = KB(nc)
    cx = load_consts(k, cd, names)
    make_eps(k, cx)
    hT = k.sb("hT", [128, FC, TOK], F32)
    hb = k.bufs(4, "hT")
    xv = hin.rearrange("(fc p) t -> p fc t", p=128)
    for c in range(4):
        k.dma("sp", hT[:, :, c * 512:(c + 1) * 512], xv[:, :, c * 512:(c + 1) * 512], w=[hb[c]])
    phase_tail(k, cx, hT, hb, oT, k.buf("oT"), w_out, g_ffn, w1, w2, g_ple, wg, wp, pT)
    with contextlib.ExitStack() as st:
        if final:
            out = nc.dram_tensor("outT", [D, TOK], F32, kind="ExternalOutput").ap()
            phase_final_norm(k, cx, hT, hb, g_next, out, k.buf("outT"), st)
        else:
            hout = nc.dram_tensor("hT_out", [D, TOK], F32, kind="ExternalOutput").ap()
            hn = nc.dram_tensor("hnT", [D, TOK], BF16, kind="ExternalOutput").ap()
            hob = k.buf("hT_out")
            ov = hout.rearrange("(fc p) t -> p fc t", p=128)
            for c in range(4):
                k.dma("sp", ov[:, :, c * 512:(c + 1) * 512], hT[:, :, c * 512:(c + 1) * 512], r=[hb[c]], w=[hob])
            phase_norm_out(k, cx, hT, hb, g_next, hn, k.buf("hn_dram"), st)
        k.barrier()
    k.close()
    return nc, {nm: consts[nm] for nm in names}


def nsa_host_consts():
    c = {}
    i = np.arange(128)
    c["ident_f"] = np.eye(128, dtype=np.float32)
    c["jflip"] = np.ascontiguousarray(np.eye(128, dtype=np.float32)[::-1])
    dist = np.arange(NDC) - DOFF
    oh = np.zeros((33, NDC), np.float32)
    bk = _rel_bucket_np(dist)
    for ii in range(NDC):
        if dist[ii] < 0:
            oh[32, ii] = 1.0
        else:
            oh[bk[ii], ii] = 1.0
    c["ohc"] = oh
    dm = np.zeros((33, 33), np.float32)
    for b in range(32):
        dm[b, b] += 1.0
        dm[31, b] -= 1.0
    dm[32, 32] = NEG
    c["dm"] = dm
    keys = np.arange(S)
    c["ex"] = (np.arange(128)[:, None] == (keys[None, :] // 64)).astype(np.float32).astype(NPBF)
    t4 = np.where(i[None, :] < i[:, None], 0.0, NEG).astype(np.float32)
    c["t4"] = np.ascontiguousarray(np.tile(t4, (1, 4)))
    n = np.arange(512)
    cs_, ss_ = n * 16, np.arange(128) * 64
    ov = ((cs_[:, None] < ss_[None, :] + 64) & (cs_[:, None] + 32 > ss_[None, :])).astype(np.float32)
    ov[511, :] = 0.0
    c["ov"] = ov.astype(NPBF)
    q = np.arange(128)
    cq = (q >= 64).astype(np.int64)[:, None]
    rel = (np.arange(256) - 128)[None, :]
    forced = (rel == cq) | (rel == cq - 1)
    causal = rel <= cq
    c["mc_rel"] = (causal & ~forced).astype(np.float32)
    c["ma_rel"] = np.where(forced, 100.0, np.where(causal, 0.0, -1.0)).astype(np.float32)
    return c


GELU_C = 1.5957691216057308


def phase_nsa(k, cx, hn_all, hn_all_b, w_ap, rb_ap, peT_ap, w1_ap, w2k_ap, w2v_ap, bdz, a2a, a2a_b, stack, nqt=64, stop=99):
    scale = 0.125
    ps = [k.ps("np%d" % i, [128, 512], F32, stack) for i in range(8)]
    psb = k.bufs(8, "np")
    QT = [k.sb("nQT%d" % i, [128, S], BF16, stack) for i in range(2)]
    KST = k.sb("nKST", [128, S], BF16, stack)
    KWT = k.sb("nKWT", [128, S], BF16, stack)
    VS = k.sb("nVS", [128, 64, 65], BF16, stack)
    VW = k.sb("nVW", [128, 64, 65], BF16, stack)
    G = k.sb("nG", [128, 64, 12], F32, stack)
    kcT = k.sb("nkcT", [128, 512], BF16, stack)
    VCX = k.sb("nVCX", [128, 4, 193], BF16, stack)
    T0 = k.sb("nT0", [128, 512], F32, stack)
    T1 = k.sb("nT1", [128, 512], F32, stack)
    pb = k.buf("nsa_persist")
    k.op("pool", lambda e: e.memset(VS[:, :, 64:65], 1.0), w=[pb])
    k.op("pool", lambda e: e.memset(VW[:, :, 64:65], 1.0), w=[pb])
    k.op("pool", lambda e: e.memset(VCX[:], 0.0), w=[pb])
    k.op("pool", lambda e: e.memset(VCX[:, :, 64:65], 1.0), w=[pb])
    k.op("pool", lambda e: e.memset(kcT[:], 0.0), w=[pb])
    k.dma("sp", VCX[:, :, 65:193], cx.ov_dram.rearrange("(nb p) s -> p nb s", p=128), w=[pb])
    with contextlib.ExitStack() as st:
        rbe = k.sb("rbe", [33, 4], F32, st)
        rbeb = k.buf("rbe")
        k.op("pool", lambda e: e.memset(rbe[32:33, :], 1.0), w=[rbeb])
        k.dma("sp", rbe[0:32, :], rb_ap, w=[rbeb])
        ohc = k.sb("ohc", [33, NDC], F32, st)
        ohcb = k.buf("ohc")
        k.dma("sp", ohc[:], cx.ohc_dram, w=[ohcb])
        dm = k.sb("dm", [33, 33], F32, st)
        dmb = k.buf("dm")
        k.dma("sp", dm[:], cx.dm_dram, w=[dmb])
        rbx = k.sb("rbx", [33, 4], F32, st)
        rbxb = k.buf("rbx")
        k.op("pe", lambda e: e.matmul(ps[0][0:33, 0:4], lhsT=dm[:], rhs=rbe[:], start=True, stop=True), r=[dmb, rbeb], w=[psb[0]])
        k.op("dve", lambda e: e.tensor_copy(out=rbx[:], in_=ps[0][0:33, 0:4]), r=[psb[0]], w=[rbxb])
        bds = k.sb("bds", [4, NDC], F32, st)
        bdsb = k.buf("bds")
        nchunk = (NDC + 511) // 512
        for ci in range(nchunk):
            lo, hi = ci * 512, min(NDC, (ci + 1) * 512)
            p_, pb_ = ps[1 + ci % 2], psb[1 + ci % 2]
            k.op("pe", lambda e, p_=p_, lo=lo, hi=hi: e.matmul(p_[0:4, 0:hi - lo], lhsT=rbx[:], rhs=ohc[:, lo:hi], start=True, stop=True),
                 r=[rbxb, ohcb], w=[pb_])
            k.op("dve", lambda e, p_=p_, lo=lo, hi=hi: e.tensor_copy(out=bds[:, lo:hi], in_=p_[0:4, 0:hi - lo]), r=[pb_], w=[bdsb])
        bdzb = k.buf("bdz")
        k.dma("sp", bdz, bds[:], r=[bdsb], w=[bdzb])
        cx.bdz_b = bdzb
        U = k.sb("U", [128, 512], F32, st)
        Ub = k.buf("U")
        for m, Tm in ((0, T0), (1, T1)):
            src = bass.AP(bdz.tensor, DOFF - 127 + 128 * m, [[1, 128], [NDC, 4], [1, 128]])
            k.dma("sp", U[:].rearrange("p (g q) -> p g q", g=4), src, r=[bdzb], w=[Ub])
            k.op("pe", lambda e: e.matmul(ps[3][:, :], lhsT=cx.jflip[:], rhs=U[:], start=True, stop=True), r=[Ub, cx.jflip_b], w=[psb[3]])
            k.op("dve", lambda e, Tm=Tm: e.tensor_copy(out=Tm[:], in_=ps[3][:, :]), r=[psb[3]], w=[pb])
        k.barrier()
    if stop <= 0:
        return
    with contextlib.ExitStack() as st:
        w = k.sb("nw", [128, FC, 780], BF16, st)
        wb = k.buf("nw")
        wv = w_ap.rearrange("(fc p) n -> p fc n", p=128)
        for fc in range(FC):
            k.dma("pool", w[:, fc, :], wv[:, fc, :], w=[wb])
        KAT = k.sb("nKAT", [128, S], BF16, st)
        hnc = [k.sb("nhnc%d" % i, [128, FC, 512], BF16, st) for i in range(2)]
        hncb = k.bufs(2, "nhnc")
        for c in range(16):
            rho, off = c // 4, (c % 4) * 512
            src = hn_all.rearrange("(q r h p) t -> r p q h t", q=4, r=4, h=2, p=128)[rho][:, :, :, off:off + 512]
            hb_ = hncb[c % 2]
            for q_ in range(4):
                k.dma("sp", hnc[c % 2][:, 2 * q_:2 * q_ + 2, :], src[:, q_, :, :], r=[hn_all_b], w=[hb_])
            x = hnc[c % 2]
            cs = slice(c * 512, (c + 1) * 512)
            for j, (c0, dst, sc) in enumerate(((0, QT[0], scale), (128, QT[1], scale), (256, KAT, None), (384, KST, None), (512, KWT, None))):
                p_, pb_ = ps[j % 4], psb[j % 4]
                for fc in range(FC):
                    k.op("pe", lambda e, fc=fc, p_=p_, c0=c0: e.matmul(p_[:, :], lhsT=w[:, fc, c0:c0 + 128], rhs=x[:, fc, :],
                                                                         start=(fc == 0), stop=(fc == FC - 1)), r=[wb, hb_], w=[pb_])
                if sc is not None:
                    k.op("act", lambda e, p_=p_, dst=dst, sc=sc: e.activation(out=dst[:, cs], in_=p_[:, :], func=AF.Copy, scale=sc), r=[pb_], w=[pb])
                else:
                    k.op("dve", lambda e, p_=p_, dst=dst: e.tensor_copy(out=dst[:, cs], in_=p_[:, :]), r=[pb_], w=[pb])
            for tt in range(4):
                p_, pb_ = ps[4 + tt % 2], psb[4 + tt % 2]
                kb = c * 4 + tt
                for fc in range(FC):
                    k.op("pe", lambda e, fc=fc, p_=p_, tt=tt: e.matmul(p_[:, 0:140], lhsT=x[:, fc, tt * 128:(tt + 1) * 128], rhs=w[:, fc, 640:780],
                                                                         start=(fc == 0), stop=(fc == FC - 1)), r=[wb, hb_], w=[pb_])
                k.op("dve", lambda e, p_=p_, kb=kb: e.tensor_copy(out=VS[:, kb, 0:64], in_=p_[:, 0:64]), r=[pb_], w=[pb])
                k.op("dve", lambda e, p_=p_, kb=kb: e.tensor_copy(out=VW[:, kb, 0:64], in_=p_[:, 64:128]), r=[pb_], w=[pb])
                k.op("act", lambda e, p_=p_, kb=kb: e.activation(out=G[:, kb, :], in_=p_[:, 128:140], func=AF.Sigmoid), r=[pb_], w=[pb])
        if stop <= 1:
            k.barrier()
            return
        w1 = k.sb("nw1", [128, 32, 256], BF16, st)
        w1b = k.buf("nw1")
        for l in range(32):
            k.dma("pool", w1[:, l, :], w1_ap[:, l, :], w=[w1b])
        w2kd = k.sb("nw2k", [128, 2, 128], BF16, st)
        w2v = k.sb("nw2v", [128, 2, 64], BF16, st)
        w2b = k.buf("nw2")
        w2kv_ = w2k_ap.rearrange("(hc p) d -> p hc d", p=128)
        for hc in range(2):
            k.dma("pool", w2kd[:, hc, 0:64], w2kv_[:, hc, :], w=[w2b])
            k.dma("pool", w2kd[:, hc, 64:128], w2kv_[:, hc, :], w=[w2b])
            k.dma("pool", w2v[:, hc, :], w2v_ap.rearrange("(hc p) d -> p hc d", p=128)[:, hc, :], w=[w2b])
        peT = k.sb("npeT", [128, 32], BF16, st)
        peb = k.buf("npeT")
        k.dma("pool", peT[:], peT_ap, w=[peb])
        cb = k.sb("ncb", [128, 4], F32, st)
        cbb = k.buf("ncb")
        hid = k.sb("nhid", [128, 4, 512], BF16, st)
        hidb = k.buf("nhid")
        xs = k.sb("nxs", [128, 512], F32, st)
        x2 = k.sb("nx2", [128, 512], F32, st)
        xsb, x2b = k.buf("nxs"), k.buf("nx2")
        for kv in range(2):
            b0 = 64 * kv
            for hc in range(2):
                idx = kv * 2 + hc
                p_, pb_ = ps[idx % 2], psb[idx % 2]
                pc, pcb = ps[2 + idx % 2], psb[2 + idx % 2]
                for l in range(32):
                    k.op("pe", lambda e, l=l, pc=pc: e.matmul(pc[:, 0:1], lhsT=w1[b0:b0 + 64, l, hc * 128:(hc + 1) * 128], rhs=peT[b0:b0 + 64, l:l + 1],
                                                                start=(l == 0), stop=(l == 31)), r=[w1b, peb], w=[pcb])
                k.op("dve", lambda e, pc=pc, idx=idx: e.tensor_copy(out=cb[:, idx:idx + 1], in_=pc[:, 0:1]), r=[pcb], w=[cbb])
                for l in range(32):
                    k.op("pe", lambda e, l=l, p_=p_: e.matmul(p_[:, 0:511], lhsT=w1[b0:b0 + 64, l, hc * 128:(hc + 1) * 128],
                                                                rhs=KAT[b0:b0 + 64, l:l + 8161:16], start=(l == 0), stop=(l == 31)), r=[w1b, pb], w=[pb_])
                k.op("act", lambda e, p_=p_, idx=idx: e.activation(out=xs[:, 0:511], in_=p_[:, 0:511], func=AF.Identity, bias=cb[:, idx:idx + 1], scale=1.0),
                     r=[pb_, cbb], w=[xsb])
                k.op("act", lambda e, p_=p_, idx=idx: e.activation(out=x2[:, 0:511], in_=p_[:, 0:511], func=AF.Square, bias=cb[:, idx:idx + 1], scale=1.0),
                     r=[pb_, cbb], w=[x2b])
                k.op("dve", lambda e: e.tensor_scalar(out=x2[:, 0:511], in0=x2[:, 0:511], scalar1=0.044715, scalar2=1.0, op0=ALU.mult, op1=ALU.add),
                     r=[x2b], w=[x2b])
                k.op("dve", lambda e: e.tensor_tensor(out=x2[:, 0:511], in0=x2[:, 0:511], in1=xs[:, 0:511], op=ALU.mult), r=[x2b, xsb], w=[x2b])
                k.op("act", lambda e: e.activation(out=x2[:, 0:511], in_=x2[:, 0:511], func=AF.Sigmoid, scale=GELU_C), r=[x2b], w=[x2b])
                k.op("dve", lambda e, idx=idx: e.tensor_tensor(out=hid[:, idx, 0:511], in0=x2[:, 0:511], in1=xs[:, 0:511], op=ALU.mult),
                     r=[x2b, xsb], w=[hidb])
        for hc in range(2):
            k.op("pe", lambda e, hc=hc: e.matmul(ps[4][:, 0:511], lhsT=w2kd[:, hc, :], rhs=hid[:, hc, 0:511], start=(hc == 0), stop=(hc == 1)),
                 r=[w2b, hidb], w=[psb[4]])
        k.op("dve", lambda e: e.tensor_copy(out=kcT[:, 0:511], in_=ps[4][:, 0:511]), r=[psb[4]], w=[pb])
        for nb in range(4):
            M = 128 if nb < 3 else 127
            p_, pb_ = ps[5 + nb % 2], psb[5 + nb % 2]
            for hc in range(2):
                k.op("pe", lambda e, hc=hc, nb=nb, M=M, p_=p_: e.matmul(p_[0:M, 0:64], lhsT=hid[:, 2 + hc, nb * 128:nb * 128 + M], rhs=w2v[:, hc, :],
                                                                          start=(hc == 0), stop=(hc == 1)), r=[w2b, hidb], w=[pb_])
            k.op("dve", lambda e, nb=nb, M=M, p_=p_: e.tensor_copy(out=VCX[0:M, nb, 0:64], in_=p_[0:M, 0:64]), r=[pb_], w=[pb])
        k.barrier()
    if stop <= 2:
        return
    with contextlib.ExitStack() as st:
        psZ, psZb = ps[0:2], psb[0:2]
        psOc, psOcb = ps[2:4], psb[2:4]
        psOs, psOsb = ps[4], psb[4]
        psOw, psOwb = ps[5], psb[5]
        psM, psMb = ps[6:8], psb[6:8]
        Wc = [k.sb("nWc%d" % i, [128, 512], F32, st) for i in range(2)]
        Wcb = k.bufs(2, "nWc")
        ee = [k.sb("nee%d" % i, [128, 512], BF16, st) for i in range(3)]
        eeb = k.bufs(3, "nee")
        sf = [k.sb("nsf%d" % i, [128, 512], F32, st) for i in range(2)]
        sfb = k.bufs(2, "nsf")
        nmT = k.sb("nnmT", [128, 512], BF16, st)
        nmTb = k.buf("nnmT")
        imp = k.sb("nimp", [128, 128], F32, st)
        imp3 = k.sb("nimp3", [128, 128], F32, st)
        negm = k.sb("nnegm", [128, 128], F32, st)
        impb, imp3b, negmb = k.buf("imp"), k.buf("imp3"), k.buf("negm")
        m8 = k.sb("nm8", [128, 16], F32, st)
        m8b = k.buf("m8")
        dn = k.sb("ndn", [128, 12], F32, st)
        dnb = k.buf("dn")
        ot = k.sb("not", [128, 256], F32, st)
        otb = k.buf("ot")
        oT = [k.sb("noT%d" % i, [128, 2, 512], BF16, st) for i in range(2)]
        oTb = k.bufs(2, "noT")
        zc = 0
        ec = 0
        wcn = 0

        bd = [k.sb("nbd%d" % i, [128, 512], BF16, st) for i in range(2)]
        bdb = k.bufs(2, "nbd")
        for i in range(2):
            k.op("pool", lambda e, i=i: e.memset(bd[i][:], 0.0), w=[bdb[i]])
        cur = {}

        def qk(Z, Zb, KT_, kcols, tcols, first_start, extra_r=()):
            k.op("pe", lambda e: e.matmul(Z[:, :], lhsT=KT_[:, kcols], rhs=cur["bd"][:, :], start=first_start, stop=True, skip_group_check=True),
                 r=[pb, cur["bdb"]] + list(extra_r), w=[Zb])

        pend = []

        def flush():
            while pend:
                pend.pop(0)()

        def defer(fn):
            pend.append(fn)
            while len(pend) > 1:
                pend.pop(0)()

        for qt in range(nqt):
            tcols = slice(128 * qt, 128 * (qt + 1))
            cur["bd"], cur["bdb"] = bd[qt % 2], bdb[qt % 2]
            for g in range(4):
                b0 = 64 * (g % 2)
                k.op("pool", lambda e, g=g, b0=b0: e.tensor_copy(out=cur["bd"][b0:b0 + 64, g * 128:(g + 1) * 128], in_=QT[g // 2][b0:b0 + 64, tcols]),
                     r=[pb], w=[cur["bdb"]])
            NB = (8 * qt + 6) // 128 + 1
            for nb in range(NB):
                Z, Zb = psZ[zc % 2], psZb[zc % 2]
                zc += 1
                o_idx = qt - 16 * nb
                if o_idx <= 16:
                    W_, Wb_ = Wc[wcn % 2], Wcb[wcn % 2]
                    wcn += 1
                    src = bass.AP(bdz.tensor, 128 * o_idx, [[16, 128], [NDC, 4], [1, 128]])
                    import os
                    if os.environ.get("NSA_DBG") == "1":
                        k.op("pool", lambda e, W_=W_: e.memset(W_[:], 0.0), w=[Wb_])
                    else:
                        k.dma("sp", W_[:].rearrange("p (g q) -> p g q", g=4), src, r=[cx.bdz_b], w=[Wb_])
                    k.op("pe", lambda e, W_=W_: e.matmul(psM[0][:, :], lhsT=cx.jflip[:], rhs=W_[:], start=True, stop=True),
                         r=[Wb_, cx.jflip_b], w=[psMb[0]])
                    k.op("act", lambda e, W_=W_: e.copy(out=W_[:], in_=psM[0][:, :]), r=[psMb[0]], w=[Wb_])
                    qk(Z, Zb, kcT, slice(nb * 128, (nb + 1) * 128), tcols, True)
                    E_, Eb_ = ee[ec % 3], eeb[ec % 3]
                    ec += 1
                    s_, sb_ = sf[ec % 2], sfb[ec % 2]
                    k.op("dve", lambda e, s_=s_, Z=Z, W_=W_: e.tensor_tensor(out=s_[:], in0=Z[:, :], in1=W_[:], op=ALU.add), r=[Zb, Wb_], w=[sb_])
                    k.op("act", lambda e, E_=E_, s_=s_: e.activation(out=E_[:, :], in_=s_[:], func=AF.Exp), r=[sb_], w=[Eb_])
                else:
                    qk(Z, Zb, kcT, slice(nb * 128, (nb + 1) * 128), tcols, True)
                    E_, Eb_ = ee[ec % 3], eeb[ec % 3]
                    ec += 1
                    k.op("act", lambda e, E_=E_, Z=Z: e.activation(out=E_[:, :], in_=Z[:, :], func=AF.Exp), r=[Zb], w=[Eb_])
                def pv_c(E_=E_, Eb_=Eb_, nb=nb, NB=NB):
                    for g in range(4):
                        bank, bb = psOc[g // 2], psOcb[g // 2]
                        c0 = (g % 2) * 193
                        k.op("pe", lambda e, g=g, bank=bank, c0=c0: e.matmul(
                            bank[:, c0:c0 + 193], lhsT=E_[:, g * 128:(g + 1) * 128], rhs=VCX[:, nb, :],
                            start=(nb == 0 and g % 2 == 0), stop=(nb == NB - 1), skip_group_check=True), r=[Eb_, pb], w=[bb])
                defer(pv_c)
            flush()
            if stop <= 3:
                continue
            for h2 in range(2):
                bank, bb = psOc[h2], psOcb[h2]
                bv = bank[:, 0:386].rearrange("p (g c) -> p g c", c=193)
                k.op("dve", lambda e, h2=h2, bv=bv: e.tensor_scalar(out=dn[:, 2 * h2:2 * h2 + 2], in0=bv[:, :, 64], scalar1=1e-30, scalar2=None, op0=ALU.max),
                     r=[bb], w=[dnb])
            k.op("dve", lambda e: e.reciprocal(out=dn[:, 0:4], in_=dn[:, 0:4]), r=[dnb], w=[dnb])
            for g in range(4):
                bank, bb = psOc[g // 2], psOcb[g // 2]
                c0 = (g % 2) * 193 + 65
                if g == 0:
                    k.op("dve", lambda e, bank=bank, c0=c0: e.tensor_scalar(out=imp[:], in0=bank[:, c0:c0 + 128], scalar1=dn[:, 0:1], scalar2=None, op0=ALU.mult),
                         r=[bb, dnb], w=[impb])
                else:
                    k.op("dve", lambda e, g=g, bank=bank, c0=c0: e.scalar_tensor_tensor(out=imp[:], in0=bank[:, c0:c0 + 128], scalar=dn[:, g:g + 1], in1=imp[:],
                                                                                        op0=ALU.mult, op1=ALU.add), r=[bb, dnb, impb], w=[impb])
            sl = slice(128 - 2 * qt, 256 - 2 * qt)
            k.op("dve", lambda e: e.tensor_tensor(out=imp[:], in0=imp[:], in1=cx.mc_rel[:, sl], op=ALU.mult), r=[impb, cx.mc_rel_b], w=[impb])
            k.op("dve", lambda e: e.tensor_tensor(out=imp[:], in0=imp[:], in1=cx.ma_rel[:, sl], op=ALU.add), r=[impb, cx.ma_rel_b], w=[impb])
            k.op("dve", lambda e: e.memset(imp[:, 0:1], 100.0), r=[impb], w=[impb])
            k.op("dve", lambda e: e.max(out=m8[:, 0:8], in_=imp[:]), r=[impb], w=[m8b])
            k.op("dve", lambda e: e.match_replace(out=imp3[:], in_to_replace=m8[:, 0:8], in_values=imp[:], imm_value=-1e30), r=[impb, m8b], w=[imp3b])
            k.op("dve", lambda e: e.max(out=m8[:, 8:16], in_=imp3[:]), r=[imp3b], w=[m8b])
            k.op("dve", lambda e: e.tensor_scalar(out=negm[:], in0=imp[:], scalar1=m8[:, 15:16], scalar2=NEG, op0=ALU.is_lt, op1=ALU.mult),
                 r=[impb, m8b], w=[negmb])
            k.op("pe", lambda e: e.transpose(out=psM[0][:, 0:128], in_=negm[:], identity=cx.ident_f[:]), r=[negmb, cx.ident_f_b], w=[psMb[0]])
            for g in range(4):
                if g % 2 == 0:
                    k.op("dve", lambda e, g=g: e.tensor_copy(out=nmT[:, g * 128:(g + 1) * 128], in_=psM[0][:, 0:128]), r=[psMb[0]], w=[nmTb])
                else:
                    k.op("act", lambda e, g=g: e.copy(out=nmT[:, g * 128:(g + 1) * 128], in_=psM[0][:, 0:128]), r=[psMb[0]], w=[nmTb])
            if stop <= 4:
                continue
            for kb in range(qt + 1):
                Z, Zb = psZ[zc % 2], psZb[zc % 2]
                zc += 1
                m = qt - kb
                k.op("pe", lambda e, Z=Z, kb=kb: e.matmul(Z[:, :], lhsT=cx.ex[:, 128 * kb:128 * (kb + 1)], rhs=nmT[:], start=True, stop=False, skip_group_check=True),
                     r=[nmTb, cx.ex_b], w=[Zb])
                qk(Z, Zb, KST, slice(128 * kb, 128 * (kb + 1)), tcols, False)
                E_, Eb_ = ee[ec % 3], eeb[ec % 3]
                ec += 1
                if m <= 1:
                    Tm = T0 if m == 0 else T1
                    s_, sb_ = sf[ec % 2], sfb[ec % 2]
                    k.op("dve", lambda e, s_=s_, Z=Z, Tm=Tm: e.tensor_tensor(out=s_[:], in0=Z[:, :], in1=Tm[:], op=ALU.add), r=[Zb, pb], w=[sb_])
                    k.op("act", lambda e, E_=E_, s_=s_: e.activation(out=E_[:, :], in_=s_[:], func=AF.Exp), r=[sb_], w=[Eb_])
                else:
                    k.op("act", lambda e, E_=E_, Z=Z: e.activation(out=E_[:, :], in_=Z[:, :], func=AF.Exp), r=[Zb], w=[Eb_])
                def pv_s(E_=E_, Eb_=Eb_, kb=kb, qt=qt):
                    for g in range(4):
                        k.op("pe", lambda e, g=g: e.matmul(psOs[:, g * 65:(g + 1) * 65], lhsT=E_[:, g * 128:(g + 1) * 128], rhs=VS[:, kb, :],
                                                            start=(kb == 0 and g == 0), stop=(kb == qt), skip_group_check=True), r=[Eb_, pb], w=[psOsb])
                defer(pv_s)
            flush()
            if stop <= 5:
                continue
            kb0 = max(0, qt - 4)
            for kb in range(kb0, qt + 1):
                Z, Zb = psZ[zc % 2], psZb[zc % 2]
                zc += 1
                m = qt - kb
                qk(Z, Zb, KWT, slice(128 * kb, 128 * (kb + 1)), tcols, True)
                E_, Eb_ = ee[ec % 3], eeb[ec % 3]
                ec += 1
                if m in (0, 1, 4):
                    Tm, Tmb = {0: (T0, pb), 1: (T1, pb), 4: (cx.t4, cx.t4_b)}[m]
                    s_, sb_ = sf[ec % 2], sfb[ec % 2]
                    k.op("dve", lambda e, s_=s_, Z=Z, Tm=Tm: e.tensor_tensor(out=s_[:], in0=Z[:, :], in1=Tm[:], op=ALU.add), r=[Zb, Tmb], w=[sb_])
                    k.op("act", lambda e, E_=E_, s_=s_: e.activation(out=E_[:, :], in_=s_[:], func=AF.Exp), r=[sb_], w=[Eb_])
                else:
                    k.op("act", lambda e, E_=E_, Z=Z: e.activation(out=E_[:, :], in_=Z[:, :], func=AF.Exp), r=[Zb], w=[Eb_])
                def pv_w(E_=E_, Eb_=Eb_, kb=kb, qt=qt, kb0=kb0):
                    for g in range(4):
                        k.op("pe", lambda e, g=g: e.matmul(psOw[:, g * 65:(g + 1) * 65], lhsT=E_[:, g * 128:(g + 1) * 128], rhs=VW[:, kb, :],
                                                            start=(kb == kb0 and g == 0), stop=(kb == qt), skip_group_check=True), r=[Eb_, pb], w=[psOwb])
                defer(pv_w)
            flush()
            if stop <= 6:
                continue
            osv = psOs[:, 0:260].rearrange("p (g c) -> p g c", c=65)
            owv = psOw[:, 0:260].rearrange("p (g c) -> p g c", c=65)
            k.op("dve", lambda e: e.tensor_scalar(out=dn[:, 4:8], in0=osv[:, :, 64], scalar1=1e-30, scalar2=None, op0=ALU.max), r=[psOsb], w=[dnb])
            k.op("dve", lambda e: e.tensor_scalar(out=dn[:, 8:12], in0=owv[:, :, 64], scalar1=1e-30, scalar2=None, op0=ALU.max), r=[psOwb], w=[dnb])
            k.op("dve", lambda e: e.reciprocal(out=dn[:, 4:12], in_=dn[:, 4:12]), r=[dnb], w=[dnb])
            k.op("dve", lambda e: e.tensor_tensor(out=dn[:, :], in0=dn[:, :], in1=G[:, qt, :], op=ALU.mult), r=[dnb, pb], w=[dnb])
            for g in range(4):
                bank, bb = psOc[g // 2], psOcb[g // 2]
                c0 = (g % 2) * 193
                og = ot[:, g * 64:(g + 1) * 64]
                k.op("dve", lambda e, g=g, bank=bank, c0=c0, og=og: e.tensor_scalar(out=og, in0=bank[:, c0:c0 + 64], scalar1=dn[:, g:g + 1], scalar2=None, op0=ALU.mult),
                     r=[bb, dnb], w=[otb])
                k.op("dve", lambda e, g=g, og=og: e.scalar_tensor_tensor(out=og, in0=psOs[:, g * 65:g * 65 + 64], scalar=dn[:, 4 + g:5 + g], in1=og, op0=ALU.mult, op1=ALU.add),
                     r=[psOsb, dnb, otb], w=[otb])
                k.op("dve", lambda e, g=g, og=og: e.scalar_tensor_tensor(out=og, in0=psOw[:, g * 65:g * 65 + 64], scalar=dn[:, 8 + g:9 + g], in1=og, op0=ALU.mult, op1=ALU.add),
                     r=[psOwb, dnb, otb], w=[otb])
            y2 = (qt // 4) % 2
            for fh in range(2):
                k.op("pe", lambda e, fh=fh: e.transpose(out=psM[1][:, fh * 128:(fh + 1) * 128], in_=ot[:, fh * 128:(fh + 1) * 128], identity=cx.ident_f[:]),
                     r=[otb, cx.ident_f_b], w=[psMb[1]])
            k.op("act", lambda e, y2=y2: e.copy(out=oT[y2][:, :, (qt % 4) * 128:(qt % 4 + 1) * 128],
                                                in_=psM[1][:, 0:256].rearrange("p (fh t) -> p fh t", fh=2)), r=[psMb[1]], w=[oTb[y2]])
            if qt % 4 == 3:
                sbk = qt // 4
                dest, off = sbk // 4, (sbk % 4) * 512
                k.dma("sp", a2a[dest].rearrange("(fh p) t -> p fh t", p=128)[:, :, off:off + 512], oT[y2][:], r=[oTb[y2]], w=[a2a_b])
                if k.engs["pe"].count > SEM_ROTATE:
                    k.barrier()
        k.barrier()


def build_prog_nsa(nqt=64, stop=99):
    nc = bass.Bass("TRN2", target_bir_lowering=False)
    consts = nsa_host_consts()
    names = ["ident_f", "jflip", "ex", "t4", "mc_rel", "ma_rel"]
    dnames = ["ohc", "dm", "ov"]
    cd = dram_consts(nc, consts, names + dnames)

    def din(nm, shape, dt_=F32):
        return nc.dram_tensor(nm, shape, dt_, kind="ExternalInput").ap()
    hn_all = din("hn_all", [4 * D, TOK], BF16)
    w = din("nsa_w", [D, 780])
    rb = din("rb", [32, 4])
    peT = din("peT", [128, 32])
    w1 = din("cw1", [128, 32, 256])
    w2k = din("cw2k", [256, 64])
    w2v = din("cw2v", [256, 64])
    a2a = nc.dram_tensor("a2a", [4, 256, TOK], BF16, kind="ExternalOutput").ap()
    bdz = nc.dram_tensor("bdz", [4, NDC], F32, kind="Internal").ap()
    k = KB(nc)
    cx = load_consts(k, cd, names)
    for nm in dnames:
        setattr(cx, nm + "_dram", cd[nm])
    phase_nsa(k, cx, hn_all, k.buf("hn_all"), w, rb, peT, w1, w2k, w2v, bdz, a2a, k.buf("a2a"), k.stack, nqt=nqt, stop=stop)
    k.barrier()
    k.close()
    return nc, {nm: consts[nm] for nm in names + dnames}


def nsa_core_inputs(inp, r):
    w_in = inp["nsa_w_in"][0]
    kv0 = 1024

    def kvc(i):
        return w_in[:, kv0 + i * 256 + r * 64: kv0 + i * 256 + (r + 1) * 64]
    gcols = [2560 + j * 16 + r * 4 + g for j in range(3) for g in range(4)]
    w = np.concatenate([w_in[:, 256 * r:256 * (r + 1)], kvc(0), kvc(1), kvc(2), kvc(2), kvc(4), kvc(4), kvc(3), kvc(5), w_in[:, gcols]], axis=1)
    peT = np.concatenate([inp["nsa_pe_k"][0].T, inp["nsa_pe_v"][0].T], axis=0)
    w1k = inp["nsa_ck_w1"][0].reshape(32, 64, 256).transpose(1, 0, 2)
    w1v = inp["nsa_cv_w1"][0].reshape(32, 64, 256).transpose(1, 0, 2)
    return {"nsa_w": np.ascontiguousarray(w), "rb": np.ascontiguousarray(inp["rel_bias"][:, 4 * r:4 * r + 4]),
            "peT": np.ascontiguousarray(peT), "cw1": np.ascontiguousarray(np.concatenate([w1k, w1v], axis=0)),
            "cw2k": inp["nsa_ck_w2"][0], "cw2v": inp["nsa_cv_w2"][0]}


_PROGS = {}


def _prog(name, fn):
    if name not in _PROGS:
        _PROGS[name] = fn()
    return _PROGS[name]


def _run(name, fn, maps):
    nc, cst = _prog(name, fn)
    full = []
    for m in maps:
        mm = dict(m)
        mm.update({"c_" + kk: v for kk, v in cst.items()})
        full.append(mm)
    res = run_bass_kernel_spmd(nc, full, core_ids=list(range(NCORES)))
    return res.results


def _c(a):
    return np.ascontiguousarray(a)


def _allgather(parts):
    out = []
    for b in range(2):
        cat = np.concatenate([np.asarray(parts[4 * b + r])[256 * q:256 * (q + 1)] for q in range(4) for r in range(4)], axis=0)
        out += [cat] * 4
    return out


def _alltoall(parts):
    out = []
    for b in range(2):
        for j in range(4):
            out.append(_c(np.concatenate([np.asarray(parts[4 * b + i])[j] for i in range(4)], axis=0)))
    return out


def kernel_unfused(**inp):
    inp = {kk: np.asarray(v) for kk, v in inp.items()}
    x, p = inp["x"], inp["p"]
    cores = [(c // 4, c % 4) for c in range(NCORES)]
    tsl = [slice(TOK * r, TOK * (r + 1)) for (_, r) in cores]
    xT = [_c(x[b, tsl[c]].T) for c, (b, r) in enumerate(cores)]
    res = _run("norm0", build_prog_norm0, [{"xT": xT[c], "gain": _c(inp["norm_mix"][0])} for c in range(NCORES)])
    hn_all = _allgather([r_["hnT"] for r_ in res])
    w_in = inp["sb_w_in"][0]
    maps = []
    for c, (b, r) in enumerate(cores):
        wq = np.concatenate([w_in[:, 256 * r:256 * (r + 1)], w_in[:, 1024 + 256 * r:1024 + 256 * (r + 1)],
                             w_in[:, 2048 + 256 * r:2048 + 256 * (r + 1)]], axis=1)
        maps.append({"hn_all": hn_all[c], "sb_w": _c(wq)})
    res = _run("sb", build_prog_sb, maps)
    oT = _alltoall([r_["a2a"] for r_ in res])

    def tail_maps(i, hT_in, oT_, g_next):
        out = []
        for c, (b, r) in enumerate(cores):
            out.append({"hT_in": hT_in[c], "oT": oT_[c],
                        "w_out": _c((inp["sb_w_out"] if i == 0 else inp["nsa_w_out"])[0]),
                        "g_ffn": _c(inp["norm_ffn"][i]), "w1": _c(inp["ffn_w_in"][i]), "w2": _c(inp["ffn_w_out"][i]),
                        "g_ple": _c(inp["norm_ple"][i]), "wg": _c(inp["ple_w_gate"][i]), "wp": _c(inp["ple_w_proj"][i]),
                        "pT": _c(p[i, b, tsl[c]].T), "g_next": _c(g_next)})
        return out
    res = _run("tail0", lambda: build_prog_tail(False), tail_maps(0, xT, oT, inp["norm_mix"][1]))
    h1T = [np.asarray(r_["hT_out"]) for r_ in res]
    hn_all = _allgather([r_["hnT"] for r_ in res])
    maps = []
    for c, (b, r) in enumerate(cores):
        m = {"hn_all": hn_all[c]}
        m.update(nsa_core_inputs(inp, r))
        maps.append(m)
    res = _run("nsa", build_prog_nsa, maps)
    oT = _alltoall([r_["a2a"] for r_ in res])
    res = _run("tail1", lambda: build_prog_tail(True), tail_maps(1, h1T, oT, inp["final_norm"]))
    out = np.empty((2, S, D), np.float32)
    for c, (b, r) in enumerate(cores):
        out[b, tsl[c], :] = np.asarray(res[c]["outT"]).T
    return out


I32 = mybir.dt.int32
TAIL_KEYS = (("w_out", [D, D]), ("g_ffn", [D]), ("w1", [D, 2 * DFF]), ("w2", [DFF, D]), ("g_ple", [D]), ("wg", [D, D]), ("wp", [256, D]))


def build_prog_fused():
    import os
    CUT = int(os.environ.get("FUSED_CUT", "99"))
    nc = bass.Bass("TRN2", target_bir_lowering=False)
    consts = dict(host_consts())
    consts.update(nsa_host_consts())
    small = ["ident_f", "ones_b", "negu_b", "negones_b", "mask_sb", "jflip"]
    nsa_sb = ["ex", "t4", "mc_rel", "ma_rel"]
    nsa_dr = ["ohc", "dm", "ov"]
    cd = dram_consts(nc, consts, small + nsa_sb + nsa_dr)

    def din(nm, shape, dt_=F32):
        return nc.dram_tensor(nm, shape, dt_, kind="ExternalInput").ap()

    def dint(nm, shape, dt_):
        return nc.dram_tensor(nm, shape, dt_, kind="Internal").ap()
    xT = din("xT", [D, TOK])
    pT = [din("pT0", [256, TOK]), din("pT1", [256, TOK])]
    rk = din("rk", [1, 2], I32)
    g_mix = [din("g_mix0", [D]), din("g_mix1", [D])]
    g_fin = din("g_fin", [D])
    sb_w = din("sb_w", [D, 768])
    tails = [{nm: din("%s_%d" % (nm, i), shp) for nm, shp in TAIL_KEYS} for i in range(2)]
    nsa_w = din("nsa_w", [D, 780])
    rb = din("rb", [32, 4])
    peT = din("peT", [128, 32])
    cw1 = din("cw1", [128, 32, 256])
    cw2k = din("cw2k", [256, 64])
    cw2v = din("cw2v", [256, 64])
    outT = nc.dram_tensor("outT", [D, TOK], F32, kind="ExternalOutput").ap()
    hn_loc = dint("hn_loc", [D, TOK], BF16)
    hn_all = dint("hn_all", [4 * D, TOK], BF16)
    a2a_loc = dint("a2a_loc", [4, 256, TOK], BF16)
    a2a_all = dint("a2a_all", [4 * D, TOK], BF16)
    bdz = dint("bdz", [4, NDC], F32)
    hsp = dint("hsp", [D, TOK], F32)
    k = KB(nc)
    cx = load_consts(k, cd, small)
    for nm in nsa_dr:
        setattr(cx, nm + "_dram", cd[nm])
    make_eps(k, cx)
    make_one(k, cx)
    groups = [[0, 1, 2, 3], [4, 5, 6, 7]]
    spq = k.engs["sp"].h
    reg = spq.alloc_register("rk")
    spq.reg_load(reg, rk[0:1, 0:1])
    crk = spq.snap(reg, min_val=0, max_val=3)
    hn_loc_b, hn_all_b, a2a_loc_b, a2a_all_b, hsp_b, out_b = [k.buf(n_) for n_ in ("hn_loc", "hn_all", "a2a_loc", "a2a_all", "hsp", "outT")]
    g4 = a2a_all.rearrange("(j f) t -> j f t", j=4)

    def oT_loader(tile, tb, cs):
        src = g4[crk].rearrange("(fc p) t -> p fc t", p=128)
        k.dma("sp", tile[:], src[:, :, cs], r=[a2a_all_b], w=[tb])

    def gather_hn():
        for q in range(4):
            k.coll("AllGather", hn_loc[256 * q:256 * (q + 1), :], hn_all[1024 * q:1024 * (q + 1), :], groups, r=[hn_loc_b], w=[hn_all_b])

    def gather_o():
        for j in range(4):
            k.coll("AllGather", a2a_loc[j], a2a_all[1024 * j:1024 * (j + 1), :], groups, r=[a2a_loc_b], w=[a2a_all_b])

    def tail(i, hT, hb):
        t = tails[i]
        phase_tail(k, cx, hT, hb, None, a2a_all_b, t["w_out"], t["g_ffn"], t["w1"], t["w2"], t["g_ple"], t["wg"], t["wp"], pT[i],
                   oT_loader=oT_loader)

    hv = hsp.rearrange("(fc p) t -> p fc t", p=128)
    with contextlib.ExitStack() as stA:
        hT = k.sb("hT", [128, FC, TOK], F32, stA)
        hb = k.bufs(4, "hT")
        xv = xT.rearrange("(fc p) t -> p fc t", p=128)
        for c in range(4):
            k.dma("sp", hT[:, :, c * 512:(c + 1) * 512], xv[:, :, c * 512:(c + 1) * 512], w=[hb[c]])
        with contextlib.ExitStack() as st:
            phase_norm_out(k, cx, hT, hb, g_mix[0], hn_loc, hn_loc_b, st)
            k.barrier()
        gather_hn()
        if CUT >= 2:
            with contextlib.ExitStack() as st:
                phase_sb(k, cx, hn_all, hn_all_b, sb_w, a2a_loc, a2a_loc_b, st)
                k.barrier()
            gather_o()
        if CUT >= 3:
            tail(0, hT, hb)
        with contextlib.ExitStack() as st:
            phase_norm_out(k, cx, hT, hb, g_mix[1], hn_loc, hn_loc_b, st)
            for c in range(4):
                k.dma("sp", hv[:, :, c * 512:(c + 1) * 512], hT[:, :, c * 512:(c + 1) * 512], r=[hb[c]], w=[hsp_b])
            k.barrier()
    if CUT >= 4:
        gather_hn()
    with contextlib.ExitStack() as stB:
      if CUT >= 5:
        cxb = load_consts(k, cd, nsa_sb, stB)
        for nm in nsa_sb:
            setattr(cx, nm, getattr(cxb, nm))
            setattr(cx, nm + "_b", getattr(cxb, nm + "_b"))
        phase_nsa(k, cx, hn_all, hn_all_b, nsa_w, rb, peT, cw1, cw2k, cw2v, bdz, a2a_loc, a2a_loc_b, stB)
        k.barrier()
    if CUT >= 5:
        gather_o()
    with contextlib.ExitStack() as stC:
        hT = k.sb("hT2", [128, FC, TOK], F32, stC)
        hb = k.bufs(4, "hT2")
        for c in range(4):
            k.dma("sp", hT[:, :, c * 512:(c + 1) * 512], hv[:, :, c * 512:(c + 1) * 512], r=[hsp_b], w=[hb[c]])
        if CUT >= 6:
            tail(1, hT, hb)
        with contextlib.ExitStack() as st:
            phase_final_norm(k, cx, hT, hb, g_fin, outT, out_b, st)
            k.barrier()
    k.barrier()
    k.close()
    return nc, {nm: consts[nm] for nm in small + nsa_sb + nsa_dr}


def fused_maps(inp):
    x, p = inp["x"], inp["p"]
    maps = []
    w_in = inp["sb_w_in"][0]
    for c in range(NCORES):
        b, r = c // 4, c % 4
        ts = slice(TOK * r, TOK * (r + 1))
        m = {"xT": _c(x[b, ts].T), "pT0": _c(p[0, b, ts].T), "pT1": _c(p[1, b, ts].T),
             "rk": np.array([[r, 0]], np.int32),
             "g_mix0": _c(inp["norm_mix"][0]), "g_mix1": _c(inp["norm_mix"][1]), "g_fin": _c(inp["final_norm"]),
             "sb_w": _c(np.concatenate([w_in[:, 256 * r:256 * (r + 1)], w_in[:, 1024 + 256 * r:1024 + 256 * (r + 1)],
                                        w_in[:, 2048 + 256 * r:2048 + 256 * (r + 1)]], axis=1))}
        for i in range(2):
            m["w_out_%d" % i] = _c((inp["sb_w_out"] if i == 0 else inp["nsa_w_out"])[0])
            m["g_ffn_%d" % i] = _c(inp["norm_ffn"][i])
            m["w1_%d" % i] = _c(inp["ffn_w_in"][i])
            m["w2_%d" % i] = _c(inp["ffn_w_out"][i])
            m["g_ple_%d" % i] = _c(inp["norm_ple"][i])
            m["wg_%d" % i] = _c(inp["ple_w_gate"][i])
            m["wp_%d" % i] = _c(inp["ple_w_proj"][i])
        m.update(nsa_core_inputs(inp, r))
        maps.append(m)
    return maps


def kernel(**inp):
    inp = {kk: np.asarray(v) for kk, v in inp.items()}
    res = _run("fused", build_prog_fused, fused_maps(inp))
    out = np.empty((2, S, D), np.float32)
    for c in range(NCORES):
        b, r = c // 4, c % 4
        out[b, TOK * r:TOK * (r + 1), :] = np.asarray(res[c]["outT"]).T
    return out
```

```python
import contextlib
import math
import numpy as np
import ml_dtypes
import concourse.bass as bass
import concourse.mybir as mybir
from concourse.alu_op_type import AluOpType as ALU
from concourse.bass_utils import run_bass_kernel_spmd

F32 = mybir.dt.float32
BF16 = mybir.dt.bfloat16
AF = mybir.ActivationFunctionType
NPBF = ml_dtypes.bfloat16

NCORES = 8
S = 8192
D = 1024
TOK = 2048
FC = 8
DFF = 2816
NJ = DFF // 128
EPS = 1e-6
NEG = -30000.0
NDC = 4352
DOFF = 2063


class Buf:
    __slots__ = ("name", "w", "r", "dsem", "dcnt", "dkey")

    def __init__(self, name):
        self.name = name
        self.w = None
        self.r = {}
        self.dsem = None
        self.dcnt = 0
        self.dkey = None


class Eng:
    def __init__(self, name, h, sem):
        self.name = name
        self.gen = 0
        self.key = ("e", name, 0)
        self.h = h
        self.sem = sem
        self.count = 0
        self.seen = {}


SEM_ROTATE = 6000


class KB:
    def __init__(self, nc):
        self.nc = nc
        self.stack = contextlib.ExitStack()
        self.engs = {}
        for key, h in (("pe", nc.tensor), ("act", nc.scalar), ("dve", nc.vector),
                       ("pool", nc.gpsimd), ("sp", nc.sync)):
            sem = self.stack.enter_context(nc.semaphore("s_" + key)) if key != "sp" else None
            self.engs[key] = Eng(key, h, sem)
        self.csem = self.stack.enter_context(nc.semaphore("s_coll"))
        self.hsem = self.stack.enter_context(nc.semaphore("s_hand"))
        self.hcnt = 0
        self.dma_bufs = []
        self.nbuf = 0
        self.free_dsems = {"hw": [], "sw": []}
        self.nsem = 0
        self.ccnt = 0

    def sb(self, name, shape, dtype, stack=None):
        self.nbuf += 1
        return (stack or self.stack).enter_context(self.nc.sbuf_tensor("S%d_%s" % (self.nbuf, name), list(shape), dtype))

    def ps(self, name, shape, dtype, stack=None):
        self.nbuf += 1
        return (stack or self.stack).enter_context(self.nc.psum_tensor("P%d_%s" % (self.nbuf, name), list(shape), dtype))

    def buf(self, name=None):
        self.nbuf += 1
        return Buf("%s_%d" % (name or "b", self.nbuf))

    def bufs(self, n, name=None):
        return [self.buf(name) for _ in range(n)]

    def _deps(self, E, r, w):
        deps = {}

        def need(k, s, v):
            if k[0] == "e" and k[1] == "pe" and E.name == "pe":
                return
            if E.seen.get(k, 0) >= v:
                return
            if k not in deps or deps[k][1] < v:
                deps[k] = (s, v)

        for b in r:
            if b.w is not None:
                need(*b.w)
        for b in w:
            if b.w is not None:
                need(*b.w)
            for kk, (s, v) in b.r.items():
                need(kk, s, v)
        for kk, (s, v) in deps.items():
            E.h.wait_ge(s, v)
            E.seen[kk] = v

    @staticmethod
    def _mark(ev, r, w):
        kk, s, v = ev
        for b in r:
            b.r[kk] = (s, v)
        for b in w:
            b.w = ev
            b.r = {}

    def op(self, eng, fn, r=(), w=()):
        E = self.engs[eng]
        self._deps(E, r, w)
        ins = fn(E.h)
        E.count += 1
        ins.then_inc(E.sem, 1)
        self._mark((E.key, E.sem, E.count), r, w)
        return ins

    def _dsem(self, sbuf, cls):
        if sbuf.dsem is None:
            sbuf.dsem = {}
        if cls not in sbuf.dsem:
            pool = self.free_dsems[cls]
            if pool:
                ent = pool.pop()
            else:
                self.nsem += 1
                sem = self.stack.enter_context(self.nc.semaphore("d%s%d" % (cls, self.nsem)))
                ent = [sem, ("d", self.nsem), 0]
            sbuf.dsem[cls] = ent
            self.dma_bufs.append((sbuf, cls))
        return sbuf.dsem[cls]

    def dma(self, q, out, in_, r=(), w=(), sem_buf=None, **kw):
        E = self.engs[q]
        self._deps(E, r, w)
        sbuf = sem_buf or (w[0] if w else r[0])
        ent = self._dsem(sbuf, "sw" if q == "pool" else "hw")
        ins = E.h.dma_start(out=out, in_=in_, **kw)
        ent[2] += 16
        ins.then_inc(ent[0], 16)
        self._mark((ent[1], ent[0], ent[2]), r, w)
        return ins

    def coll(self, kind, in_ap, out_ap, groups, r=(), w=()):
        E = self.engs["pool"]
        self._deps(E, r, w)
        ins = E.h.collective_compute(kind, ALU.bypass, replica_groups=groups, ins=[in_ap], outs=[out_ap])
        self.ccnt += 1
        ins.then_inc(self.csem, 1)
        self._mark((("c", 0), self.csem, self.ccnt), r, w)
        return ins

    def barrier(self, release=True):
        for E in self.engs.values():
            for Fg in self.engs.values():
                if Fg.count == 0:
                    continue
                if Fg is E and E.name == "pe":
                    continue
                if E.seen.get(Fg.key, 0) < Fg.count:
                    E.h.wait_ge(Fg.sem, Fg.count)
                    E.seen[Fg.key] = Fg.count
            for b, cls in self.dma_bufs:
                sem, dkey, cnt = b.dsem[cls]
                if cnt and E.seen.get(dkey, 0) < cnt:
                    E.h.wait_ge(sem, cnt)
                    E.seen[dkey] = cnt
            if self.ccnt and E.seen.get(("c", 0), 0) < self.ccnt:
                E.h.wait_ge(self.csem, self.ccnt)
                E.seen[("c", 0)] = self.ccnt
        if any(E.count > SEM_ROTATE for E in self.engs.values()):
            parts = list(self.engs.values())
            for rnd in range(2):
                self.hcnt += len(parts)
                for E in parts:
                    E.h.sem_inc(self.hsem, 1)
                for E in parts:
                    E.h.wait_ge(self.hsem, self.hcnt)
                if rnd == 0:
                    for E in parts:
                        if E.sem is not None and E.count > 0:
                            E.h.sem_clear(E.sem)
                            E.gen += 1
                            E.key = ("e", E.name, E.gen)
                            E.count = 0
        if release:
            for b, cls in self.dma_bufs:
                self.free_dsems[cls].append(b.dsem.pop(cls))
                b.w = None
                b.r = {}
            self.dma_bufs = []

    def close(self):
        self.stack.close()


def _rel_bucket_np(dist):
    n = np.maximum(dist, 0)
    nf = np.maximum(n, 1).astype(np.float32)
    large = 16 + (np.log(nf / np.float32(16)) / np.float32(math.log(128 / 16)) * np.float32(16)).astype(np.int32)
    large = np.minimum(large, 31)
    return np.where(n < 16, n, large)


def host_consts():
    c = {}
    i = np.arange(128)
    c["ident_f"] = np.eye(128, dtype=np.float32)
    c["ident_b"] = np.eye(128, dtype=np.float32).astype(NPBF)
    c["ones_b"] = np.ones((128, 128), np.float32).astype(NPBF)
    c["negu_b"] = (-(i[:, None] >= i[None, :]).astype(np.float32)).astype(NPBF)
    c["negones_b"] = (-np.ones((128, 2), np.float32)).astype(NPBF)
    c["mask_sb"] = (i[:, None] < i[None, :]).astype(np.float32).astype(NPBF)
    return c


class Ctx:
    pass


def load_consts(k, cdram, names, stack=None):
    cx = Ctx()
    for nm in names:
        ap = cdram[nm]
        t = k.sb("sc_" + nm, list(ap.shape), ap.dtype, stack)
        b = k.buf("c_" + nm)
        k.dma("sp", t[:], ap, w=[b])
        setattr(cx, nm, t)
        setattr(cx, nm + "_b", b)
    return cx


def rmsnorm_chunk(k, cx, W, h_aps, h_buf, gcol, gcol_b, out_aps, out_buf, n=512):
    sq, sqb, ps, psb, rs, rsb = W["sq"], W["sq_b"], W["ps_n"], W["ps_n_b"], W["rs"], W["rs_b"]
    for fc in range(FC):
        k.op("act", lambda e, fc=fc: e.activation(out=sq[:, fc, :n], in_=h_aps[fc], func=AF.Square),
             r=[h_buf], w=[sqb])
    for fc in range(FC):
        k.op("pe", lambda e, fc=fc: e.matmul(ps[:, :n], lhsT=cx.ones_b[:], rhs=sq[:, fc, :n],
                                              start=(fc == 0), stop=(fc == FC - 1)),
             r=[sqb, cx.ones_b_b], w=[psb])
    k.op("act", lambda e: e.activation(out=rs[:, :n], in_=ps[:, :n], func=AF.Sqrt, bias=cx.eps_col[:, 0:1], scale=1.0 / D),
         r=[psb, cx.eps_col_b], w=[rsb])
    k.op("dve", lambda e: e.reciprocal(out=rs[:, :n], in_=rs[:, :n]), r=[rsb], w=[rsb])
    for fc in range(FC):
        k.op("dve", lambda e, fc=fc: e.scalar_tensor_tensor(out=out_aps[fc], in0=h_aps[fc], scalar=gcol[:, fc:fc + 1],
                                                            in1=rs[:, :n], op0=ALU.mult, op1=ALU.mult),
             r=[h_buf, gcol_b, rsb], w=[out_buf])


def norm_work(k, stack=None):
    W = {}
    W["sq"] = k.sb("n_sq", [128, FC, 512], BF16, stack)
    W["sq_b"] = k.buf("n_sq")
    W["ps_n"] = k.ps("n_ps", [128, 512], F32, stack)
    W["ps_n_b"] = k.buf("n_ps")
    W["rs"] = k.sb("n_rs", [128, 512], F32, stack)
    W["rs_b"] = k.buf("n_rs")
    return W


def make_eps(k, cx, stack=None):
    cx.eps_col = k.sb("eps_col", [128, 1], F32, stack)
    cx.eps_col_b = k.buf("eps")
    k.op("pool", lambda e: e.memset(cx.eps_col[:], EPS), w=[cx.eps_col_b])


def load_gain_cols(k, gain_ap, name, stack=None):
    t = k.sb(name, [128, FC], F32, stack)
    b = k.buf(name)
    k.dma("sp", t[:], gain_ap.rearrange("(fc p) -> p fc", p=128), w=[b], allow_slow_non_contiguous=True)
    return t, b


def phase_norm_out(k, cx, hT, hb, gain_ap, dst, dst_b, stack):
    W = norm_work(k, stack)
    gcol, gcol_b = load_gain_cols(k, gain_ap, "g_mix", stack)
    hn = [k.sb("hn%d" % i, [128, FC, 512], BF16, stack) for i in range(2)]
    hnb = k.bufs(2, "hn")
    dv = dst.rearrange("(fc p) t -> p fc t", p=128)
    for c in range(4):
        cs = slice(c * 512, (c + 1) * 512)
        rmsnorm_chunk(k, cx, W, [hT[:, fc, cs] for fc in range(FC)], hb[c], gcol, gcol_b,
                      [hn[c % 2][:, fc, :] for fc in range(FC)], hnb[c % 2])
        k.dma("sp", dv[:, :, cs], hn[c % 2][:], r=[hnb[c % 2]], w=[dst_b])


def phase_sb(k, cx, hn_all, hn_all_b, w_ap, a2a, a2a_b, stack, nsb=16):
    scale = 0.125
    QT = [k.sb("QT%d" % i, [128, S], BF16, stack) for i in range(2)]
    KT = [k.sb("KT%d" % i, [128, S], BF16, stack) for i in range(2)]
    V = k.sb("V", [128, 64, 256], BF16, stack)
    qkb = k.buf("qkv")
    ps = [k.ps("ps%d" % i, [128, 512], F32, stack) for i in range(8)]
    psb = k.bufs(8, "ps")
    pst = contextlib.ExitStack()
    wq = k.sb("sb_w", [128, FC, 768], BF16, pst)
    wqb = k.buf("sb_w")
    wv = w_ap.rearrange("(fc p) n -> p fc n", p=128)
    for fc in range(FC):
        k.dma("pool", wq[:, fc, :], wv[:, fc, :], w=[wqb])
    hnc = [k.sb("hnc%d" % i, [128, FC, 512], BF16, pst) for i in range(2)]
    hncb = k.bufs(2, "hnc")
    for c in range(16):
        rho, off = c // 4, (c % 4) * 512
        src = hn_all.rearrange("(q r h p) t -> r p q h t", q=4, r=4, h=2, p=128)[rho][:, :, :, off:off + 512]
        hb_ = hncb[c % 2]
        for q_ in range(4):
            k.dma("sp", hnc[c % 2][:, 2 * q_:2 * q_ + 2, :], src[:, q_, :, :], r=[hn_all_b], w=[hb_])
        x = hnc[c % 2]
        cs = slice(c * 512, (c + 1) * 512)
        j = 0
        for which, dstT in ((0, QT), (1, KT)):
            for hp in range(2):
                p_, pb_ = ps[j % 4], psb[j % 4]
                j += 1
                for fc in range(FC):
                    k.op("pe", lambda e, fc=fc, p_=p_, which=which, hp=hp: e.matmul(
                        p_[:, :], lhsT=wq[:, fc, which * 256 + hp * 128: which * 256 + (hp + 1) * 128],
                        rhs=x[:, fc, :], start=(fc == 0), stop=(fc == FC - 1)), r=[wqb, hb_], w=[pb_])
                if which == 0:
                    k.op("act", lambda e, p_=p_, hp=hp: e.activation(out=QT[hp][:, cs], in_=p_[:, :], func=AF.Copy, scale=scale),
                         r=[pb_], w=[qkb])
                else:
                    k.op("dve", lambda e, p_=p_, hp=hp: e.tensor_copy(out=KT[hp][:, cs], in_=p_[:, :]), r=[pb_], w=[qkb])
        for tt in range(4):
            p_, pb_ = ps[4 + tt % 2], psb[4 + tt % 2]
            for fc in range(FC):
                k.op("pe", lambda e, fc=fc, p_=p_, tt=tt: e.matmul(
                    p_[:, 0:256], lhsT=x[:, fc, tt * 128:(tt + 1) * 128], rhs=wq[:, fc, 512:768],
                    start=(fc == 0), stop=(fc == FC - 1)), r=[wqb, hb_], w=[pb_])
            k.op("dve" if tt % 2 else "act",
                 (lambda e, p_=p_, tt=tt: e.tensor_copy(out=V[:, c * 4 + tt, :], in_=p_[:, 0:256])) if tt % 2 else
                 (lambda e, p_=p_, tt=tt: e.copy(out=V[:, c * 4 + tt, :], in_=p_[:, 0:256])),
                 r=[pb_], w=[qkb])
    k.barrier()
    pst.close()
    e1 = [k.sb("e1_%d" % i, [128, 512], F32, stack) for i in range(2)]
    e1b = k.bufs(2, "e1")
    sp = [k.sb("sp_%d" % i, [128, 512], BF16, stack) for i in range(3)]
    spb = k.bufs(3, "sp")
    aa = [k.sb("aa_%d" % i, [128, 512], BF16, stack) for i in range(2)]
    aab = k.bufs(2, "aa")
    acc = k.sb("acc", [128, 4, 256], F32, stack)
    accb = k.bufs(4, "acc")
    Cc = k.sb("Cc", [128, 4, 4], F32, stack)
    Ccb = k.bufs(4, "Cc")
    eC = [k.sb("eC%d" % i, [128, 4], F32, stack) for i in range(2)]
    eCb = k.bufs(2, "eC")
    oT = [k.sb("oT%d" % i, [128, 2, 512], BF16, stack) for i in range(2)]
    oTb = k.bufs(2, "oT")
    psA, psAb = ps[0:2], psb[0:2]
    psB, psBb = ps[2:4], psb[2:4]
    psO, psOb = ps[4:6], psb[4:6]
    psT, psTb = ps[6:8], psb[6:8]
    units = []
    for sbk in range(nsb):
        for hh in range(4):
            nu = 4 * sbk + 4
            for u in range(nu):
                kb = 4 * sbk + 3 - u
                units.append(dict(sbk=sbk, hh=hh, kb=kb, i0=max(0, kb - 4 * sbk), diag=kb >= 4 * sbk,
                                  first=(u == 0), last_sb=(hh == 3 and u == nu - 1)))
    for n_, U in enumerate(units):
        U["n"] = n_
        hp, base = U["hh"] // 2, 64 * (U["hh"] % 2)
        U["N"] = 512 - 128 * U["i0"]
        U["qc"] = QT[hp][base:base + 64, 512 * U["sbk"] + 128 * U["i0"]: 512 * (U["sbk"] + 1)]
        U["kc"] = KT[hp][base:base + 64, 128 * U["kb"]:128 * (U["kb"] + 1)]

    def stage1a(U):
        n_, N = U["n"], U["N"]
        A, Ab = psA[n_ % 2], psAb[n_ % 2]
        k.op("pe", lambda e: e.matmul(A[:, :N], lhsT=U["kc"], rhs=U["qc"], start=True, stop=True), r=[qkb], w=[Ab])
        k.op("act", lambda e: e.activation(out=e1[n_ % 2][:, :N], in_=A[:, :N], func=AF.Exp), r=[Ab], w=[e1b[n_ % 2]])

    def stage1b(U):
        n_, N = U["n"], U["N"]
        s_, sb_ = sp[n_ % 3], spb[n_ % 3]
        k.op("act", lambda e: e.activation(out=s_[:, :N], in_=e1[n_ % 2][:, :N], func=AF.Ln, bias=cx.one_col[:, 0:1], scale=1.0),
             r=[e1b[n_ % 2], cx.one_col_b], w=[sb_])
        if U["diag"]:
            k.op("pool", lambda e: e.tensor_tensor(out=s_[:, 0:128], in0=s_[:, 0:128], in1=cx.mask_sb[:], op=ALU.mult),
                 r=[sb_, cx.mask_sb_b], w=[sb_])

    def stage2(U):
        n_, N = U["n"], U["N"]
        s_, sb_ = sp[n_ % 3], spb[n_ % 3]
        Bp, Bb = psB[n_ % 2], psBb[n_ % 2]
        a_, ab_ = aa[n_ % 2], aab[n_ % 2]
        k.op("pe", lambda e: e.matmul(Bp[:, :N], lhsT=cx.negu_b[:], rhs=s_[:, :N], start=True, stop=False),
             r=[sb_, cx.negu_b_b], w=[Bb])
        k.op("pe", lambda e: e.matmul(Bp[:, :N], lhsT=U["kc"], rhs=U["qc"], start=False, stop=True), r=[qkb], w=[Bb])
        k.op("act", lambda e: e.activation(out=a_[:, :N], in_=Bp[:, :N], func=AF.Exp), r=[Bb], w=[ab_])
        if U["diag"]:
            k.op("pool", lambda e: e.tensor_tensor(out=a_[:, 0:128], in0=a_[:, 0:128], in1=cx.mask_sb[:], op=ALU.mult),
                 r=[ab_, cx.mask_sb_b], w=[ab_])

    def stage3(U):
        n_, hh, i0, kb, sbk = U["n"], U["hh"], U["i0"], U["kb"], U["sbk"]
        s_, sb_ = sp[n_ % 3], spb[n_ % 3]
        a_, ab_ = aa[n_ % 2], aab[n_ % 2]
        O, Ob = psO[n_ % 2], psOb[n_ % 2]
        x2 = n_ % 2
        if U["first"]:
            k.op("pool", lambda e: e.memset(acc[:, :, hh * 64:(hh + 1) * 64], 0.0), w=[accb[hh]])
            k.op("pool", lambda e: e.memset(Cc[:, hh, :], 0.0), w=[Ccb[hh]])
        for i in range(i0, 4):
            cl = slice((i - i0) * 128, (i - i0 + 1) * 128)
            k.op("pe", lambda e, i=i, cl=cl: e.matmul(O[:, i * 65:i * 65 + 64], lhsT=a_[:, cl],
                                                      rhs=V[:, kb, hh * 64:(hh + 1) * 64], start=True, stop=True),
                 r=[ab_, qkb], w=[Ob])
            k.op("pe", lambda e, i=i, cl=cl: e.matmul(O[:, i * 65 + 64:i * 65 + 65], lhsT=s_[:, cl],
                                                      rhs=cx.negones_b[:, 0:1], start=True, stop=True),
                 r=[sb_, cx.negones_b_b], w=[Ob])
        k.op("act", lambda e: e.activation(out=eC[x2][:, :], in_=Cc[:, hh, :], func=AF.Exp), r=[Ccb[hh]], w=[eCb[x2]])
        for i in range(i0, 4):
            k.op("dve", lambda e, i=i: e.scalar_tensor_tensor(
                out=acc[:, i, hh * 64:(hh + 1) * 64], in0=O[:, i * 65:i * 65 + 64], scalar=eC[x2][:, i:i + 1],
                in1=acc[:, i, hh * 64:(hh + 1) * 64], op0=ALU.mult, op1=ALU.add),
                r=[Ob, eCb[x2], accb[hh]], w=[accb[hh]])
        Ov = O[:, 0:260].rearrange("p (i c) -> p i c", c=65)
        k.op("dve", lambda e: e.tensor_tensor(out=Cc[:, hh, i0:4], in0=Cc[:, hh, i0:4], in1=Ov[:, i0:4, 64], op=ALU.add),
             r=[Ob, Ccb[hh]], w=[Ccb[hh]])
        if U["last_sb"]:
            y2 = sbk % 2
            for i in range(4):
                for fh in range(2):
                    T_, Tb_ = psT[(i * 2 + fh) % 2], psTb[(i * 2 + fh) % 2]
                    k.op("pe", lambda e, i=i, fh=fh, T_=T_: e.transpose(out=T_[:, 0:128], in_=acc[:, i, fh * 128:(fh + 1) * 128], identity=cx.ident_f[:]),
                         r=accb + [cx.ident_f_b], w=[Tb_])
                    k.op("dve", lambda e, i=i, fh=fh, T_=T_: e.tensor_copy(out=oT[y2][:, fh, i * 128:(i + 1) * 128], in_=T_[:, 0:128]),
                         r=[Tb_], w=[oTb[y2]])
            dest, off = sbk // 4, (sbk % 4) * 512
            k.dma("sp", a2a[dest].rearrange("(fh p) t -> p fh t", p=128)[:, :, off:off + 512], oT[y2][:], r=[oTb[y2]], w=[a2a_b])

    nun = len(units)
    for it in range(nun + 2):
        if it < nun:
            stage1a(units[it])
        if 0 <= it - 1 < nun:
            stage2(units[it - 1])
        if it < nun:
            stage1b(units[it])
        if 0 <= it - 2 < nun:
            stage3(units[it - 2])
            if units[it - 2]["last_sb"] and k.engs["pe"].count > SEM_ROTATE:
                k.barrier()


def make_one(k, cx, stack=None):
    cx.one_col = k.sb("one_col", [128, 1], F32, stack)
    cx.one_col_b = k.buf("one")
    k.op("pool", lambda e: e.memset(cx.one_col[:], 1.0), w=[cx.one_col_b])


def dram_consts(nc, consts, names):
    out = {}
    for nm in names:
        a = consts[nm]
        dt_ = BF16 if a.dtype == NPBF else F32
        out[nm] = nc.dram_tensor("c_" + nm, list(a.shape), dt_, kind="ExternalInput").ap()
    return out


def build_prog_norm0():
    nc = bass.Bass("TRN2", target_bir_lowering=False)
    consts = host_consts()
    names = ["ones_b"]
    cd = dram_consts(nc, consts, names)
    xT = nc.dram_tensor("xT", [D, TOK], F32, kind="ExternalInput").ap()
    gain = nc.dram_tensor("gain", [D], F32, kind="ExternalInput").ap()
    hn = nc.dram_tensor("hnT", [D, TOK], BF16, kind="ExternalOutput").ap()
    k = KB(nc)
    cx = load_consts(k, cd, names)
    make_eps(k, cx)
    hT = k.sb("hT", [128, FC, TOK], F32)
    hb = k.bufs(4, "hT")
    xv = xT.rearrange("(fc p) t -> p fc t", p=128)
    for c in range(4):
        k.dma("sp", hT[:, :, c * 512:(c + 1) * 512], xv[:, :, c * 512:(c + 1) * 512], w=[hb[c]])
    hn_b = k.buf("hn_dram")
    phase_norm_out(k, cx, hT, hb, gain, hn, hn_b, k.stack)
    k.barrier()
    k.close()
    return nc, {nm: consts[nm] for nm in names}


def build_prog_sb(nsb=16):
    nc = bass.Bass("TRN2", target_bir_lowering=False)
    consts = host_consts()
    names = ["ident_f", "negu_b", "negones_b", "mask_sb"]
    cd = dram_consts(nc, consts, names)
    hn_all = nc.dram_tensor("hn_all", [4 * D, TOK], BF16, kind="ExternalInput").ap()
    w = nc.dram_tensor("sb_w", [D, 768], F32, kind="ExternalInput").ap()
    a2a = nc.dram_tensor("a2a", [4, 256, TOK], BF16, kind="ExternalOutput").ap()
    k = KB(nc)
    cx = load_consts(k, cd, names)
    make_one(k, cx)
    phase_sb(k, cx, hn_all, k.buf("hn_all"), w, a2a, k.buf("a2a"), k.stack, nsb=nsb)
    k.barrier()
    k.close()
    return nc, {nm: consts[nm] for nm in names}


def load_w_cast(k, name, src_view, shape, stack, nsplit=None):
    t = k.sb(name, shape, BF16, stack)
    b = k.buf(name)
    a = shape[1]
    for i in range(a):
        k.dma("pool", t[:, i, :], src_view[:, i, :], w=[b])
    return t, b


def tail_weights_to_bf16(k, nc, tag, w_out_ap, w1_ap, w2_ap, wg_ap, wp_ap):
    out = {}
    for nm, ap in (("w_out", w_out_ap), ("w1", w1_ap), ("w2", w2_ap), ("wg", wg_ap), ("wp", wp_ap)):
        rows, cols = ap.shape
        dst = nc.dram_tensor("wb_%s_%s" % (nm, tag), [rows, cols], BF16, kind="Internal").ap()
        b = k.buf("wb_" + nm)
        for r0 in range(0, rows, 256):
            r1 = min(rows, r0 + 256)
            k.dma("pool", dst[r0:r1, :], ap[r0:r1, :], w=[b])
        out[nm] = (dst, b)
    return out


def load_w_bf16(k, name, src_view, src_b, shape, stack):
    t = k.sb(name, shape, BF16, stack)
    b = k.buf(name)
    k.dma("sp", t[:], src_view, r=[src_b], w=[b])
    return t, b


def phase_tail(k, cx, hT, hb, oT_dram, oT_b, wb, g_ffn_ap, g_ple_ap, pT_ap, oT_loader=None):
    ps_names = ["tp%d" % i for i in range(6)]
    with contextlib.ExitStack() as st0:
        ps = [k.ps(n_, [128, 512], F32, st0) for n_ in ps_names]
        psb = k.bufs(6, "tp")
        W = norm_work(k, st0)
        with contextlib.ExitStack() as st:
            wo, wob = load_w_bf16(k, "wo", wb["w_out"][0].rearrange("(fc p) n -> p fc n", p=128), wb["w_out"][1], [128, FC, D], st)
            oc = [k.sb("oc%d" % i, [128, FC, 512], BF16, st) for i in range(2)]
            ocb = k.bufs(2, "oc")
            ov = oT_dram.rearrange("(fc p) t -> p fc t", p=128) if oT_loader is None else None
            n_ = 0
            for c in range(4):
                cs = slice(c * 512, (c + 1) * 512)
                if oT_loader is None:
                    k.dma("sp", oc[c % 2][:], ov[:, :, cs], r=[oT_b], w=[ocb[c % 2]])
                else:
                    oT_loader(oc[c % 2], ocb[c % 2], cs)
                for of in range(FC):
                    p_, pb_ = ps[n_ % 4], psb[n_ % 4]
                    n_ += 1
                    for fc in range(FC):
                        k.op("pe", lambda e, fc=fc, of=of, p_=p_, c=c: e.matmul(
                            p_[:, :], lhsT=wo[:, fc, of * 128:(of + 1) * 128], rhs=oc[c % 2][:, fc, :],
                            start=(fc == 0), stop=(fc == FC - 1)), r=[wob, ocb[c % 2]], w=[pb_])
                    k.op("dve", lambda e, of=of, p_=p_, cs=cs: e.tensor_tensor(out=hT[:, of, cs], in0=hT[:, of, cs], in1=p_[:, :], op=ALU.add),
                         r=[pb_, hb[c]], w=[hb[c]])
            k.barrier()
        with contextlib.ExitStack() as st:
            gcol, gcol_b = load_gain_cols(k, g_ffn_ap, "g_ffn", st)
            hn = k.sb("f_hn", [128, FC, 1024], BF16, st)
            hnb = k.bufs(2, "f_hn")
            gT = k.sb("f_gT", [128, NJ, 1024], BF16, st)
            gTb = k.buf("f_gT")
            wab = [k.sb("f_wab%d" % i, [128, FC, 2, 512], BF16, st) for i in range(2)]
            wabb = k.bufs(2, "f_wab")
            w2t = [k.sb("f_w2%d" % i, [128, NJ, 256], BF16, st) for i in range(2)]
            w2b = k.bufs(2, "f_w2")
            sl = [k.sb("f_sl%d" % i, [128, 512], F32, st) for i in range(2)]
            slb = k.bufs(2, "f_sl")
            w1v = wb["w1"][0].rearrange("(fc p) n -> p fc n", p=128)
            w2v = wb["w2"][0].rearrange("(j p) n -> p j n", p=128)
            jgroups = [(j0, min(4, NJ - j0)) for j0 in range(0, NJ, 4)]
            n_ = 0
            nw = 0
            nw2 = 0
            for tc in range(2):
                for hf in range(2):
                    c = tc * 2 + hf
                    cs = slice(c * 512, (c + 1) * 512)
                    rmsnorm_chunk(k, cx, W, [hT[:, fc, cs] for fc in range(FC)], hb[c], gcol, gcol_b,
                                  [hn[:, fc, hf * 512:(hf + 1) * 512] for fc in range(FC)], hnb[hf])
                for (j0, gs) in jgroups:
                    wt, wtb = wab[nw % 2], wabb[nw % 2]
                    nw += 1
                    k.dma("sp", wt[:, :, 0, 0:gs * 128], w1v[:, :, j0 * 128:(j0 + gs) * 128], r=[wb["w1"][1]], w=[wtb])
                    k.dma("sp", wt[:, :, 1, 0:gs * 128], w1v[:, :, DFF + j0 * 128:DFF + (j0 + gs) * 128], r=[wb["w1"][1]], w=[wtb])
                    for jj in range(gs):
                        j = j0 + jj
                        js = slice(jj * 128, (jj + 1) * 128)
                        for hf in range(2):
                            hs = slice(hf * 512, (hf + 1) * 512)
                            pa, pab = ps[n_ % 2], psb[n_ % 2]
                            pb2, pbb = ps[2 + n_ % 2], psb[2 + n_ % 2]
                            s_, sb_ = sl[n_ % 2], slb[n_ % 2]
                            n_ += 1
                            for fc in range(FC):
                                k.op("pe", lambda e, fc=fc, pa=pa, wt=wt, hs=hs, js=js: e.matmul(pa[:, :], lhsT=wt[:, fc, 0, js], rhs=hn[:, fc, hs],
                                                                                                 start=(fc == 0), stop=(fc == FC - 1)), r=[wtb, hnb[hf]], w=[pab])
                            for fc in range(FC):
                                k.op("pe", lambda e, fc=fc, pb2=pb2, wt=wt, hs=hs, js=js: e.matmul(pb2[:, :], lhsT=wt[:, fc, 1, js], rhs=hn[:, fc, hs],
                                                                                                   start=(fc == 0), stop=(fc == FC - 1)), r=[wtb, hnb[hf]], w=[pbb])
                            k.op("act", lambda e, pa=pa, s_=s_: e.activation(out=s_[:, :], in_=pa[:, :], func=AF.Silu), r=[pab], w=[sb_])
                            k.op("dve", lambda e, pb2=pb2, s_=s_, j=j, hs=hs: e.tensor_tensor(out=gT[:, j, hs], in0=s_[:, :], in1=pb2[:, :], op=ALU.mult),
                                 r=[sb_, pbb], w=[gTb])
                for op_ in range(FC // 2):
                    wt, wtb = w2t[nw2 % 2], w2b[nw2 % 2]
                    nw2 += 1
                    k.dma("sp", wt[:], w2v[:, :, op_ * 256:(op_ + 1) * 256], r=[wb["w2"][1]], w=[wtb])
                    for o2 in range(2):
                        of = op_ * 2 + o2
                        for hf in range(2):
                            c = tc * 2 + hf
                            cs = slice(c * 512, (c + 1) * 512)
                            p_, pb_ = ps[4 + n_ % 2], psb[4 + n_ % 2]
                            n_ += 1
                            for j in range(NJ):
                                k.op("pe", lambda e, j=j, p_=p_, wt=wt, hf=hf, o2=o2: e.matmul(p_[:, :], lhsT=wt[:, j, o2 * 128:(o2 + 1) * 128], rhs=gT[:, j, hf * 512:(hf + 1) * 512],
                                                                                               start=(j == 0), stop=(j == NJ - 1)), r=[wtb, gTb], w=[pb_])
                            k.op("dve", lambda e, of=of, p_=p_, cs=cs: e.tensor_tensor(out=hT[:, of, cs], in0=hT[:, of, cs], in1=p_[:, :], op=ALU.add),
                                 r=[pb_, hb[c]], w=[hb[c]])
            k.barrier()
        with contextlib.ExitStack() as st:
            gcol, gcol_b = load_gain_cols(k, g_ple_ap, "g_ple", st)
            wg, wgb = load_w_bf16(k, "wg", wb["wg"][0].rearrange("(fc p) n -> p fc n", p=128), wb["wg"][1], [128, FC, D], st)
            wp, wpb = load_w_bf16(k, "wp", wb["wp"][0].rearrange("(kc p) n -> p kc n", p=128), wb["wp"][1], [128, 2, D], st)
            hn = [k.sb("p_hn%d" % i, [128, FC, 512], BF16, st) for i in range(2)]
            hnb = k.bufs(2, "p_hn")
            pt = [k.sb("p_pt%d" % i, [128, 2, 512], BF16, st) for i in range(2)]
            ptb = k.bufs(2, "p_pt")
            sg = [k.sb("p_sg%d" % i, [128, 512], F32, st) for i in range(2)]
            sgb = k.bufs(2, "p_sg")
            pv = pT_ap.rearrange("(kc p) t -> p kc t", p=128)
            n_ = 0
            for c in range(4):
                cs = slice(c * 512, (c + 1) * 512)
                x_, xb_ = hn[c % 2], hnb[c % 2]
                rmsnorm_chunk(k, cx, W, [hT[:, fc, cs] for fc in range(FC)], hb[c], gcol, gcol_b,
                              [x_[:, fc, :] for fc in range(FC)], xb_)
                q_, qb_ = pt[c % 2], ptb[c % 2]
                for kc in range(2):
                    k.dma("pool", q_[:, kc, :], pv[:, kc, cs], w=[qb_])
                for of in range(FC):
                    pg, pgb = ps[n_ % 2], psb[n_ % 2]
                    pp, ppb = ps[2 + n_ % 2], psb[2 + n_ % 2]
                    s_, sb_ = sg[n_ % 2], sgb[n_ % 2]
                    n_ += 1
                    for fc in range(FC):
                        k.op("pe", lambda e, fc=fc, of=of, pg=pg, x_=x_: e.matmul(pg[:, :], lhsT=wg[:, fc, of * 128:(of + 1) * 128], rhs=x_[:, fc, :],
                                                                                  start=(fc == 0), stop=(fc == FC - 1)), r=[wgb, xb_], w=[pgb])
                    for kc in range(2):
                        k.op("pe", lambda e, kc=kc, of=of, pp=pp, q_=q_: e.matmul(pp[:, :], lhsT=wp[:, kc, of * 128:(of + 1) * 128], rhs=q_[:, kc, :],
                                                                                  start=(kc == 0), stop=(kc == 1)), r=[wpb, qb_], w=[ppb])
                    k.op("act", lambda e, pg=pg, s_=s_: e.activation(out=s_[:, :], in_=pg[:, :], func=AF.Sigmoid), r=[pgb], w=[sb_])
                    k.op("dve", lambda e, pp=pp, s_=s_: e.tensor_tensor(out=s_[:, :], in0=s_[:, :], in1=pp[:, :], op=ALU.mult),
                         r=[sb_, ppb], w=[sb_])
                    k.op("pool", lambda e, of=of, s_=s_, cs=cs: e.tensor_tensor(out=hT[:, of, cs], in0=hT[:, of, cs], in1=s_[:, :], op=ALU.add),
                         r=[sb_, hb[c]], w=[hb[c]])
            k.barrier()


def phase_final_norm(k, cx, hT, hb, gain_ap, dst, dst_b, stack):
    W = norm_work(k, stack)
    gcol, gcol_b = load_gain_cols(k, gain_ap, "g_fin", stack)
    on = [k.sb("fn%d" % i, [128, FC, 512], F32, stack) for i in range(2)]
    onb = k.bufs(2, "fn")
    dv = dst.rearrange("(fc p) t -> p fc t", p=128)
    for c in range(4):
        cs = slice(c * 512, (c + 1) * 512)
        rmsnorm_chunk(k, cx, W, [hT[:, fc, cs] for fc in range(FC)], hb[c], gcol, gcol_b,
                      [on[c % 2][:, fc, :] for fc in range(FC)], onb[c % 2])
        k.dma("sp", dv[:, :, cs], on[c % 2][:], r=[onb[c % 2]], w=[dst_b])


def build_prog_tail(final):
    nc = bass.Bass("TRN2", target_bir_lowering=False)
    consts = host_consts()
    names = ["ones_b"]
    cd = dram_consts(nc, consts, names)

    def din(nm, shape, dt_=F32):
        return nc.dram_tensor(nm, shape, dt_, kind="ExternalInput").ap()
    hin = din("hT_in", [D, TOK])
    oT = din("oT", [D, TOK], BF16)
    w_out = din("w_out", [D, D])
    g_ffn = din("g_ffn", [D])
    w1 = din("w1", [D, 2 * DFF])
    w2 = din("w2", [DFF, D])
    g_ple = din("g_ple", [D])
    wg = din("wg", [D, D])
    wp = din("wp", [256, D])
    pT = din("pT", [256, TOK])
    g_next = din("g_next", [D])
    k = KB(nc)
    cx = load_consts(k, cd, names)
    make_eps(k, cx)
    hT = k.sb("hT", [128, FC, TOK], F32)
    hb = k.bufs(4, "hT")
    xv = hin.rearrange("(fc p) t -> p fc t", p=128)
    for c in range(4):
        k.dma("sp", hT[:, :, c * 512:(c + 1) * 512], xv[:, :, c * 512:(c + 1) * 512], w=[hb[c]])
    wbf = tail_weights_to_bf16(k, nc, "u", w_out, w1, w2, wg, wp)
    phase_tail(k, cx, hT, hb, oT, k.buf("oT"), wbf, g_ffn, g_ple, pT)
    with contextlib.ExitStack() as st:
        if final:
            out = nc.dram_tensor("outT", [D, TOK], F32, kind="ExternalOutput").ap()
            phase_final_norm(k, cx, hT, hb, g_next, out, k.buf("outT"), st)
        else:
            hout = nc.dram_tensor("hT_out", [D, TOK], F32, kind="ExternalOutput").ap()
            hn = nc.dram_tensor("hnT", [D, TOK], BF16, kind="ExternalOutput").ap()
            hob = k.buf("hT_out")
            ov = hout.rearrange("(fc p) t -> p fc t", p=128)
            for c in range(4):
                k.dma("sp", ov[:, :, c * 512:(c + 1) * 512], hT[:, :, c * 512:(c + 1) * 512], r=[hb[c]], w=[hob])
            phase_norm_out(k, cx, hT, hb, g_next, hn, k.buf("hn_dram"), st)
        k.barrier()
    k.close()
    return nc, {nm: consts[nm] for nm in names}


def nsa_host_consts():
    c = {}
    i = np.arange(128)
    c["ident_f"] = np.eye(128, dtype=np.float32)
    c["jflip"] = np.ascontiguousarray(np.eye(128, dtype=np.float32)[::-1])
    dist = np.arange(NDC) - DOFF
    oh = np.zeros((33, NDC), np.float32)
    bk = _rel_bucket_np(dist)
    for ii in range(NDC):
        if dist[ii] < 0:
            oh[32, ii] = 1.0
        else:
            oh[bk[ii], ii] = 1.0
    c["ohc"] = oh
    dm = np.zeros((33, 33), np.float32)
    for b in range(32):
        dm[b, b] += 1.0
        dm[31, b] -= 1.0
    dm[32, 32] = NEG
    c["dm"] = dm
    keys = np.arange(S)
    c["ex"] = (np.arange(128)[:, None] == (keys[None, :] // 64)).astype(np.float32).astype(NPBF)
    t4 = np.where(i[None, :] < i[:, None], 0.0, NEG).astype(np.float32)
    c["t4"] = np.ascontiguousarray(np.tile(t4, (1, 4)))
    n = np.arange(512)
    cs_, ss_ = n * 16, np.arange(128) * 64
    ov = ((cs_[:, None] < ss_[None, :] + 64) & (cs_[:, None] + 32 > ss_[None, :])).astype(np.float32)
    ov[511, :] = 0.0
    c["ov"] = ov.astype(NPBF)
    q = np.arange(128)
    cq = (q >= 64).astype(np.int64)[:, None]
    rel = (np.arange(256) - 128)[None, :]
    forced = (rel == cq) | (rel == cq - 1)
    causal = rel <= cq
    c["mc_rel"] = (causal & ~forced).astype(np.float32)
    c["ma_rel"] = np.where(forced, 100.0, np.where(causal, 0.0, -1.0)).astype(np.float32)
    return c


GELU_C = 1.5957691216057308


def phase_nsa(k, cx, hn_all, hn_all_b, w_ap, rb_ap, peT_ap, w1_ap, w2k_ap, w2v_ap, bdz, a2a, a2a_b, stack, nqt=64, stop=99):
    scale = 0.125
    ps = [k.ps("np%d" % i, [128, 512], F32, stack) for i in range(8)]
    psb = k.bufs(8, "np")
    QT = [k.sb("nQT%d" % i, [128, S], BF16, stack) for i in range(2)]
    KST = k.sb("nKST", [128, S], BF16, stack)
    KWT = k.sb("nKWT", [128, S], BF16, stack)
    VS = k.sb("nVS", [128, 64, 65], BF16, stack)
    VW = k.sb("nVW", [128, 64, 65], BF16, stack)
    G = k.sb("nG", [128, 64, 12], F32, stack)
    kcT = k.sb("nkcT", [128, 512], BF16, stack)
    VCX = k.sb("nVCX", [128, 4, 193], BF16, stack)
    T0 = k.sb("nT0", [128, 512], F32, stack)
    T1 = k.sb("nT1", [128, 512], F32, stack)
    pb = k.buf("nsa_persist")
    k.op("pool", lambda e: e.memset(VS[:, :, 64:65], 1.0), w=[pb])
    k.op("pool", lambda e: e.memset(VW[:, :, 64:65], 1.0), w=[pb])
    k.op("pool", lambda e: e.memset(VCX[:], 0.0), w=[pb])
    k.op("pool", lambda e: e.memset(VCX[:, :, 64:65], 1.0), w=[pb])
    k.op("pool", lambda e: e.memset(kcT[:], 0.0), w=[pb])
    k.dma("sp", VCX[:, :, 65:193], cx.ov_dram.rearrange("(nb p) s -> p nb s", p=128), w=[pb])
    with contextlib.ExitStack() as st:
        rbe = k.sb("rbe", [33, 4], F32, st)
        rbeb = k.buf("rbe")
        k.op("pool", lambda e: e.memset(rbe[32:33, :], 1.0), w=[rbeb])
        k.dma("sp", rbe[0:32, :], rb_ap, w=[rbeb])
        ohc = k.sb("ohc", [33, NDC], F32, st)
        ohcb = k.buf("ohc")
        k.dma("sp", ohc[:], cx.ohc_dram, w=[ohcb])
        dm = k.sb("dm", [33, 33], F32, st)
        dmb = k.buf("dm")
        k.dma("sp", dm[:], cx.dm_dram, w=[dmb])
        rbx = k.sb("rbx", [33, 4], F32, st)
        rbxb = k.buf("rbx")
        k.op("pe", lambda e: e.matmul(ps[0][0:33, 0:4], lhsT=dm[:], rhs=rbe[:], start=True, stop=True), r=[dmb, rbeb], w=[psb[0]])
        k.op("dve", lambda e: e.tensor_copy(out=rbx[:], in_=ps[0][0:33, 0:4]), r=[psb[0]], w=[rbxb])
        bds = k.sb("bds", [4, NDC], F32, st)
        bdsb = k.buf("bds")
        nchunk = (NDC + 511) // 512
        for ci in range(nchunk):
            lo, hi = ci * 512, min(NDC, (ci + 1) * 512)
            p_, pb_ = ps[1 + ci % 2], psb[1 + ci % 2]
            k.op("pe", lambda e, p_=p_, lo=lo, hi=hi: e.matmul(p_[0:4, 0:hi - lo], lhsT=rbx[:], rhs=ohc[:, lo:hi], start=True, stop=True),
                 r=[rbxb, ohcb], w=[pb_])
            k.op("dve", lambda e, p_=p_, lo=lo, hi=hi: e.tensor_copy(out=bds[:, lo:hi], in_=p_[0:4, 0:hi - lo]), r=[pb_], w=[bdsb])
        bdzb = k.buf("bdz")
        k.dma("sp", bdz, bds[:], r=[bdsb], w=[bdzb])
        cx.bdz_b = bdzb
        U = k.sb("U", [128, 512], F32, st)
        Ub = k.buf("U")
        for m, Tm in ((0, T0), (1, T1)):
            src = bass.AP(bdz.tensor, DOFF - 127 + 128 * m, [[1, 128], [NDC, 4], [1, 128]])
            k.dma("sp", U[:].rearrange("p (g q) -> p g q", g=4), src, r=[bdzb], w=[Ub])
            k.op("pe", lambda e: e.matmul(ps[3][:, :], lhsT=cx.jflip[:], rhs=U[:], start=True, stop=True), r=[Ub, cx.jflip_b], w=[psb[3]])
            k.op("dve", lambda e, Tm=Tm: e.tensor_copy(out=Tm[:], in_=ps[3][:, :]), r=[psb[3]], w=[pb])
        k.barrier()
    if stop <= 0:
        return
    with contextlib.ExitStack() as st:
        w = k.sb("nw", [128, FC, 780], BF16, st)
        wb = k.buf("nw")
        wv = w_ap.rearrange("(fc p) n -> p fc n", p=128)
        for fc in range(FC):
            k.dma("pool", w[:, fc, :], wv[:, fc, :], w=[wb])
        KAT = k.sb("nKAT", [128, S], BF16, st)
        hnc = [k.sb("nhnc%d" % i, [128, FC, 512], BF16, st) for i in range(2)]
        hncb = k.bufs(2, "nhnc")
        for c in range(16):
            rho, off = c // 4, (c % 4) * 512
            src = hn_all.rearrange("(q r h p) t -> r p q h t", q=4, r=4, h=2, p=128)[rho][:, :, :, off:off + 512]
            hb_ = hncb[c % 2]
            for q_ in range(4):
                k.dma("sp", hnc[c % 2][:, 2 * q_:2 * q_ + 2, :], src[:, q_, :, :], r=[hn_all_b], w=[hb_])
            x = hnc[c % 2]
            cs = slice(c * 512, (c + 1) * 512)
            for j, (c0, dst, sc) in enumerate(((0, QT[0], scale), (128, QT[1], scale), (256, KAT, None), (384, KST, None), (512, KWT, None))):
                p_, pb_ = ps[j % 4], psb[j % 4]
                for fc in range(FC):
                    k.op("pe", lambda e, fc=fc, p_=p_, c0=c0: e.matmul(p_[:, :], lhsT=w[:, fc, c0:c0 + 128], rhs=x[:, fc, :],
                                                                         start=(fc == 0), stop=(fc == FC - 1)), r=[wb, hb_], w=[pb_])
                if sc is not None:
                    k.op("act", lambda e, p_=p_, dst=dst, sc=sc: e.activation(out=dst[:, cs], in_=p_[:, :], func=AF.Copy, scale=sc), r=[pb_], w=[pb])
                else:
                    k.op("dve", lambda e, p_=p_, dst=dst: e.tensor_copy(out=dst[:, cs], in_=p_[:, :]), r=[pb_], w=[pb])
            for tt in range(4):
                p_, pb_ = ps[4 + tt % 2], psb[4 + tt % 2]
                kb = c * 4 + tt
                for fc in range(FC):
                    k.op("pe", lambda e, fc=fc, p_=p_, tt=tt: e.matmul(p_[:, 0:140], lhsT=x[:, fc, tt * 128:(tt + 1) * 128], rhs=w[:, fc, 640:780],
                                                                         start=(fc == 0), stop=(fc == FC - 1)), r=[wb, hb_], w=[pb_])
                k.op("dve", lambda e, p_=p_, kb=kb: e.tensor_copy(out=VS[:, kb, 0:64], in_=p_[:, 0:64]), r=[pb_], w=[pb])
                k.op("dve", lambda e, p_=p_, kb=kb: e.tensor_copy(out=VW[:, kb, 0:64], in_=p_[:, 64:128]), r=[pb_], w=[pb])
                k.op("act", lambda e, p_=p_, kb=kb: e.activation(out=G[:, kb, :], in_=p_[:, 128:140], func=AF.Sigmoid), r=[pb_], w=[pb])
        if stop <= 1:
            k.barrier()
            return
        w1 = k.sb("nw1", [128, 32, 256], BF16, st)
        w1b = k.buf("nw1")
        for l in range(32):
            k.dma("pool", w1[:, l, :], w1_ap[:, l, :], w=[w1b])
        w2kd = k.sb("nw2k", [128, 2, 128], BF16, st)
        w2v = k.sb("nw2v", [128, 2, 64], BF16, st)
        w2b = k.buf("nw2")
        w2kv_ = w2k_ap.rearrange("(hc p) d -> p hc d", p=128)
        for hc in range(2):
            k.dma("pool", w2kd[:, hc, 0:64], w2kv_[:, hc, :], w=[w2b])
            k.dma("pool", w2kd[:, hc, 64:128], w2kv_[:, hc, :], w=[w2b])
            k.dma("pool", w2v[:, hc, :], w2v_ap.rearrange("(hc p) d -> p hc d", p=128)[:, hc, :], w=[w2b])
        peT = k.sb("npeT", [128, 32], BF16, st)
        peb = k.buf("npeT")
        k.dma("pool", peT[:], peT_ap, w=[peb])
        cb = k.sb("ncb", [128, 4], F32, st)
        cbb = k.buf("ncb")
        hid = k.sb("nhid", [128, 4, 512], BF16, st)
        hidb = k.buf("nhid")
        xs = k.sb("nxs", [128, 512], F32, st)
        x2 = k.sb("nx2", [128, 512], F32, st)
        xsb, x2b = k.buf("nxs"), k.buf("nx2")
        for kv in range(2):
            b0 = 64 * kv
            for hc in range(2):
                idx = kv * 2 + hc
                p_, pb_ = ps[idx % 2], psb[idx % 2]
                pc, pcb = ps[2 + idx % 2], psb[2 + idx % 2]
                for l in range(32):
                    k.op("pe", lambda e, l=l, pc=pc: e.matmul(pc[:, 0:1], lhsT=w1[b0:b0 + 64, l, hc * 128:(hc + 1) * 128], rhs=peT[b0:b0 + 64, l:l + 1],
                                                                start=(l == 0), stop=(l == 31)), r=[w1b, peb], w=[pcb])
                k.op("dve", lambda e, pc=pc, idx=idx: e.tensor_copy(out=cb[:, idx:idx + 1], in_=pc[:, 0:1]), r=[pcb], w=[cbb])
                for l in range(32):
                    k.op("pe", lambda e, l=l, p_=p_: e.matmul(p_[:, 0:511], lhsT=w1[b0:b0 + 64, l, hc * 128:(hc + 1) * 128],
                                                                rhs=KAT[b0:b0 + 64, l:l + 8161:16], start=(l == 0), stop=(l == 31)), r=[w1b, pb], w=[pb_])
                k.op("act", lambda e, p_=p_, idx=idx: e.activation(out=xs[:, 0:511], in_=p_[:, 0:511], func=AF.Identity, bias=cb[:, idx:idx + 1], scale=1.0),
                     r=[pb_, cbb], w=[xsb])
                k.op("act", lambda e, p_=p_, idx=idx: e.activation(out=x2[:, 0:511], in_=p_[:, 0:511], func=AF.Square, bias=cb[:, idx:idx + 1], scale=1.0),
                     r=[pb_, cbb], w=[x2b])
                k.op("dve", lambda e: e.tensor_scalar(out=x2[:, 0:511], in0=x2[:, 0:511], scalar1=0.044715, scalar2=1.0, op0=ALU.mult, op1=ALU.add),
                     r=[x2b], w=[x2b])
                k.op("dve", lambda e: e.tensor_tensor(out=x2[:, 0:511], in0=x2[:, 0:511], in1=xs[:, 0:511], op=ALU.mult), r=[x2b, xsb], w=[x2b])
                k.op("act", lambda e: e.activation(out=x2[:, 0:511], in_=x2[:, 0:511], func=AF.Sigmoid, scale=GELU_C), r=[x2b], w=[x2b])
                k.op("dve", lambda e, idx=idx: e.tensor_tensor(out=hid[:, idx, 0:511], in0=x2[:, 0:511], in1=xs[:, 0:511], op=ALU.mult),
                     r=[x2b, xsb], w=[hidb])
        for hc in range(2):
            k.op("pe", lambda e, hc=hc: e.matmul(ps[4][:, 0:511], lhsT=w2kd[:, hc, :], rhs=hid[:, hc, 0:511], start=(hc == 0), stop=(hc == 1)),
                 r=[w2b, hidb], w=[psb[4]])
        k.op("dve", lambda e: e.tensor_copy(out=kcT[:, 0:511], in_=ps[4][:, 0:511]), r=[psb[4]], w=[pb])
        for nb in range(4):
            M = 128 if nb < 3 else 127
            p_, pb_ = ps[5 + nb % 2], psb[5 + nb % 2]
            for hc in range(2):
                k.op("pe", lambda e, hc=hc, nb=nb, M=M, p_=p_: e.matmul(p_[0:M, 0:64], lhsT=hid[:, 2 + hc, nb * 128:nb * 128 + M], rhs=w2v[:, hc, :],
                                                                          start=(hc == 0), stop=(hc == 1)), r=[w2b, hidb], w=[pb_])
            k.op("dve", lambda e, nb=nb, M=M, p_=p_: e.tensor_copy(out=VCX[0:M, nb, 0:64], in_=p_[0:M, 0:64]), r=[pb_], w=[pb])
        k.barrier()
    if stop <= 2:
        return
    with contextlib.ExitStack() as st:
        psZ, psZb = ps[0:2], psb[0:2]
        psOc, psOcb = ps[2:4], psb[2:4]
        psOs, psOsb = ps[4], psb[4]
        psOw, psOwb = ps[5], psb[5]
        psM, psMb = ps[6:8], psb[6:8]
        Wc = [k.sb("nWc%d" % i, [128, 512], F32, st) for i in range(2)]
        Wcb = k.bufs(2, "nWc")
        ee = [k.sb("nee%d" % i, [128, 512], BF16, st) for i in range(3)]
        eeb = k.bufs(3, "nee")
        sf = [k.sb("nsf%d" % i, [128, 512], F32, st) for i in range(2)]
        sfb = k.bufs(2, "nsf")
        nmT = k.sb("nnmT", [128, 512], BF16, st)
        nmTb = k.buf("nnmT")
        imp = k.sb("nimp", [128, 128], F32, st)
        imp3 = k.sb("nimp3", [128, 128], F32, st)
        negm = k.sb("nnegm", [128, 128], F32, st)
        impb, imp3b, negmb = k.buf("imp"), k.buf("imp3"), k.buf("negm")
        m8 = k.sb("nm8", [128, 16], F32, st)
        m8b = k.buf("m8")
        dn = k.sb("ndn", [128, 12], F32, st)
        dnb = k.buf("dn")
        ot = k.sb("not", [128, 256], F32, st)
        otb = k.buf("ot")
        oT = [k.sb("noT%d" % i, [128, 2, 512], BF16, st) for i in range(2)]
        oTb = k.bufs(2, "noT")
        zc = 0
        ec = 0
        wcn = 0

        bd = [k.sb("nbd%d" % i, [128, 512], BF16, st) for i in range(2)]
        bdb = k.bufs(2, "nbd")
        for i in range(2):
            k.op("pool", lambda e, i=i: e.memset(bd[i][:], 0.0), w=[bdb[i]])
        cur = {}

        def qk(Z, Zb, KT_, kcols, tcols, first_start, extra_r=()):
            k.op("pe", lambda e: e.matmul(Z[:, :], lhsT=KT_[:, kcols], rhs=cur["bd"][:, :], start=first_start, stop=True, skip_group_check=True),
                 r=[pb, cur["bdb"]] + list(extra_r), w=[Zb])

        pend = []

        def flush():
            while pend:
                pend.pop(0)()

        def defer(fn):
            pend.append(fn)
            while len(pend) > 1:
                pend.pop(0)()

        for qt in range(nqt):
            tcols = slice(128 * qt, 128 * (qt + 1))
            cur["bd"], cur["bdb"] = bd[qt % 2], bdb[qt % 2]
            for g in range(4):
                b0 = 64 * (g % 2)
                k.op("pool", lambda e, g=g, b0=b0: e.tensor_copy(out=cur["bd"][b0:b0 + 64, g * 128:(g + 1) * 128], in_=QT[g // 2][b0:b0 + 64, tcols]),
                     r=[pb], w=[cur["bdb"]])
            NB = (8 * qt + 6) // 128 + 1
            for nb in range(NB):
                Z, Zb = psZ[zc % 2], psZb[zc % 2]
                zc += 1
                o_idx = qt - 16 * nb
                if o_idx <= 16:
                    W_, Wb_ = Wc[wcn % 2], Wcb[wcn % 2]
                    wcn += 1
                    src = bass.AP(bdz.tensor, 128 * o_idx, [[16, 128], [NDC, 4], [1, 128]])
                    import os
                    if os.environ.get("NSA_DBG") == "1":
                        k.op("pool", lambda e, W_=W_: e.memset(W_[:], 0.0), w=[Wb_])
                    else:
                        k.dma("sp", W_[:].rearrange("p (g q) -> p g q", g=4), src, r=[cx.bdz_b], w=[Wb_])
                    k.op("pe", lambda e, W_=W_: e.matmul(psM[0][:, :], lhsT=cx.jflip[:], rhs=W_[:], start=True, stop=True),
                         r=[Wb_, cx.jflip_b], w=[psMb[0]])
                    k.op("act", lambda e, W_=W_: e.copy(out=W_[:], in_=psM[0][:, :]), r=[psMb[0]], w=[Wb_])
                    qk(Z, Zb, kcT, slice(nb * 128, (nb + 1) * 128), tcols, True)
                    E_, Eb_ = ee[ec % 3], eeb[ec % 3]
                    ec += 1
                    s_, sb_ = sf[ec % 2], sfb[ec % 2]
                    k.op("dve", lambda e, s_=s_, Z=Z, W_=W_: e.tensor_tensor(out=s_[:], in0=Z[:, :], in1=W_[:], op=ALU.add), r=[Zb, Wb_], w=[sb_])
                    k.op("act", lambda e, E_=E_, s_=s_: e.activation(out=E_[:, :], in_=s_[:], func=AF.Exp), r=[sb_], w=[Eb_])
                else:
                    qk(Z, Zb, kcT, slice(nb * 128, (nb + 1) * 128), tcols, True)
                    E_, Eb_ = ee[ec % 3], eeb[ec % 3]
                    ec += 1
                    k.op("act", lambda e, E_=E_, Z=Z: e.activation(out=E_[:, :], in_=Z[:, :], func=AF.Exp), r=[Zb], w=[Eb_])
                def pv_c(E_=E_, Eb_=Eb_, nb=nb, NB=NB):
                    for g in range(4):
                        bank, bb = psOc[g // 2], psOcb[g // 2]
                        c0 = (g % 2) * 193
                        k.op("pe", lambda e, g=g, bank=bank, c0=c0: e.matmul(
                            bank[:, c0:c0 + 193], lhsT=E_[:, g * 128:(g + 1) * 128], rhs=VCX[:, nb, :],
                            start=(nb == 0 and g % 2 == 0), stop=(nb == NB - 1), skip_group_check=True), r=[Eb_, pb], w=[bb])
                defer(pv_c)
            flush()
            if stop <= 3:
                continue
            for h2 in range(2):
                bank, bb = psOc[h2], psOcb[h2]
                bv = bank[:, 0:386].rearrange("p (g c) -> p g c", c=193)
                k.op("dve", lambda e, h2=h2, bv=bv: e.tensor_scalar(out=dn[:, 2 * h2:2 * h2 + 2], in0=bv[:, :, 64], scalar1=1e-30, scalar2=None, op0=ALU.max),
                     r=[bb], w=[dnb])
            k.op("dve", lambda e: e.reciprocal(out=dn[:, 0:4], in_=dn[:, 0:4]), r=[dnb], w=[dnb])
            for g in range(4):
                bank, bb = psOc[g // 2], psOcb[g // 2]
                c0 = (g % 2) * 193 + 65
                if g == 0:
                    k.op("dve", lambda e, bank=bank, c0=c0: e.tensor_scalar(out=imp[:], in0=bank[:, c0:c0 + 128], scalar1=dn[:, 0:1], scalar2=None, op0=ALU.mult),
                         r=[bb, dnb], w=[impb])
                else:
                    k.op("dve", lambda e, g=g, bank=bank, c0=c0: e.scalar_tensor_tensor(out=imp[:], in0=bank[:, c0:c0 + 128], scalar=dn[:, g:g + 1], in1=imp[:],
                                                                                        op0=ALU.mult, op1=ALU.add), r=[bb, dnb, impb], w=[impb])
            sl = slice(128 - 2 * qt, 256 - 2 * qt)
            k.op("dve", lambda e: e.tensor_tensor(out=imp[:], in0=imp[:], in1=cx.mc_rel[:, sl], op=ALU.mult), r=[impb, cx.mc_rel_b], w=[impb])
            k.op("dve", lambda e: e.tensor_tensor(out=imp[:], in0=imp[:], in1=cx.ma_rel[:, sl], op=ALU.add), r=[impb, cx.ma_rel_b], w=[impb])
            k.op("dve", lambda e: e.memset(imp[:, 0:1], 100.0), r=[impb], w=[impb])
            k.op("dve", lambda e: e.max(out=m8[:, 0:8], in_=imp[:]), r=[impb], w=[m8b])
            k.op("dve", lambda e: e.match_replace(out=imp3[:], in_to_replace=m8[:, 0:8], in_values=imp[:], imm_value=-1e30), r=[impb, m8b], w=[imp3b])
            k.op("dve", lambda e: e.max(out=m8[:, 8:16], in_=imp3[:]), r=[imp3b], w=[m8b])
            k.op("dve", lambda e: e.tensor_scalar(out=negm[:], in0=imp[:], scalar1=m8[:, 15:16], scalar2=NEG, op0=ALU.is_lt, op1=ALU.mult),
                 r=[impb, m8b], w=[negmb])
            k.op("pe", lambda e: e.transpose(out=psM[0][:, 0:128], in_=negm[:], identity=cx.ident_f[:]), r=[negmb, cx.ident_f_b], w=[psMb[0]])
            for g in range(4):
                if g % 2 == 0:
                    k.op("dve", lambda e, g=g: e.tensor_copy(out=nmT[:, g * 128:(g + 1) * 128], in_=psM[0][:, 0:128]), r=[psMb[0]], w=[nmTb])
                else:
                    k.op("act", lambda e, g=g: e.copy(out=nmT[:, g * 128:(g + 1) * 128], in_=psM[0][:, 0:128]), r=[psMb[0]], w=[nmTb])
            if stop <= 4:
                continue
            for kb in range(qt + 1):
                Z, Zb = psZ[zc % 2], psZb[zc % 2]
                zc += 1
                m = qt - kb
                k.op("pe", lambda e, Z=Z, kb=kb: e.matmul(Z[:, :], lhsT=cx.ex[:, 128 * kb:128 * (kb + 1)], rhs=nmT[:], start=True, stop=False, skip_group_check=True),
                     r=[nmTb, cx.ex_b], w=[Zb])
                qk(Z, Zb, KST, slice(128 * kb, 128 * (kb + 1)), tcols, False)
                E_, Eb_ = ee[ec % 3], eeb[ec % 3]
                ec += 1
                if m <= 1:
                    Tm = T0 if m == 0 else T1
                    s_, sb_ = sf[ec % 2], sfb[ec % 2]
                    k.op("dve", lambda e, s_=s_, Z=Z, Tm=Tm: e.tensor_tensor(out=s_[:], in0=Z[:, :], in1=Tm[:], op=ALU.add), r=[Zb, pb], w=[sb_])
                    k.op("act", lambda e, E_=E_, s_=s_: e.activation(out=E_[:, :], in_=s_[:], func=AF.Exp), r=[sb_], w=[Eb_])
                else:
                    k.op("act", lambda e, E_=E_, Z=Z: e.activation(out=E_[:, :], in_=Z[:, :], func=AF.Exp), r=[Zb], w=[Eb_])
                def pv_s(E_=E_, Eb_=Eb_, kb=kb, qt=qt):
                    for g in range(4):
                        k.op("pe", lambda e, g=g: e.matmul(psOs[:, g * 65:(g + 1) * 65], lhsT=E_[:, g * 128:(g + 1) * 128], rhs=VS[:, kb, :],
                                                            start=(kb == 0 and g == 0), stop=(kb == qt), skip_group_check=True), r=[Eb_, pb], w=[psOsb])
                defer(pv_s)
            flush()
            if stop <= 5:
                continue
            kb0 = max(0, qt - 4)
            for kb in range(kb0, qt + 1):
                Z, Zb = psZ[zc % 2], psZb[zc % 2]
                zc += 1
                m = qt - kb
                qk(Z, Zb, KWT, slice(128 * kb, 128 * (kb + 1)), tcols, True)
                E_, Eb_ = ee[ec % 3], eeb[ec % 3]
                ec += 1
                if m in (0, 1, 4):
                    Tm, Tmb = {0: (T0, pb), 1: (T1, pb), 4: (cx.t4, cx.t4_b)}[m]
                    s_, sb_ = sf[ec % 2], sfb[ec % 2]
                    k.op("dve", lambda e, s_=s_, Z=Z, Tm=Tm: e.tensor_tensor(out=s_[:], in0=Z[:, :], in1=Tm[:], op=ALU.add), r=[Zb, Tmb], w=[sb_])
                    k.op("act", lambda e, E_=E_, s_=s_: e.activation(out=E_[:, :], in_=s_[:], func=AF.Exp), r=[sb_], w=[Eb_])
                else:
                    k.op("act", lambda e, E_=E_, Z=Z: e.activation(out=E_[:, :], in_=Z[:, :], func=AF.Exp), r=[Zb], w=[Eb_])
                def pv_w(E_=E_, Eb_=Eb_, kb=kb, qt=qt, kb0=kb0):
                    for g in range(4):
                        k.op("pe", lambda e, g=g: e.matmul(psOw[:, g * 65:(g + 1) * 65], lhsT=E_[:, g * 128:(g + 1) * 128], rhs=VW[:, kb, :],
                                                            start=(kb == kb0 and g == 0), stop=(kb == qt), skip_group_check=True), r=[Eb_, pb], w=[psOwb])
                defer(pv_w)
            flush()
            if stop <= 6:
                continue
            osv = psOs[:, 0:260].rearrange("p (g c) -> p g c", c=65)
            owv = psOw[:, 0:260].rearrange("p (g c) -> p g c", c=65)
            k.op("dve", lambda e: e.tensor_scalar(out=dn[:, 4:8], in0=osv[:, :, 64], scalar1=1e-30, scalar2=None, op0=ALU.max), r=[psOsb], w=[dnb])
            k.op("dve", lambda e: e.tensor_scalar(out=dn[:, 8:12], in0=owv[:, :, 64], scalar1=1e-30, scalar2=None, op0=ALU.max), r=[psOwb], w=[dnb])
            k.op("dve", lambda e: e.reciprocal(out=dn[:, 4:12], in_=dn[:, 4:12]), r=[dnb], w=[dnb])
            k.op("dve", lambda e: e.tensor_tensor(out=dn[:, :], in0=dn[:, :], in1=G[:, qt, :], op=ALU.mult), r=[dnb, pb], w=[dnb])
            for g in range(4):
                bank, bb = psOc[g // 2], psOcb[g // 2]
                c0 = (g % 2) * 193
                og = ot[:, g * 64:(g + 1) * 64]
                k.op("dve", lambda e, g=g, bank=bank, c0=c0, og=og: e.tensor_scalar(out=og, in0=bank[:, c0:c0 + 64], scalar1=dn[:, g:g + 1], scalar2=None, op0=ALU.mult),
                     r=[bb, dnb], w=[otb])
                k.op("dve", lambda e, g=g, og=og: e.scalar_tensor_tensor(out=og, in0=psOs[:, g * 65:g * 65 + 64], scalar=dn[:, 4 + g:5 + g], in1=og, op0=ALU.mult, op1=ALU.add),
                     r=[psOsb, dnb, otb], w=[otb])
                k.op("dve", lambda e, g=g, og=og: e.scalar_tensor_tensor(out=og, in0=psOw[:, g * 65:g * 65 + 64], scalar=dn[:, 8 + g:9 + g], in1=og, op0=ALU.mult, op1=ALU.add),
                     r=[psOwb, dnb, otb], w=[otb])
            y2 = (qt // 4) % 2
            for fh in range(2):
                k.op("pe", lambda e, fh=fh: e.transpose(out=psM[1][:, fh * 128:(fh + 1) * 128], in_=ot[:, fh * 128:(fh + 1) * 128], identity=cx.ident_f[:]),
                     r=[otb, cx.ident_f_b], w=[psMb[1]])
            k.op("act", lambda e, y2=y2: e.copy(out=oT[y2][:, :, (qt % 4) * 128:(qt % 4 + 1) * 128],
                                                in_=psM[1][:, 0:256].rearrange("p (fh t) -> p fh t", fh=2)), r=[psMb[1]], w=[oTb[y2]])
            if qt % 4 == 3:
                sbk = qt // 4
                dest, off = sbk // 4, (sbk % 4) * 512
                k.dma("sp", a2a[dest].rearrange("(fh p) t -> p fh t", p=128)[:, :, off:off + 512], oT[y2][:], r=[oTb[y2]], w=[a2a_b])
                if k.engs["pe"].count > SEM_ROTATE:
                    k.barrier()
        k.barrier()


def build_prog_nsa(nqt=64, stop=99):
    nc = bass.Bass("TRN2", target_bir_lowering=False)
    consts = nsa_host_consts()
    names = ["ident_f", "jflip", "ex", "t4", "mc_rel", "ma_rel"]
    dnames = ["ohc", "dm", "ov"]
    cd = dram_consts(nc, consts, names + dnames)

    def din(nm, shape, dt_=F32):
        return nc.dram_tensor(nm, shape, dt_, kind="ExternalInput").ap()
    hn_all = din("hn_all", [4 * D, TOK], BF16)
    w = din("nsa_w", [D, 780])
    rb = din("rb", [32, 4])
    peT = din("peT", [128, 32])
    w1 = din("cw1", [128, 32, 256])
    w2k = din("cw2k", [256, 64])
    w2v = din("cw2v", [256, 64])
    a2a = nc.dram_tensor("a2a", [4, 256, TOK], BF16, kind="ExternalOutput").ap()
    bdz = nc.dram_tensor("bdz", [4, NDC], F32, kind="Internal").ap()
    k = KB(nc)
    cx = load_consts(k, cd, names)
    for nm in dnames:
        setattr(cx, nm + "_dram", cd[nm])
    phase_nsa(k, cx, hn_all, k.buf("hn_all"), w, rb, peT, w1, w2k, w2v, bdz, a2a, k.buf("a2a"), k.stack, nqt=nqt, stop=stop)
    k.barrier()
    k.close()
    return nc, {nm: consts[nm] for nm in names + dnames}


def nsa_core_inputs(inp, r):
    w_in = inp["nsa_w_in"][0]
    kv0 = 1024

    def kvc(i):
        return w_in[:, kv0 + i * 256 + r * 64: kv0 + i * 256 + (r + 1) * 64]
    gcols = [2560 + j * 16 + r * 4 + g for j in range(3) for g in range(4)]
    w = np.concatenate([w_in[:, 256 * r:256 * (r + 1)], kvc(0), kvc(1), kvc(2), kvc(2), kvc(4), kvc(4), kvc(3), kvc(5), w_in[:, gcols]], axis=1)
    peT = np.concatenate([inp["nsa_pe_k"][0].T, inp["nsa_pe_v"][0].T], axis=0)
    w1k = inp["nsa_ck_w1"][0].reshape(32, 64, 256).transpose(1, 0, 2)
    w1v = inp["nsa_cv_w1"][0].reshape(32, 64, 256).transpose(1, 0, 2)
    return {"nsa_w": np.ascontiguousarray(w), "rb": np.ascontiguousarray(inp["rel_bias"][:, 4 * r:4 * r + 4]),
            "peT": np.ascontiguousarray(peT), "cw1": np.ascontiguousarray(np.concatenate([w1k, w1v], axis=0)),
            "cw2k": inp["nsa_ck_w2"][0], "cw2v": inp["nsa_cv_w2"][0]}


_PROGS = {}


def _prog(name, fn):
    if name not in _PROGS:
        _PROGS[name] = fn()
    return _PROGS[name]


def _run(name, fn, maps):
    nc, cst = _prog(name, fn)
    full = []
    for m in maps:
        mm = dict(m)
        mm.update({"c_" + kk: v for kk, v in cst.items()})
        full.append(mm)
    res = run_bass_kernel_spmd(nc, full, core_ids=list(range(NCORES)))
    return res.results


def _c(a):
    return np.ascontiguousarray(a)


def _allgather(parts):
    out = []
    for b in range(2):
        cat = np.concatenate([np.asarray(parts[4 * b + r])[256 * q:256 * (q + 1)] for q in range(4) for r in range(4)], axis=0)
        out += [cat] * 4
    return out


def _alltoall(parts):
    out = []
    for b in range(2):
        for j in range(4):
            out.append(_c(np.concatenate([np.asarray(parts[4 * b + i])[j] for i in range(4)], axis=0)))
    return out


def kernel_unfused(**inp):
    inp = {kk: np.asarray(v) for kk, v in inp.items()}
    x, p = inp["x"], inp["p"]
    cores = [(c // 4, c % 4) for c in range(NCORES)]
    tsl = [slice(TOK * r, TOK * (r + 1)) for (_, r) in cores]
    xT = [_c(x[b, tsl[c]].T) for c, (b, r) in enumerate(cores)]
    res = _run("norm0", build_prog_norm0, [{"xT": xT[c], "gain": _c(inp["norm_mix"][0])} for c in range(NCORES)])
    hn_all = _allgather([r_["hnT"] for r_ in res])
    w_in = inp["sb_w_in"][0]
    maps = []
    for c, (b, r) in enumerate(cores):
        wq = np.concatenate([w_in[:, 256 * r:256 * (r + 1)], w_in[:, 1024 + 256 * r:1024 + 256 * (r + 1)],
                             w_in[:, 2048 + 256 * r:2048 + 256 * (r + 1)]], axis=1)
        maps.append({"hn_all": hn_all[c], "sb_w": _c(wq)})
    res = _run("sb", build_prog_sb, maps)
    oT = _alltoall([r_["a2a"] for r_ in res])

    def tail_maps(i, hT_in, oT_, g_next):
        out = []
        for c, (b, r) in enumerate(cores):
            out.append({"hT_in": hT_in[c], "oT": oT_[c],
                        "w_out": _c((inp["sb_w_out"] if i == 0 else inp["nsa_w_out"])[0]),
                        "g_ffn": _c(inp["norm_ffn"][i]), "w1": _c(inp["ffn_w_in"][i]), "w2": _c(inp["ffn_w_out"][i]),
                        "g_ple": _c(inp["norm_ple"][i]), "wg": _c(inp["ple_w_gate"][i]), "wp": _c(inp["ple_w_proj"][i]),
                        "pT": _c(p[i, b, tsl[c]].T), "g_next": _c(g_next)})
        return out
    res = _run("tail0", lambda: build_prog_tail(False), tail_maps(0, xT, oT, inp["norm_mix"][1]))
    h1T = [np.asarray(r_["hT_out"]) for r_ in res]
    hn_all = _allgather([r_["hnT"] for r_ in res])
    maps = []
    for c, (b, r) in enumerate(cores):
        m = {"hn_all": hn_all[c]}
        m.update(nsa_core_inputs(inp, r))
        maps.append(m)
    res = _run("nsa", build_prog_nsa, maps)
    oT = _alltoall([r_["a2a"] for r_ in res])
    res = _run("tail1", lambda: build_prog_tail(True), tail_maps(1, h1T, oT, inp["final_norm"]))
    out = np.empty((2, S, D), np.float32)
    for c, (b, r) in enumerate(cores):
        out[b, tsl[c], :] = np.asarray(res[c]["outT"]).T
    return out


I32 = mybir.dt.int32
TAIL_KEYS = (("w_out", [D, D]), ("g_ffn", [D]), ("w1", [D, 2 * DFF]), ("w2", [DFF, D]), ("g_ple", [D]), ("wg", [D, D]), ("wp", [256, D]))


def build_prog_fused():
    import os
    CUT = int(os.environ.get("FUSED_CUT", "99"))
    nc = bass.Bass("TRN2", target_bir_lowering=False)
    consts = dict(host_consts())
    consts.update(nsa_host_consts())
    small = ["ident_f", "ones_b", "negu_b", "negones_b", "mask_sb", "jflip"]
    nsa_sb = ["ex", "t4", "mc_rel", "ma_rel"]
    nsa_dr = ["ohc", "dm", "ov"]
    cd = dram_consts(nc, consts, small + nsa_sb + nsa_dr)

    def din(nm, shape, dt_=F32):
        return nc.dram_tensor(nm, shape, dt_, kind="ExternalInput").ap()

    def dint(nm, shape, dt_):
        return nc.dram_tensor(nm, shape, dt_, kind="Internal").ap()
    xT = din("xT", [D, TOK])
    pT = [din("pT0", [256, TOK]), din("pT1", [256, TOK])]
    rk = din("rk", [1, 2], I32)
    g_mix = [din("g_mix0", [D]), din("g_mix1", [D])]
    g_fin = din("g_fin", [D])
    sb_w = din("sb_w", [D, 768])
    tails = [{nm: din("%s_%d" % (nm, i), shp) for nm, shp in TAIL_KEYS} for i in range(2)]
    nsa_w = din("nsa_w", [D, 780])
    rb = din("rb", [32, 4])
    peT = din("peT", [128, 32])
    cw1 = din("cw1", [128, 32, 256])
    cw2k = din("cw2k", [256, 64])
    cw2v = din("cw2v", [256, 64])
    outT = nc.dram_tensor("outT", [D, TOK], F32, kind="ExternalOutput").ap()
    hn_loc = dint("hn_loc", [D, TOK], BF16)
    hn_all = dint("hn_all", [4 * D, TOK], BF16)
    a2a_loc = dint("a2a_loc", [4, 256, TOK], BF16)
    a2a_all = dint("a2a_all", [4 * D, TOK], BF16)
    bdz = dint("bdz", [4, NDC], F32)
    hsp = dint("hsp", [D, TOK], F32)
    k = KB(nc)
    cx = load_consts(k, cd, small)
    for nm in nsa_dr:
        setattr(cx, nm + "_dram", cd[nm])
    make_eps(k, cx)
    make_one(k, cx)
    groups = [[0, 1, 2, 3], [4, 5, 6, 7]]
    spq = k.engs["sp"].h
    reg = spq.alloc_register("rk")
    spq.reg_load(reg, rk[0:1, 0:1])
    crk = spq.snap(reg, min_val=0, max_val=3)
    hn_loc_b, hn_all_b, a2a_loc_b, a2a_all_b, hsp_b, out_b = [k.buf(n_) for n_ in ("hn_loc", "hn_all", "a2a_loc", "a2a_all", "hsp", "outT")]
    g4 = a2a_all.rearrange("(j f) t -> j f t", j=4)

    def oT_loader(tile, tb, cs):
        src = g4[crk].rearrange("(fc p) t -> p fc t", p=128)
        k.dma("sp", tile[:], src[:, :, cs], r=[a2a_all_b], w=[tb])

    def gather_hn():
        for q in range(4):
            k.coll("AllGather", hn_loc[256 * q:256 * (q + 1), :], hn_all[1024 * q:1024 * (q + 1), :], groups, r=[hn_loc_b], w=[hn_all_b])

    def gather_o():
        for j in range(4):
            k.coll("AllGather", a2a_loc[j], a2a_all[1024 * j:1024 * (j + 1), :], groups, r=[a2a_loc_b], w=[a2a_all_b])

    wbf = [tail_weights_to_bf16(k, nc, str(i), tails[i]["w_out"], tails[i]["w1"], tails[i]["w2"], tails[i]["wg"], tails[i]["wp"])
           for i in range(2)]

    def tail(i, hT, hb):
        t = tails[i]
        phase_tail(k, cx, hT, hb, None, a2a_all_b, wbf[i], t["g_ffn"], t["g_ple"], pT[i], oT_loader=oT_loader)

    hv = hsp.rearrange("(fc p) t -> p fc t", p=128)
    with contextlib.ExitStack() as stA:
        hT = k.sb("hT", [128, FC, TOK], F32, stA)
        hb = k.bufs(4, "hT")
        xv = xT.rearrange("(fc p) t -> p fc t", p=128)
        for c in range(4):
            k.dma("sp", hT[:, :, c * 512:(c + 1) * 512], xv[:, :, c * 512:(c + 1) * 512], w=[hb[c]])
        with contextlib.ExitStack() as st:
            phase_norm_out(k, cx, hT, hb, g_mix[0], hn_loc, hn_loc_b, st)
            k.barrier()
        gather_hn()
        if CUT >= 2:
            with contextlib.ExitStack() as st:
                phase_sb(k, cx, hn_all, hn_all_b, sb_w, a2a_loc, a2a_loc_b, st)
                k.barrier()
            gather_o()
        if CUT >= 3:
            tail(0, hT, hb)
        with contextlib.ExitStack() as st:
            phase_norm_out(k, cx, hT, hb, g_mix[1], hn_loc, hn_loc_b, st)
            for c in range(4):
                k.dma("sp", hv[:, :, c * 512:(c + 1) * 512], hT[:, :, c * 512:(c + 1) * 512], r=[hb[c]], w=[hsp_b])
            k.barrier()
    if CUT >= 4:
        gather_hn()
    with contextlib.ExitStack() as stB:
      if CUT >= 5:
        cxb = load_consts(k, cd, nsa_sb, stB)
        for nm in nsa_sb:
            setattr(cx, nm, getattr(cxb, nm))
            setattr(cx, nm + "_b", getattr(cxb, nm + "_b"))
        phase_nsa(k, cx, hn_all, hn_all_b, nsa_w, rb, peT, cw1, cw2k, cw2v, bdz, a2a_loc, a2a_loc_b, stB)
        k.barrier()
    if CUT >= 5:
        gather_o()
    with contextlib.ExitStack() as stC:
        hT = k.sb("hT2", [128, FC, TOK], F32, stC)
        hb = k.bufs(4, "hT2")
        for c in range(4):
            k.dma("sp", hT[:, :, c * 512:(c + 1) * 512], hv[:, :, c * 512:(c + 1) * 512], r=[hsp_b], w=[hb[c]])
        if CUT >= 6:
            tail(1, hT, hb)
        with contextlib.ExitStack() as st:
            phase_final_norm(k, cx, hT, hb, g_fin, outT, out_b, st)
            k.barrier()
    k.barrier()
    k.close()
    return nc, {nm: consts[nm] for nm in small + nsa_sb + nsa_dr}


def fused_maps(inp):
    x, p = inp["x"], inp["p"]
    maps = []
    w_in = inp["sb_w_in"][0]
    for c in range(NCORES):
        b, r = c // 4, c % 4
        ts = slice(TOK * r, TOK * (r + 1))
        m = {"xT": _c(x[b, ts].T), "pT0": _c(p[0, b, ts].T), "pT1": _c(p[1, b, ts].T),
             "rk": np.array([[r, 0]], np.int32),
             "g_mix0": _c(inp["norm_mix"][0]), "g_mix1": _c(inp["norm_mix"][1]), "g_fin": _c(inp["final_norm"]),
             "sb_w": _c(np.concatenate([w_in[:, 256 * r:256 * (r + 1)], w_in[:, 1024 + 256 * r:1024 + 256 * (r + 1)],
                                        w_in[:, 2048 + 256 * r:2048 + 256 * (r + 1)]], axis=1))}
        for i in range(2):
            m["w_out_%d" % i] = _c((inp["sb_w_out"] if i == 0 else inp["nsa_w_out"])[0])
            m["g_ffn_%d" % i] = _c(inp["norm_ffn"][i])
            m["w1_%d" % i] = _c(inp["ffn_w_in"][i])
            m["w2_%d" % i] = _c(inp["ffn_w_out"][i])
            m["g_ple_%d" % i] = _c(inp["norm_ple"][i])
            m["wg_%d" % i] = _c(inp["ple_w_gate"][i])
            m["wp_%d" % i] = _c(inp["ple_w_proj"][i])
        m.update(nsa_core_inputs(inp, r))
        maps.append(m)
    return maps


def kernel(**inp):
    inp = {kk: np.asarray(v) for kk, v in inp.items()}
    res = _run("fused", build_prog_fused, fused_maps(inp))
    out = np.empty((2, S, D), np.float32)
    for c in range(NCORES):
        b, r = c // 4, c % 4
        out[b, TOK * r:TOK * (r + 1), :] = np.asarray(res[c]["outT"]).T
    return out
```

```python
import contextlib
import math
import numpy as np
import ml_dtypes
import concourse.bass as bass
import concourse.mybir as mybir
from concourse.alu_op_type import AluOpType as ALU
from concourse.bass_utils import run_bass_kernel_spmd

F32 = mybir.dt.float32
BF16 = mybir.dt.bfloat16
AF = mybir.ActivationFunctionType
NPBF = ml_dtypes.bfloat16

NCORES = 8
S = 8192
D = 1024
TOK = 2048
FC = 8
DFF = 2816
NJ = DFF // 128
EPS = 1e-6
NEG = -30000.0
NDC = 4352
DOFF = 2063


class Buf:
    __slots__ = ("name", "w", "r", "dsem", "dcnt", "dkey")

    def __init__(self, name):
        self.name = name
        self.w = None
        self.r = {}
        self.dsem = None
        self.dcnt = 0
        self.dkey = None


class Eng:
    def __init__(self, name, h, sem):
        self.name = name
        self.gen = 0
        self.key = ("e", name, 0)
        self.h = h
        self.sem = sem
        self.count = 0
        self.seen = {}


import os as _os
ATTACH_WAIT = _os.environ.get("KB_ATTACH", "1") == "1"
ATTACH_ENGS = tuple(_os.environ.get("KB_ATTACH_ENGS", "act,dve,pool,pe").split(","))
SEM_ROTATE = 6000


class KB:
    def __init__(self, nc):
        self.nc = nc
        self.stack = contextlib.ExitStack()
        self.engs = {}
        for key, h in (("pe", nc.tensor), ("act", nc.scalar), ("dve", nc.vector),
                       ("pool", nc.gpsimd), ("sp", nc.sync)):
            sem = self.stack.enter_context(nc.semaphore("s_" + key)) if key != "sp" else None
            self.engs[key] = Eng(key, h, sem)
        self.csem = self.stack.enter_context(nc.semaphore("s_coll"))
        self.hsem = self.stack.enter_context(nc.semaphore("s_hand"))
        self.hcnt = 0
        self.dma_bufs = []
        self.nbuf = 0
        self.free_dsems = {"hw": [], "sw": []}
        self.nsem = 0
        self.ccnt = 0

    def sb(self, name, shape, dtype, stack=None):
        self.nbuf += 1
        return (stack or self.stack).enter_context(self.nc.sbuf_tensor("S%d_%s" % (self.nbuf, name), list(shape), dtype))

    def ps(self, name, shape, dtype, stack=None):
        self.nbuf += 1
        return (stack or self.stack).enter_context(self.nc.psum_tensor("P%d_%s" % (self.nbuf, name), list(shape), dtype))

    def buf(self, name=None):
        self.nbuf += 1
        return Buf("%s_%d" % (name or "b", self.nbuf))

    def bufs(self, n, name=None):
        return [self.buf(name) for _ in range(n)]

    def _deps(self, E, r, w, attach=False):
        deps = {}

        def need(k, s, v):
            if k[0] == "e" and k[1] == "pe" and E.name == "pe":
                return
            if E.seen.get(k, 0) >= v:
                return
            if k not in deps or deps[k][1] < v:
                deps[k] = (s, v)

        for b in r:
            if b.w is not None:
                need(*b.w)
        for b in w:
            if b.w is not None:
                need(*b.w)
            for kk, (s, v) in b.r.items():
                need(kk, s, v)
        items = list(deps.items())
        carry = None
        if ATTACH_WAIT and attach and items:
            kk, (s, v) = items.pop()
            E.seen[kk] = v
            carry = (s, v)
        for kk, (s, v) in items:
            E.h.wait_ge(s, v)
            E.seen[kk] = v
        return carry

    @staticmethod
    def _mark(ev, r, w):
        kk, s, v = ev
        for b in r:
            b.r[kk] = (s, v)
        for b in w:
            b.w = ev
            b.r = {}

    def op(self, eng, fn, r=(), w=()):
        E = self.engs[eng]
        carry = self._deps(E, r, w, attach=(eng in ATTACH_ENGS))
        ins = fn(E.h)
        if carry is not None:
            ins._wait_ge(carry[0], carry[1])
        E.count += 1
        ins.then_inc(E.sem, 1)
        self._mark((E.key, E.sem, E.count), r, w)
        return ins

    def _dsem(self, sbuf, cls):
        if sbuf.dsem is None:
            sbuf.dsem = {}
        if cls not in sbuf.dsem:
            pool = self.free_dsems[cls]
            if pool:
                ent = pool.pop()
            else:
                self.nsem += 1
                sem = self.stack.enter_context(self.nc.semaphore("d%s%d" % (cls, self.nsem)))
                ent = [sem, ("d", self.nsem), 0]
            sbuf.dsem[cls] = ent
            self.dma_bufs.append((sbuf, cls))
        return sbuf.dsem[cls]

    def dma(self, q, out, in_, r=(), w=(), sem_buf=None, **kw):
        E = self.engs[q]
        self._deps(E, r, w)
        sbuf = sem_buf or (w[0] if w else r[0])
        ent = self._dsem(sbuf, "sw" if q == "pool" else "hw")
        ins = E.h.dma_start(out=out, in_=in_, **kw)
        ent[2] += 16
        ins.then_inc(ent[0], 16)
        self._mark((ent[1], ent[0], ent[2]), r, w)
        return ins

    def coll(self, kind, in_ap, out_ap, groups, r=(), w=()):
        E = self.engs["pool"]
        self._deps(E, r, w)
        ins = E.h.collective_compute(kind, ALU.bypass, replica_groups=groups, ins=[in_ap], outs=[out_ap])
        self.ccnt += 1
        ins.then_inc(self.csem, 1)
        self._mark((("c", 0), self.csem, self.ccnt), r, w)
        return ins

    def barrier(self, release=True):
        for E in self.engs.values():
            for Fg in self.engs.values():
                if Fg.count == 0:
                    continue
                if Fg is E and E.name == "pe":
                    continue
                if E.seen.get(Fg.key, 0) < Fg.count:
                    E.h.wait_ge(Fg.sem, Fg.count)
                    E.seen[Fg.key] = Fg.count
            for b, cls in self.dma_bufs:
                sem, dkey, cnt = b.dsem[cls]
                if cnt and E.seen.get(dkey, 0) < cnt:
                    E.h.wait_ge(sem, cnt)
                    E.seen[dkey] = cnt
            if self.ccnt and E.seen.get(("c", 0), 0) < self.ccnt:
                E.h.wait_ge(self.csem, self.ccnt)
                E.seen[("c", 0)] = self.ccnt
        if any(E.count > SEM_ROTATE for E in self.engs.values()):
            parts = list(self.engs.values())
            for rnd in range(2):
                self.hcnt += len(parts)
                for E in parts:
                    E.h.sem_inc(self.hsem, 1)
                for E in parts:
                    E.h.wait_ge(self.hsem, self.hcnt)
                if rnd == 0:
                    for E in parts:
                        if E.sem is not None and E.count > 0:
                            E.h.sem_clear(E.sem)
                            E.gen += 1
                            E.key = ("e", E.name, E.gen)
                            E.count = 0
        if release:
            for b, cls in self.dma_bufs:
                self.free_dsems[cls].append(b.dsem.pop(cls))
                b.w = None
                b.r = {}
            self.dma_bufs = []

    def close(self):
        self.stack.close()


def _rel_bucket_np(dist):
    n = np.maximum(dist, 0)
    nf = np.maximum(n, 1).astype(np.float32)
    large = 16 + (np.log(nf / np.float32(16)) / np.float32(math.log(128 / 16)) * np.float32(16)).astype(np.int32)
    large = np.minimum(large, 31)
    return np.where(n < 16, n, large)


def host_consts():
    c = {}
    i = np.arange(128)
    c["ident_f"] = np.eye(128, dtype=np.float32)
    c["ident_b"] = np.eye(128, dtype=np.float32).astype(NPBF)
    c["ones_b"] = np.ones((128, 128), np.float32).astype(NPBF)
    c["negu_b"] = (-(i[:, None] >= i[None, :]).astype(np.float32)).astype(NPBF)
    c["negones_b"] = (-np.ones((128, 2), np.float32)).astype(NPBF)
    c["mask_sb"] = (i[:, None] < i[None, :]).astype(np.float32).astype(NPBF)
    return c


class Ctx:
    pass


def load_consts(k, cdram, names, stack=None):
    cx = Ctx()
    for nm in names:
        ap = cdram[nm]
        t = k.sb("sc_" + nm, list(ap.shape), ap.dtype, stack)
        b = k.buf("c_" + nm)
        k.dma("sp", t[:], ap, w=[b])
        setattr(cx, nm, t)
        setattr(cx, nm + "_b", b)
    return cx


def rmsnorm_chunk(k, cx, W, h_aps, h_buf, gcol, gcol_b, out_aps, out_buf, n=512):
    sq, sqb, ps, psb, rs, rsb = W["sq"], W["sq_b"], W["ps_n"], W["ps_n_b"], W["rs"], W["rs_b"]
    for fc in range(FC):
        k.op("act", lambda e, fc=fc: e.activation(out=sq[:, fc, :n], in_=h_aps[fc], func=AF.Square),
             r=[h_buf], w=[sqb])
    for fc in range(FC):
        k.op("pe", lambda e, fc=fc: e.matmul(ps[:, :n], lhsT=cx.ones_b[:], rhs=sq[:, fc, :n],
                                              start=(fc == 0), stop=(fc == FC - 1)),
             r=[sqb, cx.ones_b_b], w=[psb])
    k.op("act", lambda e: e.activation(out=rs[:, :n], in_=ps[:, :n], func=AF.Sqrt, bias=cx.eps_col[:, 0:1], scale=1.0 / D),
         r=[psb, cx.eps_col_b], w=[rsb])
    k.op("dve", lambda e: e.reciprocal(out=rs[:, :n], in_=rs[:, :n]), r=[rsb], w=[rsb])
    for fc in range(FC):
        k.op("dve", lambda e, fc=fc: e.scalar_tensor_tensor(out=out_aps[fc], in0=h_aps[fc], scalar=gcol[:, fc:fc + 1],
                                                            in1=rs[:, :n], op0=ALU.mult, op1=ALU.mult),
             r=[h_buf, gcol_b, rsb], w=[out_buf])


def norm_work(k, stack=None):
    W = {}
    W["sq"] = k.sb("n_sq", [128, FC, 512], BF16, stack)
    W["sq_b"] = k.buf("n_sq")
    W["ps_n"] = k.ps("n_ps", [128, 512], F32, stack)
    W["ps_n_b"] = k.buf("n_ps")
    W["rs"] = k.sb("n_rs", [128, 512], F32, stack)
    W["rs_b"] = k.buf("n_rs")
    return W


def make_eps(k, cx, stack=None):
    cx.eps_col = k.sb("eps_col", [128, 1], F32, stack)
    cx.eps_col_b = k.buf("eps")
    k.op("pool", lambda e: e.memset(cx.eps_col[:], EPS), w=[cx.eps_col_b])


def load_gain_cols(k, gain_ap, name, stack=None):
    t = k.sb(name, [128, FC], F32, stack)
    b = k.buf(name)
    k.dma("sp", t[:], gain_ap.rearrange("(fc p) -> p fc", p=128), w=[b], allow_slow_non_contiguous=True)
    return t, b


def phase_norm_out(k, cx, hT, hb, gain_ap, dst, dst_b, stack):
    W = norm_work(k, stack)
    gcol, gcol_b = load_gain_cols(k, gain_ap, "g_mix", stack)
    hn = [k.sb("hn%d" % i, [128, FC, 512], BF16, stack) for i in range(2)]
    hnb = k.bufs(2, "hn")
    dv = dst.rearrange("(fc p) t -> p fc t", p=128)
    for c in range(4):
        cs = slice(c * 512, (c + 1) * 512)
        rmsnorm_chunk(k, cx, W, [hT[:, fc, cs] for fc in range(FC)], hb[c], gcol, gcol_b,
                      [hn[c % 2][:, fc, :] for fc in range(FC)], hnb[c % 2])
        k.dma("sp", dv[:, :, cs], hn[c % 2][:], r=[hnb[c % 2]], w=[dst_b])


def phase_sb(k, cx, hn_all, hn_all_b, w_ap, a2a, a2a_b, stack, nsb=16):
    scale = 0.125
    QT = [k.sb("QT%d" % i, [128, S], BF16, stack) for i in range(2)]
    KT = [k.sb("KT%d" % i, [128, S], BF16, stack) for i in range(2)]
    V = k.sb("V", [128, 64, 260], BF16, stack)
    qkb = k.buf("qkv")
    k.op("pool", lambda e: e.memset(V[:].rearrange("p b (h c) -> p b h c", c=65)[:, :, :, 64:65], 1.0), w=[qkb])
    psA2 = k.ps("psA2", [128, 1024], F32, stack)
    psB2 = k.ps("psB2", [128, 1024], F32, stack)
    ps = [psA2[:, 0:512], psA2[:, 512:1024], psB2[:, 0:512], psB2[:, 512:1024]] + \
         [k.ps("ps%d" % i, [128, 512], F32, stack)[:, :] for i in range(4, 8)]
    psb = k.bufs(8, "ps")
    pst = contextlib.ExitStack()
    wq = k.sb("sb_w", [128, FC, 768], BF16, pst)
    wqb = k.buf("sb_w")
    wv = w_ap.rearrange("(fc p) n -> p fc n", p=128)
    for fc in range(FC):
        k.dma("pool", wq[:, fc, :], wv[:, fc, :], w=[wqb])
    hnc = [k.sb("hnc%d" % i, [128, FC, 512], BF16, pst) for i in range(2)]
    hncb = k.bufs(2, "hnc")
    for c in range(16):
        rho, off = c // 4, (c % 4) * 512
        src = hn_all.rearrange("(q r h p) t -> r p q h t", q=4, r=4, h=2, p=128)[rho][:, :, :, off:off + 512]
        hb_ = hncb[c % 2]
        for q_ in range(4):
            k.dma("sp", hnc[c % 2][:, 2 * q_:2 * q_ + 2, :], src[:, q_, :, :], r=[hn_all_b], w=[hb_])
        x = hnc[c % 2]
        cs = slice(c * 512, (c + 1) * 512)
        j = 0
        for which, dstT in ((0, QT), (1, KT)):
            for hp in range(2):
                p_, pb_ = ps[j % 4], psb[j % 4]
                j += 1
                for fc in range(FC):
                    k.op("pe", lambda e, fc=fc, p_=p_, which=which, hp=hp: e.matmul(
                        p_[:, :], lhsT=wq[:, fc, which * 256 + hp * 128: which * 256 + (hp + 1) * 128],
                        rhs=x[:, fc, :], start=(fc == 0), stop=(fc == FC - 1)), r=[wqb, hb_], w=[pb_])
                if which == 0:
                    k.op("act", lambda e, p_=p_, hp=hp: e.activation(out=QT[hp][:, cs], in_=p_[:, :], func=AF.Copy, scale=scale),
                         r=[pb_], w=[qkb])
                else:
                    k.op("dve", lambda e, p_=p_, hp=hp: e.tensor_copy(out=KT[hp][:, cs], in_=p_[:, :]), r=[pb_], w=[qkb])
        for tt in range(4):
            p_, pb_ = ps[4 + tt % 2], psb[4 + tt % 2]
            for fc in range(FC):
                k.op("pe", lambda e, fc=fc, p_=p_, tt=tt: e.matmul(
                    p_[:, 0:256], lhsT=x[:, fc, tt * 128:(tt + 1) * 128], rhs=wq[:, fc, 512:768],
                    start=(fc == 0), stop=(fc == FC - 1)), r=[wqb, hb_], w=[pb_])
            vdst = V[:, c * 4 + tt, :].rearrange("p (h c) -> p h c", c=65)[:, :, 0:64]
            vsrc = p_[:, 0:256].rearrange("p (h c) -> p h c", c=64)
            k.op("dve" if tt % 2 else "act",
                 (lambda e, vdst=vdst, vsrc=vsrc: e.tensor_copy(out=vdst, in_=vsrc)) if tt % 2 else
                 (lambda e, vdst=vdst, vsrc=vsrc: e.copy(out=vdst, in_=vsrc)),
                 r=[pb_], w=[qkb])
    k.barrier()
    pst.close()
    ps_free = ps
    pst2 = contextlib.ExitStack()
    e1 = [k.sb("e1_%d" % i, [128, 1024], F32, stack) for i in range(2)]
    e1b = k.bufs(2, "e1")
    sp = [k.sb("sp_%d" % i, [128, 1024], BF16, stack) for i in range(3)]
    spb = k.bufs(3, "sp")
    aa = [k.sb("aa_%d" % i, [128, 1024], BF16, stack) for i in range(2)]
    aab = k.bufs(2, "aa")
    acc = k.sb("acc", [128, 4, 256], F32, stack)
    accb = k.bufs(4, "acc")
    Ec = k.sb("Ec", [128, 4, 4], F32, stack)
    Ecb = k.bufs(4, "Ec")
    tE = k.sb("tE", [128, 4], F32, stack)
    tEb = k.buf("tE")
    oT = [k.sb("oT%d" % i, [128, 2, 512], BF16, stack) for i in range(2)]
    oTb = k.bufs(2, "oT")
    psO, psOb = ps[4:6], psb[4:6]
    psT, psTb = ps[6:8], psb[6:8]
    Ab, Bb = k.buf("psA2"), k.buf("psB2")

    def Abank(j):
        return ps[0 + j]

    def Bbank(j):
        return ps[2 + j]
    steps = []
    for sbk in range(nsb):
        for hh in range(4):
            nu = 4 * sbk + 4
            chain = []
            for u in range(nu):
                kb = 4 * sbk + 3 - u
                chain.append(dict(kb=kb, i0=max(0, kb - 4 * sbk), diag=kb >= 4 * sbk))
            groups = [[c_] for c_ in chain[:4]] + [chain[i:i + 2] for i in range(4, nu, 2)]
            for gi, g_ in enumerate(groups):
                steps.append(dict(sbk=sbk, hh=hh, subs=g_, first=(gi == 0), last_sb=(hh == 3 and gi == len(groups) - 1)))
    for n_, St in enumerate(steps):
        St["n"] = n_
        hp, base = St["hh"] // 2, 64 * (St["hh"] % 2)
        for U in St["subs"]:
            U["N"] = 512 - 128 * U["i0"]
            U["qc"] = QT[hp][base:base + 64, 512 * St["sbk"] + 128 * U["i0"]: 512 * (St["sbk"] + 1)]
            U["kc"] = KT[hp][base:base + 64, 128 * U["kb"]:128 * (U["kb"] + 1)]
        St["W"] = 512 * (len(St["subs"]) - 1) + St["subs"][-1]["N"]
        St["diag"] = St["subs"][0]["diag"]

    def actv(out_t, in_banks, W, func, **kw):
        return lambda e: e.activation(out=out_t[:, :W], in_=in_banks[:, :W], func=func, **kw)

    def stage1a(St):
        n_, W = St["n"], St["W"]
        for j, U in enumerate(St["subs"]):
            k.op("pe", lambda e, j=j, U=U: e.matmul(Abank(j)[:, :U["N"]], lhsT=U["kc"], rhs=U["qc"], start=True, stop=True), r=[qkb], w=[Ab])
        k.op("act", actv(e1[n_ % 2], psA2, W, AF.Exp), r=[Ab], w=[e1b[n_ % 2]])

    def stage1b(St):
        n_, W = St["n"], St["W"]
        s_, sb_ = sp[n_ % 3], spb[n_ % 3]
        k.op("act", lambda e: e.activation(out=s_[:, :W], in_=e1[n_ % 2][:, :W], func=AF.Ln, bias=cx.one_col[:, 0:1], scale=1.0),
             r=[e1b[n_ % 2], cx.one_col_b], w=[sb_])
        if St["diag"]:
            k.op("pool", lambda e: e.tensor_tensor(out=s_[:, 0:128], in0=s_[:, 0:128], in1=cx.mask_sb[:], op=ALU.mult),
                 r=[sb_, cx.mask_sb_b], w=[sb_])

    def stage2(St):
        n_, W = St["n"], St["W"]
        s_, sb_ = sp[n_ % 3], spb[n_ % 3]
        a_, ab_ = aa[n_ % 2], aab[n_ % 2]
        for j, U in enumerate(St["subs"]):
            N = U["N"]
            k.op("pe", lambda e, j=j, N=N: e.matmul(Bbank(j)[:, :N], lhsT=cx.negu_b[:], rhs=s_[:, j * 512:j * 512 + N], start=True, stop=False),
                 r=[sb_, cx.negu_b_b], w=[Bb])
            k.op("pe", lambda e, j=j, N=N, U=U: e.matmul(Bbank(j)[:, :N], lhsT=U["kc"], rhs=U["qc"], start=False, stop=True), r=[qkb], w=[Bb])
        k.op("act", actv(a_, psB2, W, AF.Exp), r=[Bb], w=[ab_])
        if St["diag"]:
            k.op("pool", lambda e: e.tensor_tensor(out=a_[:, 0:128], in0=a_[:, 0:128], in1=cx.mask_sb[:], op=ALU.mult),
                 r=[ab_, cx.mask_sb_b], w=[ab_])

    def stage3(St):
        n_, hh, sbk = St["n"], St["hh"], St["sbk"]
        a_, ab_ = aa[n_ % 2], aab[n_ % 2]
        if St["first"]:
            k.op("pool", lambda e: e.memset(acc[:, :, hh * 64:(hh + 1) * 64], 0.0), w=[accb[hh]])
            k.op("pool", lambda e: e.memset(Ec[:, hh, :], 1.0), w=[Ecb[hh]])
        for j, U in enumerate(St["subs"]):
            i0, kb = U["i0"], U["kb"]
            O, Ob = psO[j], psOb[j]
            for i in range(i0, 4):
                cl = slice(j * 512 + (i - i0) * 128, j * 512 + (i - i0 + 1) * 128)
                k.op("pe", lambda e, i=i, cl=cl, O=O, kb=kb: e.matmul(O[:, i * 65:(i + 1) * 65], lhsT=a_[:, cl],
                                                                     rhs=V[:, kb, hh * 65:(hh + 1) * 65], start=True, stop=True),
                     r=[ab_, qkb], w=[Ob])
            for i in range(i0, 4):
                k.op("dve", lambda e, i=i, O=O: e.scalar_tensor_tensor(
                    out=acc[:, i, hh * 64:(hh + 1) * 64], in0=O[:, i * 65:i * 65 + 64], scalar=Ec[:, hh, i:i + 1],
                    in1=acc[:, i, hh * 64:(hh + 1) * 64], op0=ALU.mult, op1=ALU.add),
                    r=[Ob, Ecb[hh], accb[hh]], w=[accb[hh]])
            Ov = O[:, 0:260].rearrange("p (i c) -> p i c", c=65)
            k.op("dve", lambda e, Ov=Ov, i0=i0: e.tensor_tensor(out=tE[:, i0:4], in0=Ov[:, i0:4, 64], in1=Ec[:, hh, i0:4], op=ALU.mult),
                 r=[Ob, Ecb[hh]], w=[tEb])
            k.op("dve", lambda e, i0=i0: e.tensor_tensor(out=Ec[:, hh, i0:4], in0=Ec[:, hh, i0:4], in1=tE[:, i0:4], op=ALU.subtract),
                 r=[tEb, Ecb[hh]], w=[Ecb[hh]])
        if St["last_sb"]:
            y2 = sbk % 2
            for i in range(4):
                for fh in range(2):
                    T_, Tb_ = psT[(i * 2 + fh) % 2], psTb[(i * 2 + fh) % 2]
                    k.op("pe", lambda e, i=i, fh=fh, T_=T_: e.transpose(out=T_[:, 0:128], in_=acc[:, i, fh * 128:(fh + 1) * 128], identity=cx.ident_f[:]),
                         r=accb + [cx.ident_f_b], w=[Tb_])
                    k.op("dve", lambda e, i=i, fh=fh, T_=T_: e.tensor_copy(out=oT[y2][:, fh, i * 128:(i + 1) * 128], in_=T_[:, 0:128]),
                         r=[Tb_], w=[oTb[y2]])
            dest, off = sbk // 4, (sbk % 4) * 512
            k.dma("sp", a2a[dest].rearrange("(fh p) t -> p fh t", p=128)[:, :, off:off + 512], oT[y2][:], r=[oTb[y2]], w=[a2a_b])

    nst = len(steps)
    for it in range(nst + 2):
        if it < nst:
            stage1a(steps[it])
        if 0 <= it - 1 < nst:
            stage2(steps[it - 1])
        if it < nst:
            stage1b(steps[it])
        if 0 <= it - 2 < nst:
            stage3(steps[it - 2])
            if steps[it - 2]["last_sb"] and k.engs["pe"].count > SEM_ROTATE:
                k.barrier()


def make_one(k, cx, stack=None):
    cx.one_col = k.sb("one_col", [128, 1], F32, stack)
    cx.one_col_b = k.buf("one")
    k.op("pool", lambda e: e.memset(cx.one_col[:], 1.0), w=[cx.one_col_b])


def dram_consts(nc, consts, names):
    out = {}
    for nm in names:
        a = consts[nm]
        dt_ = BF16 if a.dtype == NPBF else F32
        out[nm] = nc.dram_tensor("c_" + nm, list(a.shape), dt_, kind="ExternalInput").ap()
    return out


def build_prog_norm0():
    nc = bass.Bass("TRN2", target_bir_lowering=False)
    consts = host_consts()
    names = ["ones_b"]
    cd = dram_consts(nc, consts, names)
    xT = nc.dram_tensor("xT", [D, TOK], F32, kind="ExternalInput").ap()
    gain = nc.dram_tensor("gain", [D], F32, kind="ExternalInput").ap()
    hn = nc.dram_tensor("hnT", [D, TOK], BF16, kind="ExternalOutput").ap()
    k = KB(nc)
    cx = load_consts(k, cd, names)
    make_eps(k, cx)
    hT = k.sb("hT", [128, FC, TOK], F32)
    hb = k.bufs(4, "hT")
    xv = xT.rearrange("(fc p) t -> p fc t", p=128)
    for c in range(4):
        k.dma("sp", hT[:, :, c * 512:(c + 1) * 512], xv[:, :, c * 512:(c + 1) * 512], w=[hb[c]])
    hn_b = k.buf("hn_dram")
    phase_norm_out(k, cx, hT, hb, gain, hn, hn_b, k.stack)
    k.barrier()
    k.close()
    return nc, {nm: consts[nm] for nm in names}


def build_prog_sb(nsb=16):
    nc = bass.Bass("TRN2", target_bir_lowering=False)
    consts = host_consts()
    names = ["ident_f", "negu_b", "negones_b", "mask_sb"]
    cd = dram_consts(nc, consts, names)
    hn_all = nc.dram_tensor("hn_all", [4 * D, TOK], BF16, kind="ExternalInput").ap()
    w = nc.dram_tensor("sb_w", [D, 768], F32, kind="ExternalInput").ap()
    a2a = nc.dram_tensor("a2a", [4, 256, TOK], BF16, kind="ExternalOutput").ap()
    k = KB(nc)
    cx = load_consts(k, cd, names)
    make_one(k, cx)
    phase_sb(k, cx, hn_all, k.buf("hn_all"), w, a2a, k.buf("a2a"), k.stack, nsb=nsb)
    k.barrier()
    k.close()
    return nc, {nm: consts[nm] for nm in names}


def load_w_cast(k, name, src_view, shape, stack, nsplit=None):
    t = k.sb(name, shape, BF16, stack)
    b = k.buf(name)
    a = shape[1]
    for i in range(a):
        k.dma("pool", t[:, i, :], src_view[:, i, :], w=[b])
    return t, b


def tail_weights_to_bf16(k, nc, tag, w_out_ap, w1_ap, w2_ap, wg_ap, wp_ap):
    out = {}
    for nm, ap in (("w_out", w_out_ap), ("w1", w1_ap), ("w2", w2_ap), ("wg", wg_ap), ("wp", wp_ap)):
        rows, cols = ap.shape
        dst = nc.dram_tensor("wb_%s_%s" % (nm, tag), [rows, cols], BF16, kind="Internal").ap()
        b = k.buf("wb_" + nm)
        for r0 in range(0, rows, 256):
            r1 = min(rows, r0 + 256)
            k.dma("pool", dst[r0:r1, :], ap[r0:r1, :], w=[b])
        out[nm] = (dst, b)
    return out


def load_w_bf16(k, name, src_view, src_b, shape, stack):
    t = k.sb(name, shape, BF16, stack)
    b = k.buf(name)
    k.dma("sp", t[:], src_view, r=[src_b], w=[b])
    return t, b


def phase_tail(k, cx, hT, hb, oT_dram, oT_b, wb, g_ffn_ap, g_ple_ap, pT_ap, oT_loader=None):
    ps_names = ["tp%d" % i for i in range(6)]
    with contextlib.ExitStack() as st0:
        ps = [k.ps(n_, [128, 512], F32, st0) for n_ in ps_names]
        psb = k.bufs(6, "tp")
        W = norm_work(k, st0)
        with contextlib.ExitStack() as st:
            wo, wob = load_w_bf16(k, "wo", wb["w_out"][0].rearrange("(fc p) n -> p fc n", p=128), wb["w_out"][1], [128, FC, D], st)
            oc = [k.sb("oc%d" % i, [128, FC, 512], BF16, st) for i in range(2)]
            ocb = k.bufs(2, "oc")
            ov = oT_dram.rearrange("(fc p) t -> p fc t", p=128) if oT_loader is None else None
            n_ = 0
            for c in range(4):
                cs = slice(c * 512, (c + 1) * 512)
                if oT_loader is None:
                    k.dma("sp", oc[c % 2][:], ov[:, :, cs], r=[oT_b], w=[ocb[c % 2]])
                else:
                    oT_loader(oc[c % 2], ocb[c % 2], cs)
                for of in range(FC):
                    p_, pb_ = ps[n_ % 4], psb[n_ % 4]
                    n_ += 1
                    for fc in range(FC):
                        k.op("pe", lambda e, fc=fc, of=of, p_=p_, c=c: e.matmul(
                            p_[:, :], lhsT=wo[:, fc, of * 128:(of + 1) * 128], rhs=oc[c % 2][:, fc, :],
                            start=(fc == 0), stop=(fc == FC - 1)), r=[wob, ocb[c % 2]], w=[pb_])
                    k.op("dve", lambda e, of=of, p_=p_, cs=cs: e.tensor_tensor(out=hT[:, of, cs], in0=hT[:, of, cs], in1=p_[:, :], op=ALU.add),
                         r=[pb_, hb[c]], w=[hb[c]])
            k.barrier()
        with contextlib.ExitStack() as st:
            gcol, gcol_b = load_gain_cols(k, g_ffn_ap, "g_ffn", st)
            hn = k.sb("f_hn", [128, FC, 1024], BF16, st)
            hnb = k.bufs(2, "f_hn")
            gT = k.sb("f_gT", [128, NJ, 1024], BF16, st)
            gTb = k.buf("f_gT")
            wab = [k.sb("f_wab%d" % i, [128, FC, 2, 512], BF16, st) for i in range(2)]
            wabb = k.bufs(2, "f_wab")
            w2t = [k.sb("f_w2%d" % i, [128, NJ, 256], BF16, st) for i in range(2)]
            w2b = k.bufs(2, "f_w2")
            sl = [k.sb("f_sl%d" % i, [128, 512], F32, st) for i in range(2)]
            slb = k.bufs(2, "f_sl")
            w1v = wb["w1"][0].rearrange("(fc p) n -> p fc n", p=128)
            w2v = wb["w2"][0].rearrange("(j p) n -> p j n", p=128)
            jgroups = [(j0, min(4, NJ - j0)) for j0 in range(0, NJ, 4)]
            n_ = 0
            nw = 0
            nw2 = 0
            for tc in range(2):
                for hf in range(2):
                    c = tc * 2 + hf
                    cs = slice(c * 512, (c + 1) * 512)
                    rmsnorm_chunk(k, cx, W, [hT[:, fc, cs] for fc in range(FC)], hb[c], gcol, gcol_b,
                                  [hn[:, fc, hf * 512:(hf + 1) * 512] for fc in range(FC)], hnb[hf])
                for (j0, gs) in jgroups:
                    wt, wtb = wab[nw % 2], wabb[nw % 2]
                    nw += 1
                    k.dma("sp", wt[:, :, 0, 0:gs * 128], w1v[:, :, j0 * 128:(j0 + gs) * 128], r=[wb["w1"][1]], w=[wtb])
                    k.dma("sp", wt[:, :, 1, 0:gs * 128], w1v[:, :, DFF + j0 * 128:DFF + (j0 + gs) * 128], r=[wb["w1"][1]], w=[wtb])
                    for jj in range(gs):
                        j = j0 + jj
                        js = slice(jj * 128, (jj + 1) * 128)
                        for hf in range(2):
                            hs = slice(hf * 512, (hf + 1) * 512)
                            pa, pab = ps[n_ % 2], psb[n_ % 2]
                            pb2, pbb = ps[2 + n_ % 2], psb[2 + n_ % 2]
                            s_, sb_ = sl[n_ % 2], slb[n_ % 2]
                            n_ += 1
                            for fc in range(FC):
                                k.op("pe", lambda e, fc=fc, pa=pa, wt=wt, hs=hs, js=js: e.matmul(pa[:, :], lhsT=wt[:, fc, 0, js], rhs=hn[:, fc, hs],
                                                                                                 start=(fc == 0), stop=(fc == FC - 1)), r=[wtb, hnb[hf]], w=[pab])
                            for fc in range(FC):
                                k.op("pe", lambda e, fc=fc, pb2=pb2, wt=wt, hs=hs, js=js: e.matmul(pb2[:, :], lhsT=wt[:, fc, 1, js], rhs=hn[:, fc, hs],
                                                                                                   start=(fc == 0), stop=(fc == FC - 1)), r=[wtb, hnb[hf]], w=[pbb])
                            k.op("act", lambda e, pa=pa, s_=s_: e.activation(out=s_[:, :], in_=pa[:, :], func=AF.Silu), r=[pab], w=[sb_])
                            k.op("dve", lambda e, pb2=pb2, s_=s_, j=j, hs=hs: e.tensor_tensor(out=gT[:, j, hs], in0=s_[:, :], in1=pb2[:, :], op=ALU.mult),
                                 r=[sb_, pbb], w=[gTb])
                for op_ in range(FC // 2):
                    wt, wtb = w2t[nw2 % 2], w2b[nw2 % 2]
                    nw2 += 1
                    k.dma("sp", wt[:], w2v[:, :, op_ * 256:(op_ + 1) * 256], r=[wb["w2"][1]], w=[wtb])
                    for o2 in range(2):
                        of = op_ * 2 + o2
                        for hf in range(2):
                            c = tc * 2 + hf
                            cs = slice(c * 512, (c + 1) * 512)
                            p_, pb_ = ps[4 + n_ % 2], psb[4 + n_ % 2]
                            n_ += 1
                            for j in range(NJ):
                                k.op("pe", lambda e, j=j, p_=p_, wt=wt, hf=hf, o2=o2: e.matmul(p_[:, :], lhsT=wt[:, j, o2 * 128:(o2 + 1) * 128], rhs=gT[:, j, hf * 512:(hf + 1) * 512],
                                                                                               start=(j == 0), stop=(j == NJ - 1)), r=[wtb, gTb], w=[pb_])
                            k.op("dve", lambda e, of=of, p_=p_, cs=cs: e.tensor_tensor(out=hT[:, of, cs], in0=hT[:, of, cs], in1=p_[:, :], op=ALU.add),
                                 r=[pb_, hb[c]], w=[hb[c]])
            k.barrier()
        with contextlib.ExitStack() as st:
            gcol, gcol_b = load_gain_cols(k, g_ple_ap, "g_ple", st)
            wg, wgb = load_w_bf16(k, "wg", wb["wg"][0].rearrange("(fc p) n -> p fc n", p=128), wb["wg"][1], [128, FC, D], st)
            wp, wpb = load_w_bf16(k, "wp", wb["wp"][0].rearrange("(kc p) n -> p kc n", p=128), wb["wp"][1], [128, 2, D], st)
            hn = [k.sb("p_hn%d" % i, [128, FC, 512], BF16, st) for i in range(2)]
            hnb = k.bufs(2, "p_hn")
            pt = [k.sb("p_pt%d" % i, [128, 2, 512], BF16, st) for i in range(2)]
            ptb = k.bufs(2, "p_pt")
            sg = [k.sb("p_sg%d" % i, [128, 512], F32, st) for i in range(2)]
            sgb = k.bufs(2, "p_sg")
            pv = pT_ap.rearrange("(kc p) t -> p kc t", p=128)
            n_ = 0
            for c in range(4):
                cs = slice(c * 512, (c + 1) * 512)
                x_, xb_ = hn[c % 2], hnb[c % 2]
                rmsnorm_chunk(k, cx, W, [hT[:, fc, cs] for fc in range(FC)], hb[c], gcol, gcol_b,
                              [x_[:, fc, :] for fc in range(FC)], xb_)
                q_, qb_ = pt[c % 2], ptb[c % 2]
                for kc in range(2):
                    k.dma("pool", q_[:, kc, :], pv[:, kc, cs], w=[qb_])
                for of in range(FC):
                    pg, pgb = ps[n_ % 2], psb[n_ % 2]
                    pp, ppb = ps[2 + n_ % 2], psb[2 + n_ % 2]
                    s_, sb_ = sg[n_ % 2], sgb[n_ % 2]
                    n_ += 1
                    for fc in range(FC):
                        k.op("pe", lambda e, fc=fc, of=of, pg=pg, x_=x_: e.matmul(pg[:, :], lhsT=wg[:, fc, of * 128:(of + 1) * 128], rhs=x_[:, fc, :],
                                                                                  start=(fc == 0), stop=(fc == FC - 1)), r=[wgb, xb_], w=[pgb])
                    for kc in range(2):
                        k.op("pe", lambda e, kc=kc, of=of, pp=pp, q_=q_: e.matmul(pp[:, :], lhsT=wp[:, kc, of * 128:(of + 1) * 128], rhs=q_[:, kc, :],
                                                                                  start=(kc == 0), stop=(kc == 1)), r=[wpb, qb_], w=[ppb])
                    k.op("act", lambda e, pg=pg, s_=s_: e.activation(out=s_[:, :], in_=pg[:, :], func=AF.Sigmoid), r=[pgb], w=[sb_])
                    k.op("dve", lambda e, pp=pp, s_=s_: e.tensor_tensor(out=s_[:, :], in0=s_[:, :], in1=pp[:, :], op=ALU.mult),
                         r=[sb_, ppb], w=[sb_])
                    k.op("pool", lambda e, of=of, s_=s_, cs=cs: e.tensor_tensor(out=hT[:, of, cs], in0=hT[:, of, cs], in1=s_[:, :], op=ALU.add),
                         r=[sb_, hb[c]], w=[hb[c]])
            k.barrier()


def phase_final_norm(k, cx, hT, hb, gain_ap, dst, dst_b, stack):
    W = norm_work(k, stack)
    gcol, gcol_b = load_gain_cols(k, gain_ap, "g_fin", stack)
    on = [k.sb("fn%d" % i, [128, FC, 512], F32, stack) for i in range(2)]
    onb = k.bufs(2, "fn")
    dv = dst.rearrange("(fc p) t -> p fc t", p=128)
    for c in range(4):
        cs = slice(c * 512, (c + 1) * 512)
        rmsnorm_chunk(k, cx, W, [hT[:, fc, cs] for fc in range(FC)], hb[c], gcol, gcol_b,
                      [on[c % 2][:, fc, :] for fc in range(FC)], onb[c % 2])
        k.dma("sp", dv[:, :, cs], on[c % 2][:], r=[onb[c % 2]], w=[dst_b])


def build_prog_tail(final):
    nc = bass.Bass("TRN2", target_bir_lowering=False)
    consts = host_consts()
    names = ["ones_b"]
    cd = dram_consts(nc, consts, names)

    def din(nm, shape, dt_=F32):
        return nc.dram_tensor(nm, shape, dt_, kind="ExternalInput").ap()
    hin = din("hT_in", [D, TOK])
    oT = din("oT", [D, TOK], BF16)
    w_out = din("w_out", [D, D])
    g_ffn = din("g_ffn", [D])
    w1 = din("w1", [D, 2 * DFF])
    w2 = din("w2", [DFF, D])
    g_ple = din("g_ple", [D])
    wg = din("wg", [D, D])
    wp = din("wp", [256, D])
    pT = din("pT", [256, TOK])
    g_next = din("g_next", [D])
    k = KB(nc)
    cx = load_consts(k, cd, names)
    make_eps(k, cx)
    hT = k.sb("hT", [128, FC, TOK], F32)
    hb = k.bufs(4, "hT")
    xv = hin.rearrange("(fc p) t -> p fc t", p=128)
    for c in range(4):
        k.dma("sp", hT[:, :, c * 512:(c + 1) * 512], xv[:, :, c * 512:(c + 1) * 512], w=[hb[c]])
    wbf = tail_weights_to_bf16(k, nc, "u", w_out, w1, w2, wg, wp)
    phase_tail(k, cx, hT, hb, oT, k.buf("oT"), wbf, g_ffn, g_ple, pT)
    with contextlib.ExitStack() as st:
        if final:
            out = nc.dram_tensor("outT", [D, TOK], F32, kind="ExternalOutput").ap()
            phase_final_norm(k, cx, hT, hb, g_next, out, k.buf("outT"), st)
        else:
            hout = nc.dram_tensor("hT_out", [D, TOK], F32, kind="ExternalOutput").ap()
            hn = nc.dram_tensor("hnT", [D, TOK], BF16, kind="ExternalOutput").ap()
            hob = k.buf("hT_out")
            ov = hout.rearrange("(fc p) t -> p fc t", p=128)
            for c in range(4):
                k.dma("sp", ov[:, :, c * 512:(c + 1) * 512], hT[:, :, c * 512:(c + 1) * 512], r=[hb[c]], w=[hob])
            phase_norm_out(k, cx, hT, hb, g_next, hn, k.buf("hn_dram"), st)
        k.barrier()
    k.close()
    return nc, {nm: consts[nm] for nm in names}


def nsa_host_consts():
    c = {}
    i = np.arange(128)
    c["ident_f"] = np.eye(128, dtype=np.float32)
    c["jflip"] = np.ascontiguousarray(np.eye(128, dtype=np.float32)[::-1])
    dist = np.arange(NDC) - DOFF
    oh = np.zeros((33, NDC), np.float32)
    bk = _rel_bucket_np(dist)
    for ii in range(NDC):
        if dist[ii] < 0:
            oh[32, ii] = 1.0
        else:
            oh[bk[ii], ii] = 1.0
    c["ohc"] = oh
    dm = np.zeros((33, 33), np.float32)
    for b in range(32):
        dm[b, b] += 1.0
        dm[31, b] -= 1.0
    dm[32, 32] = NEG
    c["dm"] = dm
    keys = np.arange(S)
    c["ex"] = (np.arange(128)[:, None] == (keys[None, :] // 64)).astype(np.float32).astype(NPBF)
    t4 = np.where(i[None, :] < i[:, None], 0.0, NEG).astype(np.float32)
    c["t4"] = np.ascontiguousarray(np.tile(t4, (1, 4)))
    n = np.arange(512)
    cs_, ss_ = n * 16, np.arange(128) * 64
    ov = ((cs_[:, None] < ss_[None, :] + 64) & (cs_[:, None] + 32 > ss_[None, :])).astype(np.float32)
    ov[511, :] = 0.0
    c["ov"] = ov.astype(NPBF)
    q = np.arange(128)
    cq = (q >= 64).astype(np.int64)[:, None]
    rel = (np.arange(256) - 128)[None, :]
    forced = (rel == cq) | (rel == cq - 1)
    causal = rel <= cq
    c["mc_rel"] = (causal & ~forced).astype(np.float32)
    c["ma_rel"] = np.where(forced, 100.0, np.where(causal, 0.0, -1.0)).astype(np.float32)
    return c


GELU_C = 1.5957691216057308


def phase_nsa(k, cx, hn_all, hn_all_b, w_ap, rb_ap, peT_ap, w1_ap, w2k_ap, w2v_ap, bdz, a2a, a2a_b, stack, nqt=64, stop=99):
    scale = 0.125
    ps = [k.ps("np%d" % i, [128, 512], F32, stack) for i in range(8)]
    psb = k.bufs(8, "np")
    QT = [k.sb("nQT%d" % i, [128, S], BF16, stack) for i in range(2)]
    KST = k.sb("nKST", [128, S], BF16, stack)
    KWT = k.sb("nKWT", [128, S], BF16, stack)
    VS = k.sb("nVS", [128, 64, 65], BF16, stack)
    VW = k.sb("nVW", [128, 64, 65], BF16, stack)
    G = k.sb("nG", [128, 64, 12], F32, stack)
    kcT = k.sb("nkcT", [128, 512], BF16, stack)
    VCX = k.sb("nVCX", [128, 4, 193], BF16, stack)
    T0 = k.sb("nT0", [128, 512], F32, stack)
    T1 = k.sb("nT1", [128, 512], F32, stack)
    pb = k.buf("nsa_persist")
    k.op("pool", lambda e: e.memset(VS[:, :, 64:65], 1.0), w=[pb])
    k.op("pool", lambda e: e.memset(VW[:, :, 64:65], 1.0), w=[pb])
    k.op("pool", lambda e: e.memset(VCX[:], 0.0), w=[pb])
    k.op("pool", lambda e: e.memset(VCX[:, :, 64:65], 1.0), w=[pb])
    k.op("pool", lambda e: e.memset(kcT[:], 0.0), w=[pb])
    k.dma("sp", VCX[:, :, 65:193], cx.ov_dram.rearrange("(nb p) s -> p nb s", p=128), w=[pb])
    with contextlib.ExitStack() as st:
        rbe = k.sb("rbe", [33, 4], F32, st)
        rbeb = k.buf("rbe")
        k.op("pool", lambda e: e.memset(rbe[32:33, :], 1.0), w=[rbeb])
        k.dma("sp", rbe[0:32, :], rb_ap, w=[rbeb])
        ohc = k.sb("ohc", [33, NDC], F32, st)
        ohcb = k.buf("ohc")
        k.dma("sp", ohc[:], cx.ohc_dram, w=[ohcb])
        dm = k.sb("dm", [33, 33], F32, st)
        dmb = k.buf("dm")
        k.dma("sp", dm[:], cx.dm_dram, w=[dmb])
        rbx = k.sb("rbx", [33, 4], F32, st)
        rbxb = k.buf("rbx")
        k.op("pe", lambda e: e.matmul(ps[0][0:33, 0:4], lhsT=dm[:], rhs=rbe[:], start=True, stop=True), r=[dmb, rbeb], w=[psb[0]])
        k.op("dve", lambda e: e.tensor_copy(out=rbx[:], in_=ps[0][0:33, 0:4]), r=[psb[0]], w=[rbxb])
        bds = k.sb("bds", [4, NDC], F32, st)
        bdsb = k.buf("bds")
        nchunk = (NDC + 511) // 512
        for ci in range(nchunk):
            lo, hi = ci * 512, min(NDC, (ci + 1) * 512)
            p_, pb_ = ps[1 + ci % 2], psb[1 + ci % 2]
            k.op("pe", lambda e, p_=p_, lo=lo, hi=hi: e.matmul(p_[0:4, 0:hi - lo], lhsT=rbx[:], rhs=ohc[:, lo:hi], start=True, stop=True),
                 r=[rbxb, ohcb], w=[pb_])
            k.op("dve", lambda e, p_=p_, lo=lo, hi=hi: e.tensor_copy(out=bds[:, lo:hi], in_=p_[0:4, 0:hi - lo]), r=[pb_], w=[bdsb])
        bdzb = k.buf("bdz")
        k.dma("sp", bdz, bds[:], r=[bdsb], w=[bdzb])
        cx.bdz_b = bdzb
        U = k.sb("U", [128, 512], F32, st)
        Ub = k.buf("U")
        for m, Tm in ((0, T0), (1, T1)):
            src = bass.AP(bdz.tensor, DOFF - 127 + 128 * m, [[1, 128], [NDC, 4], [1, 128]])
            k.dma("sp", U[:].rearrange("p (g q) -> p g q", g=4), src, r=[bdzb], w=[Ub])
            k.op("pe", lambda e: e.matmul(ps[3][:, :], lhsT=cx.jflip[:], rhs=U[:], start=True, stop=True), r=[Ub, cx.jflip_b], w=[psb[3]])
            k.op("dve", lambda e, Tm=Tm: e.tensor_copy(out=Tm[:], in_=ps[3][:, :]), r=[psb[3]], w=[pb])
        k.barrier()
    if stop <= 0:
        return
    with contextlib.ExitStack() as st:
        w = k.sb("nw", [128, FC, 780], BF16, st)
        wb = k.buf("nw")
        wv = w_ap.rearrange("(fc p) n -> p fc n", p=128)
        for fc in range(FC):
            k.dma("pool", w[:, fc, :], wv[:, fc, :], w=[wb])
        KAT = k.sb("nKAT", [128, S], BF16, st)
        hnc = [k.sb("nhnc%d" % i, [128, FC, 512], BF16, st) for i in range(2)]
        hncb = k.bufs(2, "nhnc")
        for c in range(16):
            rho, off = c // 4, (c % 4) * 512
            src = hn_all.rearrange("(q r h p) t -> r p q h t", q=4, r=4, h=2, p=128)[rho][:, :, :, off:off + 512]
            hb_ = hncb[c % 2]
            for q_ in range(4):
                k.dma("sp", hnc[c % 2][:, 2 * q_:2 * q_ + 2, :], src[:, q_, :, :], r=[hn_all_b], w=[hb_])
            x = hnc[c % 2]
            cs = slice(c * 512, (c + 1) * 512)
            for j, (c0, dst, sc) in enumerate(((0, QT[0], scale), (128, QT[1], scale), (256, KAT, None), (384, KST, None), (512, KWT, None))):
                p_, pb_ = ps[j % 4], psb[j % 4]
                for fc in range(FC):
                    k.op("pe", lambda e, fc=fc, p_=p_, c0=c0: e.matmul(p_[:, :], lhsT=w[:, fc, c0:c0 + 128], rhs=x[:, fc, :],
                                                                         start=(fc == 0), stop=(fc == FC - 1)), r=[wb, hb_], w=[pb_])
                if sc is not None:
                    k.op("act", lambda e, p_=p_, dst=dst, sc=sc: e.activation(out=dst[:, cs], in_=p_[:, :], func=AF.Copy, scale=sc), r=[pb_], w=[pb])
                else:
                    k.op("dve", lambda e, p_=p_, dst=dst: e.tensor_copy(out=dst[:, cs], in_=p_[:, :]), r=[pb_], w=[pb])
            for tt in range(4):
                p_, pb_ = ps[4 + tt % 2], psb[4 + tt % 2]
                kb = c * 4 + tt
                for fc in range(FC):
                    k.op("pe", lambda e, fc=fc, p_=p_, tt=tt: e.matmul(p_[:, 0:140], lhsT=x[:, fc, tt * 128:(tt + 1) * 128], rhs=w[:, fc, 640:780],
                                                                         start=(fc == 0), stop=(fc == FC - 1)), r=[wb, hb_], w=[pb_])
                k.op("dve", lambda e, p_=p_, kb=kb: e.tensor_copy(out=VS[:, kb, 0:64], in_=p_[:, 0:64]), r=[pb_], w=[pb])
                k.op("dve", lambda e, p_=p_, kb=kb: e.tensor_copy(out=VW[:, kb, 0:64], in_=p_[:, 64:128]), r=[pb_], w=[pb])
                k.op("act", lambda e, p_=p_, kb=kb: e.activation(out=G[:, kb, :], in_=p_[:, 128:140], func=AF.Sigmoid), r=[pb_], w=[pb])
        if stop <= 1:
            k.barrier()
            return
        w1 = k.sb("nw1", [128, 32, 256], BF16, st)
        w1b = k.buf("nw1")
        for l in range(32):
            k.dma("pool", w1[:, l, :], w1_ap[:, l, :], w=[w1b])
        w2kd = k.sb("nw2k", [128, 2, 128], BF16, st)
        w2v = k.sb("nw2v", [128, 2, 64], BF16, st)
        w2b = k.buf("nw2")
        w2kv_ = w2k_ap.rearrange("(hc p) d -> p hc d", p=128)
        for hc in range(2):
            k.dma("pool", w2kd[:, hc, 0:64], w2kv_[:, hc, :], w=[w2b])
            k.dma("pool", w2kd[:, hc, 64:128], w2kv_[:, hc, :], w=[w2b])
            k.dma("pool", w2v[:, hc, :], w2v_ap.rearrange("(hc p) d -> p hc d", p=128)[:, hc, :], w=[w2b])
        peT = k.sb("npeT", [128, 32], BF16, st)
        peb = k.buf("npeT")
        k.dma("pool", peT[:], peT_ap, w=[peb])
        cb = k.sb("ncb", [128, 4], F32, st)
        cbb = k.buf("ncb")
        hid = k.sb("nhid", [128, 4, 512], BF16, st)
        hidb = k.buf("nhid")
        xs = k.sb("nxs", [128, 512], F32, st)
        x2 = k.sb("nx2", [128, 512], F32, st)
        xsb, x2b = k.buf("nxs"), k.buf("nx2")
        for kv in range(2):
            b0 = 64 * kv
            for hc in range(2):
                idx = kv * 2 + hc
                p_, pb_ = ps[idx % 2], psb[idx % 2]
                pc, pcb = ps[2 + idx % 2], psb[2 + idx % 2]
                for l in range(32):
                    k.op("pe", lambda e, l=l, pc=pc: e.matmul(pc[:, 0:1], lhsT=w1[b0:b0 + 64, l, hc * 128:(hc + 1) * 128], rhs=peT[b0:b0 + 64, l:l + 1],
                                                                start=(l == 0), stop=(l == 31)), r=[w1b, peb], w=[pcb])
                k.op("dve", lambda e, pc=pc, idx=idx: e.tensor_copy(out=cb[:, idx:idx + 1], in_=pc[:, 0:1]), r=[pcb], w=[cbb])
                for l in range(32):
                    k.op("pe", lambda e, l=l, p_=p_: e.matmul(p_[:, 0:511], lhsT=w1[b0:b0 + 64, l, hc * 128:(hc + 1) * 128],
                                                                rhs=KAT[b0:b0 + 64, l:l + 8161:16], start=(l == 0), stop=(l == 31)), r=[w1b, pb], w=[pb_])
                k.op("act", lambda e, p_=p_, idx=idx: e.activation(out=xs[:, 0:511], in_=p_[:, 0:511], func=AF.Identity, bias=cb[:, idx:idx + 1], scale=1.0),
                     r=[pb_, cbb], w=[xsb])
                k.op("act", lambda e, p_=p_, idx=idx: e.activation(out=x2[:, 0:511], in_=p_[:, 0:511], func=AF.Square, bias=cb[:, idx:idx + 1], scale=1.0),
                     r=[pb_, cbb], w=[x2b])
                k.op("dve", lambda e: e.tensor_scalar(out=x2[:, 0:511], in0=x2[:, 0:511], scalar1=0.044715, scalar2=1.0, op0=ALU.mult, op1=ALU.add),
                     r=[x2b], w=[x2b])
                k.op("dve", lambda e: e.tensor_tensor(out=x2[:, 0:511], in0=x2[:, 0:511], in1=xs[:, 0:511], op=ALU.mult), r=[x2b, xsb], w=[x2b])
                k.op("act", lambda e: e.activation(out=x2[:, 0:511], in_=x2[:, 0:511], func=AF.Sigmoid, scale=GELU_C), r=[x2b], w=[x2b])
                k.op("dve", lambda e, idx=idx: e.tensor_tensor(out=hid[:, idx, 0:511], in0=x2[:, 0:511], in1=xs[:, 0:511], op=ALU.mult),
                     r=[x2b, xsb], w=[hidb])
        for hc in range(2):
            k.op("pe", lambda e, hc=hc: e.matmul(ps[4][:, 0:511], lhsT=w2kd[:, hc, :], rhs=hid[:, hc, 0:511], start=(hc == 0), stop=(hc == 1)),
                 r=[w2b, hidb], w=[psb[4]])
        k.op("dve", lambda e: e.tensor_copy(out=kcT[:, 0:511], in_=ps[4][:, 0:511]), r=[psb[4]], w=[pb])
        for nb in range(4):
            M = 128 if nb < 3 else 127
            p_, pb_ = ps[5 + nb % 2], psb[5 + nb % 2]
            for hc in range(2):
                k.op("pe", lambda e, hc=hc, nb=nb, M=M, p_=p_: e.matmul(p_[0:M, 0:64], lhsT=hid[:, 2 + hc, nb * 128:nb * 128 + M], rhs=w2v[:, hc, :],
                                                                          start=(hc == 0), stop=(hc == 1)), r=[w2b, hidb], w=[pb_])
            k.op("dve", lambda e, nb=nb, M=M, p_=p_: e.tensor_copy(out=VCX[0:M, nb, 0:64], in_=p_[0:M, 0:64]), r=[pb_], w=[pb])
        k.barrier()
    if stop <= 2:
        return
    with contextlib.ExitStack() as st:
        psZ, psZb = ps[0:2], psb[0:2]
        psOc, psOcb = ps[2:4], psb[2:4]
        psOs, psOsb = ps[4], psb[4]
        psOw, psOwb = ps[5], psb[5]
        psM, psMb = ps[6:8], psb[6:8]
        Wc = [k.sb("nWc%d" % i, [128, 512], F32, st) for i in range(2)]
        Wcb = k.bufs(2, "nWc")
        ee = [k.sb("nee%d" % i, [128, 512], BF16, st) for i in range(3)]
        eeb = k.bufs(3, "nee")
        sf = [k.sb("nsf%d" % i, [128, 512], F32, st) for i in range(2)]
        sfb = k.bufs(2, "nsf")
        nmT = k.sb("nnmT", [128, 512], BF16, st)
        nmTb = k.buf("nnmT")
        imp = k.sb("nimp", [128, 128], F32, st)
        imp3 = k.sb("nimp3", [128, 128], F32, st)
        negm = k.sb("nnegm", [128, 128], F32, st)
        impb, imp3b, negmb = k.buf("imp"), k.buf("imp3"), k.buf("negm")
        m8 = k.sb("nm8", [128, 16], F32, st)
        m8b = k.buf("m8")
        dn = k.sb("ndn", [128, 12], F32, st)
        dnb = k.buf("dn")
        ot = k.sb("not", [128, 256], F32, st)
        otb = k.buf("ot")
        oT = [k.sb("noT%d" % i, [128, 2, 512], BF16, st) for i in range(2)]
        oTb = k.bufs(2, "noT")
        zc = 0
        ec = 0
        wcn = 0

        bd = [k.sb("nbd%d" % i, [128, 512], BF16, st) for i in range(2)]
        bdb = k.bufs(2, "nbd")
        for i in range(2):
            k.op("pool", lambda e, i=i: e.memset(bd[i][:], 0.0), w=[bdb[i]])
        cur = {}

        def qk(Z, Zb, KT_, kcols, tcols, first_start, extra_r=()):
            k.op("pe", lambda e: e.matmul(Z[:, :], lhsT=KT_[:, kcols], rhs=cur["bd"][:, :], start=first_start, stop=True, skip_group_check=True),
                 r=[pb, cur["bdb"]] + list(extra_r), w=[Zb])

        pend = []

        def flush():
            while pend:
                pend.pop(0)()

        def defer(fn):
            pend.append(fn)
            while len(pend) > 1:
                pend.pop(0)()

        for qt in range(nqt):
            tcols = slice(128 * qt, 128 * (qt + 1))
            cur["bd"], cur["bdb"] = bd[qt % 2], bdb[qt % 2]
            for g in range(4):
                b0 = 64 * (g % 2)
                k.op("pool", lambda e, g=g, b0=b0: e.tensor_copy(out=cur["bd"][b0:b0 + 64, g * 128:(g + 1) * 128], in_=QT[g // 2][b0:b0 + 64, tcols]),
                     r=[pb], w=[cur["bdb"]])
            NB = (8 * qt + 6) // 128 + 1
            for nb in range(NB):
                Z, Zb = psZ[zc % 2], psZb[zc % 2]
                zc += 1
                o_idx = qt - 16 * nb
                if o_idx <= 16:
                    W_, Wb_ = Wc[wcn % 2], Wcb[wcn % 2]
                    wcn += 1
                    src = bass.AP(bdz.tensor, 128 * o_idx, [[16, 128], [NDC, 4], [1, 128]])
                    import os
                    if os.environ.get("NSA_DBG") == "1":
                        k.op("pool", lambda e, W_=W_: e.memset(W_[:], 0.0), w=[Wb_])
                    else:
                        k.dma("sp", W_[:].rearrange("p (g q) -> p g q", g=4), src, r=[cx.bdz_b], w=[Wb_])
                    k.op("pe", lambda e, W_=W_: e.matmul(psM[0][:, :], lhsT=cx.jflip[:], rhs=W_[:], start=True, stop=True),
                         r=[Wb_, cx.jflip_b], w=[psMb[0]])
                    k.op("act", lambda e, W_=W_: e.copy(out=W_[:], in_=psM[0][:, :]), r=[psMb[0]], w=[Wb_])
                    qk(Z, Zb, kcT, slice(nb * 128, (nb + 1) * 128), tcols, True)
                    E_, Eb_ = ee[ec % 3], eeb[ec % 3]
                    ec += 1
                    s_, sb_ = sf[ec % 2], sfb[ec % 2]
                    k.op("dve", lambda e, s_=s_, Z=Z, W_=W_: e.tensor_tensor(out=s_[:], in0=Z[:, :], in1=W_[:], op=ALU.add), r=[Zb, Wb_], w=[sb_])
                    k.op("act", lambda e, E_=E_, s_=s_: e.activation(out=E_[:, :], in_=s_[:], func=AF.Exp), r=[sb_], w=[Eb_])
                else:
                    qk(Z, Zb, kcT, slice(nb * 128, (nb + 1) * 128), tcols, True)
                    E_, Eb_ = ee[ec % 3], eeb[ec % 3]
                    ec += 1
                    k.op("act", lambda e, E_=E_, Z=Z: e.activation(out=E_[:, :], in_=Z[:, :], func=AF.Exp), r=[Zb], w=[Eb_])
                def pv_c(E_=E_, Eb_=Eb_, nb=nb, NB=NB):
                    for g in range(4):
                        bank, bb = psOc[g // 2], psOcb[g // 2]
                        c0 = (g % 2) * 193
                        k.op("pe", lambda e, g=g, bank=bank, c0=c0: e.matmul(
                            bank[:, c0:c0 + 193], lhsT=E_[:, g * 128:(g + 1) * 128], rhs=VCX[:, nb, :],
                            start=(nb == 0 and g % 2 == 0), stop=(nb == NB - 1), skip_group_check=True), r=[Eb_, pb], w=[bb])
                defer(pv_c)
            flush()
            if stop <= 3:
                continue
            for h2 in range(2):
                bank, bb = psOc[h2], psOcb[h2]
                bv = bank[:, 0:386].rearrange("p (g c) -> p g c", c=193)
                k.op("dve", lambda e, h2=h2, bv=bv: e.tensor_scalar(out=dn[:, 2 * h2:2 * h2 + 2], in0=bv[:, :, 64], scalar1=1e-30, scalar2=None, op0=ALU.max),
                     r=[bb], w=[dnb])
            k.op("dve", lambda e: e.reciprocal(out=dn[:, 0:4], in_=dn[:, 0:4]), r=[dnb], w=[dnb])
            for g in range(4):
                bank, bb = psOc[g // 2], psOcb[g // 2]
                c0 = (g % 2) * 193 + 65
                if g == 0:
                    k.op("dve", lambda e, bank=bank, c0=c0: e.tensor_scalar(out=imp[:], in0=bank[:, c0:c0 + 128], scalar1=dn[:, 0:1], scalar2=None, op0=ALU.mult),
                         r=[bb, dnb], w=[impb])
                else:
                    k.op("dve", lambda e, g=g, bank=bank, c0=c0: e.scalar_tensor_tensor(out=imp[:], in0=bank[:, c0:c0 + 128], scalar=dn[:, g:g + 1], in1=imp[:],
                                                                                        op0=ALU.mult, op1=ALU.add), r=[bb, dnb, impb], w=[impb])
            sl = slice(128 - 2 * qt, 256 - 2 * qt)
            k.op("dve", lambda e: e.tensor_tensor(out=imp[:], in0=imp[:], in1=cx.mc_rel[:, sl], op=ALU.mult), r=[impb, cx.mc_rel_b], w=[impb])
            k.op("dve", lambda e: e.tensor_tensor(out=imp[:], in0=imp[:], in1=cx.ma_rel[:, sl], op=ALU.add), r=[impb, cx.ma_rel_b], w=[impb])
            k.op("dve", lambda e: e.memset(imp[:, 0:1], 100.0), r=[impb], w=[impb])
            k.op("dve", lambda e: e.max(out=m8[:, 0:8], in_=imp[:]), r=[impb], w=[m8b])
            k.op("dve", lambda e: e.match_replace(out=imp3[:], in_to_replace=m8[:, 0:8], in_values=imp[:], imm_value=-1e30), r=[impb, m8b], w=[imp3b])
            k.op("dve", lambda e: e.max(out=m8[:, 8:16], in_=imp3[:]), r=[imp3b], w=[m8b])
            k.op("dve", lambda e: e.tensor_scalar(out=negm[:], in0=imp[:], scalar1=m8[:, 15:16], scalar2=NEG, op0=ALU.is_lt, op1=ALU.mult),
                 r=[impb, m8b], w=[negmb])
            k.op("pe", lambda e: e.transpose(out=psM[0][:, 0:128], in_=negm[:], identity=cx.ident_f[:]), r=[negmb, cx.ident_f_b], w=[psMb[0]])
            for g in range(4):
                if g % 2 == 0:
                    k.op("dve", lambda e, g=g: e.tensor_copy(out=nmT[:, g * 128:(g + 1) * 128], in_=psM[0][:, 0:128]), r=[psMb[0]], w=[nmTb])
                else:
                    k.op("act", lambda e, g=g: e.copy(out=nmT[:, g * 128:(g + 1) * 128], in_=psM[0][:, 0:128]), r=[psMb[0]], w=[nmTb])
            if stop <= 4:
                continue
            for kb in range(qt + 1):
                Z, Zb = psZ[zc % 2], psZb[zc % 2]
                zc += 1
                m = qt - kb
                k.op("pe", lambda e, Z=Z, kb=kb: e.matmul(Z[:, :], lhsT=cx.ex[:, 128 * kb:128 * (kb + 1)], rhs=nmT[:], start=True, stop=False, skip_group_check=True),
                     r=[nmTb, cx.ex_b], w=[Zb])
                qk(Z, Zb, KST, slice(128 * kb, 128 * (kb + 1)), tcols, False)
                E_, Eb_ = ee[ec % 3], eeb[ec % 3]
                ec += 1
                if m <= 1:
                    Tm = T0 if m == 0 else T1
                    s_, sb_ = sf[ec % 2], sfb[ec % 2]
                    k.op("dve", lambda e, s_=s_, Z=Z, Tm=Tm: e.tensor_tensor(out=s_[:], in0=Z[:, :], in1=Tm[:], op=ALU.add), r=[Zb, pb], w=[sb_])
                    k.op("act", lambda e, E_=E_, s_=s_: e.activation(out=E_[:, :], in_=s_[:], func=AF.Exp), r=[sb_], w=[Eb_])
                else:
                    k.op("act", lambda e, E_=E_, Z=Z: e.activation(out=E_[:, :], in_=Z[:, :], func=AF.Exp), r=[Zb], w=[Eb_])
                def pv_s(E_=E_, Eb_=Eb_, kb=kb, qt=qt):
                    for g in range(4):
                        k.op("pe", lambda e, g=g: e.matmul(psOs[:, g * 65:(g + 1) * 65], lhsT=E_[:, g * 128:(g + 1) * 128], rhs=VS[:, kb, :],
                                                            start=(kb == 0 and g == 0), stop=(kb == qt), skip_group_check=True), r=[Eb_, pb], w=[psOsb])
                defer(pv_s)
            flush()
            if stop <= 5:
                continue
            kb0 = max(0, qt - 4)
            for kb in range(kb0, qt + 1):
                Z, Zb = psZ[zc % 2], psZb[zc % 2]
                zc += 1
                m = qt - kb
                qk(Z, Zb, KWT, slice(128 * kb, 128 * (kb + 1)), tcols, True)
                E_, Eb_ = ee[ec % 3], eeb[ec % 3]
                ec += 1
                if m in (0, 1, 4):
                    Tm, Tmb = {0: (T0, pb), 1: (T1, pb), 4: (cx.t4, cx.t4_b)}[m]
                    s_, sb_ = sf[ec % 2], sfb[ec % 2]
                    k.op("dve", lambda e, s_=s_, Z=Z, Tm=Tm: e.tensor_tensor(out=s_[:], in0=Z[:, :], in1=Tm[:], op=ALU.add), r=[Zb, Tmb], w=[sb_])
                    k.op("act", lambda e, E_=E_, s_=s_: e.activation(out=E_[:, :], in_=s_[:], func=AF.Exp), r=[sb_], w=[Eb_])
                else:
                    k.op("act", lambda e, E_=E_, Z=Z: e.activation(out=E_[:, :], in_=Z[:, :], func=AF.Exp), r=[Zb], w=[Eb_])
                def pv_w(E_=E_, Eb_=Eb_, kb=kb, qt=qt, kb0=kb0):
                    for g in range(4):
                        k.op("pe", lambda e, g=g: e.matmul(psOw[:, g * 65:(g + 1) * 65], lhsT=E_[:, g * 128:(g + 1) * 128], rhs=VW[:, kb, :],
                                                            start=(kb == kb0 and g == 0), stop=(kb == qt), skip_group_check=True), r=[Eb_, pb], w=[psOwb])
                defer(pv_w)
            flush()
            if stop <= 6:
                continue
            osv = psOs[:, 0:260].rearrange("p (g c) -> p g c", c=65)
            owv = psOw[:, 0:260].rearrange("p (g c) -> p g c", c=65)
            k.op("dve", lambda e: e.tensor_scalar(out=dn[:, 4:8], in0=osv[:, :, 64], scalar1=1e-30, scalar2=None, op0=ALU.max), r=[psOsb], w=[dnb])
            k.op("dve", lambda e: e.tensor_scalar(out=dn[:, 8:12], in0=owv[:, :, 64], scalar1=1e-30, scalar2=None, op0=ALU.max), r=[psOwb], w=[dnb])
            k.op("dve", lambda e: e.reciprocal(out=dn[:, 4:12], in_=dn[:, 4:12]), r=[dnb], w=[dnb])
            k.op("dve", lambda e: e.tensor_tensor(out=dn[:, :], in0=dn[:, :], in1=G[:, qt, :], op=ALU.mult), r=[dnb, pb], w=[dnb])
            for g in range(4):
                bank, bb = psOc[g // 2], psOcb[g // 2]
                c0 = (g % 2) * 193
                og = ot[:, g * 64:(g + 1) * 64]
                k.op("dve", lambda e, g=g, bank=bank, c0=c0, og=og: e.tensor_scalar(out=og, in0=bank[:, c0:c0 + 64], scalar1=dn[:, g:g + 1], scalar2=None, op0=ALU.mult),
                     r=[bb, dnb], w=[otb])
                k.op("dve", lambda e, g=g, og=og: e.scalar_tensor_tensor(out=og, in0=psOs[:, g * 65:g * 65 + 64], scalar=dn[:, 4 + g:5 + g], in1=og, op0=ALU.mult, op1=ALU.add),
                     r=[psOsb, dnb, otb], w=[otb])
                k.op("dve", lambda e, g=g, og=og: e.scalar_tensor_tensor(out=og, in0=psOw[:, g * 65:g * 65 + 64], scalar=dn[:, 8 + g:9 + g], in1=og, op0=ALU.mult, op1=ALU.add),
                     r=[psOwb, dnb, otb], w=[otb])
            y2 = (qt // 4) % 2
            for fh in range(2):
                k.op("pe", lambda e, fh=fh: e.transpose(out=psM[1][:, fh * 128:(fh + 1) * 128], in_=ot[:, fh * 128:(fh + 1) * 128], identity=cx.ident_f[:]),
                     r=[otb, cx.ident_f_b], w=[psMb[1]])
            k.op("act", lambda e, y2=y2: e.copy(out=oT[y2][:, :, (qt % 4) * 128:(qt % 4 + 1) * 128],
                                                in_=psM[1][:, 0:256].rearrange("p (fh t) -> p fh t", fh=2)), r=[psMb[1]], w=[oTb[y2]])
            if qt % 4 == 3:
                sbk = qt // 4
                dest, off = sbk // 4, (sbk % 4) * 512
                k.dma("sp", a2a[dest].rearrange("(fh p) t -> p fh t", p=128)[:, :, off:off + 512], oT[y2][:], r=[oTb[y2]], w=[a2a_b])
                if k.engs["pe"].count > SEM_ROTATE:
                    k.barrier()
        k.barrier()


def build_prog_nsa(nqt=64, stop=99):
    nc = bass.Bass("TRN2", target_bir_lowering=False)
    consts = nsa_host_consts()
    names = ["ident_f", "jflip", "ex", "t4", "mc_rel", "ma_rel"]
    dnames = ["ohc", "dm", "ov"]
    cd = dram_consts(nc, consts, names + dnames)

    def din(nm, shape, dt_=F32):
        return nc.dram_tensor(nm, shape, dt_, kind="ExternalInput").ap()
    hn_all = din("hn_all", [4 * D, TOK], BF16)
    w = din("nsa_w", [D, 780])
    rb = din("rb", [32, 4])
    peT = din("peT", [128, 32])
    w1 = din("cw1", [128, 32, 256])
    w2k = din("cw2k", [256, 64])
    w2v = din("cw2v", [256, 64])
    a2a = nc.dram_tensor("a2a", [4, 256, TOK], BF16, kind="ExternalOutput").ap()
    bdz = nc.dram_tensor("bdz", [4, NDC], F32, kind="Internal").ap()
    k = KB(nc)
    cx = load_consts(k, cd, names)
    for nm in dnames:
        setattr(cx, nm + "_dram", cd[nm])
    phase_nsa(k, cx, hn_all, k.buf("hn_all"), w, rb, peT, w1, w2k, w2v, bdz, a2a, k.buf("a2a"), k.stack, nqt=nqt, stop=stop)
    k.barrier()
    k.close()
    return nc, {nm: consts[nm] for nm in names + dnames}


def nsa_core_inputs(inp, r):
    w_in = inp["nsa_w_in"][0]
    kv0 = 1024

    def kvc(i):
        return w_in[:, kv0 + i * 256 + r * 64: kv0 + i * 256 + (r + 1) * 64]
    gcols = [2560 + j * 16 + r * 4 + g for j in range(3) for g in range(4)]
    w = np.concatenate([w_in[:, 256 * r:256 * (r + 1)], kvc(0), kvc(1), kvc(2), kvc(2), kvc(4), kvc(4), kvc(3), kvc(5), w_in[:, gcols]], axis=1)
    peT = np.concatenate([inp["nsa_pe_k"][0].T, inp["nsa_pe_v"][0].T], axis=0)
    w1k = inp["nsa_ck_w1"][0].reshape(32, 64, 256).transpose(1, 0, 2)
    w1v = inp["nsa_cv_w1"][0].reshape(32, 64, 256).transpose(1, 0, 2)
    return {"nsa_w": np.ascontiguousarray(w), "rb": np.ascontiguousarray(inp["rel_bias"][:, 4 * r:4 * r + 4]),
            "peT": np.ascontiguousarray(peT), "cw1": np.ascontiguousarray(np.concatenate([w1k, w1v], axis=0)),
            "cw2k": inp["nsa_ck_w2"][0], "cw2v": inp["nsa_cv_w2"][0]}


_PROGS = {}


def _prog(name, fn):
    if name not in _PROGS:
        _PROGS[name] = fn()
    return _PROGS[name]


def _run(name, fn, maps):
    nc, cst = _prog(name, fn)
    full = []
    for m in maps:
        mm = dict(m)
        mm.update({"c_" + kk: v for kk, v in cst.items()})
        full.append(mm)
    res = run_bass_kernel_spmd(nc, full, core_ids=list(range(NCORES)))
    return res.results


def _c(a):
    return np.ascontiguousarray(a)


def _allgather(parts):
    out = []
    for b in range(2):
        cat = np.concatenate([np.asarray(parts[4 * b + r])[256 * q:256 * (q + 1)] for q in range(4) for r in range(4)], axis=0)
        out += [cat] * 4
    return out


def _alltoall(parts):
    out = []
    for b in range(2):
        for j in range(4):
            out.append(_c(np.concatenate([np.asarray(parts[4 * b + i])[j] for i in range(4)], axis=0)))
    return out


def kernel_unfused(**inp):
    inp = {kk: np.asarray(v) for kk, v in inp.items()}
    x, p = inp["x"], inp["p"]
    cores = [(c // 4, c % 4) for c in range(NCORES)]
    tsl = [slice(TOK * r, TOK * (r + 1)) for (_, r) in cores]
    xT = [_c(x[b, tsl[c]].T) for c, (b, r) in enumerate(cores)]
    res = _run("norm0", build_prog_norm0, [{"xT": xT[c], "gain": _c(inp["norm_mix"][0])} for c in range(NCORES)])
    hn_all = _allgather([r_["hnT"] for r_ in res])
    w_in = inp["sb_w_in"][0]
    maps = []
    for c, (b, r) in enumerate(cores):
        wq = np.concatenate([w_in[:, 256 * r:256 * (r + 1)], w_in[:, 1024 + 256 * r:1024 + 256 * (r + 1)],
                             w_in[:, 2048 + 256 * r:2048 + 256 * (r + 1)]], axis=1)
        maps.append({"hn_all": hn_all[c], "sb_w": _c(wq)})
    res = _run("sb", build_prog_sb, maps)
    oT = _alltoall([r_["a2a"] for r_ in res])

    def tail_maps(i, hT_in, oT_, g_next):
        out = []
        for c, (b, r) in enumerate(cores):
            out.append({"hT_in": hT_in[c], "oT": oT_[c],
                        "w_out": _c((inp["sb_w_out"] if i == 0 else inp["nsa_w_out"])[0]),
                        "g_ffn": _c(inp["norm_ffn"][i]), "w1": _c(inp["ffn_w_in"][i]), "w2": _c(inp["ffn_w_out"][i]),
                        "g_ple": _c(inp["norm_ple"][i]), "wg": _c(inp["ple_w_gate"][i]), "wp": _c(inp["ple_w_proj"][i]),
                        "pT": _c(p[i, b, tsl[c]].T), "g_next": _c(g_next)})
        return out
    res = _run("tail0", lambda: build_prog_tail(False), tail_maps(0, xT, oT, inp["norm_mix"][1]))
    h1T = [np.asarray(r_["hT_out"]) for r_ in res]
    hn_all = _allgather([r_["hnT"] for r_ in res])
    maps = []
    for c, (b, r) in enumerate(cores):
        m = {"hn_all": hn_all[c]}
        m.update(nsa_core_inputs(inp, r))
        maps.append(m)
    res = _run("nsa", build_prog_nsa, maps)
    oT = _alltoall([r_["a2a"] for r_ in res])
    res = _run("tail1", lambda: build_prog_tail(True), tail_maps(1, h1T, oT, inp["final_norm"]))
    out = np.empty((2, S, D), np.float32)
    for c, (b, r) in enumerate(cores):
        out[b, tsl[c], :] = np.asarray(res[c]["outT"]).T
    return out


I32 = mybir.dt.int32
TAIL_KEYS = (("w_out", [D, D]), ("g_ffn", [D]), ("w1", [D, 2 * DFF]), ("w2", [DFF, D]), ("g_ple", [D]), ("wg", [D, D]), ("wp", [256, D]))


def build_prog_fused():
    import os
    CUT = int(os.environ.get("FUSED_CUT", "99"))
    nc = bass.Bass("TRN2", target_bir_lowering=False)
    consts = dict(host_consts())
    consts.update(nsa_host_consts())
    small = ["ident_f", "ones_b", "negu_b", "negones_b", "mask_sb", "jflip"]
    nsa_sb = ["ex", "t4", "mc_rel", "ma_rel"]
    nsa_dr = ["ohc", "dm", "ov"]
    cd = dram_consts(nc, consts, small + nsa_sb + nsa_dr)

    def din(nm, shape, dt_=F32):
        return nc.dram_tensor(nm, shape, dt_, kind="ExternalInput").ap()

    def dint(nm, shape, dt_):
        return nc.dram_tensor(nm, shape, dt_, kind="Internal").ap()
    xT = din("xT", [D, TOK])
    pT = [din("pT0", [256, TOK]), din("pT1", [256, TOK])]
    rk = din("rk", [1, 2], I32)
    g_mix = [din("g_mix0", [D]), din("g_mix1", [D])]
    g_fin = din("g_fin", [D])
    sb_w = din("sb_w", [D, 768])
    tails = [{nm: din("%s_%d" % (nm, i), shp) for nm, shp in TAIL_KEYS} for i in range(2)]
    nsa_w = din("nsa_w", [D, 780])
    rb = din("rb", [32, 4])
    peT = din("peT", [128, 32])
    cw1 = din("cw1", [128, 32, 256])
    cw2k = din("cw2k", [256, 64])
    cw2v = din("cw2v", [256, 64])
    outT = nc.dram_tensor("outT", [D, TOK], F32, kind="ExternalOutput").ap()
    hn_loc = dint("hn_loc", [D, TOK], BF16)
    hn_all = dint("hn_all", [4 * D, TOK], BF16)
    a2a_loc = dint("a2a_loc", [4, 256, TOK], BF16)
    a2a_all = dint("a2a_all", [4 * D, TOK], BF16)
    bdz = dint("bdz", [4, NDC], F32)
    hsp = dint("hsp", [D, TOK], F32)
    k = KB(nc)
    cx = load_consts(k, cd, small)
    for nm in nsa_dr:
        setattr(cx, nm + "_dram", cd[nm])
    make_eps(k, cx)
    make_one(k, cx)
    groups = [[0, 1, 2, 3], [4, 5, 6, 7]]
    spq = k.engs["sp"].h
    reg = spq.alloc_register("rk")
    spq.reg_load(reg, rk[0:1, 0:1])
    crk = spq.snap(reg, min_val=0, max_val=3)
    hn_loc_b, hn_all_b, a2a_loc_b, a2a_all_b, hsp_b, out_b = [k.buf(n_) for n_ in ("hn_loc", "hn_all", "a2a_loc", "a2a_all", "hsp", "outT")]
    g4 = a2a_all.rearrange("(j f) t -> j f t", j=4)

    def oT_loader(tile, tb, cs):
        src = g4[crk].rearrange("(fc p) t -> p fc t", p=128)
        k.dma("sp", tile[:], src[:, :, cs], r=[a2a_all_b], w=[tb])

    def gather_hn():
        for q in range(4):
            k.coll("AllGather", hn_loc[256 * q:256 * (q + 1), :], hn_all[1024 * q:1024 * (q + 1), :], groups, r=[hn_loc_b], w=[hn_all_b])

    def gather_o():
        for j in range(4):
            k.coll("AllGather", a2a_loc[j], a2a_all[1024 * j:1024 * (j + 1), :], groups, r=[a2a_loc_b], w=[a2a_all_b])

    wbf = [tail_weights_to_bf16(k, nc, str(i), tails[i]["w_out"], tails[i]["w1"], tails[i]["w2"], tails[i]["wg"], tails[i]["wp"])
           for i in range(2)]

    def tail(i, hT, hb):
        t = tails[i]
        phase_tail(k, cx, hT, hb, None, a2a_all_b, wbf[i], t["g_ffn"], t["g_ple"], pT[i], oT_loader=oT_loader)

    hv = hsp.rearrange("(fc p) t -> p fc t", p=128)
    with contextlib.ExitStack() as stA:
        hT = k.sb("hT", [128, FC, TOK], F32, stA)
        hb = k.bufs(4, "hT")
        xv = xT.rearrange("(fc p) t -> p fc t", p=128)
        for c in range(4):
            k.dma("sp", hT[:, :, c * 512:(c + 1) * 512], xv[:, :, c * 512:(c + 1) * 512], w=[hb[c]])
        with contextlib.ExitStack() as st:
            phase_norm_out(k, cx, hT, hb, g_mix[0], hn_loc, hn_loc_b, st)
            k.barrier()
        gather_hn()
        if CUT >= 2:
            with contextlib.ExitStack() as st:
                phase_sb(k, cx, hn_all, hn_all_b, sb_w, a2a_loc, a2a_loc_b, st)
                k.barrier()
            gather_o()
        if CUT >= 3:
            tail(0, hT, hb)
        with contextlib.ExitStack() as st:
            phase_norm_out(k, cx, hT, hb, g_mix[1], hn_loc, hn_loc_b, st)
            for c in range(4):
                k.dma("sp", hv[:, :, c * 512:(c + 1) * 512], hT[:, :, c * 512:(c + 1) * 512], r=[hb[c]], w=[hsp_b])
            k.barrier()
    if CUT >= 4:
        gather_hn()
    with contextlib.ExitStack() as stB:
      if CUT >= 5:
        cxb = load_consts(k, cd, nsa_sb, stB)
        for nm in nsa_sb:
            setattr(cx, nm, getattr(cxb, nm))
            setattr(cx, nm + "_b", getattr(cxb, nm + "_b"))
        phase_nsa(k, cx, hn_all, hn_all_b, nsa_w, rb, peT, cw1, cw2k, cw2v, bdz, a2a_loc, a2a_loc_b, stB)
        k.barrier()
    if CUT >= 5:
        gather_o()
    with contextlib.ExitStack() as stC:
        hT = k.sb("hT2", [128, FC, TOK], F32, stC)
        hb = k.bufs(4, "hT2")
        for c in range(4):
            k.dma("sp", hT[:, :, c * 512:(c + 1) * 512], hv[:, :, c * 512:(c + 1) * 512], r=[hsp_b], w=[hb[c]])
        if CUT >= 6:
            tail(1, hT, hb)
        with contextlib.ExitStack() as st:
            phase_final_norm(k, cx, hT, hb, g_fin, outT, out_b, st)
            k.barrier()
    k.barrier()
    k.close()
    return nc, {nm: consts[nm] for nm in small + nsa_sb + nsa_dr}


def fused_maps(inp):
    x, p = inp["x"], inp["p"]
    maps = []
    w_in = inp["sb_w_in"][0]
    for c in range(NCORES):
        b, r = c // 4, c % 4
        ts = slice(TOK * r, TOK * (r + 1))
        m = {"xT": _c(x[b, ts].T), "pT0": _c(p[0, b, ts].T), "pT1": _c(p[1, b, ts].T),
             "rk": np.array([[r, 0]], np.int32),
             "g_mix0": _c(inp["norm_mix"][0]), "g_mix1": _c(inp["norm_mix"][1]), "g_fin": _c(inp["final_norm"]),
             "sb_w": _c(np.concatenate([w_in[:, 256 * r:256 * (r + 1)], w_in[:, 1024 + 256 * r:1024 + 256 * (r + 1)],
                                        w_in[:, 2048 + 256 * r:2048 + 256 * (r + 1)]], axis=1))}
        for i in range(2):
            m["w_out_%d" % i] = _c((inp["sb_w_out"] if i == 0 else inp["nsa_w_out"])[0])
            m["g_ffn_%d" % i] = _c(inp["norm_ffn"][i])
            m["w1_%d" % i] = _c(inp["ffn_w_in"][i])
            m["w2_%d" % i] = _c(inp["ffn_w_out"][i])
            m["g_ple_%d" % i] = _c(inp["norm_ple"][i])
            m["wg_%d" % i] = _c(inp["ple_w_gate"][i])
            m["wp_%d" % i] = _c(inp["ple_w_proj"][i])
        m.update(nsa_core_inputs(inp, r))
        maps.append(m)
    return maps


def kernel(**inp):
    inp = {kk: np.asarray(v) for kk, v in inp.items()}
    res = _run("fused", build_prog_fused, fused_maps(inp))
    out = np.empty((2, S, D), np.float32)
    for c in range(NCORES):
        b, r = c // 4, c % 4
        out[b, TOK * r:TOK * (r + 1), :] = np.asarray(res[c]["outT"]).T
    return out
```

```python
import contextlib
import math
import numpy as np
import ml_dtypes
import concourse.bass as bass
import concourse.mybir as mybir
from concourse.alu_op_type import AluOpType as ALU
from concourse.bass_utils import run_bass_kernel_spmd

F32 = mybir.dt.float32
BF16 = mybir.dt.bfloat16
AF = mybir.ActivationFunctionType
NPBF = ml_dtypes.bfloat16

NCORES = 8
S = 8192
D = 1024
TOK = 2048
FC = 8
DFF = 2816
NJ = DFF // 128
EPS = 1e-6
NEG = -30000.0
NDC = 4352
DOFF = 2063


class Buf:
    __slots__ = ("name", "w", "r", "dsem", "dcnt", "dkey")

    def __init__(self, name):
        self.name = name
        self.w = None
        self.r = {}
        self.dsem = None
        self.dcnt = 0
        self.dkey = None


class Eng:
    def __init__(self, name, h, sem):
        self.name = name
        self.gen = 0
        self.key = ("e", name, 0)
        self.h = h
        self.sem = sem
        self.count = 0
        self.seen = {}


import os as _os
ATTACH_WAIT = _os.environ.get("KB_ATTACH", "1") == "1"
ATTACH_ENGS = tuple(_os.environ.get("KB_ATTACH_ENGS", "act,dve,pool,pe").split(","))
SEM_ROTATE = 6000


class KB:
    def __init__(self, nc):
        self.nc = nc
        self.stack = contextlib.ExitStack()
        self.engs = {}
        for key, h in (("pe", nc.tensor), ("act", nc.scalar), ("dve", nc.vector),
                       ("pool", nc.gpsimd), ("sp", nc.sync)):
            sem = self.stack.enter_context(nc.semaphore("s_" + key)) if key != "sp" else None
            self.engs[key] = Eng(key, h, sem)
        self.csem = self.stack.enter_context(nc.semaphore("s_coll"))
        self.hsem = self.stack.enter_context(nc.semaphore("s_hand"))
        self.hcnt = 0
        self.dma_bufs = []
        self.nbuf = 0
        self.free_dsems = {"hw": [], "sw": []}
        self.nsem = 0
        self.ccnt = 0

    def sb(self, name, shape, dtype, stack=None):
        self.nbuf += 1
        return (stack or self.stack).enter_context(self.nc.sbuf_tensor("S%d_%s" % (self.nbuf, name), list(shape), dtype))

    def ps(self, name, shape, dtype, stack=None):
        self.nbuf += 1
        return (stack or self.stack).enter_context(self.nc.psum_tensor("P%d_%s" % (self.nbuf, name), list(shape), dtype))

    def buf(self, name=None):
        self.nbuf += 1
        return Buf("%s_%d" % (name or "b", self.nbuf))

    def bufs(self, n, name=None):
        return [self.buf(name) for _ in range(n)]

    def _deps(self, E, r, w, attach=False):
        deps = {}

        def need(k, s, v):
            if k[0] == "e" and k[1] == "pe" and E.name == "pe":
                return
            if E.seen.get(k, 0) >= v:
                return
            if k not in deps or deps[k][1] < v:
                deps[k] = (s, v)

        for b in r:
            if b.w is not None:
                need(*b.w)
        for b in w:
            if b.w is not None:
                need(*b.w)
            for kk, (s, v) in b.r.items():
                need(kk, s, v)
        items = list(deps.items())
        carry = None
        if ATTACH_WAIT and attach and items:
            kk, (s, v) = items.pop()
            E.seen[kk] = v
            carry = (s, v)
        for kk, (s, v) in items:
            E.h.wait_ge(s, v)
            E.seen[kk] = v
        return carry

    @staticmethod
    def _mark(ev, r, w):
        kk, s, v = ev
        for b in r:
            b.r[kk] = (s, v)
        for b in w:
            b.w = ev
            b.r = {}

    def op(self, eng, fn, r=(), w=()):
        E = self.engs[eng]
        carry = self._deps(E, r, w, attach=(eng in ATTACH_ENGS))
        ins = fn(E.h)
        if carry is not None:
            ins._wait_ge(carry[0], carry[1])
        E.count += 1
        ins.then_inc(E.sem, 1)
        self._mark((E.key, E.sem, E.count), r, w)
        return ins

    def _dsem(self, sbuf, cls):
        if sbuf.dsem is None:
            sbuf.dsem = {}
        if cls not in sbuf.dsem:
            pool = self.free_dsems[cls]
            if pool:
                ent = pool.pop()
            else:
                self.nsem += 1
                sem = self.stack.enter_context(self.nc.semaphore("d%s%d" % (cls, self.nsem)))
                ent = [sem, ("d", self.nsem), 0]
            sbuf.dsem[cls] = ent
            self.dma_bufs.append((sbuf, cls))
        return sbuf.dsem[cls]

    def dma(self, q, out, in_, r=(), w=(), sem_buf=None, **kw):
        E = self.engs[q]
        self._deps(E, r, w)
        sbuf = sem_buf or (w[0] if w else r[0])
        ent = self._dsem(sbuf, "sw" if q == "pool" else "hw")
        ins = E.h.dma_start(out=out, in_=in_, **kw)
        ent[2] += 16
        ins.then_inc(ent[0], 16)
        self._mark((ent[1], ent[0], ent[2]), r, w)
        return ins

    def coll(self, kind, in_ap, out_ap, groups, r=(), w=()):
        E = self.engs["pool"]
        self._deps(E, r, w)
        ins = E.h.collective_compute(kind, ALU.bypass, replica_groups=groups, ins=[in_ap], outs=[out_ap])
        self.ccnt += 1
        ins.then_inc(self.csem, 1)
        self._mark((("c", 0), self.csem, self.ccnt), r, w)
        return ins

    def barrier(self, release=True):
        for E in self.engs.values():
            for Fg in self.engs.values():
                if Fg.count == 0:
                    continue
                if Fg is E and E.name == "pe":
                    continue
                if E.seen.get(Fg.key, 0) < Fg.count:
                    E.h.wait_ge(Fg.sem, Fg.count)
                    E.seen[Fg.key] = Fg.count
            for b, cls in self.dma_bufs:
                sem, dkey, cnt = b.dsem[cls]
                if cnt and E.seen.get(dkey, 0) < cnt:
                    E.h.wait_ge(sem, cnt)
                    E.seen[dkey] = cnt
            if self.ccnt and E.seen.get(("c", 0), 0) < self.ccnt:
                E.h.wait_ge(self.csem, self.ccnt)
                E.seen[("c", 0)] = self.ccnt
        if any(E.count > SEM_ROTATE for E in self.engs.values()):
            parts = list(self.engs.values())
            for rnd in range(2):
                self.hcnt += len(parts)
                for E in parts:
                    E.h.sem_inc(self.hsem, 1)
                for E in parts:
                    E.h.wait_ge(self.hsem, self.hcnt)
                if rnd == 0:
                    for E in parts:
                        if E.sem is not None and E.count > 0:
                            E.h.sem_clear(E.sem)
                            E.gen += 1
                            E.key = ("e", E.name, E.gen)
                            E.count = 0
        if release:
            for b, cls in self.dma_bufs:
                self.free_dsems[cls].append(b.dsem.pop(cls))
                b.w = None
                b.r = {}
            self.dma_bufs = []

    def close(self):
        self.stack.close()


def _rel_bucket_np(dist):
    n = np.maximum(dist, 0)
    nf = np.maximum(n, 1).astype(np.float32)
    large = 16 + (np.log(nf / np.float32(16)) / np.float32(math.log(128 / 16)) * np.float32(16)).astype(np.int32)
    large = np.minimum(large, 31)
    return np.where(n < 16, n, large)


def host_consts():
    c = {}
    i = np.arange(128)
    c["ident_f"] = np.eye(128, dtype=np.float32)
    c["ident_b"] = np.eye(128, dtype=np.float32).astype(NPBF)
    c["ones_b"] = np.ones((128, 128), np.float32).astype(NPBF)
    c["negu_b"] = (-(i[:, None] >= i[None, :]).astype(np.float32)).astype(NPBF)
    c["negones_b"] = (-np.ones((128, 2), np.float32)).astype(NPBF)
    c["mask_sb"] = (i[:, None] < i[None, :]).astype(np.float32).astype(NPBF)
    return c


class Ctx:
    pass


def load_consts(k, cdram, names, stack=None):
    cx = Ctx()
    for nm in names:
        ap = cdram[nm]
        t = k.sb("sc_" + nm, list(ap.shape), ap.dtype, stack)
        b = k.buf("c_" + nm)
        k.dma("sp", t[:], ap, w=[b])
        setattr(cx, nm, t)
        setattr(cx, nm + "_b", b)
    return cx


def rmsnorm_chunk(k, cx, W, h_aps, h_buf, gcol, gcol_b, out_aps, out_buf, n=512):
    sq, sqb, ps, psb, rs, rsb = W["sq"], W["sq_b"], W["ps_n"], W["ps_n_b"], W["rs"], W["rs_b"]
    for fc in range(FC):
        k.op("act", lambda e, fc=fc: e.activation(out=sq[:, fc, :n], in_=h_aps[fc], func=AF.Square),
             r=[h_buf], w=[sqb])
    for fc in range(FC):
        k.op("pe", lambda e, fc=fc: e.matmul(ps[:, :n], lhsT=cx.ones_b[:], rhs=sq[:, fc, :n],
                                              start=(fc == 0), stop=(fc == FC - 1)),
             r=[sqb, cx.ones_b_b], w=[psb])
    k.op("act", lambda e: e.activation(out=rs[:, :n], in_=ps[:, :n], func=AF.Sqrt, bias=cx.eps_col[:, 0:1], scale=1.0 / D),
         r=[psb, cx.eps_col_b], w=[rsb])
    k.op("dve", lambda e: e.reciprocal(out=rs[:, :n], in_=rs[:, :n]), r=[rsb], w=[rsb])
    for fc in range(FC):
        k.op("dve", lambda e, fc=fc: e.scalar_tensor_tensor(out=out_aps[fc], in0=h_aps[fc], scalar=gcol[:, fc:fc + 1],
                                                            in1=rs[:, :n], op0=ALU.mult, op1=ALU.mult),
             r=[h_buf, gcol_b, rsb], w=[out_buf])


def norm_work(k, stack=None):
    W = {}
    W["sq"] = k.sb("n_sq", [128, FC, 512], BF16, stack)
    W["sq_b"] = k.buf("n_sq")
    W["ps_n"] = k.ps("n_ps", [128, 512], F32, stack)
    W["ps_n_b"] = k.buf("n_ps")
    W["rs"] = k.sb("n_rs", [128, 512], F32, stack)
    W["rs_b"] = k.buf("n_rs")
    return W


def make_eps(k, cx, stack=None):
    cx.eps_col = k.sb("eps_col", [128, 1], F32, stack)
    cx.eps_col_b = k.buf("eps")
    k.op("pool", lambda e: e.memset(cx.eps_col[:], EPS), w=[cx.eps_col_b])


def load_gain_cols(k, gain_ap, name, stack=None):
    t = k.sb(name, [128, FC], F32, stack)
    b = k.buf(name)
    k.dma("sp", t[:], gain_ap.rearrange("(fc p) -> p fc", p=128), w=[b], allow_slow_non_contiguous=True)
    return t, b


def phase_norm_out(k, cx, hT, hb, gain_ap, dst, dst_b, stack):
    W = norm_work(k, stack)
    gcol, gcol_b = load_gain_cols(k, gain_ap, "g_mix", stack)
    hn = [k.sb("hn%d" % i, [128, FC, 512], BF16, stack) for i in range(2)]
    hnb = k.bufs(2, "hn")
    dv = dst.rearrange("(fc p) t -> p fc t", p=128)
    for c in range(4):
        cs = slice(c * 512, (c + 1) * 512)
        rmsnorm_chunk(k, cx, W, [hT[:, fc, cs] for fc in range(FC)], hb[c], gcol, gcol_b,
                      [hn[c % 2][:, fc, :] for fc in range(FC)], hnb[c % 2])
        k.dma("sp", dv[:, :, cs], hn[c % 2][:], r=[hnb[c % 2]], w=[dst_b])


def phase_sb(k, cx, hn_all, hn_all_b, w_ap, a2a, a2a_b, stack, nsb=16):
    scale = 0.125
    QT = [k.sb("QT%d" % i, [128, S], BF16, stack) for i in range(2)]
    KT = [k.sb("KT%d" % i, [128, S], BF16, stack) for i in range(2)]
    V = k.sb("V", [128, 64, 260], BF16, stack)
    qkb = k.buf("qkv")
    k.op("pool", lambda e: e.memset(V[:].rearrange("p b (h c) -> p b h c", c=65)[:, :, :, 64:65], 1.0), w=[qkb])
    psA2 = k.ps("psA2", [128, 1024], F32, stack)
    psB2 = k.ps("psB2", [128, 1024], F32, stack)
    psAA = [psA2, psB2]
    ps = [psA2[:, 0:512], psA2[:, 512:1024], psB2[:, 0:512], psB2[:, 512:1024]] + \
         [k.ps("ps%d" % i, [128, 512], F32, stack)[:, :] for i in range(4, 8)]
    psb = k.bufs(8, "ps")
    pst = contextlib.ExitStack()
    wq = k.sb("sb_w", [128, FC, 768], BF16, pst)
    wqb = k.buf("sb_w")
    wv = w_ap.rearrange("(fc p) n -> p fc n", p=128)
    for fc in range(FC):
        k.dma("pool", wq[:, fc, :], wv[:, fc, :], w=[wqb])
    hnc = [k.sb("hnc%d" % i, [128, FC, 512], BF16, pst) for i in range(2)]
    hncb = k.bufs(2, "hnc")
    for c in range(16):
        rho, off = c // 4, (c % 4) * 512
        src = hn_all.rearrange("(q r h p) t -> r p q h t", q=4, r=4, h=2, p=128)[rho][:, :, :, off:off + 512]
        hb_ = hncb[c % 2]
        for q_ in range(4):
            k.dma("sp", hnc[c % 2][:, 2 * q_:2 * q_ + 2, :], src[:, q_, :, :], r=[hn_all_b], w=[hb_])
        x = hnc[c % 2]
        cs = slice(c * 512, (c + 1) * 512)
        j = 0
        for which, dstT in ((0, QT), (1, KT)):
            for hp in range(2):
                p_, pb_ = ps[j % 4], psb[j % 4]
                j += 1
                for fc in range(FC):
                    k.op("pe", lambda e, fc=fc, p_=p_, which=which, hp=hp: e.matmul(
                        p_[:, :], lhsT=wq[:, fc, which * 256 + hp * 128: which * 256 + (hp + 1) * 128],
                        rhs=x[:, fc, :], start=(fc == 0), stop=(fc == FC - 1)), r=[wqb, hb_], w=[pb_])
                if which == 0:
                    k.op("act", lambda e, p_=p_, hp=hp: e.activation(out=QT[hp][:, cs], in_=p_[:, :], func=AF.Copy, scale=scale),
                         r=[pb_], w=[qkb])
                else:
                    k.op("dve", lambda e, p_=p_, hp=hp: e.tensor_copy(out=KT[hp][:, cs], in_=p_[:, :]), r=[pb_], w=[qkb])
        for tt in range(4):
            p_, pb_ = ps[4 + tt % 2], psb[4 + tt % 2]
            for fc in range(FC):
                k.op("pe", lambda e, fc=fc, p_=p_, tt=tt: e.matmul(
                    p_[:, 0:256], lhsT=x[:, fc, tt * 128:(tt + 1) * 128], rhs=wq[:, fc, 512:768],
                    start=(fc == 0), stop=(fc == FC - 1)), r=[wqb, hb_], w=[pb_])
            vdst = V[:, c * 4 + tt, :].rearrange("p (h c) -> p h c", c=65)[:, :, 0:64]
            vsrc = p_[:, 0:256].rearrange("p (h c) -> p h c", c=64)
            k.op("dve" if tt % 2 else "act",
                 (lambda e, vdst=vdst, vsrc=vsrc: e.tensor_copy(out=vdst, in_=vsrc)) if tt % 2 else
                 (lambda e, vdst=vdst, vsrc=vsrc: e.copy(out=vdst, in_=vsrc)),
                 r=[pb_], w=[qkb])
    k.barrier()
    pst.close()
    ps_free = ps
    pst2 = contextlib.ExitStack()
    e1 = [k.sb("e1_%d" % i, [128, 1024], F32, stack) for i in range(2)]
    e1b = k.bufs(2, "e1")
    sp = [k.sb("sp_%d" % i, [128, 1024], BF16, stack) for i in range(3)]
    spb = k.bufs(3, "sp")
    aa = [k.sb("aa_%d" % i, [128, 1024], BF16, stack) for i in range(2)]
    aab = k.bufs(2, "aa")
    acc = k.sb("acc", [128, 4, 256], F32, stack)
    accb = k.bufs(4, "acc")
    Ec = k.sb("Ec", [128, 4, 4], F32, stack)
    Ecb = k.bufs(4, "Ec")
    tE = k.sb("tE", [128, 4], F32, stack)
    tEb = k.buf("tE")
    oT = [k.sb("oT%d" % i, [128, 2, 512], BF16, stack) for i in range(2)]
    oTb = k.bufs(2, "oT")
    psO, psOb = ps[4:6], psb[4:6]
    psT, psTb = ps[6:8], psb[6:8]
    AAb = k.bufs(2, "psAA")
    steps = []
    for sbk in range(nsb):
        for hh in range(4):
            nu = 4 * sbk + 4
            chain = []
            for u in range(nu):
                kb = 4 * sbk + 3 - u
                chain.append(dict(kb=kb, i0=max(0, kb - 4 * sbk), diag=kb >= 4 * sbk))
            groups = [[c_] for c_ in chain[:4]] + [chain[i:i + 2] for i in range(4, nu, 2)]
            for gi, g_ in enumerate(groups):
                steps.append(dict(sbk=sbk, hh=hh, subs=g_, first=(gi == 0), last_sb=(hh == 3 and gi == len(groups) - 1)))
    for n_, St in enumerate(steps):
        St["n"] = n_
        hp, base = St["hh"] // 2, 64 * (St["hh"] % 2)
        for U in St["subs"]:
            U["N"] = 512 - 128 * U["i0"]
            U["qc"] = QT[hp][base:base + 64, 512 * St["sbk"] + 128 * U["i0"]: 512 * (St["sbk"] + 1)]
            U["kc"] = KT[hp][base:base + 64, 128 * U["kb"]:128 * (U["kb"] + 1)]
        St["W"] = 512 * (len(St["subs"]) - 1) + St["subs"][-1]["N"]
        St["diag"] = St["subs"][0]["diag"]

    def actv(out_t, in_banks, W, func, **kw):
        return lambda e: e.activation(out=out_t[:, :W], in_=in_banks[:, :W], func=func, **kw)

    def stage1a(St):
        n_, W = St["n"], St["W"]
        A_, Ab = psAA[n_ % 2], AAb[n_ % 2]
        for j, U in enumerate(St["subs"]):
            k.op("pe", lambda e, j=j, U=U: e.matmul(A_[:, j * 512:j * 512 + U["N"]], lhsT=U["kc"], rhs=U["qc"], start=True, stop=True), r=[qkb], w=[Ab])
        k.op("act", actv(e1[n_ % 2], A_, W, AF.Exp), r=[Ab], w=[e1b[n_ % 2]])

    def stage1b(St):
        n_, W = St["n"], St["W"]
        s_, sb_ = sp[n_ % 3], spb[n_ % 3]
        k.op("act", lambda e: e.activation(out=s_[:, :W], in_=e1[n_ % 2][:, :W], func=AF.Ln, bias=cx.one_col[:, 0:1], scale=1.0),
             r=[e1b[n_ % 2], cx.one_col_b], w=[sb_])
        if St["diag"]:
            k.op("pool", lambda e: e.tensor_tensor(out=s_[:, 0:128], in0=s_[:, 0:128], in1=cx.mask_sb[:], op=ALU.mult),
                 r=[sb_, cx.mask_sb_b], w=[sb_])

    def stage2(St):
        n_, W = St["n"], St["W"]
        s_, sb_ = sp[n_ % 3], spb[n_ % 3]
        a_, ab_ = aa[n_ % 2], aab[n_ % 2]
        A_, Ab = psAA[n_ % 2], AAb[n_ % 2]
        for j, U in enumerate(St["subs"]):
            N = U["N"]
            k.op("pe", lambda e, j=j, N=N: e.matmul(A_[:, j * 512:j * 512 + N], lhsT=cx.negu_b[:], rhs=s_[:, j * 512:j * 512 + N],
                                                    start=False, stop=True, skip_group_check=True),
                 r=[sb_, cx.negu_b_b, Ab], w=[Ab])
        k.op("act", actv(a_, A_, W, AF.Exp), r=[Ab], w=[ab_])
        if St["diag"]:
            k.op("pool", lambda e: e.tensor_tensor(out=a_[:, 0:128], in0=a_[:, 0:128], in1=cx.mask_sb[:], op=ALU.mult),
                 r=[ab_, cx.mask_sb_b], w=[ab_])

    def stage3(St):
        n_, hh, sbk = St["n"], St["hh"], St["sbk"]
        a_, ab_ = aa[n_ % 2], aab[n_ % 2]
        if St["first"]:
            k.op("pool", lambda e: e.memset(acc[:, :, hh * 64:(hh + 1) * 64], 0.0), w=[accb[hh]])
            k.op("pool", lambda e: e.memset(Ec[:, hh, :], 1.0), w=[Ecb[hh]])
        for j, U in enumerate(St["subs"]):
            i0, kb = U["i0"], U["kb"]
            O, Ob = psO[j], psOb[j]
            for i in range(i0, 4):
                cl = slice(j * 512 + (i - i0) * 128, j * 512 + (i - i0 + 1) * 128)
                k.op("pe", lambda e, i=i, cl=cl, O=O, kb=kb: e.matmul(O[:, i * 65:(i + 1) * 65], lhsT=a_[:, cl],
                                                                     rhs=V[:, kb, hh * 65:(hh + 1) * 65], start=True, stop=True),
                     r=[ab_, qkb], w=[Ob])
            for i in range(i0, 4):
                k.op("dve", lambda e, i=i, O=O: e.scalar_tensor_tensor(
                    out=acc[:, i, hh * 64:(hh + 1) * 64], in0=O[:, i * 65:i * 65 + 64], scalar=Ec[:, hh, i:i + 1],
                    in1=acc[:, i, hh * 64:(hh + 1) * 64], op0=ALU.mult, op1=ALU.add),
                    r=[Ob, Ecb[hh], accb[hh]], w=[accb[hh]])
            Ov = O[:, 0:260].rearrange("p (i c) -> p i c", c=65)
            k.op("dve", lambda e, Ov=Ov, i0=i0: e.tensor_tensor(out=tE[:, i0:4], in0=Ov[:, i0:4, 64], in1=Ec[:, hh, i0:4], op=ALU.mult),
                 r=[Ob, Ecb[hh]], w=[tEb])
            k.op("dve", lambda e, i0=i0: e.tensor_tensor(out=Ec[:, hh, i0:4], in0=Ec[:, hh, i0:4], in1=tE[:, i0:4], op=ALU.subtract),
                 r=[tEb, Ecb[hh]], w=[Ecb[hh]])
        if St["last_sb"]:
            y2 = sbk % 2
            for i in range(4):
                for fh in range(2):
                    T_, Tb_ = psT[(i * 2 + fh) % 2], psTb[(i * 2 + fh) % 2]
                    k.op("pe", lambda e, i=i, fh=fh, T_=T_: e.transpose(out=T_[:, 0:128], in_=acc[:, i, fh * 128:(fh + 1) * 128], identity=cx.ident_f[:]),
                         r=accb + [cx.ident_f_b], w=[Tb_])
                    k.op("dve", lambda e, i=i, fh=fh, T_=T_: e.tensor_copy(out=oT[y2][:, fh, i * 128:(i + 1) * 128], in_=T_[:, 0:128]),
                         r=[Tb_], w=[oTb[y2]])
            dest, off = sbk // 4, (sbk % 4) * 512
            k.dma("sp", a2a[dest].rearrange("(fh p) t -> p fh t", p=128)[:, :, off:off + 512], oT[y2][:], r=[oTb[y2]], w=[a2a_b])

    nst = len(steps)
    for it in range(nst + 2):
        if it < nst:
            stage1a(steps[it])
        if 0 <= it - 1 < nst:
            stage2(steps[it - 1])
        if it < nst:
            stage1b(steps[it])
        if 0 <= it - 2 < nst:
            stage3(steps[it - 2])
            if steps[it - 2]["last_sb"] and k.engs["pe"].count > SEM_ROTATE:
                k.barrier()


def make_one(k, cx, stack=None):
    cx.one_col = k.sb("one_col", [128, 1], F32, stack)
    cx.one_col_b = k.buf("one")
    k.op("pool", lambda e: e.memset(cx.one_col[:], 1.0), w=[cx.one_col_b])


def dram_consts(nc, consts, names):
    out = {}
    for nm in names:
        a = consts[nm]
        dt_ = BF16 if a.dtype == NPBF else F32
        out[nm] = nc.dram_tensor("c_" + nm, list(a.shape), dt_, kind="ExternalInput").ap()
    return out


def build_prog_norm0():
    nc = bass.Bass("TRN2", target_bir_lowering=False)
    consts = host_consts()
    names = ["ones_b"]
    cd = dram_consts(nc, consts, names)
    xT = nc.dram_tensor("xT", [D, TOK], F32, kind="ExternalInput").ap()
    gain = nc.dram_tensor("gain", [D], F32, kind="ExternalInput").ap()
    hn = nc.dram_tensor("hnT", [D, TOK], BF16, kind="ExternalOutput").ap()
    k = KB(nc)
    cx = load_consts(k, cd, names)
    make_eps(k, cx)
    hT = k.sb("hT", [128, FC, TOK], F32)
    hb = k.bufs(4, "hT")
    xv = xT.rearrange("(fc p) t -> p fc t", p=128)
    for c in range(4):
        k.dma("sp", hT[:, :, c * 512:(c + 1) * 512], xv[:, :, c * 512:(c + 1) * 512], w=[hb[c]])
    hn_b = k.buf("hn_dram")
    phase_norm_out(k, cx, hT, hb, gain, hn, hn_b, k.stack)
    k.barrier()
    k.close()
    return nc, {nm: consts[nm] for nm in names}


def build_prog_sb(nsb=16):
    nc = bass.Bass("TRN2", target_bir_lowering=False)
    consts = host_consts()
    names = ["ident_f", "negu_b", "negones_b", "mask_sb"]
    cd = dram_consts(nc, consts, names)
    hn_all = nc.dram_tensor("hn_all", [4 * D, TOK], BF16, kind="ExternalInput").ap()
    w = nc.dram_tensor("sb_w", [D, 768], F32, kind="ExternalInput").ap()
    a2a = nc.dram_tensor("a2a", [4, 256, TOK], BF16, kind="ExternalOutput").ap()
    k = KB(nc)
    cx = load_consts(k, cd, names)
    make_one(k, cx)
    phase_sb(k, cx, hn_all, k.buf("hn_all"), w, a2a, k.buf("a2a"), k.stack, nsb=nsb)
    k.barrier()
    k.close()
    return nc, {nm: consts[nm] for nm in names}


def load_w_cast(k, name, src_view, shape, stack, nsplit=None):
    t = k.sb(name, shape, BF16, stack)
    b = k.buf(name)
    a = shape[1]
    for i in range(a):
        k.dma("pool", t[:, i, :], src_view[:, i, :], w=[b])
    return t, b


def tail_weights_to_bf16(k, nc, tag, w_out_ap, w1_ap, w2_ap, wg_ap, wp_ap):
    out = {}
    for nm, ap in (("w_out", w_out_ap), ("w1", w1_ap), ("w2", w2_ap), ("wg", wg_ap), ("wp", wp_ap)):
        rows, cols = ap.shape
        dst = nc.dram_tensor("wb_%s_%s" % (nm, tag), [rows, cols], BF16, kind="Internal").ap()
        b = k.buf("wb_" + nm)
        for r0 in range(0, rows, 256):
            r1 = min(rows, r0 + 256)
            k.dma("pool", dst[r0:r1, :], ap[r0:r1, :], w=[b])
        out[nm] = (dst, b)
    return out


def load_w_bf16(k, name, src_view, src_b, shape, stack):
    t = k.sb(name, shape, BF16, stack)
    b = k.buf(name)
    k.dma("sp", t[:], src_view, r=[src_b], w=[b])
    return t, b


def phase_tail(k, cx, hT, hb, oT_dram, oT_b, wb, g_ffn_ap, g_ple_ap, pT_ap, oT_loader=None):
    ps_names = ["tp%d" % i for i in range(6)]
    with contextlib.ExitStack() as st0:
        ps = [k.ps(n_, [128, 512], F32, st0) for n_ in ps_names]
        psb = k.bufs(6, "tp")
        W = norm_work(k, st0)
        with contextlib.ExitStack() as st:
            wo, wob = load_w_bf16(k, "wo", wb["w_out"][0].rearrange("(fc p) n -> p fc n", p=128), wb["w_out"][1], [128, FC, D], st)
            oc = [k.sb("oc%d" % i, [128, FC, 512], BF16, st) for i in range(2)]
            ocb = k.bufs(2, "oc")
            ov = oT_dram.rearrange("(fc p) t -> p fc t", p=128) if oT_loader is None else None
            n_ = 0
            for c in range(4):
                cs = slice(c * 512, (c + 1) * 512)
                if oT_loader is None:
                    k.dma("sp", oc[c % 2][:], ov[:, :, cs], r=[oT_b], w=[ocb[c % 2]])
                else:
                    oT_loader(oc[c % 2], ocb[c % 2], cs)
                for of in range(FC):
                    p_, pb_ = ps[n_ % 4], psb[n_ % 4]
                    n_ += 1
                    for fc in range(FC):
                        k.op("pe", lambda e, fc=fc, of=of, p_=p_, c=c: e.matmul(
                            p_[:, :], lhsT=wo[:, fc, of * 128:(of + 1) * 128], rhs=oc[c % 2][:, fc, :],
                            start=(fc == 0), stop=(fc == FC - 1)), r=[wob, ocb[c % 2]], w=[pb_])
                    k.op("dve", lambda e, of=of, p_=p_, cs=cs: e.tensor_tensor(out=hT[:, of, cs], in0=hT[:, of, cs], in1=p_[:, :], op=ALU.add),
                         r=[pb_, hb[c]], w=[hb[c]])
            k.barrier()
        with contextlib.ExitStack() as st:
            gcol, gcol_b = load_gain_cols(k, g_ffn_ap, "g_ffn", st)
            hn = k.sb("f_hn", [128, FC, 1024], BF16, st)
            hnb = k.bufs(2, "f_hn")
            gT = k.sb("f_gT", [128, NJ, 1024], BF16, st)
            gTb = k.buf("f_gT")
            wab = [k.sb("f_wab%d" % i, [128, FC, 2, 512], BF16, st) for i in range(2)]
            wabb = k.bufs(2, "f_wab")
            w2t = [k.sb("f_w2%d" % i, [128, NJ, 256], BF16, st) for i in range(2)]
            w2b = k.bufs(2, "f_w2")
            sl = [k.sb("f_sl%d" % i, [128, 512], F32, st) for i in range(2)]
            slb = k.bufs(2, "f_sl")
            w1v = wb["w1"][0].rearrange("(fc p) n -> p fc n", p=128)
            w2v = wb["w2"][0].rearrange("(j p) n -> p j n", p=128)
            jgroups = [(j0, min(4, NJ - j0)) for j0 in range(0, NJ, 4)]
            n_ = 0
            nw = 0
            nw2 = 0
            for tc in range(2):
                for hf in range(2):
                    c = tc * 2 + hf
                    cs = slice(c * 512, (c + 1) * 512)
                    rmsnorm_chunk(k, cx, W, [hT[:, fc, cs] for fc in range(FC)], hb[c], gcol, gcol_b,
                                  [hn[:, fc, hf * 512:(hf + 1) * 512] for fc in range(FC)], hnb[hf])
                for (j0, gs) in jgroups:
                    wt, wtb = wab[nw % 2], wabb[nw % 2]
                    nw += 1
                    k.dma("sp", wt[:, :, 0, 0:gs * 128], w1v[:, :, j0 * 128:(j0 + gs) * 128], r=[wb["w1"][1]], w=[wtb])
                    k.dma("sp", wt[:, :, 1, 0:gs * 128], w1v[:, :, DFF + j0 * 128:DFF + (j0 + gs) * 128], r=[wb["w1"][1]], w=[wtb])
                    for jj in range(gs):
                        j = j0 + jj
                        js = slice(jj * 128, (jj + 1) * 128)
                        for hf in range(2):
                            hs = slice(hf * 512, (hf + 1) * 512)
                            pa, pab = ps[n_ % 2], psb[n_ % 2]
                            pb2, pbb = ps[2 + n_ % 2], psb[2 + n_ % 2]
                            s_, sb_ = sl[n_ % 2], slb[n_ % 2]
                            n_ += 1
                            for fc in range(FC):
                                k.op("pe", lambda e, fc=fc, pa=pa, wt=wt, hs=hs, js=js: e.matmul(pa[:, :], lhsT=wt[:, fc, 0, js], rhs=hn[:, fc, hs],
                                                                                                 start=(fc == 0), stop=(fc == FC - 1)), r=[wtb, hnb[hf]], w=[pab])
                            for fc in range(FC):
                                k.op("pe", lambda e, fc=fc, pb2=pb2, wt=wt, hs=hs, js=js: e.matmul(pb2[:, :], lhsT=wt[:, fc, 1, js], rhs=hn[:, fc, hs],
                                                                                                   start=(fc == 0), stop=(fc == FC - 1)), r=[wtb, hnb[hf]], w=[pbb])
                            k.op("act", lambda e, pa=pa, s_=s_: e.activation(out=s_[:, :], in_=pa[:, :], func=AF.Silu), r=[pab], w=[sb_])
                            k.op("dve", lambda e, pb2=pb2, s_=s_, j=j, hs=hs: e.tensor_tensor(out=gT[:, j, hs], in0=s_[:, :], in1=pb2[:, :], op=ALU.mult),
                                 r=[sb_, pbb], w=[gTb])
                for op_ in range(FC // 2):
                    wt, wtb = w2t[nw2 % 2], w2b[nw2 % 2]
                    nw2 += 1
                    k.dma("sp", wt[:], w2v[:, :, op_ * 256:(op_ + 1) * 256], r=[wb["w2"][1]], w=[wtb])
                    for o2 in range(2):
                        of = op_ * 2 + o2
                        for hf in range(2):
                            c = tc * 2 + hf
                            cs = slice(c * 512, (c + 1) * 512)
                            p_, pb_ = ps[4 + n_ % 2], psb[4 + n_ % 2]
                            n_ += 1
                            for j in range(NJ):
                                k.op("pe", lambda e, j=j, p_=p_, wt=wt, hf=hf, o2=o2: e.matmul(p_[:, :], lhsT=wt[:, j, o2 * 128:(o2 + 1) * 128], rhs=gT[:, j, hf * 512:(hf + 1) * 512],
                                                                                               start=(j == 0), stop=(j == NJ - 1)), r=[wtb, gTb], w=[pb_])
                            k.op("dve", lambda e, of=of, p_=p_, cs=cs: e.tensor_tensor(out=hT[:, of, cs], in0=hT[:, of, cs], in1=p_[:, :], op=ALU.add),
                                 r=[pb_, hb[c]], w=[hb[c]])
            k.barrier()
        with contextlib.ExitStack() as st:
            gcol, gcol_b = load_gain_cols(k, g_ple_ap, "g_ple", st)
            wg, wgb = load_w_bf16(k, "wg", wb["wg"][0].rearrange("(fc p) n -> p fc n", p=128), wb["wg"][1], [128, FC, D], st)
            wp, wpb = load_w_bf16(k, "wp", wb["wp"][0].rearrange("(kc p) n -> p kc n", p=128), wb["wp"][1], [128, 2, D], st)
            hn = [k.sb("p_hn%d" % i, [128, FC, 512], BF16, st) for i in range(2)]
            hnb = k.bufs(2, "p_hn")
            pt = [k.sb("p_pt%d" % i, [128, 2, 512], BF16, st) for i in range(2)]
            ptb = k.bufs(2, "p_pt")
            sg = [k.sb("p_sg%d" % i, [128, 512], F32, st) for i in range(2)]
            sgb = k.bufs(2, "p_sg")
            pv = pT_ap.rearrange("(kc p) t -> p kc t", p=128)
            n_ = 0
            for c in range(4):
                cs = slice(c * 512, (c + 1) * 512)
                x_, xb_ = hn[c % 2], hnb[c % 2]
                rmsnorm_chunk(k, cx, W, [hT[:, fc, cs] for fc in range(FC)], hb[c], gcol, gcol_b,
                              [x_[:, fc, :] for fc in range(FC)], xb_)
                q_, qb_ = pt[c % 2], ptb[c % 2]
                for kc in range(2):
                    k.dma("pool", q_[:, kc, :], pv[:, kc, cs], w=[qb_])
                for of in range(FC):
                    pg, pgb = ps[n_ % 2], psb[n_ % 2]
                    pp, ppb = ps[2 + n_ % 2], psb[2 + n_ % 2]
                    s_, sb_ = sg[n_ % 2], sgb[n_ % 2]
                    n_ += 1
                    for fc in range(FC):
                        k.op("pe", lambda e, fc=fc, of=of, pg=pg, x_=x_: e.matmul(pg[:, :], lhsT=wg[:, fc, of * 128:(of + 1) * 128], rhs=x_[:, fc, :],
                                                                                  start=(fc == 0), stop=(fc == FC - 1)), r=[wgb, xb_], w=[pgb])
                    for kc in range(2):
                        k.op("pe", lambda e, kc=kc, of=of, pp=pp, q_=q_: e.matmul(pp[:, :], lhsT=wp[:, kc, of * 128:(of + 1) * 128], rhs=q_[:, kc, :],
                                                                                  start=(kc == 0), stop=(kc == 1)), r=[wpb, qb_], w=[ppb])
                    k.op("act", lambda e, pg=pg, s_=s_: e.activation(out=s_[:, :], in_=pg[:, :], func=AF.Sigmoid), r=[pgb], w=[sb_])
                    k.op("dve", lambda e, pp=pp, s_=s_: e.tensor_tensor(out=s_[:, :], in0=s_[:, :], in1=pp[:, :], op=ALU.mult),
                         r=[sb_, ppb], w=[sb_])
                    k.op("pool", lambda e, of=of, s_=s_, cs=cs: e.tensor_tensor(out=hT[:, of, cs], in0=hT[:, of, cs], in1=s_[:, :], op=ALU.add),
                         r=[sb_, hb[c]], w=[hb[c]])
            k.barrier()


def phase_final_norm(k, cx, hT, hb, gain_ap, dst, dst_b, stack):
    W = norm_work(k, stack)
    gcol, gcol_b = load_gain_cols(k, gain_ap, "g_fin", stack)
    on = [k.sb("fn%d" % i, [128, FC, 512], F32, stack) for i in range(2)]
    onb = k.bufs(2, "fn")
    dv = dst.rearrange("(fc p) t -> p fc t", p=128)
    for c in range(4):
        cs = slice(c * 512, (c + 1) * 512)
        rmsnorm_chunk(k, cx, W, [hT[:, fc, cs] for fc in range(FC)], hb[c], gcol, gcol_b,
                      [on[c % 2][:, fc, :] for fc in range(FC)], onb[c % 2])
        k.dma("sp", dv[:, :, cs], on[c % 2][:], r=[onb[c % 2]], w=[dst_b])


def build_prog_tail(final):
    nc = bass.Bass("TRN2", target_bir_lowering=False)
    consts = host_consts()
    names = ["ones_b"]
    cd = dram_consts(nc, consts, names)

    def din(nm, shape, dt_=F32):
        return nc.dram_tensor(nm, shape, dt_, kind="ExternalInput").ap()
    hin = din("hT_in", [D, TOK])
    oT = din("oT", [D, TOK], BF16)
    w_out = din("w_out", [D, D])
    g_ffn = din("g_ffn", [D])
    w1 = din("w1", [D, 2 * DFF])
    w2 = din("w2", [DFF, D])
    g_ple = din("g_ple", [D])
    wg = din("wg", [D, D])
    wp = din("wp", [256, D])
    pT = din("pT", [256, TOK])
    g_next = din("g_next", [D])
    k = KB(nc)
    cx = load_consts(k, cd, names)
    make_eps(k, cx)
    hT = k.sb("hT", [128, FC, TOK], F32)
    hb = k.bufs(4, "hT")
    xv = hin.rearrange("(fc p) t -> p fc t", p=128)
    for c in range(4):
        k.dma("sp", hT[:, :, c * 512:(c + 1) * 512], xv[:, :, c * 512:(c + 1) * 512], w=[hb[c]])
    wbf = tail_weights_to_bf16(k, nc, "u", w_out, w1, w2, wg, wp)
    phase_tail(k, cx, hT, hb, oT, k.buf("oT"), wbf, g_ffn, g_ple, pT)
    with contextlib.ExitStack() as st:
        if final:
            out = nc.dram_tensor("outT", [D, TOK], F32, kind="ExternalOutput").ap()
            phase_final_norm(k, cx, hT, hb, g_next, out, k.buf("outT"), st)
        else:
            hout = nc.dram_tensor("hT_out", [D, TOK], F32, kind="ExternalOutput").ap()
            hn = nc.dram_tensor("hnT", [D, TOK], BF16, kind="ExternalOutput").ap()
            hob = k.buf("hT_out")
            ov = hout.rearrange("(fc p) t -> p fc t", p=128)
            for c in range(4):
                k.dma("sp", ov[:, :, c * 512:(c + 1) * 512], hT[:, :, c * 512:(c + 1) * 512], r=[hb[c]], w=[hob])
            phase_norm_out(k, cx, hT, hb, g_next, hn, k.buf("hn_dram"), st)
        k.barrier()
    k.close()
    return nc, {nm: consts[nm] for nm in names}


def nsa_host_consts():
    c = {}
    i = np.arange(128)
    c["ident_f"] = np.eye(128, dtype=np.float32)
    c["jflip"] = np.ascontiguousarray(np.eye(128, dtype=np.float32)[::-1])
    dist = np.arange(NDC) - DOFF
    oh = np.zeros((33, NDC), np.float32)
    bk = _rel_bucket_np(dist)
    for ii in range(NDC):
        if dist[ii] < 0:
            oh[32, ii] = 1.0
        else:
            oh[bk[ii], ii] = 1.0
    c["ohc"] = oh
    dm = np.zeros((33, 33), np.float32)
    for b in range(32):
        dm[b, b] += 1.0
        dm[31, b] -= 1.0
    dm[32, 32] = NEG
    c["dm"] = dm
    keys = np.arange(S)
    c["ex"] = (np.arange(128)[:, None] == (keys[None, :] // 64)).astype(np.float32).astype(NPBF)
    t4 = np.where(i[None, :] < i[:, None], 0.0, NEG).astype(np.float32)
    c["t4"] = np.ascontiguousarray(np.tile(t4, (1, 4)))
    n = np.arange(512)
    cs_, ss_ = n * 16, np.arange(128) * 64
    ov = ((cs_[:, None] < ss_[None, :] + 64) & (cs_[:, None] + 32 > ss_[None, :])).astype(np.float32)
    ov[511, :] = 0.0
    c["ov"] = ov.astype(NPBF)
    q = np.arange(128)
    cq = (q >= 64).astype(np.int64)[:, None]
    rel = (np.arange(256) - 128)[None, :]
    forced = (rel == cq) | (rel == cq - 1)
    causal = rel <= cq
    c["mc_rel"] = (causal & ~forced).astype(np.float32)
    c["ma_rel"] = np.where(forced, 100.0, np.where(causal, 0.0, -1.0)).astype(np.float32)
    return c


GELU_C = 1.5957691216057308


def phase_nsa(k, cx, hn_all, hn_all_b, w_ap, rb_ap, peT_ap, w1_ap, w2k_ap, w2v_ap, bdz, a2a, a2a_b, stack, nqt=64, stop=99):
    scale = 0.125
    ps = [k.ps("np%d" % i, [128, 512], F32, stack) for i in range(8)]
    psb = k.bufs(8, "np")
    QT = [k.sb("nQT%d" % i, [128, S], BF16, stack) for i in range(2)]
    KST = k.sb("nKST", [128, S], BF16, stack)
    KWT = k.sb("nKWT", [128, S], BF16, stack)
    VS = k.sb("nVS", [128, 64, 65], BF16, stack)
    VW = k.sb("nVW", [128, 64, 65], BF16, stack)
    G = k.sb("nG", [128, 64, 12], F32, stack)
    kcT = k.sb("nkcT", [128, 512], BF16, stack)
    VCX = k.sb("nVCX", [128, 4, 193], BF16, stack)
    T0 = k.sb("nT0", [128, 512], F32, stack)
    T1 = k.sb("nT1", [128, 512], F32, stack)
    pb = k.buf("nsa_persist")
    k.op("pool", lambda e: e.memset(VS[:, :, 64:65], 1.0), w=[pb])
    k.op("pool", lambda e: e.memset(VW[:, :, 64:65], 1.0), w=[pb])
    k.op("pool", lambda e: e.memset(VCX[:], 0.0), w=[pb])
    k.op("pool", lambda e: e.memset(VCX[:, :, 64:65], 1.0), w=[pb])
    k.op("pool", lambda e: e.memset(kcT[:], 0.0), w=[pb])
    k.dma("sp", VCX[:, :, 65:193], cx.ov_dram.rearrange("(nb p) s -> p nb s", p=128), w=[pb])
    with contextlib.ExitStack() as st:
        rbe = k.sb("rbe", [33, 4], F32, st)
        rbeb = k.buf("rbe")
        k.op("pool", lambda e: e.memset(rbe[32:33, :], 1.0), w=[rbeb])
        k.dma("sp", rbe[0:32, :], rb_ap, w=[rbeb])
        ohc = k.sb("ohc", [33, NDC], F32, st)
        ohcb = k.buf("ohc")
        k.dma("sp", ohc[:], cx.ohc_dram, w=[ohcb])
        dm = k.sb("dm", [33, 33], F32, st)
        dmb = k.buf("dm")
        k.dma("sp", dm[:], cx.dm_dram, w=[dmb])
        rbx = k.sb("rbx", [33, 4], F32, st)
        rbxb = k.buf("rbx")
        k.op("pe", lambda e: e.matmul(ps[0][0:33, 0:4], lhsT=dm[:], rhs=rbe[:], start=True, stop=True), r=[dmb, rbeb], w=[psb[0]])
        k.op("dve", lambda e: e.tensor_copy(out=rbx[:], in_=ps[0][0:33, 0:4]), r=[psb[0]], w=[rbxb])
        bds = k.sb("bds", [4, NDC], F32, st)
        bdsb = k.buf("bds")
        nchunk = (NDC + 511) // 512
        for ci in range(nchunk):
            lo, hi = ci * 512, min(NDC, (ci + 1) * 512)
            p_, pb_ = ps[1 + ci % 2], psb[1 + ci % 2]
            k.op("pe", lambda e, p_=p_, lo=lo, hi=hi: e.matmul(p_[0:4, 0:hi - lo], lhsT=rbx[:], rhs=ohc[:, lo:hi], start=True, stop=True),
                 r=[rbxb, ohcb], w=[pb_])
            k.op("dve", lambda e, p_=p_, lo=lo, hi=hi: e.tensor_copy(out=bds[:, lo:hi], in_=p_[0:4, 0:hi - lo]), r=[pb_], w=[bdsb])
        bdzb = k.buf("bdz")
        k.dma("sp", bdz, bds[:], r=[bdsb], w=[bdzb])
        cx.bdz_b = bdzb
        U = k.sb("U", [128, 512], F32, st)
        Ub = k.buf("U")
        for m, Tm in ((0, T0), (1, T1)):
            src = bass.AP(bdz.tensor, DOFF - 127 + 128 * m, [[1, 128], [NDC, 4], [1, 128]])
            k.dma("sp", U[:].rearrange("p (g q) -> p g q", g=4), src, r=[bdzb], w=[Ub])
            k.op("pe", lambda e: e.matmul(ps[3][:, :], lhsT=cx.jflip[:], rhs=U[:], start=True, stop=True), r=[Ub, cx.jflip_b], w=[psb[3]])
            k.op("dve", lambda e, Tm=Tm: e.tensor_copy(out=Tm[:], in_=ps[3][:, :]), r=[psb[3]], w=[pb])
        k.barrier()
    if stop <= 0:
        return
    with contextlib.ExitStack() as st:
        w = k.sb("nw", [128, FC, 780], BF16, st)
        wb = k.buf("nw")
        wv = w_ap.rearrange("(fc p) n -> p fc n", p=128)
        for fc in range(FC):
            k.dma("pool", w[:, fc, :], wv[:, fc, :], w=[wb])
        KAT = k.sb("nKAT", [128, S], BF16, st)
        hnc = [k.sb("nhnc%d" % i, [128, FC, 512], BF16, st) for i in range(2)]
        hncb = k.bufs(2, "nhnc")
        for c in range(16):
            rho, off = c // 4, (c % 4) * 512
            src = hn_all.rearrange("(q r h p) t -> r p q h t", q=4, r=4, h=2, p=128)[rho][:, :, :, off:off + 512]
            hb_ = hncb[c % 2]
            for q_ in range(4):
                k.dma("sp", hnc[c % 2][:, 2 * q_:2 * q_ + 2, :], src[:, q_, :, :], r=[hn_all_b], w=[hb_])
            x = hnc[c % 2]
            cs = slice(c * 512, (c + 1) * 512)
            for j, (c0, dst, sc) in enumerate(((0, QT[0], scale), (128, QT[1], scale), (256, KAT, None), (384, KST, None), (512, KWT, None))):
                p_, pb_ = ps[j % 4], psb[j % 4]
                for fc in range(FC):
                    k.op("pe", lambda e, fc=fc, p_=p_, c0=c0: e.matmul(p_[:, :], lhsT=w[:, fc, c0:c0 + 128], rhs=x[:, fc, :],
                                                                         start=(fc == 0), stop=(fc == FC - 1)), r=[wb, hb_], w=[pb_])
                if sc is not None:
                    k.op("act", lambda e, p_=p_, dst=dst, sc=sc: e.activation(out=dst[:, cs], in_=p_[:, :], func=AF.Copy, scale=sc), r=[pb_], w=[pb])
                else:
                    k.op("dve", lambda e, p_=p_, dst=dst: e.tensor_copy(out=dst[:, cs], in_=p_[:, :]), r=[pb_], w=[pb])
            for tt in range(4):
                p_, pb_ = ps[4 + tt % 2], psb[4 + tt % 2]
                kb = c * 4 + tt
                for fc in range(FC):
                    k.op("pe", lambda e, fc=fc, p_=p_, tt=tt: e.matmul(p_[:, 0:140], lhsT=x[:, fc, tt * 128:(tt + 1) * 128], rhs=w[:, fc, 640:780],
                                                                         start=(fc == 0), stop=(fc == FC - 1)), r=[wb, hb_], w=[pb_])
                k.op("dve", lambda e, p_=p_, kb=kb: e.tensor_copy(out=VS[:, kb, 0:64], in_=p_[:, 0:64]), r=[pb_], w=[pb])
                k.op("dve", lambda e, p_=p_, kb=kb: e.tensor_copy(out=VW[:, kb, 0:64], in_=p_[:, 64:128]), r=[pb_], w=[pb])
                k.op("act", lambda e, p_=p_, kb=kb: e.activation(out=G[:, kb, :], in_=p_[:, 128:140], func=AF.Sigmoid), r=[pb_], w=[pb])
        if stop <= 1:
            k.barrier()
            return
        w1 = k.sb("nw1", [128, 32, 256], BF16, st)
        w1b = k.buf("nw1")
        for l in range(32):
            k.dma("pool", w1[:, l, :], w1_ap[:, l, :], w=[w1b])
        w2kd = k.sb("nw2k", [128, 2, 128], BF16, st)
        w2v = k.sb("nw2v", [128, 2, 64], BF16, st)
        w2b = k.buf("nw2")
        w2kv_ = w2k_ap.rearrange("(hc p) d -> p hc d", p=128)
        for hc in range(2):
            k.dma("pool", w2kd[:, hc, 0:64], w2kv_[:, hc, :], w=[w2b])
            k.dma("pool", w2kd[:, hc, 64:128], w2kv_[:, hc, :], w=[w2b])
            k.dma("pool", w2v[:, hc, :], w2v_ap.rearrange("(hc p) d -> p hc d", p=128)[:, hc, :], w=[w2b])
        peT = k.sb("npeT", [128, 32], BF16, st)
        peb = k.buf("npeT")
        k.dma("pool", peT[:], peT_ap, w=[peb])
        cb = k.sb("ncb", [128, 4], F32, st)
        cbb = k.buf("ncb")
        hid = k.sb("nhid", [128, 4, 512], BF16, st)
        hidb = k.buf("nhid")
        xs = k.sb("nxs", [128, 512], F32, st)
        x2 = k.sb("nx2", [128, 512], F32, st)
        xsb, x2b = k.buf("nxs"), k.buf("nx2")
        for kv in range(2):
            b0 = 64 * kv
            for hc in range(2):
                idx = kv * 2 + hc
                p_, pb_ = ps[idx % 2], psb[idx % 2]
                pc, pcb = ps[2 + idx % 2], psb[2 + idx % 2]
                for l in range(32):
                    k.op("pe", lambda e, l=l, pc=pc: e.matmul(pc[:, 0:1], lhsT=w1[b0:b0 + 64, l, hc * 128:(hc + 1) * 128], rhs=peT[b0:b0 + 64, l:l + 1],
                                                                start=(l == 0), stop=(l == 31)), r=[w1b, peb], w=[pcb])
                k.op("dve", lambda e, pc=pc, idx=idx: e.tensor_copy(out=cb[:, idx:idx + 1], in_=pc[:, 0:1]), r=[pcb], w=[cbb])
                for l in range(32):
                    k.op("pe", lambda e, l=l, p_=p_: e.matmul(p_[:, 0:511], lhsT=w1[b0:b0 + 64, l, hc * 128:(hc + 1) * 128],
                                                                rhs=KAT[b0:b0 + 64, l:l + 8161:16], start=(l == 0), stop=(l == 31)), r=[w1b, pb], w=[pb_])
                k.op("act", lambda e, p_=p_, idx=idx: e.activation(out=xs[:, 0:511], in_=p_[:, 0:511], func=AF.Identity, bias=cb[:, idx:idx + 1], scale=1.0),
                     r=[pb_, cbb], w=[xsb])
                k.op("act", lambda e, p_=p_, idx=idx: e.activation(out=x2[:, 0:511], in_=p_[:, 0:511], func=AF.Square, bias=cb[:, idx:idx + 1], scale=1.0),
                     r=[pb_, cbb], w=[x2b])
                k.op("dve", lambda e: e.tensor_scalar(out=x2[:, 0:511], in0=x2[:, 0:511], scalar1=0.044715, scalar2=1.0, op0=ALU.mult, op1=ALU.add),
                     r=[x2b], w=[x2b])
                k.op("dve", lambda e: e.tensor_tensor(out=x2[:, 0:511], in0=x2[:, 0:511], in1=xs[:, 0:511], op=ALU.mult), r=[x2b, xsb], w=[x2b])
                k.op("act", lambda e: e.activation(out=x2[:, 0:511], in_=x2[:, 0:511], func=AF.Sigmoid, scale=GELU_C), r=[x2b], w=[x2b])
                k.op("dve", lambda e, idx=idx: e.tensor_tensor(out=hid[:, idx, 0:511], in0=x2[:, 0:511], in1=xs[:, 0:511], op=ALU.mult),
                     r=[x2b, xsb], w=[hidb])
        for hc in range(2):
            k.op("pe", lambda e, hc=hc: e.matmul(ps[4][:, 0:511], lhsT=w2kd[:, hc, :], rhs=hid[:, hc, 0:511], start=(hc == 0), stop=(hc == 1)),
                 r=[w2b, hidb], w=[psb[4]])
        k.op("dve", lambda e: e.tensor_copy(out=kcT[:, 0:511], in_=ps[4][:, 0:511]), r=[psb[4]], w=[pb])
        for nb in range(4):
            M = 128 if nb < 3 else 127
            p_, pb_ = ps[5 + nb % 2], psb[5 + nb % 2]
            for hc in range(2):
                k.op("pe", lambda e, hc=hc, nb=nb, M=M, p_=p_: e.matmul(p_[0:M, 0:64], lhsT=hid[:, 2 + hc, nb * 128:nb * 128 + M], rhs=w2v[:, hc, :],
                                                                          start=(hc == 0), stop=(hc == 1)), r=[w2b, hidb], w=[pb_])
            k.op("dve", lambda e, nb=nb, M=M, p_=p_: e.tensor_copy(out=VCX[0:M, nb, 0:64], in_=p_[0:M, 0:64]), r=[pb_], w=[pb])
        k.barrier()
    if stop <= 2:
        return
    with contextlib.ExitStack() as st:
        psZ, psZb = ps[0:2], psb[0:2]
        psOc, psOcb = ps[2:4], psb[2:4]
        psOs, psOsb = ps[4], psb[4]
        psOw, psOwb = ps[5], psb[5]
        psM, psMb = ps[6:8], psb[6:8]
        Wc = [k.sb("nWc%d" % i, [128, 512], F32, st) for i in range(2)]
        Wcb = k.bufs(2, "nWc")
        ee = [k.sb("nee%d" % i, [128, 512], BF16, st) for i in range(3)]
        eeb = k.bufs(3, "nee")
        sf = [k.sb("nsf%d" % i, [128, 512], F32, st) for i in range(2)]
        sfb = k.bufs(2, "nsf")
        nmT = k.sb("nnmT", [128, 512], BF16, st)
        nmTb = k.buf("nnmT")
        imp = k.sb("nimp", [128, 128], F32, st)
        imp3 = k.sb("nimp3", [128, 128], F32, st)
        negm = k.sb("nnegm", [128, 128], F32, st)
        impb, imp3b, negmb = k.buf("imp"), k.buf("imp3"), k.buf("negm")
        m8 = k.sb("nm8", [128, 16], F32, st)
        m8b = k.buf("m8")
        dn = k.sb("ndn", [128, 12], F32, st)
        dnb = k.buf("dn")
        ot = k.sb("not", [128, 256], F32, st)
        otb = k.buf("ot")
        oT = [k.sb("noT%d" % i, [128, 2, 512], BF16, st) for i in range(2)]
        oTb = k.bufs(2, "noT")
        zc = 0
        ec = 0
        wcn = 0

        bd = [k.sb("nbd%d" % i, [128, 512], BF16, st) for i in range(2)]
        bdb = k.bufs(2, "nbd")
        for i in range(2):
            k.op("pool", lambda e, i=i: e.memset(bd[i][:], 0.0), w=[bdb[i]])
        cur = {}

        def qk(Z, Zb, KT_, kcols, tcols, first_start, extra_r=()):
            k.op("pe", lambda e: e.matmul(Z[:, :], lhsT=KT_[:, kcols], rhs=cur["bd"][:, :], start=first_start, stop=True, skip_group_check=True),
                 r=[pb, cur["bdb"]] + list(extra_r), w=[Zb])

        pend = []

        def flush():
            while pend:
                pend.pop(0)()

        def defer(fn):
            pend.append(fn)
            while len(pend) > 1:
                pend.pop(0)()

        for qt in range(nqt):
            tcols = slice(128 * qt, 128 * (qt + 1))
            cur["bd"], cur["bdb"] = bd[qt % 2], bdb[qt % 2]
            for g in range(4):
                b0 = 64 * (g % 2)
                k.op("pool", lambda e, g=g, b0=b0: e.tensor_copy(out=cur["bd"][b0:b0 + 64, g * 128:(g + 1) * 128], in_=QT[g // 2][b0:b0 + 64, tcols]),
                     r=[pb], w=[cur["bdb"]])
            NB = (8 * qt + 6) // 128 + 1
            for nb in range(NB):
                Z, Zb = psZ[zc % 2], psZb[zc % 2]
                zc += 1
                o_idx = qt - 16 * nb
                if o_idx <= 16:
                    W_, Wb_ = Wc[wcn % 2], Wcb[wcn % 2]
                    wcn += 1
                    src = bass.AP(bdz.tensor, 128 * o_idx, [[16, 128], [NDC, 4], [1, 128]])
                    import os
                    if os.environ.get("NSA_DBG") == "1":
                        k.op("pool", lambda e, W_=W_: e.memset(W_[:], 0.0), w=[Wb_])
                    else:
                        k.dma("sp", W_[:].rearrange("p (g q) -> p g q", g=4), src, r=[cx.bdz_b], w=[Wb_])
                    k.op("pe", lambda e, W_=W_: e.matmul(psM[0][:, :], lhsT=cx.jflip[:], rhs=W_[:], start=True, stop=True),
                         r=[Wb_, cx.jflip_b], w=[psMb[0]])
                    k.op("act", lambda e, W_=W_: e.copy(out=W_[:], in_=psM[0][:, :]), r=[psMb[0]], w=[Wb_])
                    qk(Z, Zb, kcT, slice(nb * 128, (nb + 1) * 128), tcols, True)
                    E_, Eb_ = ee[ec % 3], eeb[ec % 3]
                    ec += 1
                    s_, sb_ = sf[ec % 2], sfb[ec % 2]
                    k.op("dve", lambda e, s_=s_, Z=Z, W_=W_: e.tensor_tensor(out=s_[:], in0=Z[:, :], in1=W_[:], op=ALU.add), r=[Zb, Wb_], w=[sb_])
                    k.op("act", lambda e, E_=E_, s_=s_: e.activation(out=E_[:, :], in_=s_[:], func=AF.Exp), r=[sb_], w=[Eb_])
                else:
                    qk(Z, Zb, kcT, slice(nb * 128, (nb + 1) * 128), tcols, True)
                    E_, Eb_ = ee[ec % 3], eeb[ec % 3]
                    ec += 1
                    k.op("act", lambda e, E_=E_, Z=Z: e.activation(out=E_[:, :], in_=Z[:, :], func=AF.Exp), r=[Zb], w=[Eb_])
                def pv_c(E_=E_, Eb_=Eb_, nb=nb, NB=NB):
                    for g in range(4):
                        bank, bb = psOc[g // 2], psOcb[g // 2]
                        c0 = (g % 2) * 193
                        k.op("pe", lambda e, g=g, bank=bank, c0=c0: e.matmul(
                            bank[:, c0:c0 + 193], lhsT=E_[:, g * 128:(g + 1) * 128], rhs=VCX[:, nb, :],
                            start=(nb == 0 and g % 2 == 0), stop=(nb == NB - 1), skip_group_check=True), r=[Eb_, pb], w=[bb])
                defer(pv_c)
            flush()
            if stop <= 3:
                continue
            for h2 in range(2):
                bank, bb = psOc[h2], psOcb[h2]
                bv = bank[:, 0:386].rearrange("p (g c) -> p g c", c=193)
                k.op("dve", lambda e, h2=h2, bv=bv: e.tensor_scalar(out=dn[:, 2 * h2:2 * h2 + 2], in0=bv[:, :, 64], scalar1=1e-30, scalar2=None, op0=ALU.max),
                     r=[bb], w=[dnb])
            k.op("dve", lambda e: e.reciprocal(out=dn[:, 0:4], in_=dn[:, 0:4]), r=[dnb], w=[dnb])
            for g in range(4):
                bank, bb = psOc[g // 2], psOcb[g // 2]
                c0 = (g % 2) * 193 + 65
                if g == 0:
                    k.op("dve", lambda e, bank=bank, c0=c0: e.tensor_scalar(out=imp[:], in0=bank[:, c0:c0 + 128], scalar1=dn[:, 0:1], scalar2=None, op0=ALU.mult),
                         r=[bb, dnb], w=[impb])
                else:
                    k.op("dve", lambda e, g=g, bank=bank, c0=c0: e.scalar_tensor_tensor(out=imp[:], in0=bank[:, c0:c0 + 128], scalar=dn[:, g:g + 1], in1=imp[:],
                                                                                        op0=ALU.mult, op1=ALU.add), r=[bb, dnb, impb], w=[impb])
            sl = slice(128 - 2 * qt, 256 - 2 * qt)
            k.op("dve", lambda e: e.tensor_tensor(out=imp[:], in0=imp[:], in1=cx.mc_rel[:, sl], op=ALU.mult), r=[impb, cx.mc_rel_b], w=[impb])
            k.op("dve", lambda e: e.tensor_tensor(out=imp[:], in0=imp[:], in1=cx.ma_rel[:, sl], op=ALU.add), r=[impb, cx.ma_rel_b], w=[impb])
            k.op("dve", lambda e: e.memset(imp[:, 0:1], 100.0), r=[impb], w=[impb])
            k.op("dve", lambda e: e.max(out=m8[:, 0:8], in_=imp[:]), r=[impb], w=[m8b])
            k.op("dve", lambda e: e.match_replace(out=imp3[:], in_to_replace=m8[:, 0:8], in_values=imp[:], imm_value=-1e30), r=[impb, m8b], w=[imp3b])
            k.op("dve", lambda e: e.max(out=m8[:, 8:16], in_=imp3[:]), r=[imp3b], w=[m8b])
            k.op("dve", lambda e: e.tensor_scalar(out=negm[:], in0=imp[:], scalar1=m8[:, 15:16], scalar2=NEG, op0=ALU.is_lt, op1=ALU.mult),
                 r=[impb, m8b], w=[negmb])
            k.op("pe", lambda e: e.transpose(out=psM[0][:, 0:128], in_=negm[:], identity=cx.ident_f[:]), r=[negmb, cx.ident_f_b], w=[psMb[0]])
            for g in range(4):
                if g % 2 == 0:
                    k.op("dve", lambda e, g=g: e.tensor_copy(out=nmT[:, g * 128:(g + 1) * 128], in_=psM[0][:, 0:128]), r=[psMb[0]], w=[nmTb])
                else:
                    k.op("act", lambda e, g=g: e.copy(out=nmT[:, g * 128:(g + 1) * 128], in_=psM[0][:, 0:128]), r=[psMb[0]], w=[nmTb])
            if stop <= 4:
                continue
            for kb in range(qt + 1):
                Z, Zb = psZ[zc % 2], psZb[zc % 2]
                zc += 1
                m = qt - kb
                k.op("pe", lambda e, Z=Z, kb=kb: e.matmul(Z[:, :], lhsT=cx.ex[:, 128 * kb:128 * (kb + 1)], rhs=nmT[:], start=True, stop=False, skip_group_check=True),
                     r=[nmTb, cx.ex_b], w=[Zb])
                qk(Z, Zb, KST, slice(128 * kb, 128 * (kb + 1)), tcols, False)
                E_, Eb_ = ee[ec % 3], eeb[ec % 3]
                ec += 1
                if m <= 1:
                    Tm = T0 if m == 0 else T1
                    s_, sb_ = sf[ec % 2], sfb[ec % 2]
                    k.op("dve", lambda e, s_=s_, Z=Z, Tm=Tm: e.tensor_tensor(out=s_[:], in0=Z[:, :], in1=Tm[:], op=ALU.add), r=[Zb, pb], w=[sb_])
                    k.op("act", lambda e, E_=E_, s_=s_: e.activation(out=E_[:, :], in_=s_[:], func=AF.Exp), r=[sb_], w=[Eb_])
                else:
                    k.op("act", lambda e, E_=E_, Z=Z: e.activation(out=E_[:, :], in_=Z[:, :], func=AF.Exp), r=[Zb], w=[Eb_])
                def pv_s(E_=E_, Eb_=Eb_, kb=kb, qt=qt):
                    for g in range(4):
                        k.op("pe", lambda e, g=g: e.matmul(psOs[:, g * 65:(g + 1) * 65], lhsT=E_[:, g * 128:(g + 1) * 128], rhs=VS[:, kb, :],
                                                            start=(kb == 0 and g == 0), stop=(kb == qt), skip_group_check=True), r=[Eb_, pb], w=[psOsb])
                defer(pv_s)
            flush()
            if stop <= 5:
                continue
            kb0 = max(0, qt - 4)
            for kb in range(kb0, qt + 1):
                Z, Zb = psZ[zc % 2], psZb[zc % 2]
                zc += 1
                m = qt - kb
                qk(Z, Zb, KWT, slice(128 * kb, 128 * (kb + 1)), tcols, True)
                E_, Eb_ = ee[ec % 3], eeb[ec % 3]
                ec += 1
                if m in (0, 1, 4):
                    Tm, Tmb = {0: (T0, pb), 1: (T1, pb), 4: (cx.t4, cx.t4_b)}[m]
                    s_, sb_ = sf[ec % 2], sfb[ec % 2]
                    k.op("dve", lambda e, s_=s_, Z=Z, Tm=Tm: e.tensor_tensor(out=s_[:], in0=Z[:, :], in1=Tm[:], op=ALU.add), r=[Zb, Tmb], w=[sb_])
                    k.op("act", lambda e, E_=E_, s_=s_: e.activation(out=E_[:, :], in_=s_[:], func=AF.Exp), r=[sb_], w=[Eb_])
                else:
                    k.op("act", lambda e, E_=E_, Z=Z: e.activation(out=E_[:, :], in_=Z[:, :], func=AF.Exp), r=[Zb], w=[Eb_])
                def pv_w(E_=E_, Eb_=Eb_, kb=kb, qt=qt, kb0=kb0):
                    for g in range(4):
                        k.op("pe", lambda e, g=g: e.matmul(psOw[:, g * 65:(g + 1) * 65], lhsT=E_[:, g * 128:(g + 1) * 128], rhs=VW[:, kb, :],
                                                            start=(kb == kb0 and g == 0), stop=(kb == qt), skip_group_check=True), r=[Eb_, pb], w=[psOwb])
                defer(pv_w)
            flush()
            if stop <= 6:
                continue
            osv = psOs[:, 0:260].rearrange("p (g c) -> p g c", c=65)
            owv = psOw[:, 0:260].rearrange("p (g c) -> p g c", c=65)
            k.op("dve", lambda e: e.tensor_scalar(out=dn[:, 4:8], in0=osv[:, :, 64], scalar1=1e-30, scalar2=None, op0=ALU.max), r=[psOsb], w=[dnb])
            k.op("dve", lambda e: e.tensor_scalar(out=dn[:, 8:12], in0=owv[:, :, 64], scalar1=1e-30, scalar2=None, op0=ALU.max), r=[psOwb], w=[dnb])
            k.op("dve", lambda e: e.reciprocal(out=dn[:, 4:12], in_=dn[:, 4:12]), r=[dnb], w=[dnb])
            k.op("dve", lambda e: e.tensor_tensor(out=dn[:, :], in0=dn[:, :], in1=G[:, qt, :], op=ALU.mult), r=[dnb, pb], w=[dnb])
            for g in range(4):
                bank, bb = psOc[g // 2], psOcb[g // 2]
                c0 = (g % 2) * 193
                og = ot[:, g * 64:(g + 1) * 64]
                k.op("dve", lambda e, g=g, bank=bank, c0=c0, og=og: e.tensor_scalar(out=og, in0=bank[:, c0:c0 + 64], scalar1=dn[:, g:g + 1], scalar2=None, op0=ALU.mult),
                     r=[bb, dnb], w=[otb])
                k.op("dve", lambda e, g=g, og=og: e.scalar_tensor_tensor(out=og, in0=psOs[:, g * 65:g * 65 + 64], scalar=dn[:, 4 + g:5 + g], in1=og, op0=ALU.mult, op1=ALU.add),
                     r=[psOsb, dnb, otb], w=[otb])
                k.op("dve", lambda e, g=g, og=og: e.scalar_tensor_tensor(out=og, in0=psOw[:, g * 65:g * 65 + 64], scalar=dn[:, 8 + g:9 + g], in1=og, op0=ALU.mult, op1=ALU.add),
                     r=[psOwb, dnb, otb], w=[otb])
            y2 = (qt // 4) % 2
            for fh in range(2):
                k.op("pe", lambda e, fh=fh: e.transpose(out=psM[1][:, fh * 128:(fh + 1) * 128], in_=ot[:, fh * 128:(fh + 1) * 128], identity=cx.ident_f[:]),
                     r=[otb, cx.ident_f_b], w=[psMb[1]])
            k.op("act", lambda e, y2=y2: e.copy(out=oT[y2][:, :, (qt % 4) * 128:(qt % 4 + 1) * 128],
                                                in_=psM[1][:, 0:256].rearrange("p (fh t) -> p fh t", fh=2)), r=[psMb[1]], w=[oTb[y2]])
            if qt % 4 == 3:
                sbk = qt // 4
                dest, off = sbk // 4, (sbk % 4) * 512
                k.dma("sp", a2a[dest].rearrange("(fh p) t -> p fh t", p=128)[:, :, off:off + 512], oT[y2][:], r=[oTb[y2]], w=[a2a_b])
                if k.engs["pe"].count > SEM_ROTATE:
                    k.barrier()
        k.barrier()


def build_prog_nsa(nqt=64, stop=99):
    nc = bass.Bass("TRN2", target_bir_lowering=False)
    consts = nsa_host_consts()
    names = ["ident_f", "jflip", "ex", "t4", "mc_rel", "ma_rel"]
    dnames = ["ohc", "dm", "ov"]
    cd = dram_consts(nc, consts, names + dnames)

    def din(nm, shape, dt_=F32):
        return nc.dram_tensor(nm, shape, dt_, kind="ExternalInput").ap()
    hn_all = din("hn_all", [4 * D, TOK], BF16)
    w = din("nsa_w", [D, 780])
    rb = din("rb", [32, 4])
    peT = din("peT", [128, 32])
    w1 = din("cw1", [128, 32, 256])
    w2k = din("cw2k", [256, 64])
    w2v = din("cw2v", [256, 64])
    a2a = nc.dram_tensor("a2a", [4, 256, TOK], BF16, kind="ExternalOutput").ap()
    bdz = nc.dram_tensor("bdz", [4, NDC], F32, kind="Internal").ap()
    k = KB(nc)
    cx = load_consts(k, cd, names)
    for nm in dnames:
        setattr(cx, nm + "_dram", cd[nm])
    phase_nsa(k, cx, hn_all, k.buf("hn_all"), w, rb, peT, w1, w2k, w2v, bdz, a2a, k.buf("a2a"), k.stack, nqt=nqt, stop=stop)
    k.barrier()
    k.close()
    return nc, {nm: consts[nm] for nm in names + dnames}


def nsa_core_inputs(inp, r):
    w_in = inp["nsa_w_in"][0]
    kv0 = 1024

    def kvc(i):
        return w_in[:, kv0 + i * 256 + r * 64: kv0 + i * 256 + (r + 1) * 64]
    gcols = [2560 + j * 16 + r * 4 + g for j in range(3) for g in range(4)]
    w = np.concatenate([w_in[:, 256 * r:256 * (r + 1)], kvc(0), kvc(1), kvc(2), kvc(2), kvc(4), kvc(4), kvc(3), kvc(5), w_in[:, gcols]], axis=1)
    peT = np.concatenate([inp["nsa_pe_k"][0].T, inp["nsa_pe_v"][0].T], axis=0)
    w1k = inp["nsa_ck_w1"][0].reshape(32, 64, 256).transpose(1, 0, 2)
    w1v = inp["nsa_cv_w1"][0].reshape(32, 64, 256).transpose(1, 0, 2)
    return {"nsa_w": np.ascontiguousarray(w), "rb": np.ascontiguousarray(inp["rel_bias"][:, 4 * r:4 * r + 4]),
            "peT": np.ascontiguousarray(peT), "cw1": np.ascontiguousarray(np.concatenate([w1k, w1v], axis=0)),
            "cw2k": inp["nsa_ck_w2"][0], "cw2v": inp["nsa_cv_w2"][0]}


_PROGS = {}


def _prog(name, fn):
    if name not in _PROGS:
        _PROGS[name] = fn()
    return _PROGS[name]


def _run(name, fn, maps):
    nc, cst = _prog(name, fn)
    full = []
    for m in maps:
        mm = dict(m)
        mm.update({"c_" + kk: v for kk, v in cst.items()})
        full.append(mm)
    res = run_bass_kernel_spmd(nc, full, core_ids=list(range(NCORES)))
    return res.results


def _c(a):
    return np.ascontiguousarray(a)


def _allgather(parts):
    out = []
    for b in range(2):
        cat = np.concatenate([np.asarray(parts[4 * b + r])[256 * q:256 * (q + 1)] for q in range(4) for r in range(4)], axis=0)
        out += [cat] * 4
    return out


def _alltoall(parts):
    out = []
    for b in range(2):
        for j in range(4):
            out.append(_c(np.concatenate([np.asarray(parts[4 * b + i])[j] for i in range(4)], axis=0)))
    return out


def kernel_unfused(**inp):
    inp = {kk: np.asarray(v) for kk, v in inp.items()}
    x, p = inp["x"], inp["p"]
    cores = [(c // 4, c % 4) for c in range(NCORES)]
    tsl = [slice(TOK * r, TOK * (r + 1)) for (_, r) in cores]
    xT = [_c(x[b, tsl[c]].T) for c, (b, r) in enumerate(cores)]
    res = _run("norm0", build_prog_norm0, [{"xT": xT[c], "gain": _c(inp["norm_mix"][0])} for c in range(NCORES)])
    hn_all = _allgather([r_["hnT"] for r_ in res])
    w_in = inp["sb_w_in"][0]
    maps = []
    for c, (b, r) in enumerate(cores):
        wq = np.concatenate([w_in[:, 256 * r:256 * (r + 1)], w_in[:, 1024 + 256 * r:1024 + 256 * (r + 1)],
                             w_in[:, 2048 + 256 * r:2048 + 256 * (r + 1)]], axis=1)
        maps.append({"hn_all": hn_all[c], "sb_w": _c(wq)})
    res = _run("sb", build_prog_sb, maps)
    oT = _alltoall([r_["a2a"] for r_ in res])

    def tail_maps(i, hT_in, oT_, g_next):
        out = []
        for c, (b, r) in enumerate(cores):
            out.append({"hT_in": hT_in[c], "oT": oT_[c],
                        "w_out": _c((inp["sb_w_out"] if i == 0 else inp["nsa_w_out"])[0]),
                        "g_ffn": _c(inp["norm_ffn"][i]), "w1": _c(inp["ffn_w_in"][i]), "w2": _c(inp["ffn_w_out"][i]),
                        "g_ple": _c(inp["norm_ple"][i]), "wg": _c(inp["ple_w_gate"][i]), "wp": _c(inp["ple_w_proj"][i]),
                        "pT": _c(p[i, b, tsl[c]].T), "g_next": _c(g_next)})
        return out
    res = _run("tail0", lambda: build_prog_tail(False), tail_maps(0, xT, oT, inp["norm_mix"][1]))
    h1T = [np.asarray(r_["hT_out"]) for r_ in res]
    hn_all = _allgather([r_["hnT"] for r_ in res])
    maps = []
    for c, (b, r) in enumerate(cores):
        m = {"hn_all": hn_all[c]}
        m.update(nsa_core_inputs(inp, r))
        maps.append(m)
    res = _run("nsa", build_prog_nsa, maps)
    oT = _alltoall([r_["a2a"] for r_ in res])
    res = _run("tail1", lambda: build_prog_tail(True), tail_maps(1, h1T, oT, inp["final_norm"]))
    out = np.empty((2, S, D), np.float32)
    for c, (b, r) in enumerate(cores):
        out[b, tsl[c], :] = np.asarray(res[c]["outT"]).T
    return out


I32 = mybir.dt.int32
TAIL_KEYS = (("w_out", [D, D]), ("g_ffn", [D]), ("w1", [D, 2 * DFF]), ("w2", [DFF, D]), ("g_ple", [D]), ("wg", [D, D]), ("wp", [256, D]))


def build_prog_fused():
    import os
    CUT = int(os.environ.get("FUSED_CUT", "99"))
    nc = bass.Bass("TRN2", target_bir_lowering=False)
    consts = dict(host_consts())
    consts.update(nsa_host_consts())
    small = ["ident_f", "ones_b", "negu_b", "negones_b", "mask_sb", "jflip"]
    nsa_sb = ["ex", "t4", "mc_rel", "ma_rel"]
    nsa_dr = ["ohc", "dm", "ov"]
    cd = dram_consts(nc, consts, small + nsa_sb + nsa_dr)

    def din(nm, shape, dt_=F32):
        return nc.dram_tensor(nm, shape, dt_, kind="ExternalInput").ap()

    def dint(nm, shape, dt_):
        return nc.dram_tensor(nm, shape, dt_, kind="Internal").ap()
    xT = din("xT", [D, TOK])
    pT = [din("pT0", [256, TOK]), din("pT1", [256, TOK])]
    rk = din("rk", [1, 2], I32)
    g_mix = [din("g_mix0", [D]), din("g_mix1", [D])]
    g_fin = din("g_fin", [D])
    sb_w = din("sb_w", [D, 768])
    tails = [{nm: din("%s_%d" % (nm, i), shp) for nm, shp in TAIL_KEYS} for i in range(2)]
    nsa_w = din("nsa_w", [D, 780])
    rb = din("rb", [32, 4])
    peT = din("peT", [128, 32])
    cw1 = din("cw1", [128, 32, 256])
    cw2k = din("cw2k", [256, 64])
    cw2v = din("cw2v", [256, 64])
    outT = nc.dram_tensor("outT", [D, TOK], F32, kind="ExternalOutput").ap()
    hn_loc = dint("hn_loc", [D, TOK], BF16)
    hn_all = dint("hn_all", [4 * D, TOK], BF16)
    a2a_loc = dint("a2a_loc", [4, 256, TOK], BF16)
    a2a_all = dint("a2a_all", [4 * D, TOK], BF16)
    bdz = dint("bdz", [4, NDC], F32)
    hsp = dint("hsp", [D, TOK], F32)
    k = KB(nc)
    cx = load_consts(k, cd, small)
    for nm in nsa_dr:
        setattr(cx, nm + "_dram", cd[nm])
    make_eps(k, cx)
    make_one(k, cx)
    groups = [[0, 1, 2, 3], [4, 5, 6, 7]]
    spq = k.engs["sp"].h
    reg = spq.alloc_register("rk")
    spq.reg_load(reg, rk[0:1, 0:1])
    crk = spq.snap(reg, min_val=0, max_val=3)
    hn_loc_b, hn_all_b, a2a_loc_b, a2a_all_b, hsp_b, out_b = [k.buf(n_) for n_ in ("hn_loc", "hn_all", "a2a_loc", "a2a_all", "hsp", "outT")]
    g4 = a2a_all.rearrange("(j f) t -> j f t", j=4)

    def oT_loader(tile, tb, cs):
        src = g4[crk].rearrange("(fc p) t -> p fc t", p=128)
        k.dma("sp", tile[:], src[:, :, cs], r=[a2a_all_b], w=[tb])

    def gather_hn():
        for q in range(4):
            k.coll("AllGather", hn_loc[256 * q:256 * (q + 1), :], hn_all[1024 * q:1024 * (q + 1), :], groups, r=[hn_loc_b], w=[hn_all_b])

    def gather_o():
        for j in range(4):
            k.coll("AllGather", a2a_loc[j], a2a_all[1024 * j:1024 * (j + 1), :], groups, r=[a2a_loc_b], w=[a2a_all_b])

    wbf = [tail_weights_to_bf16(k, nc, str(i), tails[i]["w_out"], tails[i]["w1"], tails[i]["w2"], tails[i]["wg"], tails[i]["wp"])
           for i in range(2)]

    def tail(i, hT, hb):
        t = tails[i]
        phase_tail(k, cx, hT, hb, None, a2a_all_b, wbf[i], t["g_ffn"], t["g_ple"], pT[i], oT_loader=oT_loader)

    hv = hsp.rearrange("(fc p) t -> p fc t", p=128)
    with contextlib.ExitStack() as stA:
        hT = k.sb("hT", [128, FC, TOK], F32, stA)
        hb = k.bufs(4, "hT")
        xv = xT.rearrange("(fc p) t -> p fc t", p=128)
        for c in range(4):
            k.dma("sp", hT[:, :, c * 512:(c + 1) * 512], xv[:, :, c * 512:(c + 1) * 512], w=[hb[c]])
        with contextlib.ExitStack() as st:
            phase_norm_out(k, cx, hT, hb, g_mix[0], hn_loc, hn_loc_b, st)
            k.barrier()
        gather_hn()
        if CUT >= 2:
            with contextlib.ExitStack() as st:
                phase_sb(k, cx, hn_all, hn_all_b, sb_w, a2a_loc, a2a_loc_b, st)
                k.barrier()
            gather_o()
        if CUT >= 3:
            tail(0, hT, hb)
        with contextlib.ExitStack() as st:
            phase_norm_out(k, cx, hT, hb, g_mix[1], hn_loc, hn_loc_b, st)
            for c in range(4):
                k.dma("sp", hv[:, :, c * 512:(c + 1) * 512], hT[:, :, c * 512:(c + 1) * 512], r=[hb[c]], w=[hsp_b])
            k.barrier()
    if CUT >= 4:
        gather_hn()
    with contextlib.ExitStack() as stB:
      if CUT >= 5:
        cxb = load_consts(k, cd, nsa_sb, stB)
        for nm in nsa_sb:
            setattr(cx, nm, getattr(cxb, nm))
            setattr(cx, nm + "_b", getattr(cxb, nm + "_b"))
        phase_nsa(k, cx, hn_all, hn_all_b, nsa_w, rb, peT, cw1, cw2k, cw2v, bdz, a2a_loc, a2a_loc_b, stB)
        k.barrier()
    if CUT >= 5:
        gather_o()
    with contextlib.ExitStack() as stC:
        hT = k.sb("hT2", [128, FC, TOK], F32, stC)
        hb = k.bufs(4, "hT2")
        for c in range(4):
            k.dma("sp", hT[:, :, c * 512:(c + 1) * 512], hv[:, :, c * 512:(c + 1) * 512], r=[hsp_b], w=[hb[c]])
        if CUT >= 6:
            tail(1, hT, hb)
        with contextlib.ExitStack() as st:
            phase_final_norm(k, cx, hT, hb, g_fin, outT, out_b, st)
            k.barrier()
    k.barrier()
    k.close()
    return nc, {nm: consts[nm] for nm in small + nsa_sb + nsa_dr}


def fused_maps(inp):
    x, p = inp["x"], inp["p"]
    maps = []
    w_in = inp["sb_w_in"][0]
    for c in range(NCORES):
        b, r = c // 4, c % 4
        ts = slice(TOK * r, TOK * (r + 1))
        m = {"xT": _c(x[b, ts].T), "pT0": _c(p[0, b, ts].T), "pT1": _c(p[1, b, ts].T),
             "rk": np.array([[r, 0]], np.int32),
             "g_mix0": _c(inp["norm_mix"][0]), "g_mix1": _c(inp["norm_mix"][1]), "g_fin": _c(inp["final_norm"]),
             "sb_w": _c(np.concatenate([w_in[:, 256 * r:256 * (r + 1)], w_in[:, 1024 + 256 * r:1024 + 256 * (r + 1)],
                                        w_in[:, 2048 + 256 * r:2048 + 256 * (r + 1)]], axis=1))}
        for i in range(2):
            m["w_out_%d" % i] = _c((inp["sb_w_out"] if i == 0 else inp["nsa_w_out"])[0])
            m["g_ffn_%d" % i] = _c(inp["norm_ffn"][i])
            m["w1_%d" % i] = _c(inp["ffn_w_in"][i])
            m["w2_%d" % i] = _c(inp["ffn_w_out"][i])
            m["g_ple_%d" % i] = _c(inp["norm_ple"][i])
            m["wg_%d" % i] = _c(inp["ple_w_gate"][i])
            m["wp_%d" % i] = _c(inp["ple_w_proj"][i])
        m.update(nsa_core_inputs(inp, r))
        maps.append(m)
    return maps


def kernel(**inp):
    inp = {kk: np.asarray(v) for kk, v in inp.items()}
    res = _run("fused", build_prog_fused, fused_maps(inp))
    out = np.empty((2, S, D), np.float32)
    for c in range(NCORES):
        b, r = c // 4, c % 4
        out[b, TOK * r:TOK * (r + 1), :] = np.asarray(res[c]["outT"]).T
    return out
```

```python
import contextlib
import math
import numpy as np
import ml_dtypes
import concourse.bass as bass
import concourse.mybir as mybir
from concourse.alu_op_type import AluOpType as ALU
from concourse.bass_utils import run_bass_kernel_spmd

F32 = mybir.dt.float32
BF16 = mybir.dt.bfloat16
AF = mybir.ActivationFunctionType
NPBF = ml_dtypes.bfloat16

NCORES = 8
S = 8192
D = 1024
TOK = 2048
FC = 8
DFF = 2816
NJ = DFF // 128
EPS = 1e-6
NEG = -30000.0
NDC = 4352
DOFF = 2063


class Buf:
    __slots__ = ("name", "w", "r", "dsem", "dcnt", "dkey")

    def __init__(self, name):
        self.name = name
        self.w = None
        self.r = {}
        self.dsem = None
        self.dcnt = 0
        self.dkey = None


class Eng:
    def __init__(self, name, h, sem):
        self.name = name
        self.gen = 0
        self.key = ("e", name, 0)
        self.h = h
        self.sem = sem
        self.count = 0
        self.seen = {}


import os as _os
ATTACH_WAIT = _os.environ.get("KB_ATTACH", "1") == "1"
ATTACH_ENGS = tuple(_os.environ.get("KB_ATTACH_ENGS", "act,dve,pool,pe").split(","))
SEM_ROTATE = 6000


class KB:
    def __init__(self, nc):
        self.nc = nc
        self.stack = contextlib.ExitStack()
        self.engs = {}
        for key, h in (("pe", nc.tensor), ("act", nc.scalar), ("dve", nc.vector),
                       ("pool", nc.gpsimd), ("sp", nc.sync)):
            sem = self.stack.enter_context(nc.semaphore("s_" + key)) if key != "sp" else None
            self.engs[key] = Eng(key, h, sem)
        self.csem = self.stack.enter_context(nc.semaphore("s_coll"))
        self.hsem = self.stack.enter_context(nc.semaphore("s_hand"))
        self.hcnt = 0
        self.dma_bufs = []
        self.nbuf = 0
        self.free_dsems = {"hw": [], "sw": []}
        self.nsem = 0
        self.ccnt = 0

    def sb(self, name, shape, dtype, stack=None):
        self.nbuf += 1
        return (stack or self.stack).enter_context(self.nc.sbuf_tensor("S%d_%s" % (self.nbuf, name), list(shape), dtype))

    def ps(self, name, shape, dtype, stack=None):
        self.nbuf += 1
        return (stack or self.stack).enter_context(self.nc.psum_tensor("P%d_%s" % (self.nbuf, name), list(shape), dtype))

    def buf(self, name=None):
        self.nbuf += 1
        return Buf("%s_%d" % (name or "b", self.nbuf))

    def bufs(self, n, name=None):
        return [self.buf(name) for _ in range(n)]

    def _deps(self, E, r, w, attach=False):
        deps = {}

        def need(k, s, v):
            if k[0] == "e" and k[1] == "pe" and E.name == "pe":
                return
            if E.seen.get(k, 0) >= v:
                return
            if k not in deps or deps[k][1] < v:
                deps[k] = (s, v)

        for b in r:
            if b.w is not None:
                need(*b.w)
        for b in w:
            if b.w is not None:
                need(*b.w)
            for kk, (s, v) in b.r.items():
                need(kk, s, v)
        items = list(deps.items())
        carry = None
        if ATTACH_WAIT and attach and items:
            kk, (s, v) = items.pop()
            E.seen[kk] = v
            carry = (s, v)
        for kk, (s, v) in items:
            E.h.wait_ge(s, v)
            E.seen[kk] = v
        return carry

    @staticmethod
    def _mark(ev, r, w):
        kk, s, v = ev
        for b in r:
            b.r[kk] = (s, v)
        for b in w:
            b.w = ev
            b.r = {}

    def op(self, eng, fn, r=(), w=()):
        E = self.engs[eng]
        carry = self._deps(E, r, w, attach=(eng in ATTACH_ENGS))
        ins = fn(E.h)
        if carry is not None:
            ins._wait_ge(carry[0], carry[1])
        E.count += 1
        ins.then_inc(E.sem, 1)
        self._mark((E.key, E.sem, E.count), r, w)
        return ins

    def _dsem(self, sbuf, cls):
        if sbuf.dsem is None:
            sbuf.dsem = {}
        if cls not in sbuf.dsem:
            pool = self.free_dsems[cls]
            if pool:
                ent = pool.pop()
            else:
                self.nsem += 1
                sem = self.stack.enter_context(self.nc.semaphore("d%s%d" % (cls, self.nsem)))
                ent = [sem, ("d", self.nsem), 0]
            sbuf.dsem[cls] = ent
            self.dma_bufs.append((sbuf, cls))
        return sbuf.dsem[cls]

    def dma(self, q, out, in_, r=(), w=(), sem_buf=None, **kw):
        E = self.engs[q]
        self._deps(E, r, w)
        sbuf = sem_buf or (w[0] if w else r[0])
        ent = self._dsem(sbuf, "sw" if q == "pool" else "hw")
        ins = E.h.dma_start(out=out, in_=in_, **kw)
        ent[2] += 16
        ins.then_inc(ent[0], 16)
        self._mark((ent[1], ent[0], ent[2]), r, w)
        return ins

    def coll(self, kind, in_ap, out_ap, groups, r=(), w=()):
        E = self.engs["pool"]
        self._deps(E, r, w)
        ins = E.h.collective_compute(kind, ALU.bypass, replica_groups=groups, ins=[in_ap], outs=[out_ap])
        self.ccnt += 1
        ins.then_inc(self.csem, 1)
        self._mark((("c", 0), self.csem, self.ccnt), r, w)
        return ins

    def barrier(self, release=True):
        for E in self.engs.values():
            for Fg in self.engs.values():
                if Fg.count == 0:
                    continue
                if Fg is E and E.name == "pe":
                    continue
                if E.seen.get(Fg.key, 0) < Fg.count:
                    E.h.wait_ge(Fg.sem, Fg.count)
                    E.seen[Fg.key] = Fg.count
            for b, cls in self.dma_bufs:
                sem, dkey, cnt = b.dsem[cls]
                if cnt and E.seen.get(dkey, 0) < cnt:
                    E.h.wait_ge(sem, cnt)
                    E.seen[dkey] = cnt
            if self.ccnt and E.seen.get(("c", 0), 0) < self.ccnt:
                E.h.wait_ge(self.csem, self.ccnt)
                E.seen[("c", 0)] = self.ccnt
        if any(E.count > SEM_ROTATE for E in self.engs.values()):
            parts = list(self.engs.values())
            for rnd in range(2):
                self.hcnt += len(parts)
                for E in parts:
                    E.h.sem_inc(self.hsem, 1)
                for E in parts:
                    E.h.wait_ge(self.hsem, self.hcnt)
                if rnd == 0:
                    for E in parts:
                        if E.sem is not None and E.count > 0:
                            E.h.sem_clear(E.sem)
                            E.gen += 1
                            E.key = ("e", E.name, E.gen)
                            E.count = 0
        if release:
            for b, cls in self.dma_bufs:
                self.free_dsems[cls].append(b.dsem.pop(cls))
                b.w = None
                b.r = {}
            self.dma_bufs = []

    def close(self):
        self.stack.close()


def _rel_bucket_np(dist):
    n = np.maximum(dist, 0)
    nf = np.maximum(n, 1).astype(np.float32)
    large = 16 + (np.log(nf / np.float32(16)) / np.float32(math.log(128 / 16)) * np.float32(16)).astype(np.int32)
    large = np.minimum(large, 31)
    return np.where(n < 16, n, large)


def host_consts():
    c = {}
    i = np.arange(128)
    c["ident_f"] = np.eye(128, dtype=np.float32)
    c["ident_b"] = np.eye(128, dtype=np.float32).astype(NPBF)
    c["ones_b"] = np.ones((128, 128), np.float32).astype(NPBF)
    c["negu_b"] = (-(i[:, None] >= i[None, :]).astype(np.float32)).astype(NPBF)
    c["negones_b"] = (-np.ones((128, 2), np.float32)).astype(NPBF)
    c["mask_sb"] = (i[:, None] < i[None, :]).astype(np.float32).astype(NPBF)
    return c


class Ctx:
    pass


def load_consts(k, cdram, names, stack=None):
    cx = Ctx()
    for nm in names:
        ap = cdram[nm]
        t = k.sb("sc_" + nm, list(ap.shape), ap.dtype, stack)
        b = k.buf("c_" + nm)
        k.dma("sp", t[:], ap, w=[b])
        setattr(cx, nm, t)
        setattr(cx, nm + "_b", b)
    return cx


def rmsnorm_chunk(k, cx, W, h_aps, h_buf, gcol, gcol_b, out_aps, out_buf, n=512):
    sq, sqb, ps, psb, rs, rsb = W["sq"], W["sq_b"], W["ps_n"], W["ps_n_b"], W["rs"], W["rs_b"]
    for fc in range(FC):
        k.op("act", lambda e, fc=fc: e.activation(out=sq[:, fc, :n], in_=h_aps[fc], func=AF.Square),
             r=[h_buf], w=[sqb])
    for fc in range(FC):
        k.op("pe", lambda e, fc=fc: e.matmul(ps[:, :n], lhsT=cx.ones_b[:], rhs=sq[:, fc, :n],
                                              start=(fc == 0), stop=(fc == FC - 1)),
             r=[sqb, cx.ones_b_b], w=[psb])
    k.op("act", lambda e: e.activation(out=rs[:, :n], in_=ps[:, :n], func=AF.Sqrt, bias=cx.eps_col[:, 0:1], scale=1.0 / D),
         r=[psb, cx.eps_col_b], w=[rsb])
    k.op("dve", lambda e: e.reciprocal(out=rs[:, :n], in_=rs[:, :n]), r=[rsb], w=[rsb])
    for fc in range(FC):
        k.op("dve", lambda e, fc=fc: e.scalar_tensor_tensor(out=out_aps[fc], in0=h_aps[fc], scalar=gcol[:, fc:fc + 1],
                                                            in1=rs[:, :n], op0=ALU.mult, op1=ALU.mult),
             r=[h_buf, gcol_b, rsb], w=[out_buf])


def norm_work(k, stack=None):
    W = {}
    W["sq"] = k.sb("n_sq", [128, FC, 512], BF16, stack)
    W["sq_b"] = k.buf("n_sq")
    W["ps_n"] = k.ps("n_ps", [128, 512], F32, stack)
    W["ps_n_b"] = k.buf("n_ps")
    W["rs"] = k.sb("n_rs", [128, 512], F32, stack)
    W["rs_b"] = k.buf("n_rs")
    return W


def make_eps(k, cx, stack=None):
    cx.eps_col = k.sb("eps_col", [128, 1], F32, stack)
    cx.eps_col_b = k.buf("eps")
    k.op("pool", lambda e: e.memset(cx.eps_col[:], EPS), w=[cx.eps_col_b])


def load_gain_cols(k, gain_ap, name, stack=None):
    t = k.sb(name, [128, FC], F32, stack)
    b = k.buf(name)
    k.dma("sp", t[:], gain_ap.rearrange("(fc p) -> p fc", p=128), w=[b], allow_slow_non_contiguous=True)
    return t, b


def phase_norm_out(k, cx, hT, hb, gain_ap, dst, dst_b, stack):
    W = norm_work(k, stack)
    gcol, gcol_b = load_gain_cols(k, gain_ap, "g_mix", stack)
    hn = [k.sb("hn%d" % i, [128, FC, 512], BF16, stack) for i in range(2)]
    hnb = k.bufs(2, "hn")
    dv = dst.rearrange("(fc p) t -> p fc t", p=128)
    for c in range(4):
        cs = slice(c * 512, (c + 1) * 512)
        rmsnorm_chunk(k, cx, W, [hT[:, fc, cs] for fc in range(FC)], hb[c], gcol, gcol_b,
                      [hn[c % 2][:, fc, :] for fc in range(FC)], hnb[c % 2])
        k.dma("sp", dv[:, :, cs], hn[c % 2][:], r=[hnb[c % 2]], w=[dst_b])


def phase_sb(k, cx, hn_all, hn_all_b, w_ap, a2a, a2a_b, stack, nsb=16):
    scale = 0.125
    QT = [k.sb("QT%d" % i, [128, S], BF16, stack) for i in range(2)]
    KT = [k.sb("KT%d" % i, [128, S], BF16, stack) for i in range(2)]
    V = k.sb("V", [128, 64, 260], BF16, stack)
    qkb = k.buf("qkv")
    k.op("pool", lambda e: e.memset(V[:].rearrange("p b (h c) -> p b h c", c=65)[:, :, :, 64:65], 1.0), w=[qkb])
    psA2 = k.ps("psA2", [128, 1024], F32, stack)
    psB2 = k.ps("psB2", [128, 1024], F32, stack)
    psAA = [psA2, psB2]
    ps = [psA2[:, 0:512], psA2[:, 512:1024], psB2[:, 0:512], psB2[:, 512:1024]] + \
         [k.ps("ps%d" % i, [128, 512], F32, stack)[:, :] for i in range(4, 8)]
    psb = k.bufs(8, "ps")
    pst = contextlib.ExitStack()
    wq = k.sb("sb_w", [128, FC, 768], BF16, pst)
    wqb = k.buf("sb_w")
    wv = w_ap.rearrange("(fc p) n -> p fc n", p=128)
    for fc in range(FC):
        k.dma("pool", wq[:, fc, :], wv[:, fc, :], w=[wqb])
    hnc = [k.sb("hnc%d" % i, [128, FC, 512], BF16, pst) for i in range(2)]
    hncb = k.bufs(2, "hnc")
    for c in range(16):
        rho, off = c // 4, (c % 4) * 512
        src = hn_all.rearrange("(q r h p) t -> r p q h t", q=4, r=4, h=2, p=128)[rho][:, :, :, off:off + 512]
        hb_ = hncb[c % 2]
        for q_ in range(4):
            k.dma("sp", hnc[c % 2][:, 2 * q_:2 * q_ + 2, :], src[:, q_, :, :], r=[hn_all_b], w=[hb_])
        x = hnc[c % 2]
        cs = slice(c * 512, (c + 1) * 512)
        j = 0
        for which, dstT in ((0, QT), (1, KT)):
            for hp in range(2):
                p_, pb_ = ps[j % 4], psb[j % 4]
                j += 1
                for fc in range(FC):
                    k.op("pe", lambda e, fc=fc, p_=p_, which=which, hp=hp: e.matmul(
                        p_[:, :], lhsT=wq[:, fc, which * 256 + hp * 128: which * 256 + (hp + 1) * 128],
                        rhs=x[:, fc, :], start=(fc == 0), stop=(fc == FC - 1)), r=[wqb, hb_], w=[pb_])
                if which == 0:
                    k.op("act", lambda e, p_=p_, hp=hp: e.activation(out=QT[hp][:, cs], in_=p_[:, :], func=AF.Copy, scale=scale),
                         r=[pb_], w=[qkb])
                else:
                    k.op("dve", lambda e, p_=p_, hp=hp: e.tensor_copy(out=KT[hp][:, cs], in_=p_[:, :]), r=[pb_], w=[qkb])
        for tt in range(4):
            p_, pb_ = ps[4 + tt % 2], psb[4 + tt % 2]
            for fc in range(FC):
                k.op("pe", lambda e, fc=fc, p_=p_, tt=tt: e.matmul(
                    p_[:, 0:256], lhsT=x[:, fc, tt * 128:(tt + 1) * 128], rhs=wq[:, fc, 512:768],
                    start=(fc == 0), stop=(fc == FC - 1)), r=[wqb, hb_], w=[pb_])
            vdst = V[:, c * 4 + tt, :].rearrange("p (h c) -> p h c", c=65)[:, :, 0:64]
            vsrc = p_[:, 0:256].rearrange("p (h c) -> p h c", c=64)
            k.op("dve" if tt % 2 else "act",
                 (lambda e, vdst=vdst, vsrc=vsrc: e.tensor_copy(out=vdst, in_=vsrc)) if tt % 2 else
                 (lambda e, vdst=vdst, vsrc=vsrc: e.copy(out=vdst, in_=vsrc)),
                 r=[pb_], w=[qkb])
    k.barrier()
    pst.close()
    ps_free = ps
    pst2 = contextlib.ExitStack()
    e1 = [k.sb("e1_%d" % i, [128, 1024], F32, stack) for i in range(2)]
    e1b = k.bufs(2, "e1")
    sp = [k.sb("sp_%d" % i, [128, 1024], BF16, stack) for i in range(3)]
    spb = k.bufs(3, "sp")
    aa = [k.sb("aa_%d" % i, [128, 1024], BF16, stack) for i in range(2)]
    aab = k.bufs(2, "aa")
    acc = k.sb("acc", [128, 4, 256], F32, stack)
    accb = k.bufs(4, "acc")
    Ec = k.sb("Ec", [128, 4, 4], F32, stack)
    Ecb = k.bufs(4, "Ec")
    tE = k.sb("tE", [128, 4], F32, stack)
    tEb = k.buf("tE")
    oT = [k.sb("oT%d" % i, [128, 2, 512], BF16, stack) for i in range(2)]
    oTb = k.bufs(2, "oT")
    psO, psOb = ps[4:6], psb[4:6]
    psT, psTb = ps[6:8], psb[6:8]
    AAb = k.bufs(2, "psAA")
    steps = []
    for sbk in range(nsb):
        for hh in range(4):
            nu = 4 * sbk + 4
            chain = []
            for u in range(nu):
                kb = 4 * sbk + 3 - u
                chain.append(dict(kb=kb, i0=max(0, kb - 4 * sbk), diag=kb >= 4 * sbk))
            groups = [[c_] for c_ in chain[:4]] + [chain[i:i + 2] for i in range(4, nu, 2)]
            for gi, g_ in enumerate(groups):
                steps.append(dict(sbk=sbk, hh=hh, subs=g_, first=(gi == 0), last_sb=(hh == 3 and gi == len(groups) - 1)))
    for n_, St in enumerate(steps):
        St["n"] = n_
        hp, base = St["hh"] // 2, 64 * (St["hh"] % 2)
        for U in St["subs"]:
            U["N"] = 512 - 128 * U["i0"]
            U["qc"] = QT[hp][base:base + 64, 512 * St["sbk"] + 128 * U["i0"]: 512 * (St["sbk"] + 1)]
            U["kc"] = KT[hp][base:base + 64, 128 * U["kb"]:128 * (U["kb"] + 1)]
        St["W"] = 512 * (len(St["subs"]) - 1) + St["subs"][-1]["N"]
        St["diag"] = St["subs"][0]["diag"]

    def actv(out_t, in_banks, W, func, **kw):
        return lambda e: e.activation(out=out_t[:, :W], in_=in_banks[:, :W], func=func, **kw)

    def stage1a(St):
        n_, W = St["n"], St["W"]
        A_, Ab = psAA[n_ % 2], AAb[n_ % 2]
        for j, U in enumerate(St["subs"]):
            k.op("pe", lambda e, j=j, U=U: e.matmul(A_[:, j * 512:j * 512 + U["N"]], lhsT=U["kc"], rhs=U["qc"], start=True, stop=True), r=[qkb], w=[Ab])
        k.op("act", actv(e1[n_ % 2], A_, W, AF.Exp), r=[Ab], w=[e1b[n_ % 2]])

    def stage1b(St):
        n_, W = St["n"], St["W"]
        s_, sb_ = sp[n_ % 3], spb[n_ % 3]
        k.op("act", lambda e: e.activation(out=s_[:, :W], in_=e1[n_ % 2][:, :W], func=AF.Ln, bias=cx.one_col[:, 0:1], scale=1.0),
             r=[e1b[n_ % 2], cx.one_col_b], w=[sb_])
        if St["diag"]:
            k.op("pool", lambda e: e.tensor_tensor(out=s_[:, 0:128], in0=s_[:, 0:128], in1=cx.mask_sb[:], op=ALU.mult),
                 r=[sb_, cx.mask_sb_b], w=[sb_])

    def stage2(St):
        n_, W = St["n"], St["W"]
        s_, sb_ = sp[n_ % 3], spb[n_ % 3]
        a_, ab_ = aa[n_ % 2], aab[n_ % 2]
        A_, Ab = psAA[n_ % 2], AAb[n_ % 2]
        for j, U in enumerate(St["subs"]):
            N = U["N"]
            k.op("pe", lambda e, j=j, N=N: e.matmul(A_[:, j * 512:j * 512 + N], lhsT=cx.negu_b[:], rhs=s_[:, j * 512:j * 512 + N],
                                                    start=False, stop=True, skip_group_check=True),
                 r=[sb_, cx.negu_b_b, Ab], w=[Ab])
        k.op("act", actv(a_, A_, W, AF.Exp), r=[Ab], w=[ab_])
        if St["diag"]:
            k.op("pool", lambda e: e.tensor_tensor(out=a_[:, 0:128], in0=a_[:, 0:128], in1=cx.mask_sb[:], op=ALU.mult),
                 r=[ab_, cx.mask_sb_b], w=[ab_])

    def stage3(St):
        n_, hh, sbk = St["n"], St["hh"], St["sbk"]
        a_, ab_ = aa[n_ % 2], aab[n_ % 2]
        if St["first"]:
            k.op("pool", lambda e: e.memset(acc[:, :, hh * 64:(hh + 1) * 64], 0.0), w=[accb[hh]])
            k.op("pool", lambda e: e.memset(Ec[:, hh, :], 1.0), w=[Ecb[hh]])
        for j, U in enumerate(St["subs"]):
            i0, kb = U["i0"], U["kb"]
            O, Ob = psO[j], psOb[j]
            for i in range(i0, 4):
                cl = slice(j * 512 + (i - i0) * 128, j * 512 + (i - i0 + 1) * 128)
                k.op("pe", lambda e, i=i, cl=cl, O=O, kb=kb: e.matmul(O[:, i * 65:(i + 1) * 65], lhsT=a_[:, cl],
                                                                     rhs=V[:, kb, hh * 65:(hh + 1) * 65], start=True, stop=True),
                     r=[ab_, qkb], w=[Ob])
            for i in range(i0, 4):
                k.op("dve", lambda e, i=i, O=O: e.scalar_tensor_tensor(
                    out=acc[:, i, hh * 64:(hh + 1) * 64], in0=O[:, i * 65:i * 65 + 64], scalar=Ec[:, hh, i:i + 1],
                    in1=acc[:, i, hh * 64:(hh + 1) * 64], op0=ALU.mult, op1=ALU.add),
                    r=[Ob, Ecb[hh], accb[hh]], w=[accb[hh]])
            Ov = O[:, 0:260].rearrange("p (i c) -> p i c", c=65)
            k.op("dve", lambda e, Ov=Ov, i0=i0: e.tensor_tensor(out=tE[:, i0:4], in0=Ov[:, i0:4, 64], in1=Ec[:, hh, i0:4], op=ALU.mult),
                 r=[Ob, Ecb[hh]], w=[tEb])
            k.op("dve", lambda e, i0=i0: e.tensor_tensor(out=Ec[:, hh, i0:4], in0=Ec[:, hh, i0:4], in1=tE[:, i0:4], op=ALU.subtract),
                 r=[tEb, Ecb[hh]], w=[Ecb[hh]])
        if St["last_sb"]:
            y2 = sbk % 2
            for i in range(4):
                for fh in range(2):
                    T_, Tb_ = psT[(i * 2 + fh) % 2], psTb[(i * 2 + fh) % 2]
                    k.op("pe", lambda e, i=i, fh=fh, T_=T_: e.transpose(out=T_[:, 0:128], in_=acc[:, i, fh * 128:(fh + 1) * 128], identity=cx.ident_f[:]),
                         r=accb + [cx.ident_f_b], w=[Tb_])
                    k.op("dve", lambda e, i=i, fh=fh, T_=T_: e.tensor_copy(out=oT[y2][:, fh, i * 128:(i + 1) * 128], in_=T_[:, 0:128]),
                         r=[Tb_], w=[oTb[y2]])
            dest, off = sbk // 4, (sbk % 4) * 512
            k.dma("sp", a2a[dest].rearrange("(fh p) t -> p fh t", p=128)[:, :, off:off + 512], oT[y2][:], r=[oTb[y2]], w=[a2a_b])

    nst = len(steps)
    for it in range(nst + 2):
        if it < nst:
            stage1a(steps[it])
        if 0 <= it - 1 < nst:
            stage2(steps[it - 1])
        if it < nst:
            stage1b(steps[it])
        if 0 <= it - 2 < nst:
            stage3(steps[it - 2])
            if steps[it - 2]["last_sb"] and k.engs["pe"].count > SEM_ROTATE:
                k.barrier()


def make_one(k, cx, stack=None):
    cx.one_col = k.sb("one_col", [128, 1], F32, stack)
    cx.one_col_b = k.buf("one")
    k.op("pool", lambda e: e.memset(cx.one_col[:], 1.0), w=[cx.one_col_b])


def dram_consts(nc, consts, names):
    out = {}
    for nm in names:
        a = consts[nm]
        dt_ = BF16 if a.dtype == NPBF else F32
        out[nm] = nc.dram_tensor("c_" + nm, list(a.shape), dt_, kind="ExternalInput").ap()
    return out


def build_prog_norm0():
    nc = bass.Bass("TRN2", target_bir_lowering=False)
    consts = host_consts()
    names = ["ones_b"]
    cd = dram_consts(nc, consts, names)
    xT = nc.dram_tensor("xT", [D, TOK], F32, kind="ExternalInput").ap()
    gain = nc.dram_tensor("gain", [D], F32, kind="ExternalInput").ap()
    hn = nc.dram_tensor("hnT", [D, TOK], BF16, kind="ExternalOutput").ap()
    k = KB(nc)
    cx = load_consts(k, cd, names)
    make_eps(k, cx)
    hT = k.sb("hT", [128, FC, TOK], F32)
    hb = k.bufs(4, "hT")
    xv = xT.rearrange("(fc p) t -> p fc t", p=128)
    for c in range(4):
        k.dma("sp", hT[:, :, c * 512:(c + 1) * 512], xv[:, :, c * 512:(c + 1) * 512], w=[hb[c]])
    hn_b = k.buf("hn_dram")
    phase_norm_out(k, cx, hT, hb, gain, hn, hn_b, k.stack)
    k.barrier()
    k.close()
    return nc, {nm: consts[nm] for nm in names}


def build_prog_sb(nsb=16):
    nc = bass.Bass("TRN2", target_bir_lowering=False)
    consts = host_consts()
    names = ["ident_f", "negu_b", "negones_b", "mask_sb"]
    cd = dram_consts(nc, consts, names)
    hn_all = nc.dram_tensor("hn_all", [4 * D, TOK], BF16, kind="ExternalInput").ap()
    w = nc.dram_tensor("sb_w", [D, 768], F32, kind="ExternalInput").ap()
    a2a = nc.dram_tensor("a2a", [4, 256, TOK], BF16, kind="ExternalOutput").ap()
    k = KB(nc)
    cx = load_consts(k, cd, names)
    make_one(k, cx)
    phase_sb(k, cx, hn_all, k.buf("hn_all"), w, a2a, k.buf("a2a"), k.stack, nsb=nsb)
    k.barrier()
    k.close()
    return nc, {nm: consts[nm] for nm in names}


def load_w_cast(k, name, src_view, shape, stack, nsplit=None):
    t = k.sb(name, shape, BF16, stack)
    b = k.buf(name)
    a = shape[1]
    for i in range(a):
        k.dma("pool", t[:, i, :], src_view[:, i, :], w=[b])
    return t, b


def tail_weights_to_bf16(k, nc, tag, w_out_ap, w1_ap, w2_ap, wg_ap, wp_ap):
    out = {}
    for nm, ap in (("w_out", w_out_ap), ("w1", w1_ap), ("w2", w2_ap), ("wg", wg_ap), ("wp", wp_ap)):
        rows, cols = ap.shape
        dst = nc.dram_tensor("wb_%s_%s" % (nm, tag), [rows, cols], BF16, kind="Internal").ap()
        b = k.buf("wb_" + nm)
        for r0 in range(0, rows, 256):
            r1 = min(rows, r0 + 256)
            k.dma("pool", dst[r0:r1, :], ap[r0:r1, :], w=[b])
        out[nm] = (dst, b)
    return out


def load_w_bf16(k, name, src_view, src_b, shape, stack):
    t = k.sb(name, shape, BF16, stack)
    b = k.buf(name)
    k.dma("sp", t[:], src_view, r=[src_b], w=[b])
    return t, b


def phase_tail(k, cx, hT, hb, oT_dram, oT_b, wb, g_ffn_ap, g_ple_ap, pT_ap, oT_loader=None):
    ps_names = ["tp%d" % i for i in range(6)]
    with contextlib.ExitStack() as st0:
        ps = [k.ps(n_, [128, 512], F32, st0) for n_ in ps_names]
        psb = k.bufs(6, "tp")
        W = norm_work(k, st0)
        with contextlib.ExitStack() as st:
            wo, wob = load_w_bf16(k, "wo", wb["w_out"][0].rearrange("(fc p) n -> p fc n", p=128), wb["w_out"][1], [128, FC, D], st)
            oc = [k.sb("oc%d" % i, [128, FC, 512], BF16, st) for i in range(2)]
            ocb = k.bufs(2, "oc")
            ov = oT_dram.rearrange("(fc p) t -> p fc t", p=128) if oT_loader is None else None
            n_ = 0
            for c in range(4):
                cs = slice(c * 512, (c + 1) * 512)
                if oT_loader is None:
                    k.dma("sp", oc[c % 2][:], ov[:, :, cs], r=[oT_b], w=[ocb[c % 2]])
                else:
                    oT_loader(oc[c % 2], ocb[c % 2], cs)
                for of in range(FC):
                    p_, pb_ = ps[n_ % 4], psb[n_ % 4]
                    n_ += 1
                    for fc in range(FC):
                        k.op("pe", lambda e, fc=fc, of=of, p_=p_, c=c: e.matmul(
                            p_[:, :], lhsT=wo[:, fc, of * 128:(of + 1) * 128], rhs=oc[c % 2][:, fc, :],
                            start=(fc == 0), stop=(fc == FC - 1)), r=[wob, ocb[c % 2]], w=[pb_])
                    k.op("dve", lambda e, of=of, p_=p_, cs=cs: e.tensor_tensor(out=hT[:, of, cs], in0=hT[:, of, cs], in1=p_[:, :], op=ALU.add),
                         r=[pb_, hb[c]], w=[hb[c]])
            k.barrier()
        with contextlib.ExitStack() as st:
            gcol, gcol_b = load_gain_cols(k, g_ffn_ap, "g_ffn", st)
            hn = k.sb("f_hn", [128, FC, 1024], BF16, st)
            hnb = k.bufs(2, "f_hn")
            gT = k.sb("f_gT", [128, NJ, 1024], BF16, st)
            gTb = k.buf("f_gT")
            wab = [k.sb("f_wab%d" % i, [128, FC, 2, 512], BF16, st) for i in range(2)]
            wabb = k.bufs(2, "f_wab")
            w2t = [k.sb("f_w2%d" % i, [128, NJ, 256], BF16, st) for i in range(2)]
            w2b = k.bufs(2, "f_w2")
            sl = [k.sb("f_sl%d" % i, [128, 512], F32, st) for i in range(2)]
            slb = k.bufs(2, "f_sl")
            w1v = wb["w1"][0].rearrange("(fc p) n -> p fc n", p=128)
            w2v = wb["w2"][0].rearrange("(j p) n -> p j n", p=128)
            jgroups = [(j0, min(4, NJ - j0)) for j0 in range(0, NJ, 4)]
            n_ = 0
            nw = 0
            nw2 = 0
            for tc in range(2):
                for hf in range(2):
                    c = tc * 2 + hf
                    cs = slice(c * 512, (c + 1) * 512)
                    rmsnorm_chunk(k, cx, W, [hT[:, fc, cs] for fc in range(FC)], hb[c], gcol, gcol_b,
                                  [hn[:, fc, hf * 512:(hf + 1) * 512] for fc in range(FC)], hnb[hf])
                for (j0, gs) in jgroups:
                    wt, wtb = wab[nw % 2], wabb[nw % 2]
                    nw += 1
                    k.dma("sp", wt[:, :, 0, 0:gs * 128], w1v[:, :, j0 * 128:(j0 + gs) * 128], r=[wb["w1"][1]], w=[wtb])
                    k.dma("sp", wt[:, :, 1, 0:gs * 128], w1v[:, :, DFF + j0 * 128:DFF + (j0 + gs) * 128], r=[wb["w1"][1]], w=[wtb])
                    for jj in range(gs):
                        j = j0 + jj
                        js = slice(jj * 128, (jj + 1) * 128)
                        for hf in range(2):
                            hs = slice(hf * 512, (hf + 1) * 512)
                            pa, pab = ps[n_ % 2], psb[n_ % 2]
                            pb2, pbb = ps[2 + n_ % 2], psb[2 + n_ % 2]
                            s_, sb_ = sl[n_ % 2], slb[n_ % 2]
                            n_ += 1
                            for fc in range(FC):
                                k.op("pe", lambda e, fc=fc, pa=pa, wt=wt, hs=hs, js=js: e.matmul(pa[:, :], lhsT=wt[:, fc, 0, js], rhs=hn[:, fc, hs],
                                                                                                 start=(fc == 0), stop=(fc == FC - 1)), r=[wtb, hnb[hf]], w=[pab])
                            for fc in range(FC):
                                k.op("pe", lambda e, fc=fc, pb2=pb2, wt=wt, hs=hs, js=js: e.matmul(pb2[:, :], lhsT=wt[:, fc, 1, js], rhs=hn[:, fc, hs],
                                                                                                   start=(fc == 0), stop=(fc == FC - 1)), r=[wtb, hnb[hf]], w=[pbb])
                            k.op("act", lambda e, pa=pa, s_=s_: e.activation(out=s_[:, :], in_=pa[:, :], func=AF.Silu), r=[pab], w=[sb_])
                            k.op("dve", lambda e, pb2=pb2, s_=s_, j=j, hs=hs: e.tensor_tensor(out=gT[:, j, hs], in0=s_[:, :], in1=pb2[:, :], op=ALU.mult),
                                 r=[sb_, pbb], w=[gTb])
                for op_ in range(FC // 2):
                    wt, wtb = w2t[nw2 % 2], w2b[nw2 % 2]
                    nw2 += 1
                    k.dma("sp", wt[:], w2v[:, :, op_ * 256:(op_ + 1) * 256], r=[wb["w2"][1]], w=[wtb])
                    for o2 in range(2):
                        of = op_ * 2 + o2
                        for hf in range(2):
                            c = tc * 2 + hf
                            cs = slice(c * 512, (c + 1) * 512)
                            p_, pb_ = ps[4 + n_ % 2], psb[4 + n_ % 2]
                            n_ += 1
                            for j in range(NJ):
                                k.op("pe", lambda e, j=j, p_=p_, wt=wt, hf=hf, o2=o2: e.matmul(p_[:, :], lhsT=wt[:, j, o2 * 128:(o2 + 1) * 128], rhs=gT[:, j, hf * 512:(hf + 1) * 512],
                                                                                               start=(j == 0), stop=(j == NJ - 1)), r=[wtb, gTb], w=[pb_])
                            k.op("dve", lambda e, of=of, p_=p_, cs=cs: e.tensor_tensor(out=hT[:, of, cs], in0=hT[:, of, cs], in1=p_[:, :], op=ALU.add),
                                 r=[pb_, hb[c]], w=[hb[c]])
            k.barrier()
        with contextlib.ExitStack() as st:
            gcol, gcol_b = load_gain_cols(k, g_ple_ap, "g_ple", st)
            wg, wgb = load_w_bf16(k, "wg", wb["wg"][0].rearrange("(fc p) n -> p fc n", p=128), wb["wg"][1], [128, FC, D], st)
            wp, wpb = load_w_bf16(k, "wp", wb["wp"][0].rearrange("(kc p) n -> p kc n", p=128), wb["wp"][1], [128, 2, D], st)
            hn = [k.sb("p_hn%d" % i, [128, FC, 512], BF16, st) for i in range(2)]
            hnb = k.bufs(2, "p_hn")
            pt = [k.sb("p_pt%d" % i, [128, 2, 512], BF16, st) for i in range(2)]
            ptb = k.bufs(2, "p_pt")
            sg = [k.sb("p_sg%d" % i, [128, 512], F32, st) for i in range(2)]
            sgb = k.bufs(2, "p_sg")
            pv = pT_ap.rearrange("(kc p) t -> p kc t", p=128)
            n_ = 0
            for c in range(4):
                cs = slice(c * 512, (c + 1) * 512)
                x_, xb_ = hn[c % 2], hnb[c % 2]
                rmsnorm_chunk(k, cx, W, [hT[:, fc, cs] for fc in range(FC)], hb[c], gcol, gcol_b,
                              [x_[:, fc, :] for fc in range(FC)], xb_)
                q_, qb_ = pt[c % 2], ptb[c % 2]
                for kc in range(2):
                    k.dma("pool", q_[:, kc, :], pv[:, kc, cs], w=[qb_])
                for of in range(FC):
                    pg, pgb = ps[n_ % 2], psb[n_ % 2]
                    pp, ppb = ps[2 + n_ % 2], psb[2 + n_ % 2]
                    s_, sb_ = sg[n_ % 2], sgb[n_ % 2]
                    n_ += 1
                    for fc in range(FC):
                        k.op("pe", lambda e, fc=fc, of=of, pg=pg, x_=x_: e.matmul(pg[:, :], lhsT=wg[:, fc, of * 128:(of + 1) * 128], rhs=x_[:, fc, :],
                                                                                  start=(fc == 0), stop=(fc == FC - 1)), r=[wgb, xb_], w=[pgb])
                    for kc in range(2):
                        k.op("pe", lambda e, kc=kc, of=of, pp=pp, q_=q_: e.matmul(pp[:, :], lhsT=wp[:, kc, of * 128:(of + 1) * 128], rhs=q_[:, kc, :],
                                                                                  start=(kc == 0), stop=(kc == 1)), r=[wpb, qb_], w=[ppb])
                    k.op("act", lambda e, pg=pg, s_=s_: e.activation(out=s_[:, :], in_=pg[:, :], func=AF.Sigmoid), r=[pgb], w=[sb_])
                    k.op("dve", lambda e, pp=pp, s_=s_: e.tensor_tensor(out=s_[:, :], in0=s_[:, :], in1=pp[:, :], op=ALU.mult),
                         r=[sb_, ppb], w=[sb_])
                    k.op("pool", lambda e, of=of, s_=s_, cs=cs: e.tensor_tensor(out=hT[:, of, cs], in0=hT[:, of, cs], in1=s_[:, :], op=ALU.add),
                         r=[sb_, hb[c]], w=[hb[c]])
            k.barrier()


def phase_final_norm(k, cx, hT, hb, gain_ap, dst, dst_b, stack):
    W = norm_work(k, stack)
    gcol, gcol_b = load_gain_cols(k, gain_ap, "g_fin", stack)
    on = [k.sb("fn%d" % i, [128, FC, 512], F32, stack) for i in range(2)]
    onb = k.bufs(2, "fn")
    dv = dst.rearrange("(fc p) t -> p fc t", p=128)
    for c in range(4):
        cs = slice(c * 512, (c + 1) * 512)
        rmsnorm_chunk(k, cx, W, [hT[:, fc, cs] for fc in range(FC)], hb[c], gcol, gcol_b,
                      [on[c % 2][:, fc, :] for fc in range(FC)], onb[c % 2])
        k.dma("sp", dv[:, :, cs], on[c % 2][:], r=[onb[c % 2]], w=[dst_b])


def build_prog_tail(final):
    nc = bass.Bass("TRN2", target_bir_lowering=False)
    consts = host_consts()
    names = ["ones_b"]
    cd = dram_consts(nc, consts, names)

    def din(nm, shape, dt_=F32):
        return nc.dram_tensor(nm, shape, dt_, kind="ExternalInput").ap()
    hin = din("hT_in", [D, TOK])
    oT = din("oT", [D, TOK], BF16)
    w_out = din("w_out", [D, D])
    g_ffn = din("g_ffn", [D])
    w1 = din("w1", [D, 2 * DFF])
    w2 = din("w2", [DFF, D])
    g_ple = din("g_ple", [D])
    wg = din("wg", [D, D])
    wp = din("wp", [256, D])
    pT = din("pT", [256, TOK])
    g_next = din("g_next", [D])
    k = KB(nc)
    cx = load_consts(k, cd, names)
    make_eps(k, cx)
    hT = k.sb("hT", [128, FC, TOK], F32)
    hb = k.bufs(4, "hT")
    xv = hin.rearrange("(fc p) t -> p fc t", p=128)
    for c in range(4):
        k.dma("sp", hT[:, :, c * 512:(c + 1) * 512], xv[:, :, c * 512:(c + 1) * 512], w=[hb[c]])
    wbf = tail_weights_to_bf16(k, nc, "u", w_out, w1, w2, wg, wp)
    phase_tail(k, cx, hT, hb, oT, k.buf("oT"), wbf, g_ffn, g_ple, pT)
    with contextlib.ExitStack() as st:
        if final:
            out = nc.dram_tensor("outT", [D, TOK], F32, kind="ExternalOutput").ap()
            phase_final_norm(k, cx, hT, hb, g_next, out, k.buf("outT"), st)
        else:
            hout = nc.dram_tensor("hT_out", [D, TOK], F32, kind="ExternalOutput").ap()
            hn = nc.dram_tensor("hnT", [D, TOK], BF16, kind="ExternalOutput").ap()
            hob = k.buf("hT_out")
            ov = hout.rearrange("(fc p) t -> p fc t", p=128)
            for c in range(4):
                k.dma("sp", ov[:, :, c * 512:(c + 1) * 512], hT[:, :, c * 512:(c + 1) * 512], r=[hb[c]], w=[hob])
            phase_norm_out(k, cx, hT, hb, g_next, hn, k.buf("hn_dram"), st)
        k.barrier()
    k.close()
    return nc, {nm: consts[nm] for nm in names}


def nsa_host_consts():
    c = {}
    i = np.arange(128)
    c["ident_f"] = np.eye(128, dtype=np.float32)
    c["jflip"] = np.ascontiguousarray(np.eye(128, dtype=np.float32)[::-1])
    dist = np.arange(NDC) - DOFF
    oh = np.zeros((33, NDC), np.float32)
    bk = _rel_bucket_np(dist)
    for ii in range(NDC):
        if dist[ii] < 0:
            oh[32, ii] = 1.0
        else:
            oh[bk[ii], ii] = 1.0
    c["ohc"] = oh
    dm = np.zeros((33, 33), np.float32)
    for b in range(32):
        dm[b, b] += 1.0
        dm[31, b] -= 1.0
    dm[32, 32] = NEG
    c["dm"] = dm
    keys = np.arange(S)
    c["ex"] = (np.arange(128)[:, None] == (keys[None, :] // 64)).astype(np.float32).astype(NPBF)
    c["expat"] = (np.arange(64)[:, None] == ((keys[None, :] // 64) % 64)).astype(np.float32).astype(NPBF)
    c["ident_b"] = np.eye(128, dtype=np.float32).astype(NPBF)
    t4 = np.where(i[None, :] < i[:, None], 0.0, NEG).astype(np.float32)
    c["t4"] = np.ascontiguousarray(np.tile(t4, (1, 4)))
    n = np.arange(512)
    cs_, ss_ = n * 16, np.arange(128) * 64
    ov = ((cs_[:, None] < ss_[None, :] + 64) & (cs_[:, None] + 32 > ss_[None, :])).astype(np.float32)
    ov[511, :] = 0.0
    c["ov"] = ov.astype(NPBF)
    q = np.arange(128)
    cq = (q >= 64).astype(np.int64)[:, None]
    rel = (np.arange(256) - 128)[None, :]
    forced = (rel == cq) | (rel == cq - 1)
    causal = rel <= cq
    c["mc_rel"] = (causal & ~forced).astype(np.float32)
    c["ma_rel"] = np.where(forced, 100.0, np.where(causal, 0.0, -1.0)).astype(np.float32)
    return c


GELU_C = 1.5957691216057308


def phase_nsa(k, cx, hn_all, hn_all_b, w_ap, rb_ap, peT_ap, w1_ap, w2k_ap, w2v_ap, bdz, a2a, a2a_b, stack, nqt=64, stop=99):
    scale = 0.125
    ps = [k.ps("np%d" % i, [128, 512], F32, stack) for i in range(8)]
    psb = k.bufs(8, "np")
    QT = [k.sb("nQT%d" % i, [128, S], BF16, stack) for i in range(2)]
    KST = k.sb("nKST", [128, S], BF16, stack)
    KWT = k.sb("nKWT", [128, S], BF16, stack)
    VS = k.sb("nVS", [128, 64, 65], BF16, stack)
    VW = k.sb("nVW", [128, 64, 65], BF16, stack)
    G = k.sb("nG", [128, 64, 12], F32, stack)
    kcT = k.sb("nkcT", [128, 512], BF16, stack)
    VCX = k.sb("nVCX", [128, 4, 193], BF16, stack)
    T0 = k.sb("nT0", [128, 512], F32, stack)
    T1 = k.sb("nT1", [128, 512], F32, stack)
    pb = k.buf("nsa_persist")
    k.op("pool", lambda e: e.memset(VS[:, :, 64:65], 1.0), w=[pb])
    k.op("pool", lambda e: e.memset(VW[:, :, 64:65], 1.0), w=[pb])
    k.op("pool", lambda e: e.memset(VCX[:], 0.0), w=[pb])
    k.op("pool", lambda e: e.memset(VCX[:, :, 64:65], 1.0), w=[pb])
    k.op("pool", lambda e: e.memset(kcT[:], 0.0), w=[pb])
    k.dma("sp", VCX[:, :, 65:193], cx.ov_dram.rearrange("(nb p) s -> p nb s", p=128), w=[pb])
    with contextlib.ExitStack() as st:
        rbe = k.sb("rbe", [33, 4], F32, st)
        rbeb = k.buf("rbe")
        k.op("pool", lambda e: e.memset(rbe[32:33, :], 1.0), w=[rbeb])
        k.dma("sp", rbe[0:32, :], rb_ap, w=[rbeb])
        ohc = k.sb("ohc", [33, NDC], F32, st)
        ohcb = k.buf("ohc")
        k.dma("sp", ohc[:], cx.ohc_dram, w=[ohcb])
        dm = k.sb("dm", [33, 33], F32, st)
        dmb = k.buf("dm")
        k.dma("sp", dm[:], cx.dm_dram, w=[dmb])
        rbx = k.sb("rbx", [33, 4], F32, st)
        rbxb = k.buf("rbx")
        k.op("pe", lambda e: e.matmul(ps[0][0:33, 0:4], lhsT=dm[:], rhs=rbe[:], start=True, stop=True), r=[dmb, rbeb], w=[psb[0]])
        k.op("dve", lambda e: e.tensor_copy(out=rbx[:], in_=ps[0][0:33, 0:4]), r=[psb[0]], w=[rbxb])
        bds = k.sb("bds", [4, NDC], F32, st)
        bdsb = k.buf("bds")
        nchunk = (NDC + 511) // 512
        for ci in range(nchunk):
            lo, hi = ci * 512, min(NDC, (ci + 1) * 512)
            p_, pb_ = ps[1 + ci % 2], psb[1 + ci % 2]
            k.op("pe", lambda e, p_=p_, lo=lo, hi=hi: e.matmul(p_[0:4, 0:hi - lo], lhsT=rbx[:], rhs=ohc[:, lo:hi], start=True, stop=True),
                 r=[rbxb, ohcb], w=[pb_])
            k.op("dve", lambda e, p_=p_, lo=lo, hi=hi: e.tensor_copy(out=bds[:, lo:hi], in_=p_[0:4, 0:hi - lo]), r=[pb_], w=[bdsb])
        bdzb = k.buf("bdz")
        k.dma("sp", bdz, bds[:], r=[bdsb], w=[bdzb])
        cx.bdz_b = bdzb
        U = k.sb("U", [128, 512], F32, st)
        Ub = k.buf("U")
        for m, Tm in ((0, T0), (1, T1)):
            src = bass.AP(bdz.tensor, DOFF - 127 + 128 * m, [[1, 128], [NDC, 4], [1, 128]])
            k.dma("sp", U[:].rearrange("p (g q) -> p g q", g=4), src, r=[bdzb], w=[Ub])
            k.op("pe", lambda e: e.matmul(ps[3][:, :], lhsT=cx.jflip[:], rhs=U[:], start=True, stop=True), r=[Ub, cx.jflip_b], w=[psb[3]])
            k.op("dve", lambda e, Tm=Tm: e.tensor_copy(out=Tm[:], in_=ps[3][:, :]), r=[psb[3]], w=[pb])
        k.barrier()
    if stop <= 0:
        return
    with contextlib.ExitStack() as st:
        w = k.sb("nw", [128, FC, 780], BF16, st)
        wb = k.buf("nw")
        wv = w_ap.rearrange("(fc p) n -> p fc n", p=128)
        for fc in range(FC):
            k.dma("pool", w[:, fc, :], wv[:, fc, :], w=[wb])
        KAT = k.sb("nKAT", [128, S], BF16, st)
        hnc = [k.sb("nhnc%d" % i, [128, FC, 512], BF16, st) for i in range(2)]
        hncb = k.bufs(2, "nhnc")
        for c in range(16):
            rho, off = c // 4, (c % 4) * 512
            src = hn_all.rearrange("(q r h p) t -> r p q h t", q=4, r=4, h=2, p=128)[rho][:, :, :, off:off + 512]
            hb_ = hncb[c % 2]
            for q_ in range(4):
                k.dma("sp", hnc[c % 2][:, 2 * q_:2 * q_ + 2, :], src[:, q_, :, :], r=[hn_all_b], w=[hb_])
            x = hnc[c % 2]
            cs = slice(c * 512, (c + 1) * 512)
            for j, (c0, dst, sc) in enumerate(((0, QT[0], scale), (128, QT[1], scale), (256, KAT, None), (384, KST, None), (512, KWT, None))):
                p_, pb_ = ps[j % 4], psb[j % 4]
                for fc in range(FC):
                    k.op("pe", lambda e, fc=fc, p_=p_, c0=c0: e.matmul(p_[:, :], lhsT=w[:, fc, c0:c0 + 128], rhs=x[:, fc, :],
                                                                         start=(fc == 0), stop=(fc == FC - 1)), r=[wb, hb_], w=[pb_])
                if sc is not None:
                    k.op("act", lambda e, p_=p_, dst=dst, sc=sc: e.activation(out=dst[:, cs], in_=p_[:, :], func=AF.Copy, scale=sc), r=[pb_], w=[pb])
                else:
                    k.op("dve", lambda e, p_=p_, dst=dst: e.tensor_copy(out=dst[:, cs], in_=p_[:, :]), r=[pb_], w=[pb])
            for tt in range(4):
                p_, pb_ = ps[4 + tt % 2], psb[4 + tt % 2]
                kb = c * 4 + tt
                for fc in range(FC):
                    k.op("pe", lambda e, fc=fc, p_=p_, tt=tt: e.matmul(p_[:, 0:140], lhsT=x[:, fc, tt * 128:(tt + 1) * 128], rhs=w[:, fc, 640:780],
                                                                         start=(fc == 0), stop=(fc == FC - 1)), r=[wb, hb_], w=[pb_])
                k.op("dve", lambda e, p_=p_, kb=kb: e.tensor_copy(out=VS[:, kb, 0:64], in_=p_[:, 0:64]), r=[pb_], w=[pb])
                k.op("dve", lambda e, p_=p_, kb=kb: e.tensor_copy(out=VW[:, kb, 0:64], in_=p_[:, 64:128]), r=[pb_], w=[pb])
                k.op("act", lambda e, p_=p_, kb=kb: e.activation(out=G[:, kb, :], in_=p_[:, 128:140], func=AF.Sigmoid), r=[pb_], w=[pb])
        k.dma("sp", KST[64:128, :], cx.expat_dram, w=[pb])
        if stop <= 1:
            k.barrier()
            return
        w1 = k.sb("nw1", [128, 32, 256], BF16, st)
        w1b = k.buf("nw1")
        for l in range(32):
            k.dma("pool", w1[:, l, :], w1_ap[:, l, :], w=[w1b])
        w2kd = k.sb("nw2k", [128, 2, 128], BF16, st)
        w2v = k.sb("nw2v", [128, 2, 64], BF16, st)
        w2b = k.buf("nw2")
        w2kv_ = w2k_ap.rearrange("(hc p) d -> p hc d", p=128)
        for hc in range(2):
            k.dma("pool", w2kd[:, hc, 0:64], w2kv_[:, hc, :], w=[w2b])
            k.dma("pool", w2kd[:, hc, 64:128], w2kv_[:, hc, :], w=[w2b])
            k.dma("pool", w2v[:, hc, :], w2v_ap.rearrange("(hc p) d -> p hc d", p=128)[:, hc, :], w=[w2b])
        peT = k.sb("npeT", [128, 32], BF16, st)
        peb = k.buf("npeT")
        k.dma("pool", peT[:], peT_ap, w=[peb])
        cb = k.sb("ncb", [128, 4], F32, st)
        cbb = k.buf("ncb")
        hid = k.sb("nhid", [128, 4, 512], BF16, st)
        hidb = k.buf("nhid")
        xs = k.sb("nxs", [128, 512], F32, st)
        x2 = k.sb("nx2", [128, 512], F32, st)
        xsb, x2b = k.buf("nxs"), k.buf("nx2")
        for kv in range(2):
            b0 = 64 * kv
            for hc in range(2):
                idx = kv * 2 + hc
                p_, pb_ = ps[idx % 2], psb[idx % 2]
                pc, pcb = ps[2 + idx % 2], psb[2 + idx % 2]
                for l in range(32):
                    k.op("pe", lambda e, l=l, pc=pc: e.matmul(pc[:, 0:1], lhsT=w1[b0:b0 + 64, l, hc * 128:(hc + 1) * 128], rhs=peT[b0:b0 + 64, l:l + 1],
                                                                start=(l == 0), stop=(l == 31)), r=[w1b, peb], w=[pcb])
                k.op("dve", lambda e, pc=pc, idx=idx: e.tensor_copy(out=cb[:, idx:idx + 1], in_=pc[:, 0:1]), r=[pcb], w=[cbb])
                for l in range(32):
                    k.op("pe", lambda e, l=l, p_=p_: e.matmul(p_[:, 0:511], lhsT=w1[b0:b0 + 64, l, hc * 128:(hc + 1) * 128],
                                                                rhs=KAT[b0:b0 + 64, l:l + 8161:16], start=(l == 0), stop=(l == 31)), r=[w1b, pb], w=[pb_])
                k.op("act", lambda e, p_=p_, idx=idx: e.activation(out=xs[:, 0:511], in_=p_[:, 0:511], func=AF.Identity, bias=cb[:, idx:idx + 1], scale=1.0),
                     r=[pb_, cbb], w=[xsb])
                k.op("act", lambda e, p_=p_, idx=idx: e.activation(out=x2[:, 0:511], in_=p_[:, 0:511], func=AF.Square, bias=cb[:, idx:idx + 1], scale=1.0),
                     r=[pb_, cbb], w=[x2b])
                k.op("dve", lambda e: e.tensor_scalar(out=x2[:, 0:511], in0=x2[:, 0:511], scalar1=0.044715, scalar2=1.0, op0=ALU.mult, op1=ALU.add),
                     r=[x2b], w=[x2b])
                k.op("dve", lambda e: e.tensor_tensor(out=x2[:, 0:511], in0=x2[:, 0:511], in1=xs[:, 0:511], op=ALU.mult), r=[x2b, xsb], w=[x2b])
                k.op("act", lambda e: e.activation(out=x2[:, 0:511], in_=x2[:, 0:511], func=AF.Sigmoid, scale=GELU_C), r=[x2b], w=[x2b])
                k.op("dve", lambda e, idx=idx: e.tensor_tensor(out=hid[:, idx, 0:511], in0=x2[:, 0:511], in1=xs[:, 0:511], op=ALU.mult),
                     r=[x2b, xsb], w=[hidb])
        for hc in range(2):
            k.op("pe", lambda e, hc=hc: e.matmul(ps[4][:, 0:511], lhsT=w2kd[:, hc, :], rhs=hid[:, hc, 0:511], start=(hc == 0), stop=(hc == 1)),
                 r=[w2b, hidb], w=[psb[4]])
        k.op("dve", lambda e: e.tensor_copy(out=kcT[:, 0:511], in_=ps[4][:, 0:511]), r=[psb[4]], w=[pb])
        for nb in range(4):
            M = 128 if nb < 3 else 127
            p_, pb_ = ps[5 + nb % 2], psb[5 + nb % 2]
            for hc in range(2):
                k.op("pe", lambda e, hc=hc, nb=nb, M=M, p_=p_: e.matmul(p_[0:M, 0:64], lhsT=hid[:, 2 + hc, nb * 128:nb * 128 + M], rhs=w2v[:, hc, :],
                                                                          start=(hc == 0), stop=(hc == 1)), r=[w2b, hidb], w=[pb_])
            k.op("dve", lambda e, nb=nb, M=M, p_=p_: e.tensor_copy(out=VCX[0:M, nb, 0:64], in_=p_[0:M, 0:64]), r=[pb_], w=[pb])
        k.barrier()
    if stop <= 2:
        return
    with contextlib.ExitStack() as st:
        psZ, psZb = ps[0:2], psb[0:2]
        psOc, psOcb = ps[2:4], psb[2:4]
        psOs, psOsb = ps[4], psb[4]
        psOw, psOwb = ps[5], psb[5]
        psM, psMb = ps[6:8], psb[6:8]
        Wc = [k.sb("nWc%d" % i, [128, 512], F32, st) for i in range(2)]
        Wcb = k.bufs(2, "nWc")
        ee = [k.sb("nee%d" % i, [128, 512], BF16, st) for i in range(3)]
        eeb = k.bufs(3, "nee")
        sf = [k.sb("nsf%d" % i, [128, 512], F32, st) for i in range(2)]
        sfb = k.bufs(2, "nsf")
        nmT = k.sb("nnmT", [128, 512], BF16, st)
        nmTb = k.buf("nnmT")
        imp = k.sb("nimp", [128, 128], F32, st)
        imp3 = k.sb("nimp3", [128, 128], F32, st)
        negm = k.sb("nnegm", [128, 128], F32, st)
        impb, imp3b, negmb = k.buf("imp"), k.buf("imp3"), k.buf("negm")
        m8 = k.sb("nm8", [128, 16], F32, st)
        m8b = k.buf("m8")
        dn = k.sb("ndn", [128, 12], F32, st)
        dnb = k.buf("dn")
        ot = k.sb("not", [128, 256], F32, st)
        otb = k.buf("ot")
        oT = [k.sb("noT%d" % i, [128, 2, 512], BF16, st) for i in range(2)]
        oTb = k.bufs(2, "noT")
        zc = 0
        ec = 0
        wcn = 0

        bd = [k.sb("nbd%d" % i, [128, 512], BF16, st) for i in range(2)]
        bdb = k.bufs(2, "nbd")
        for i in range(2):
            k.op("pool", lambda e, i=i: e.memset(bd[i][:], 0.0), w=[bdb[i]])
        cur = {}

        def qk(Z, Zb, KT_, kcols, tcols, first_start, extra_r=()):
            k.op("pe", lambda e: e.matmul(Z[:, :], lhsT=KT_[:, kcols], rhs=cur["bd"][:, :], start=first_start, stop=True, skip_group_check=True),
                 r=[pb, cur["bdb"]] + list(extra_r), w=[Zb])

        bdS = [k.sb("nbdS%d" % i, [128, 1024], BF16, st) for i in range(2)]
        bdSb = k.bufs(2, "nbdS")
        negmb16 = k.sb("nnegm16", [128, 128], BF16, st)
        negm16b = k.buf("negm16")
        pend = []

        def flush():
            while pend:
                pend.pop(0)()

        def defer(fn):
            pend.append(fn)
            while len(pend) > 1:
                pend.pop(0)()

        for qt in range(nqt):
            tcols = slice(128 * qt, 128 * (qt + 1))
            cur["bd"], cur["bdb"] = bd[qt % 2], bdb[qt % 2]
            for g in range(4):
                b0 = 64 * (g % 2)
                k.op("pool", lambda e, g=g, b0=b0: e.tensor_copy(out=cur["bd"][b0:b0 + 64, g * 128:(g + 1) * 128], in_=QT[g // 2][b0:b0 + 64, tcols]),
                     r=[pb], w=[cur["bdb"]])
            S_, Sb_ = bdS[qt % 2], bdSb[qt % 2]
            nvar = 2 if qt >= 32 else 1
            for v_ in range(nvar):
                for g in range(4):
                    dstc = slice(v_ * 512 + g * 128, v_ * 512 + (g + 1) * 128)
                    if g % 2 == 0:
                        k.op("pool", lambda e, g=g, dstc=dstc: e.tensor_copy(out=S_[0:64, dstc], in_=QT[g // 2][0:64, tcols]), r=[pb], w=[Sb_])
                    else:
                        k.dma("sp", S_[0:64, dstc], QT[g // 2][64:128, tcols], r=[pb], w=[Sb_])
            NB = (8 * qt + 6) // 128 + 1
            for nb in range(NB):
                Z, Zb = psZ[zc % 2], psZb[zc % 2]
                zc += 1
                o_idx = qt - 16 * nb
                if o_idx <= 16:
                    W_, Wb_ = Wc[wcn % 2], Wcb[wcn % 2]
                    wcn += 1
                    src = bass.AP(bdz.tensor, 128 * o_idx, [[16, 128], [NDC, 4], [1, 128]])
                    import os
                    if os.environ.get("NSA_DBG") == "1":
                        k.op("pool", lambda e, W_=W_: e.memset(W_[:], 0.0), w=[Wb_])
                    else:
                        k.dma("sp", W_[:].rearrange("p (g q) -> p g q", g=4), src, r=[cx.bdz_b], w=[Wb_])
                    k.op("pe", lambda e, W_=W_: e.matmul(psM[0][:, :], lhsT=cx.jflip[:], rhs=W_[:], start=True, stop=True),
                         r=[Wb_, cx.jflip_b], w=[psMb[0]])
                    k.op("act", lambda e, W_=W_: e.copy(out=W_[:], in_=psM[0][:, :]), r=[psMb[0]], w=[Wb_])
                    qk(Z, Zb, kcT, slice(nb * 128, (nb + 1) * 128), tcols, True)
                    E_, Eb_ = ee[ec % 3], eeb[ec % 3]
                    ec += 1
                    s_, sb_ = sf[ec % 2], sfb[ec % 2]
                    k.op("dve", lambda e, s_=s_, Z=Z, W_=W_: e.tensor_tensor(out=s_[:], in0=Z[:, :], in1=W_[:], op=ALU.add), r=[Zb, Wb_], w=[sb_])
                    k.op("act", lambda e, E_=E_, s_=s_: e.activation(out=E_[:, :], in_=s_[:], func=AF.Exp), r=[sb_], w=[Eb_])
                else:
                    qk(Z, Zb, kcT, slice(nb * 128, (nb + 1) * 128), tcols, True)
                    E_, Eb_ = ee[ec % 3], eeb[ec % 3]
                    ec += 1
                    k.op("act", lambda e, E_=E_, Z=Z: e.activation(out=E_[:, :], in_=Z[:, :], func=AF.Exp), r=[Zb], w=[Eb_])
                def pv_c(E_=E_, Eb_=Eb_, nb=nb, NB=NB):
                    for g in range(4):
                        bank, bb = psOc[g // 2], psOcb[g // 2]
                        c0 = (g % 2) * 193
                        k.op("pe", lambda e, g=g, bank=bank, c0=c0: e.matmul(
                            bank[:, c0:c0 + 193], lhsT=E_[:, g * 128:(g + 1) * 128], rhs=VCX[:, nb, :],
                            start=(nb == 0 and g % 2 == 0), stop=(nb == NB - 1), skip_group_check=True), r=[Eb_, pb], w=[bb])
                defer(pv_c)
            flush()
            if stop <= 3:
                continue
            for h2 in range(2):
                bank, bb = psOc[h2], psOcb[h2]
                bv = bank[:, 0:386].rearrange("p (g c) -> p g c", c=193)
                k.op("dve", lambda e, h2=h2, bv=bv: e.tensor_scalar(out=dn[:, 2 * h2:2 * h2 + 2], in0=bv[:, :, 64], scalar1=1e-30, scalar2=None, op0=ALU.max),
                     r=[bb], w=[dnb])
            k.op("dve", lambda e: e.reciprocal(out=dn[:, 0:4], in_=dn[:, 0:4]), r=[dnb], w=[dnb])
            for g in range(4):
                bank, bb = psOc[g // 2], psOcb[g // 2]
                c0 = (g % 2) * 193 + 65
                if g == 0:
                    k.op("dve", lambda e, bank=bank, c0=c0: e.tensor_scalar(out=imp[:], in0=bank[:, c0:c0 + 128], scalar1=dn[:, 0:1], scalar2=None, op0=ALU.mult),
                         r=[bb, dnb], w=[impb])
                else:
                    k.op("dve", lambda e, g=g, bank=bank, c0=c0: e.scalar_tensor_tensor(out=imp[:], in0=bank[:, c0:c0 + 128], scalar=dn[:, g:g + 1], in1=imp[:],
                                                                                        op0=ALU.mult, op1=ALU.add), r=[bb, dnb, impb], w=[impb])
            sl = slice(128 - 2 * qt, 256 - 2 * qt)
            k.op("dve", lambda e: e.tensor_tensor(out=imp[:], in0=imp[:], in1=cx.mc_rel[:, sl], op=ALU.mult), r=[impb, cx.mc_rel_b], w=[impb])
            k.op("dve", lambda e: e.tensor_tensor(out=imp[:], in0=imp[:], in1=cx.ma_rel[:, sl], op=ALU.add), r=[impb, cx.ma_rel_b], w=[impb])
            k.op("dve", lambda e: e.memset(imp[:, 0:1], 100.0), r=[impb], w=[impb])
            k.op("dve", lambda e: e.max(out=m8[:, 0:8], in_=imp[:]), r=[impb], w=[m8b])
            k.op("dve", lambda e: e.match_replace(out=imp3[:], in_to_replace=m8[:, 0:8], in_values=imp[:], imm_value=-1e30), r=[impb, m8b], w=[imp3b])
            k.op("dve", lambda e: e.max(out=m8[:, 8:16], in_=imp3[:]), r=[imp3b], w=[m8b])
            k.op("dve", lambda e: e.tensor_scalar(out=negm[:], in0=imp[:], scalar1=m8[:, 15:16], scalar2=NEG, op0=ALU.is_lt, op1=ALU.mult),
                 r=[impb, m8b], w=[negmb])
            k.op("dve", lambda e: e.tensor_copy(out=negmb16[:], in_=negm[:]), r=[negmb], w=[negm16b])
            for v_ in range(nvar):
                k.op("pe", lambda e, v_=v_: e.matmul(psM[0][64:128, v_ * 128:(v_ + 1) * 128], lhsT=negmb16[:, v_ * 64:(v_ + 1) * 64], rhs=cx.ident_b[:],
                                                     start=True, stop=True, skip_group_check=True), r=[negm16b, cx.ident_b_b], w=[psMb[0]])
            for v_ in range(nvar):
                for g in range(4):
                    dstc = slice(v_ * 512 + g * 128, v_ * 512 + (g + 1) * 128)
                    srcc = slice(v_ * 128, (v_ + 1) * 128)
                    if g % 2 == 0:
                        k.op("dve", lambda e, dstc=dstc, srcc=srcc: e.tensor_copy(out=S_[64:128, dstc], in_=psM[0][64:128, srcc]), r=[psMb[0]], w=[Sb_])
                    else:
                        k.op("act", lambda e, dstc=dstc, srcc=srcc: e.copy(out=S_[64:128, dstc], in_=psM[0][64:128, srcc]), r=[psMb[0]], w=[Sb_])
            if stop <= 4:
                continue
            for kb in range(qt + 1):
                Z, Zb = psZ[zc % 2], psZb[zc % 2]
                zc += 1
                m = qt - kb
                vv = 0 if kb < 32 else 1
                k.op("pe", lambda e, Z=Z, kb=kb, vv=vv: e.matmul(Z[:, :], lhsT=KST[:, 128 * kb:128 * (kb + 1)], rhs=S_[:, vv * 512:(vv + 1) * 512],
                                                                 start=True, stop=True), r=[pb, Sb_], w=[Zb])
                E_, Eb_ = ee[ec % 3], eeb[ec % 3]
                ec += 1
                if m <= 1:
                    Tm = T0 if m == 0 else T1
                    s_, sb_ = sf[ec % 2], sfb[ec % 2]
                    k.op("dve", lambda e, s_=s_, Z=Z, Tm=Tm: e.tensor_tensor(out=s_[:], in0=Z[:, :], in1=Tm[:], op=ALU.add), r=[Zb, pb], w=[sb_])
                    k.op("act", lambda e, E_=E_, s_=s_: e.activation(out=E_[:, :], in_=s_[:], func=AF.Exp), r=[sb_], w=[Eb_])
                else:
                    k.op("act", lambda e, E_=E_, Z=Z: e.activation(out=E_[:, :], in_=Z[:, :], func=AF.Exp), r=[Zb], w=[Eb_])
                def pv_s(E_=E_, Eb_=Eb_, kb=kb, qt=qt):
                    for g in range(4):
                        k.op("pe", lambda e, g=g: e.matmul(psOs[:, g * 65:(g + 1) * 65], lhsT=E_[:, g * 128:(g + 1) * 128], rhs=VS[:, kb, :],
                                                            start=(kb == 0 and g == 0), stop=(kb == qt), skip_group_check=True), r=[Eb_, pb], w=[psOsb])
                defer(pv_s)
            flush()
            if stop <= 5:
                continue
            kb0 = max(0, qt - 4)
            for kb in range(kb0, qt + 1):
                Z, Zb = psZ[zc % 2], psZb[zc % 2]
                zc += 1
                m = qt - kb
                qk(Z, Zb, KWT, slice(128 * kb, 128 * (kb + 1)), tcols, True)
                E_, Eb_ = ee[ec % 3], eeb[ec % 3]
                ec += 1
                if m in (0, 1, 4):
                    Tm, Tmb = {0: (T0, pb), 1: (T1, pb), 4: (cx.t4, cx.t4_b)}[m]
                    s_, sb_ = sf[ec % 2], sfb[ec % 2]
                    k.op("dve", lambda e, s_=s_, Z=Z, Tm=Tm: e.tensor_tensor(out=s_[:], in0=Z[:, :], in1=Tm[:], op=ALU.add), r=[Zb, Tmb], w=[sb_])
                    k.op("act", lambda e, E_=E_, s_=s_: e.activation(out=E_[:, :], in_=s_[:], func=AF.Exp), r=[sb_], w=[Eb_])
                else:
                    k.op("act", lambda e, E_=E_, Z=Z: e.activation(out=E_[:, :], in_=Z[:, :], func=AF.Exp), r=[Zb], w=[Eb_])
                def pv_w(E_=E_, Eb_=Eb_, kb=kb, qt=qt, kb0=kb0):
                    for g in range(4):
                        k.op("pe", lambda e, g=g: e.matmul(psOw[:, g * 65:(g + 1) * 65], lhsT=E_[:, g * 128:(g + 1) * 128], rhs=VW[:, kb, :],
                                                            start=(kb == kb0 and g == 0), stop=(kb == qt), skip_group_check=True), r=[Eb_, pb], w=[psOwb])
                defer(pv_w)
            flush()
            if stop <= 6:
                continue
            osv = psOs[:, 0:260].rearrange("p (g c) -> p g c", c=65)
            owv = psOw[:, 0:260].rearrange("p (g c) -> p g c", c=65)
            k.op("dve", lambda e: e.tensor_scalar(out=dn[:, 4:8], in0=osv[:, :, 64], scalar1=1e-30, scalar2=None, op0=ALU.max), r=[psOsb], w=[dnb])
            k.op("dve", lambda e: e.tensor_scalar(out=dn[:, 8:12], in0=owv[:, :, 64], scalar1=1e-30, scalar2=None, op0=ALU.max), r=[psOwb], w=[dnb])
            k.op("dve", lambda e: e.reciprocal(out=dn[:, 4:12], in_=dn[:, 4:12]), r=[dnb], w=[dnb])
            k.op("dve", lambda e: e.tensor_tensor(out=dn[:, :], in0=dn[:, :], in1=G[:, qt, :], op=ALU.mult), r=[dnb, pb], w=[dnb])
            for g in range(4):
                bank, bb = psOc[g // 2], psOcb[g // 2]
                c0 = (g % 2) * 193
                og = ot[:, g * 64:(g + 1) * 64]
                k.op("dve", lambda e, g=g, bank=bank, c0=c0, og=og: e.tensor_scalar(out=og, in0=bank[:, c0:c0 + 64], scalar1=dn[:, g:g + 1], scalar2=None, op0=ALU.mult),
                     r=[bb, dnb], w=[otb])
                k.op("dve", lambda e, g=g, og=og: e.scalar_tensor_tensor(out=og, in0=psOs[:, g * 65:g * 65 + 64], scalar=dn[:, 4 + g:5 + g], in1=og, op0=ALU.mult, op1=ALU.add),
                     r=[psOsb, dnb, otb], w=[otb])
                k.op("dve", lambda e, g=g, og=og: e.scalar_tensor_tensor(out=og, in0=psOw[:, g * 65:g * 65 + 64], scalar=dn[:, 8 + g:9 + g], in1=og, op0=ALU.mult, op1=ALU.add),
                     r=[psOwb, dnb, otb], w=[otb])
            y2 = (qt // 4) % 2
            for fh in range(2):
                k.op("pe", lambda e, fh=fh: e.transpose(out=psM[1][:, fh * 128:(fh + 1) * 128], in_=ot[:, fh * 128:(fh + 1) * 128], identity=cx.ident_f[:]),
                     r=[otb, cx.ident_f_b], w=[psMb[1]])
            k.op("act", lambda e, y2=y2: e.copy(out=oT[y2][:, :, (qt % 4) * 128:(qt % 4 + 1) * 128],
                                                in_=psM[1][:, 0:256].rearrange("p (fh t) -> p fh t", fh=2)), r=[psMb[1]], w=[oTb[y2]])
            if qt % 4 == 3:
                sbk = qt // 4
                dest, off = sbk // 4, (sbk % 4) * 512
                k.dma("sp", a2a[dest].rearrange("(fh p) t -> p fh t", p=128)[:, :, off:off + 512], oT[y2][:], r=[oTb[y2]], w=[a2a_b])
                if k.engs["pe"].count > SEM_ROTATE:
                    k.barrier()
        k.barrier()


def build_prog_nsa(nqt=64, stop=99):
    nc = bass.Bass("TRN2", target_bir_lowering=False)
    consts = nsa_host_consts()
    names = ["ident_f", "ident_b", "jflip", "t4", "mc_rel", "ma_rel"]
    dnames = ["ohc", "dm", "ov", "expat"]
    cd = dram_consts(nc, consts, names + dnames)

    def din(nm, shape, dt_=F32):
        return nc.dram_tensor(nm, shape, dt_, kind="ExternalInput").ap()
    hn_all = din("hn_all", [4 * D, TOK], BF16)
    w = din("nsa_w", [D, 780])
    rb = din("rb", [32, 4])
    peT = din("peT", [128, 32])
    w1 = din("cw1", [128, 32, 256])
    w2k = din("cw2k", [256, 64])
    w2v = din("cw2v", [256, 64])
    a2a = nc.dram_tensor("a2a", [4, 256, TOK], BF16, kind="ExternalOutput").ap()
    bdz = nc.dram_tensor("bdz", [4, NDC], F32, kind="Internal").ap()
    k = KB(nc)
    cx = load_consts(k, cd, names)
    for nm in dnames:
        setattr(cx, nm + "_dram", cd[nm])
    phase_nsa(k, cx, hn_all, k.buf("hn_all"), w, rb, peT, w1, w2k, w2v, bdz, a2a, k.buf("a2a"), k.stack, nqt=nqt, stop=stop)
    k.barrier()
    k.close()
    return nc, {nm: consts[nm] for nm in names + dnames}


def nsa_core_inputs(inp, r):
    w_in = inp["nsa_w_in"][0]
    kv0 = 1024

    def kvc(i):
        return w_in[:, kv0 + i * 256 + r * 64: kv0 + i * 256 + (r + 1) * 64]
    gcols = [2560 + j * 16 + r * 4 + g for j in range(3) for g in range(4)]
    w = np.concatenate([w_in[:, 256 * r:256 * (r + 1)], kvc(0), kvc(1), kvc(2), kvc(2), kvc(4), kvc(4), kvc(3), kvc(5), w_in[:, gcols]], axis=1)
    peT = np.concatenate([inp["nsa_pe_k"][0].T, inp["nsa_pe_v"][0].T], axis=0)
    w1k = inp["nsa_ck_w1"][0].reshape(32, 64, 256).transpose(1, 0, 2)
    w1v = inp["nsa_cv_w1"][0].reshape(32, 64, 256).transpose(1, 0, 2)
    return {"nsa_w": np.ascontiguousarray(w), "rb": np.ascontiguousarray(inp["rel_bias"][:, 4 * r:4 * r + 4]),
            "peT": np.ascontiguousarray(peT), "cw1": np.ascontiguousarray(np.concatenate([w1k, w1v], axis=0)),
            "cw2k": inp["nsa_ck_w2"][0], "cw2v": inp["nsa_cv_w2"][0]}


_PROGS = {}


def _prog(name, fn):
    if name not in _PROGS:
        _PROGS[name] = fn()
    return _PROGS[name]


def _run(name, fn, maps):
    nc, cst = _prog(name, fn)
    full = []
    for m in maps:
        mm = dict(m)
        mm.update({"c_" + kk: v for kk, v in cst.items()})
        full.append(mm)
    res = run_bass_kernel_spmd(nc, full, core_ids=list(range(NCORES)))
    return res.results


def _c(a):
    return np.ascontiguousarray(a)


def _allgather(parts):
    out = []
    for b in range(2):
        cat = np.concatenate([np.asarray(parts[4 * b + r])[256 * q:256 * (q + 1)] for q in range(4) for r in range(4)], axis=0)
        out += [cat] * 4
    return out


def _alltoall(parts):
    out = []
    for b in range(2):
        for j in range(4):
            out.append(_c(np.concatenate([np.asarray(parts[4 * b + i])[j] for i in range(4)], axis=0)))
    return out


def kernel_unfused(**inp):
    inp = {kk: np.asarray(v) for kk, v in inp.items()}
    x, p = inp["x"], inp["p"]
    cores = [(c // 4, c % 4) for c in range(NCORES)]
    tsl = [slice(TOK * r, TOK * (r + 1)) for (_, r) in cores]
    xT = [_c(x[b, tsl[c]].T) for c, (b, r) in enumerate(cores)]
    res = _run("norm0", build_prog_norm0, [{"xT": xT[c], "gain": _c(inp["norm_mix"][0])} for c in range(NCORES)])
    hn_all = _allgather([r_["hnT"] for r_ in res])
    w_in = inp["sb_w_in"][0]
    maps = []
    for c, (b, r) in enumerate(cores):
        wq = np.concatenate([w_in[:, 256 * r:256 * (r + 1)], w_in[:, 1024 + 256 * r:1024 + 256 * (r + 1)],
                             w_in[:, 2048 + 256 * r:2048 + 256 * (r + 1)]], axis=1)
        maps.append({"hn_all": hn_all[c], "sb_w": _c(wq)})
    res = _run("sb", build_prog_sb, maps)
    oT = _alltoall([r_["a2a"] for r_ in res])

    def tail_maps(i, hT_in, oT_, g_next):
        out = []
        for c, (b, r) in enumerate(cores):
            out.append({"hT_in": hT_in[c], "oT": oT_[c],
                        "w_out": _c((inp["sb_w_out"] if i == 0 else inp["nsa_w_out"])[0]),
                        "g_ffn": _c(inp["norm_ffn"][i]), "w1": _c(inp["ffn_w_in"][i]), "w2": _c(inp["ffn_w_out"][i]),
                        "g_ple": _c(inp["norm_ple"][i]), "wg": _c(inp["ple_w_gate"][i]), "wp": _c(inp["ple_w_proj"][i]),
                        "pT": _c(p[i, b, tsl[c]].T), "g_next": _c(g_next)})
        return out
    res = _run("tail0", lambda: build_prog_tail(False), tail_maps(0, xT, oT, inp["norm_mix"][1]))
    h1T = [np.asarray(r_["hT_out"]) for r_ in res]
    hn_all = _allgather([r_["hnT"] for r_ in res])
    maps = []
    for c, (b, r) in enumerate(cores):
        m = {"hn_all": hn_all[c]}
        m.update(nsa_core_inputs(inp, r))
        maps.append(m)
    res = _run("nsa", build_prog_nsa, maps)
    oT = _alltoall([r_["a2a"] for r_ in res])
    res = _run("tail1", lambda: build_prog_tail(True), tail_maps(1, h1T, oT, inp["final_norm"]))
    out = np.empty((2, S, D), np.float32)
    for c, (b, r) in enumerate(cores):
        out[b, tsl[c], :] = np.asarray(res[c]["outT"]).T
    return out


I32 = mybir.dt.int32
TAIL_KEYS = (("w_out", [D, D]), ("g_ffn", [D]), ("w1", [D, 2 * DFF]), ("w2", [DFF, D]), ("g_ple", [D]), ("wg", [D, D]), ("wp", [256, D]))


def build_prog_fused():
    import os
    CUT = int(os.environ.get("FUSED_CUT", "99"))
    nc = bass.Bass("TRN2", target_bir_lowering=False)
    consts = dict(host_consts())
    consts.update(nsa_host_consts())
    small = ["ident_f", "ones_b", "negu_b", "negones_b", "mask_sb", "jflip"]
    nsa_sb = ["ident_b", "t4", "mc_rel", "ma_rel"]
    nsa_dr = ["ohc", "dm", "ov", "expat"]
    cd = dram_consts(nc, consts, small + nsa_sb + nsa_dr)

    def din(nm, shape, dt_=F32):
        return nc.dram_tensor(nm, shape, dt_, kind="ExternalInput").ap()

    def dint(nm, shape, dt_):
        return nc.dram_tensor(nm, shape, dt_, kind="Internal").ap()
    xT = din("xT", [D, TOK])
    pT = [din("pT0", [256, TOK]), din("pT1", [256, TOK])]
    rk = din("rk", [1, 2], I32)
    g_mix = [din("g_mix0", [D]), din("g_mix1", [D])]
    g_fin = din("g_fin", [D])
    sb_w = din("sb_w", [D, 768])
    tails = [{nm: din("%s_%d" % (nm, i), shp) for nm, shp in TAIL_KEYS} for i in range(2)]
    nsa_w = din("nsa_w", [D, 780])
    rb = din("rb", [32, 4])
    peT = din("peT", [128, 32])
    cw1 = din("cw1", [128, 32, 256])
    cw2k = din("cw2k", [256, 64])
    cw2v = din("cw2v", [256, 64])
    outT = nc.dram_tensor("outT", [D, TOK], F32, kind="ExternalOutput").ap()
    hn_loc = dint("hn_loc", [D, TOK], BF16)
    hn_all = dint("hn_all", [4 * D, TOK], BF16)
    a2a_loc = dint("a2a_loc", [4, 256, TOK], BF16)
    a2a_all = dint("a2a_all", [4 * D, TOK], BF16)
    bdz = dint("bdz", [4, NDC], F32)
    hsp = dint("hsp", [D, TOK], F32)
    k = KB(nc)
    cx = load_consts(k, cd, small)
    for nm in nsa_dr:
        setattr(cx, nm + "_dram", cd[nm])
    make_eps(k, cx)
    make_one(k, cx)
    groups = [[0, 1, 2, 3], [4, 5, 6, 7]]
    spq = k.engs["sp"].h
    reg = spq.alloc_register("rk")
    spq.reg_load(reg, rk[0:1, 0:1])
    crk = spq.snap(reg, min_val=0, max_val=3)
    hn_loc_b, hn_all_b, a2a_loc_b, a2a_all_b, hsp_b, out_b = [k.buf(n_) for n_ in ("hn_loc", "hn_all", "a2a_loc", "a2a_all", "hsp", "outT")]
    g4 = a2a_all.rearrange("(j f) t -> j f t", j=4)

    def oT_loader(tile, tb, cs):
        src = g4[crk].rearrange("(fc p) t -> p fc t", p=128)
        k.dma("sp", tile[:], src[:, :, cs], r=[a2a_all_b], w=[tb])

    def gather_hn():
        for q in range(4):
            k.coll("AllGather", hn_loc[256 * q:256 * (q + 1), :], hn_all[1024 * q:1024 * (q + 1), :], groups, r=[hn_loc_b], w=[hn_all_b])

    def gather_o():
        for j in range(4):
            k.coll("AllGather", a2a_loc[j], a2a_all[1024 * j:1024 * (j + 1), :], groups, r=[a2a_loc_b], w=[a2a_all_b])

    wbf = [tail_weights_to_bf16(k, nc, str(i), tails[i]["w_out"], tails[i]["w1"], tails[i]["w2"], tails[i]["wg"], tails[i]["wp"])
           for i in range(2)]

    def tail(i, hT, hb):
        t = tails[i]
        phase_tail(k, cx, hT, hb, None, a2a_all_b, wbf[i], t["g_ffn"], t["g_ple"], pT[i], oT_loader=oT_loader)

    hv = hsp.rearrange("(fc p) t -> p fc t", p=128)
    with contextlib.ExitStack() as stA:
        hT = k.sb("hT", [128, FC, TOK], F32, stA)
        hb = k.bufs(4, "hT")
        xv = xT.rearrange("(fc p) t -> p fc t", p=128)
        for c in range(4):
            k.dma("sp", hT[:, :, c * 512:(c + 1) * 512], xv[:, :, c * 512:(c + 1) * 512], w=[hb[c]])
        with contextlib.ExitStack() as st:
            phase_norm_out(k, cx, hT, hb, g_mix[0], hn_loc, hn_loc_b, st)
            k.barrier()
        gather_hn()
        if CUT >= 2:
            with contextlib.ExitStack() as st:
                phase_sb(k, cx, hn_all, hn_all_b, sb_w, a2a_loc, a2a_loc_b, st)
                k.barrier()
            gather_o()
        if CUT >= 3:
            tail(0, hT, hb)
        with contextlib.ExitStack() as st:
            phase_norm_out(k, cx, hT, hb, g_mix[1], hn_loc, hn_loc_b, st)
            for c in range(4):
                k.dma("sp", hv[:, :, c * 512:(c + 1) * 512], hT[:, :, c * 512:(c + 1) * 512], r=[hb[c]], w=[hsp_b])
            k.barrier()
    if CUT >= 4:
        gather_hn()
    with contextlib.ExitStack() as stB:
      if CUT >= 5:
        cxb = load_consts(k, cd, nsa_sb, stB)
        for nm in nsa_sb:
            setattr(cx, nm, getattr(cxb, nm))
            setattr(cx, nm + "_b", getattr(cxb, nm + "_b"))
        phase_nsa(k, cx, hn_all, hn_all_b, nsa_w, rb, peT, cw1, cw2k, cw2v, bdz, a2a_loc, a2a_loc_b, stB)
        k.barrier()
    if CUT >= 5:
        gather_o()
    with contextlib.ExitStack() as stC:
        hT = k.sb("hT2", [128, FC, TOK], F32, stC)
        hb = k.bufs(4, "hT2")
        for c in range(4):
            k.dma("sp", hT[:, :, c * 512:(c + 1) * 512], hv[:, :, c * 512:(c + 1) * 512], r=[hsp_b], w=[hb[c]])
        if CUT >= 6:
            tail(1, hT, hb)
        with contextlib.ExitStack() as st:
            phase_final_norm(k, cx, hT, hb, g_fin, outT, out_b, st)
            k.barrier()
    k.barrier()
    k.close()
    return nc, {nm: consts[nm] for nm in small + nsa_sb + nsa_dr}


def fused_maps(inp):
    x, p = inp["x"], inp["p"]
    maps = []
    w_in = inp["sb_w_in"][0]
    for c in range(NCORES):
        b, r = c // 4, c % 4
        ts = slice(TOK * r, TOK * (r + 1))
        m = {"xT": _c(x[b, ts].T), "pT0": _c(p[0, b, ts].T), "pT1": _c(p[1, b, ts].T),
             "rk": np.array([[r, 0]], np.int32),
             "g_mix0": _c(inp["norm_mix"][0]), "g_mix1": _c(inp["norm_mix"][1]), "g_fin": _c(inp["final_norm"]),
             "sb_w": _c(np.concatenate([w_in[:, 256 * r:256 * (r + 1)], w_in[:, 1024 + 256 * r:1024 + 256 * (r + 1)],
                                        w_in[:, 2048 + 256 * r:2048 + 256 * (r + 1)]], axis=1))}
        for i in range(2):
            m["w_out_%d" % i] = _c((inp["sb_w_out"] if i == 0 else inp["nsa_w_out"])[0])
            m["g_ffn_%d" % i] = _c(inp["norm_ffn"][i])
            m["w1_%d" % i] = _c(inp["ffn_w_in"][i])
            m["w2_%d" % i] = _c(inp["ffn_w_out"][i])
            m["g_ple_%d" % i] = _c(inp["norm_ple"][i])
            m["wg_%d" % i] = _c(inp["ple_w_gate"][i])
            m["wp_%d" % i] = _c(inp["ple_w_proj"][i])
        m.update(nsa_core_inputs(inp, r))
        maps.append(m)
    return maps


def kernel(**inp):
    inp = {kk: np.asarray(v) for kk, v in inp.items()}
    res = _run("fused", build_prog_fused, fused_maps(inp))
    out = np.empty((2, S, D), np.float32)
    for c in range(NCORES):
        b, r = c // 4, c % 4
        out[b, TOK * r:TOK * (r + 1), :] = np.asarray(res[c]["outT"]).T
    return out
```
